# Optimizing a Trainium2 kernel written in Bass

```python
import math
import jax, jax.numpy as jnp
from jax import lax
import numpy as np

D_MODEL = 1024
BATCH = 4
SEQ = 4096
DEPTH = 4

N_MIXERS = 3
D_INNER = D_MODEL
PLE_DIM = 256
CONV_WIDTH = 3
SSM_GROUP = 16
SSM_STATE = 64
SSM_GROUPS = D_INNER // SSM_GROUP
DT_MIN = 1e-3
DT_MAX = 1e-1
RWKV_HEAD = 64
RWKV_HEADS = D_INNER // RWKV_HEAD
DECAY_LORA = 64
ICLR_LORA = 64
RWKV_GN_EPS = 64e-5
LN_EPS = 1e-5
DN_ALPHA = (2 * DEPTH) ** 0.25
DN_BETA = (8 * DEPTH) ** -0.25
N_CONV = (DEPTH + 2) // 3
N_SSM = (DEPTH + 1) // 3
N_RWKV = DEPTH // 3

kernel_name = "hybrid_conv_s5_rwkv7_deepnorm_trunk"


def layer_norm(h, g, b, eps=LN_EPS):
    h32 = h.astype(jnp.float32)
    mu = h32.mean(-1, keepdims=True)
    var = jnp.square(h32 - mu).mean(-1, keepdims=True)
    return ((h32 - mu) * lax.rsqrt(var + eps) * g.astype(jnp.float32) + b.astype(jnp.float32)).astype(h.dtype)


def token_shift(x):
    return jnp.pad(x, ((0, 0), (1, 0), (0, 0)))[:, :-1]


def short_conv_mixer(x, w_in, conv_k, w_out):
    bg, cg, h, z = jnp.split(x @ w_in, 4, axis=-1)
    u = cg * h
    conv = lax.conv_general_dilated(
        u, conv_k[:, None, :].astype(u.dtype), window_strides=(1,),
        padding=[(CONV_WIDTH - 1, 0)], dimension_numbers=("NWC", "WIO", "NWC"),
        feature_group_count=D_INNER)
    return (bg * conv * jax.nn.silu(z)) @ w_out


def _ssm_combine(left, right):
    a_l, b_l = left
    a_r, b_r = right
    return a_r * a_l, a_r * b_l + b_r


def s5_mixer(x, w_in, lam_re, lam_im, log_dt, b_re, b_im, c_re, c_im, d_skip, w_glu, b_glu, w_out):
    f32 = jnp.float32
    u, z = jnp.split(x @ w_in, 2, axis=-1)
    bsz, seqlen, _ = u.shape
    ug = u.astype(f32).reshape(bsz, seqlen, SSM_GROUPS, SSM_GROUP)
    lam = lax.complex(lam_re.astype(f32), lam_im.astype(f32))
    dt = jnp.exp(log_dt.astype(f32))[:, None]
    a_bar = jnp.exp(lam * dt)
    b_mat = lax.complex(b_re.astype(f32), b_im.astype(f32))
    b_bar = ((a_bar - 1.0) / lam)[..., None] * b_mat
    c_mat = lax.complex(c_re.astype(f32), c_im.astype(f32))
    bu = jnp.einsum("blgh,gph->blgp", ug, b_bar)
    a_seq = jnp.broadcast_to(a_bar, bu.shape)
    states = lax.associative_scan(_ssm_combine, (a_seq, bu), axis=1)[1]
    y = jnp.einsum("blgp,ghp->blgh", states, c_mat).real + d_skip.astype(f32).reshape(SSM_GROUPS, SSM_GROUP) * ug
    y = jax.nn.gelu(y.reshape(bsz, seqlen, D_INNER).astype(x.dtype))
    y = y * jax.nn.sigmoid(y @ w_glu + b_glu)
    return (y * jax.nn.silu(z)) @ w_out


def _rwkv_step(state, inp):
    r_t, w_t, k_t, v_t, kk_t, a_t = inp
    sa = jnp.einsum("bhvk,bhk->bhv", state, -kk_t)
    state = (state * w_t[:, :, None, :] + sa[..., None] * (kk_t * a_t)[:, :, None, :]
             + v_t[..., None] * k_t[:, :, None, :])
    out = jnp.einsum("bhvk,bhk->bhv", state, r_t)
    return state, out


def rwkv7_mixer(x, mu, w_rkvz, w0, w1, w2, a0, a1, a2, k_k, k_a, r_k, lnx_g, lnx_b, w_out):
    f32 = jnp.float32
    bsz, seqlen, _ = x.shape
    dx = token_shift(x) - x
    xs = x[None] + dx[None] * mu[:, None, None, :]
    r, k, v, z = jnp.einsum("nbld,nde->nble", xs[:4], w_rkvz)
    xw, xa = xs[4], xs[5]
    w_log = -jax.nn.softplus(-(w0 + jnp.tanh(xw @ w1) @ w2)) - 0.5
    decay = jnp.exp(-jnp.exp(w_log.astype(f32)))
    a = jax.nn.sigmoid(a0 + (xa @ a1) @ a2).astype(f32)
    heads = lambda t: t.astype(f32).reshape(bsz, seqlen, RWKV_HEADS, RWKV_HEAD)
    kk = heads(k * k_k)
    kk = kk / jnp.maximum(jnp.linalg.norm(kk, axis=-1, keepdims=True), 1e-12)
    k = heads(k * (1.0 + (a.astype(k.dtype) - 1.0) * k_a))
    r, v, decay, a = heads(r), heads(v), heads(decay), heads(a)
    seq_first = lambda t: jnp.moveaxis(t, 1, 0)
    s0 = jnp.zeros((bsz, RWKV_HEADS, RWKV_HEAD, RWKV_HEAD), f32)
    _, out = lax.scan(_rwkv_step, s0, tuple(seq_first(t) for t in (r, decay, k, v, kk, a)))
    out = jnp.moveaxis(out, 0, 1)
    mu_o = out.mean(-1, keepdims=True)
    var_o = jnp.square(out - mu_o).mean(-1, keepdims=True)
    out = ((out - mu_o) * lax.rsqrt(var_o + RWKV_GN_EPS)).reshape(bsz, seqlen, D_INNER)
    out = out * lnx_g.astype(f32) + lnx_b.astype(f32)
    bonus = jnp.sum(r * k * r_k.astype(f32), axis=-1, keepdims=True) * v
    out = (out + bonus.reshape(bsz, seqlen, D_INNER)).astype(x.dtype)
    return (out * jax.nn.silu(z)) @ w_out


def setup_inputs(seed: int = 0) -> dict:
    key = jax.random.key(seed)
    ks = iter(jax.random.split(key, 48))
    nrm = lambda shape, s: jax.random.normal(next(ks), shape, jnp.float32) * s
    D, E, G, P, H, N = D_MODEL, D_INNER, SSM_GROUPS, SSM_STATE, RWKV_HEADS, RWKV_HEAD
    ramp = jnp.linspace(0.0, 1.0, E, dtype=jnp.float32) ** 0.85
    inp = {}
    inp["x"] = nrm((BATCH, SEQ, D), 1.0)
    inp["p"] = nrm((DEPTH, BATCH, SEQ, PLE_DIM), 1.0)
    inp["conv_w_in"] = nrm((N_CONV, D, 4 * E), D ** -0.5)
    inp["conv_k"] = nrm((N_CONV, CONV_WIDTH, E), CONV_WIDTH ** -0.5)
    inp["conv_w_out"] = nrm((N_CONV, E, D), E ** -0.5 * DN_BETA)
    inp["ssm_w_in"] = nrm((N_SSM, D, 2 * E), D ** -0.5)
    inp["ssm_lam_re"] = -0.5 + nrm((N_SSM, G, P), 0.01)
    inp["ssm_lam_im"] = math.pi * jnp.arange(P, dtype=jnp.float32) + nrm((N_SSM, G, P), 0.01)
    inp["ssm_log_dt"] = jax.random.uniform(next(ks), (N_SSM, G), jnp.float32, math.log(DT_MIN), math.log(DT_MAX))
    inp["ssm_b_re"] = nrm((N_SSM, G, P, SSM_GROUP), (2 * SSM_GROUP) ** -0.5)
    inp["ssm_b_im"] = nrm((N_SSM, G, P, SSM_GROUP), (2 * SSM_GROUP) ** -0.5)
    inp["ssm_c_re"] = nrm((N_SSM, G, SSM_GROUP, P), (2 * P) ** -0.5)
    inp["ssm_c_im"] = nrm((N_SSM, G, SSM_GROUP, P), (2 * P) ** -0.5)
    inp["ssm_d"] = nrm((N_SSM, E), 1.0)
    inp["ssm_w_glu"] = nrm((N_SSM, E, E), E ** -0.5)
    inp["ssm_b_glu"] = nrm((N_SSM, E), 0.01)
    inp["ssm_w_out"] = nrm((N_SSM, E, D), E ** -0.5 * DN_BETA)
    inp["rwkv_mu"] = jax.random.uniform(next(ks), (N_RWKV, 6, D), jnp.float32)
    inp["rwkv_w_rkvz"] = nrm((N_RWKV, 4, D, E), D ** -0.5)
    inp["rwkv_w0"] = -6.0 + 5.0 * ramp + nrm((N_RWKV, E), 0.1)
    inp["rwkv_w1"] = nrm((N_RWKV, D, DECAY_LORA), D ** -0.5)
    inp["rwkv_w2"] = nrm((N_RWKV, DECAY_LORA, E), 0.1 * DECAY_LORA ** -0.5)
    inp["rwkv_a0"] = nrm((N_RWKV, E), 0.1)
    inp["rwkv_a1"] = nrm((N_RWKV, D, ICLR_LORA), D ** -0.5)
    inp["rwkv_a2"] = nrm((N_RWKV, ICLR_LORA, E), 0.1 * ICLR_LORA ** -0.5)
    inp["rwkv_k_k"] = 0.85 + nrm((N_RWKV, E), 0.05)
    inp["rwkv_k_a"] = 1.0 + nrm((N_RWKV, E), 0.05)
    inp["rwkv_r_k"] = nrm((N_RWKV, H, N), 0.1)
    inp["rwkv_lnx_g"] = 1.0 + nrm((N_RWKV, E), 0.05)
    inp["rwkv_lnx_b"] = nrm((N_RWKV, E), 0.01)
    inp["rwkv_w_out"] = nrm((N_RWKV, E, D), E ** -0.5 * DN_BETA)
    inp["ple_proj"] = nrm((DEPTH, PLE_DIM, D), PLE_DIM ** -0.5)
    inp["ple_gate"] = nrm((DEPTH, D, D), D ** -0.5)
    inp["ln_g"] = 1.0 + nrm((DEPTH, D), 0.05)
    inp["ln_b"] = nrm((DEPTH, D), 0.01)
    return inp


def reference(x, p, conv_w_in, conv_k, conv_w_out,
              ssm_w_in, ssm_lam_re, ssm_lam_im, ssm_log_dt, ssm_b_re, ssm_b_im, ssm_c_re, ssm_c_im,
              ssm_d, ssm_w_glu, ssm_b_glu, ssm_w_out,
              rwkv_mu, rwkv_w_rkvz, rwkv_w0, rwkv_w1, rwkv_w2, rwkv_a0, rwkv_a1, rwkv_a2,
              rwkv_k_k, rwkv_k_a, rwkv_r_k, rwkv_lnx_g, rwkv_lnx_b, rwkv_w_out,
              ple_proj, ple_gate, ln_g, ln_b):
    for i in range(DEPTH):
        kind, j = i % N_MIXERS, i // N_MIXERS
        if kind == 0:
            y = short_conv_mixer(x, conv_w_in[j], conv_k[j], conv_w_out[j])
        elif kind == 1:
            y = s5_mixer(x, ssm_w_in[j], ssm_lam_re[j], ssm_lam_im[j], ssm_log_dt[j],
                         ssm_b_re[j], ssm_b_im[j], ssm_c_re[j], ssm_c_im[j], ssm_d[j],
                         ssm_w_glu[j], ssm_b_glu[j], ssm_w_out[j])
        else:
            y = rwkv7_mixer(x, rwkv_mu[j], rwkv_w_rkvz[j], rwkv_w0[j], rwkv_w1[j], rwkv_w2[j],
                            rwkv_a0[j], rwkv_a1[j], rwkv_a2[j], rwkv_k_k[j], rwkv_k_a[j],
                            rwkv_r_k[j], rwkv_lnx_g[j], rwkv_lnx_b[j], rwkv_w_out[j])
        r = DN_ALPHA * x + y
        r = r + (p[i] @ ple_proj[i]) * jax.nn.sigmoid(r @ ple_gate[i])
        x = layer_norm(r, ln_g[i], ln_b[i])
    return x
```

```python
from contextlib import ExitStack

import numpy as np
import concourse.bass as bass
import concourse.mybir as mybir
from concourse.bass_utils import run_bass_kernel_spmd

F32 = mybir.dt.float32
BF16 = mybir.dt.bfloat16
AF = mybir.ActivationFunctionType
ALU = mybir.AluOpType
AX = mybir.AxisListType

D = 1024
DEPTH = 4
DN_ALPHA = (2 * DEPTH) ** 0.25
LN_EPS = 1e-5
NCORES = 8


class Prog:
    COMPUTE = ("pe", "act", "dve", "pool")
    BLOCKNAME = {"pe": "tensor", "act": "scalar", "dve": "vector", "pool": "gpsimd", "sp": "sync"}
    NDMASEM = 12

    def __init__(self):
        self.nc = bass.Bass("TRN2", target_bir_lowering=False)
        self.ops = []
        self.stack = ExitStack()
        self._rr = {}
        self.prefix = ""
        self.bound = {}
        self.psum_keys = set()

    def dram(self, name, shape, dtype=F32, out=False):
        name = self.prefix + name
        if name in self.bound:
            ap = self.bound[name]
            assert list(ap.shape) == list(shape), (name, ap.shape, shape)
            return ap
        return self.nc.dram_tensor(name, list(shape), dtype,
                                   kind="ExternalOutput" if out else "ExternalInput").ap()

    def scratch(self, name, shape, dtype=F32):
        return self.nc.dram_tensor(name, list(shape), dtype).ap()

    def stage(self, prefix, bind=None):
        prog = self

        class _S:
            def __enter__(s_):
                s_.old = (prog.stack, prog.prefix)
                prog.stack = ExitStack()
                prog.prefix = prefix
                for k, v in (bind or {}).items():
                    prog.bound[prefix + k] = v

            def __exit__(s_, *a):
                prog.stack.close()
                prog.stack, prog.prefix = s_.old
                prog.ops.append(dict(barrier=True, eng=None, fn=None, r=[], w=[], dma=False))
                return False
        return _S()

    def sb(self, name, shape, dtype=F32):
        return self.stack.enter_context(self.nc.sbuf_tensor("s_" + self.prefix + name, list(shape), dtype))

    def ps(self, name, shape, dtype=F32):
        return self.stack.enter_context(self.nc.psum_tensor("p_" + self.prefix + name, list(shape), dtype))

    def pool(self, name, n, shape, dtype=F32, psum=False):
        mk = self.ps if psum else self.sb
        if psum:
            shape = [128, 512]
            self.psum_keys |= {f"{self.prefix}{name}{i}" for i in range(n)}
        tiles = [(mk(f"{name}{i}", shape, dtype), f"{self.prefix}{name}{i}") for i in range(n)]
        self._rr[name] = [tiles, 0]
        return tiles

    def nxt(self, name):
        ent = self._rr[name]
        t = ent[0][ent[1] % len(ent[0])]
        ent[1] += 1
        return t

    def add(self, eng, fn, r=(), w=(), dma=False):
        self.ops.append(dict(eng=eng, fn=fn, r=list(r), w=list(w), dma=dma))

    def dma(self, out, in_, r=(), w=(), q="sp", slow=False):
        if slow:
            self.add(q, lambda e: e.dma_start(out=out, in_=in_, allow_slow_non_contiguous=True), r=r, w=w, dma=True)
        else:
            self.add(q, lambda e: e.dma_start(out=out, in_=in_), r=r, w=w, dma=True)

    def cc_allgather(self, in_ap, out_ap, groups, r=(), w=()):
        self.ops.append(dict(eng="pool", fn=lambda e: e.collective_compute("AllGather", ALU.bypass, replica_groups=groups,
                                                                            ins=[in_ap.opt()], outs=[out_ap.opt()]),
                             r=list(r), w=list(w), dma=False, cc=True))

    def finalize(self):
        nc, ops = self.nc, self.ops
        last_w, readers = {}, {}
        bar = set()
        ccs = []
        last_c, last_d, dslot = {}, {}, {}
        for i, op in enumerate(ops):
            if op.get("barrier"):
                bar = set(last_c.values()) | set(last_d.values()) | set(ccs)
                ccs = []
                op["deps"] = set()
                continue
            if op.get("cc"):
                ccs.append(i)
            elif op["dma"]:
                sl_ = dslot.get(op["eng"], 0)
                dslot[op["eng"]] = sl_ + 1
                last_d[(op["eng"], sl_ % self.NDMASEM)] = i
            elif op["fn"] is not None:
                last_c[op["eng"]] = i
            deps = set(bar)
            raw = set()
            pk = self.psum_keys
            for k in op["r"]:
                if k in last_w:
                    deps.add(last_w[k]); raw.add(last_w[k])
                if k in pk:
                    for j in readers.get(k, ()):
                        if ops[j]["eng"] != op["eng"]:
                            deps.add(j)
            for k in op["w"]:
                if k in last_w:
                    deps.add(last_w[k])
                for j in readers.get(k, ()):
                    deps.add(j)
            deps.discard(i)
            op["deps"] = set(deps)
            if op["fn"] is not None:
                for k in op["r"]:
                    readers.setdefault(k, []).append(i)
            for k in op["w"]:
                last_w[k] = i
                readers[k] = []
        needed = set()
        for op in ops:
            needed |= op["deps"]
        engines = list(self.COMPUTE) + ["sp"]
        SEMMAX = 4000
        ccount = {e: 0 for e in engines}
        dcount = {}
        dn = {e: 0 for e in engines}
        semkeys = set()
        for i, op in enumerate(ops):
            e = op["eng"]
            if op.get("barrier"):
                op["sig"] = None
                continue
            if op.get("cc"):
                op["sig"] = (("cc", i), 1)
                semkeys.add(op["sig"][0])
                continue
            if op["dma"]:
                s = dn[e] % self.NDMASEM
                dn[e] += 1
                c = dcount.get((e, s), 0)
                per = SEMMAX // 16
                op["dprev"] = (("d", e, s, (c - 1) // per), 16 * ((c - 1) % per + 1)) if c > 0 else None
                dcount[(e, s)] = c + 1
                op["sig"] = (("d", e, s, c // per), 16 * (c % per + 1))
                semkeys.add(op["sig"][0])
            elif i in needed:
                c = ccount[e]
                ccount[e] += 1
                op["sig"] = (("c", e, c // SEMMAX), c % SEMMAX + 1)
                semkeys.add(op["sig"][0])
            else:
                op["sig"] = None
        self.nwait = {}
        with ExitStack() as st:
            sems = {k: st.enter_context(nc.semaphore("s_" + "_".join(str(x) for x in k))) for k in sorted(semkeys)}
            block = st.enter_context(nc.Block())
            for e in engines:
                mine = [op for op in ops if op["eng"] == e and not op.get("barrier")]
                if not mine:
                    continue

                def body(eng, mine=mine, eng_name=e):
                    known = {}

                    def wait(sig):
                        if sig is None:
                            return
                        key, val = sig
                        if known.get(key, 0) < val:
                            eng.wait_ge(sems[key], val)
                            known[key] = val
                            self.nwait[eng_name] = self.nwait.get(eng_name, 0) + 1

                    for op in mine:
                        need = {}
                        for j in op["deps"]:
                            sg = ops[j]["sig"]
                            if sg is not None and need.get(sg[0], 0) < sg[1]:
                                need[sg[0]] = sg[1]
                        if op["dma"] and op["dprev"] is not None:
                            sg = op["dprev"]
                            if need.get(sg[0], 0) < sg[1]:
                                need[sg[0]] = sg[1]
                        for kk_, vv_ in sorted(need.items(), key=lambda t: str(t[0])):
                            wait((kk_, vv_))
                        if op["fn"] is None:
                            continue
                        ins = op["fn"](eng)
                        if op["sig"] is not None:
                            ins.then_inc(sems[op["sig"][0]], 16 if op["dma"] else 1)

                getattr(block, self.BLOCKNAME[e])(body)
        self.counts = dict(ccount)
        self.stack.close()
        return nc


def pp(v):
    v = np.asarray(v, np.float32).reshape(-1, 128)
    return np.ascontiguousarray(v.T)


def load_weight_bf16(P, name, w_dram, K, N, stage_pool):
    kt = K // 128
    wt = P.sb(name, [128, kt, N], BF16)
    for k in range(kt):
        for c0 in range(0, N, 1024):
            c1 = min(N, c0 + 1024)
            st, sk = P.nxt(stage_pool)
            P.dma(st[:, : c1 - c0], w_dram[k * 128:(k + 1) * 128, c0:c1], w=[sk])
            P.add("pool", lambda e, st=st, k=k, c0=c0, c1=c1: e.tensor_copy(out=wt[:, k, c0:c1], in_=st[:, : c1 - c0]),
                  r=[sk], w=[(name, k)])
    return wt


def tail(P, TT, x_res, xk, y_ps_fn, pT_bf, pk, w_pg, w_pe, ones, lng, lnb, out_tile, ok, tag):
    r, rk = P.nxt("r")
    rb, rbk = P.nxt("rb")
    for d in range(8):
        yp, yk = y_ps_fn(d)
        P.add("dve", lambda e, d=d, yp=yp: e.scalar_tensor_tensor(out=r[:, d, :], in0=x_res[:, d, :], scalar=float(DN_ALPHA),
                                                                 in1=yp, op0=ALU.mult, op1=ALU.add),
              r=[xk, yk], w=[(rk, d)])
        P.add("pool", lambda e, d=d: e.tensor_copy(out=rb[:, d, :], in_=r[:, d, :]), r=[(rk, d)], w=[(rbk, d)])
    sq, sqk = P.nxt("sq")
    mean_ps, mk = P.nxt("psA")
    msq_ps, qk = P.nxt("psA")
    for d in range(8):
        gp, gk = P.nxt("ps")
        P.add("pe", lambda e, d=d, gp=gp: [e.matmul(gp[:, :TT], lhsT=w_pg[:, k, d * 128:(d + 1) * 128], rhs=rb[:, k, :],
                                                    start=(k == 0), stop=(k == 7)) for k in range(8)][-1],
              r=[(rbk, k) for k in range(8)] + [("w_pg", k) for k in range(8)], w=[gk])
        sg, sgk = P.nxt("tmp")
        P.add("act", lambda e, gp=gp, sg=sg: e.activation(out=sg[:, :TT], in_=gp[:, :TT], func=AF.Sigmoid), r=[gk], w=[sgk])
        ep, ek = P.nxt("ps")
        P.add("pe", lambda e, d=d, ep=ep: [e.matmul(ep[:, :TT], lhsT=w_pe[:, k, d * 128:(d + 1) * 128], rhs=pT_bf[:, k, :],
                                                    start=(k == 0), stop=(k == 1)) for k in range(2)][-1],
              r=[pk] + [("w_pe", k) for k in range(2)], w=[ek])
        P.add("dve", lambda e, ep=ep, sg=sg: e.tensor_tensor(out=sg[:, :TT], in0=ep[:, :TT], in1=sg[:, :TT], op=ALU.mult),
              r=[ek, sgk], w=[sgk])
        P.add("dve", lambda e, d=d, sg=sg: e.tensor_tensor(out=r[:, d, :], in0=r[:, d, :], in1=sg[:, :TT], op=ALU.add),
              r=[sgk, (rk, d)], w=[(rk, d)])
        P.add("act", lambda e, d=d: e.activation(out=sq[:, d, :], in_=r[:, d, :], func=AF.Square), r=[(rk, d)], w=[(sqk, d)])
    P.add("pe", lambda e: [e.matmul(mean_ps[:, :TT], lhsT=ones[:, :], rhs=r[:, k, :], start=(k == 0), stop=(k == 7))
                           for k in range(8)][-1], r=[(rk, k) for k in range(8)] + ["ones"], w=[mk])
    P.add("pe", lambda e: [e.matmul(msq_ps[:, :TT], lhsT=ones[:, :], rhs=sq[:, k, :], start=(k == 0), stop=(k == 7))
                           for k in range(8)][-1], r=[(sqk, k) for k in range(8)] + ["ones"], w=[qk])
    mean, mnk = P.nxt("st")
    rstd, rsk = P.nxt("st")
    P.add("act", lambda e: e.activation(out=mean[:, :TT], in_=mean_ps[:, :TT], func=AF.Copy), r=[mk], w=[mnk])
    P.add("dve", lambda e: e.tensor_tensor(out=rstd[:, :TT], in0=mean[:, :TT], in1=mean[:, :TT], op=ALU.mult), r=[mnk], w=[rsk])
    P.add("dve", lambda e: e.tensor_tensor(out=rstd[:, :TT], in0=msq_ps[:, :TT], in1=rstd[:, :TT], op=ALU.subtract),
          r=[qk, rsk], w=[rsk])
    P.add("dve", lambda e: e.tensor_scalar(out=rstd[:, :TT], in0=rstd[:, :TT], scalar1=float(LN_EPS), scalar2=None, op0=ALU.add),
          r=[rsk], w=[rsk])
    P.add("act", lambda e: e.activation(out=rstd[:, :TT], in_=rstd[:, :TT], func=AF.Sqrt), r=[rsk], w=[rsk])
    P.add("dve", lambda e: e.reciprocal(out=rstd[:, :TT], in_=rstd[:, :TT]), r=[rsk], w=[rsk])
    for d in range(8):
        P.add("dve", lambda e, d=d: e.tensor_tensor(out=r[:, d, :], in0=r[:, d, :], in1=mean[:, :TT], op=ALU.subtract),
              r=[(rk, d), mnk], w=[(rk, d)])
        P.add("dve", lambda e, d=d: e.tensor_tensor(out=r[:, d, :], in0=r[:, d, :], in1=rstd[:, :TT], op=ALU.mult),
              r=[(rk, d), rsk], w=[(rk, d)])
        P.add("dve", lambda e, d=d: e.tensor_scalar(out=out_tile[:, d, :], in0=r[:, d, :], scalar1=lng[:, d:d + 1],
                                                    scalar2=lnb[:, d:d + 1], op0=ALU.mult, op1=ALU.add),
              r=[(rk, d), "lnp"], w=[(ok, d)])


def tail_setup(P, TT, w_pg_d, w_pe_d, lng_d, lnb_d):
    w_pg = load_weight_bf16(P, "w_pg", w_pg_d, 1024, 1024, "wst")
    w_pe = load_weight_bf16(P, "w_pe", w_pe_d, 256, 1024, "wst")
    ones = P.sb("ones", [128, 128], F32)
    P.add("pool", lambda e: e.memset(ones[:, :], 1.0 / D), w=["ones"])
    lng = P.sb("lng", [128, 8], F32)
    lnb = P.sb("lnb", [128, 8], F32)
    P.dma(lng[:, :], lng_d, w=["lnp"])
    P.dma(lnb[:, :], lnb_d, w=["lnp"])
    P.pool("r", 1, [128, 8, TT], F32)
    P.pool("rb", 1, [128, 8, TT], BF16)
    P.pool("sq", 1, [128, 8, TT], F32)
    P.pool("st", 2, [128, TT], F32)
    P.pool("tmp", 6, [128, TT + 2], F32)
    return w_pg, w_pe, ones, lng, lnb


def build_conv(NT=2048, TT=256, P=None):
    own = P is None
    P = P or Prog()
    xT = P.dram("xT", [D, NT + 2])
    pT = P.dram("pT", [256, NT])
    w_in_d = P.dram("w_in", [D, 4 * D])
    ck_d = P.dram("conv_k", [128, 24])
    w_out_d = P.dram("w_out", [D, D])
    w_pg_d = P.dram("w_pg", [D, D])
    w_pe_d = P.dram("w_pe", [256, D])
    lng_d = P.dram("ln_g", [128, 8])
    lnb_d = P.dram("ln_b", [128, 8])
    oT = P.dram("oT", [D, NT], out=True)

    P.pool("wst", 2, [128, 1024], F32)
    P.pool("ps", 6, [128, 512], F32, psum=True)
    P.pool("psA", 2, [128, 512], F32, psum=True)
    w_in = load_weight_bf16(P, "w_in", w_in_d, D, 4 * D, "wst")
    w_out = load_weight_bf16(P, "w_out", w_out_d, D, D, "wst")
    w_pg, w_pe, ones, lng, lnb = tail_setup(P, TT, w_pg_d, w_pe_d, lng_d, lnb_d)
    ck = P.sb("ck", [128, 24], F32)
    P.dma(ck[:, :], ck_d, w=["ck"])
    P.pool("x", 2, [128, 8, TT + 2], F32)
    P.pool("xb", 2, [128, 8, TT + 2], BF16)
    P.pool("p", 2, [128, 2, TT], F32)
    P.pool("pb", 2, [128, 2, TT], BF16)
    P.pool("m", 1, [128, 8, TT], BF16)
    P.pool("o", 2, [128, 8, TT], F32)
    xTv = xT.rearrange("(k p) t -> p k t", p=128)
    pTv = pT.rearrange("(k p) t -> p k t", p=128)
    oTv = oT.rearrange("(k p) t -> p k t", p=128)
    W = TT + 2
    outkeys = []
    for j in range(NT // TT):
        x, xk = P.nxt("x")
        xb, xbk = P.nxt("xb")
        pt, ptk = P.nxt("p")
        pb, pbk = P.nxt("pb")
        P.dma(x[:, :, :], xTv[:, :, j * TT: j * TT + W], w=[xk])
        P.dma(pt[:, :, :], pTv[:, :, j * TT:(j + 1) * TT], w=[ptk])
        P.add("pool", lambda e, x=x, xb=xb: e.tensor_copy(out=xb[:, :, :], in_=x[:, :, :]), r=[xk], w=[xbk])
        P.add("pool", lambda e, pt=pt, pb=pb: e.tensor_copy(out=pb[:, :, :], in_=pt[:, :, :]), r=[ptk], w=[pbk])
        m, mk = P.nxt("m")
        for et in range(8):
            pss = {}
            for qi, q in enumerate(("bg", "cg", "h", "z")):
                pz, pzk = P.nxt("ps")
                c0 = qi * D + et * 128
                lo = 0 if q in ("cg", "h") else 2
                P.add("pe", lambda e, pz=pz, c0=c0, lo=lo, xb=xb: [
                    e.matmul(pz[:, : W - lo], lhsT=w_in[:, k, c0:c0 + 128], rhs=xb[:, k, lo:W], start=(k == 0), stop=(k == 7))
                    for k in range(8)][-1], r=[xbk] + [("w_in", k) for k in range(8)], w=[pzk])
                pss[q] = (pz, pzk)
            cgs, cgk = P.nxt("tmp")
            P.add("act", lambda e, cgs=cgs, a=pss["cg"][0]: e.activation(out=cgs[:, :W], in_=a[:, :W], func=AF.Copy),
                  r=[pss["cg"][1]], w=[cgk])
            P.add("dve", lambda e, cgs=cgs, a=pss["h"][0]: e.tensor_tensor(out=cgs[:, :W], in0=a[:, :W], in1=cgs[:, :W], op=ALU.mult),
                  r=[pss["h"][1], cgk], w=[cgk])
            sz, szk = P.nxt("tmp")
            P.add("act", lambda e, sz=sz, a=pss["z"][0]: e.activation(out=sz[:, :TT], in_=a[:, :TT], func=AF.Silu),
                  r=[pss["z"][1]], w=[szk])
            P.add("dve", lambda e, sz=sz, a=pss["bg"][0]: e.tensor_tensor(out=sz[:, :TT], in0=a[:, :TT], in1=sz[:, :TT], op=ALU.mult),
                  r=[pss["bg"][1], szk], w=[szk])
            cv, cvk = P.nxt("tmp")
            P.add("dve", lambda e, cv=cv, cgs=cgs, et=et: e.tensor_scalar(out=cv[:, :TT], in0=cgs[:, 2:W], scalar1=ck[:, 16 + et:17 + et],
                                                                         scalar2=None, op0=ALU.mult), r=[cgk, "ck"], w=[cvk])
            P.add("dve", lambda e, cv=cv, cgs=cgs, et=et: e.scalar_tensor_tensor(out=cv[:, :TT], in0=cgs[:, 1:W - 1], scalar=ck[:, 8 + et:9 + et],
                                                                                in1=cv[:, :TT], op0=ALU.mult, op1=ALU.add),
                  r=[cgk, cvk, "ck"], w=[cvk])
            P.add("dve", lambda e, cv=cv, cgs=cgs, et=et: e.scalar_tensor_tensor(out=cv[:, :TT], in0=cgs[:, 0:TT], scalar=ck[:, et:et + 1],
                                                                                in1=cv[:, :TT], op0=ALU.mult, op1=ALU.add),
                  r=[cgk, cvk, "ck"], w=[cvk])
            P.add("dve", lambda e, cv=cv, sz=sz, et=et, m=m: e.tensor_tensor(out=m[:, et, :], in0=cv[:, :TT], in1=sz[:, :TT], op=ALU.mult),
                  r=[cvk, szk], w=[(mk, et)])

        def y_ps(d, m=m, mk=mk):
            yp, yk = P.nxt("ps")
            P.add("pe", lambda e: [e.matmul(yp[:, :TT], lhsT=w_out[:, k, d * 128:(d + 1) * 128], rhs=m[:, k, :],
                                            start=(k == 0), stop=(k == 7)) for k in range(8)][-1],
                  r=[(mk, k) for k in range(8)] + [("w_out", k) for k in range(8)], w=[yk])
            return yp[:, :TT], yk

        o, ok = P.nxt("o")
        tail(P, TT, x[:, :, 2:W], xk, y_ps, pb, pbk, w_pg, w_pe, ones, lng, lnb, o, ok, "c")
        P.dma(oTv[:, :, j * TT:(j + 1) * TT], o[:, :, :], r=[(ok, d) for d in range(8)], w=[("out", j)])
        outkeys.append(("out", j))
    P.add("sp", None, r=outkeys)
    return P.finalize() if own else None


def build_s5b(NT=2048, TT=256, perm_g=False, sel=False, P=None):
    own = P is None
    P = P or Prog()
    xT = P.dram("xT", [D, NT])
    gcols = NT * (2 if sel else 1)
    gT = P.dram("gT", [D, 8, gcols // 8] if perm_g else [D, gcols])
    pT = P.dram("pT", [256, NT])
    w_z_d = P.dram("w_z", [D, D])
    w_glu_d = P.dram("w_glu", [D, D])
    bglu_d = P.dram("b_glu", [128, 8])
    w_out_d = P.dram("w_out", [D, D])
    w_pg_d = P.dram("w_pg", [D, D])
    w_pe_d = P.dram("w_pe", [256, D])
    lng_d = P.dram("ln_g", [128, 8])
    lnb_d = P.dram("ln_b", [128, 8])
    oT = P.dram("oT", [D, NT], out=True)
    P.pool("wst", 2, [128, 1024], F32)
    P.pool("ps", 6, [128, 512], F32, psum=True)
    P.pool("psA", 2, [128, 512], F32, psum=True)
    w_z = load_weight_bf16(P, "w_z", w_z_d, D, D, "wst")
    w_glu = load_weight_bf16(P, "w_glu", w_glu_d, D, D, "wst")
    w_out = load_weight_bf16(P, "w_out", w_out_d, D, D, "wst")
    w_pg, w_pe, ones, lng, lnb = tail_setup(P, TT, w_pg_d, w_pe_d, lng_d, lnb_d)
    bglu = P.sb("bglu", [128, 8], F32)
    P.dma(bglu[:, :], bglu_d, w=["bglu"])
    P.pool("x", 2, [128, 8, TT], F32)
    P.pool("xb", 2, [128, 8, TT], BF16)
    P.pool("g", 2, [128, 8, TT], F32)
    P.pool("gb", 2, [128, 8, TT], BF16)
    P.pool("p", 2, [128, 2, TT], F32)
    P.pool("pb", 2, [128, 2, TT], BF16)
    P.pool("m", 1, [128, 8, TT], BF16)
    P.pool("o", 2, [128, 8, TT], F32)
    if sel:
        P.pool("g2", 1, [128, 8, TT], F32)
        selt = P.sb("sel", [128, 2], F32)
        P.dma(selt[:, :], P.dram("sel", [128, 2]), w=["sel"])
    xTv = xT.rearrange("(k p) t -> p k t", p=128)
    gTv = gT.rearrange("(k p) t n -> p k t n", p=128) if perm_g else gT.rearrange("(k p) t -> p k t", p=128)
    pTv = pT.rearrange("(k p) t -> p k t", p=128)
    oTv = oT.rearrange("(k p) t -> p k t", p=128)
    outkeys = []
    CN = TT // 8
    gseq = (lambda ap: ap.rearrange("p (t n) -> p n t", t=8)) if perm_g else (lambda ap: ap)
    sseq = (lambda ap: ap.rearrange("p (n t) -> p n t", t=8)) if perm_g else (lambda ap: ap)
    for j in range(NT // TT):
        sl = slice(j * TT, (j + 1) * TT)
        x, xk = P.nxt("x"); xb, xbk = P.nxt("xb")
        g, gk = P.nxt("g"); gb, gbk = P.nxt("gb")
        pt, ptk = P.nxt("p"); pb, pbk = P.nxt("pb")
        P.dma(x[:, :, :], xTv[:, :, sl], w=[xk])
        if perm_g:
            for k in range(8):
                P.dma(g[:, k, :].rearrange("p (t n) -> p t n", t=8), gTv[:, k, :, j * CN:(j + 1) * CN], w=[gk])
            if sel:
                g2, g2k = P.nxt("g2")
                for k in range(8):
                    P.dma(g2[:, k, :].rearrange("p (t n) -> p t n", t=8), gTv[:, k, :, NT // 8 + j * CN:NT // 8 + (j + 1) * CN], w=[g2k])
                P.add("pool", lambda e, g=g: e.tensor_scalar(out=g[:, :, :], in0=g[:, :, :], scalar1=selt[:, 0:1], scalar2=None, op0=ALU.mult),
                      r=[gk, "sel"], w=[gk])
                P.add("dve", lambda e, g=g, g2=g2: e.scalar_tensor_tensor(out=g[:, :, :], in0=g2[:, :, :], scalar=selt[:, 1:2], in1=g[:, :, :],
                                                                          op0=ALU.mult, op1=ALU.add), r=[gk, g2k, "sel"], w=[gk])
        else:
            P.dma(g[:, :, :], gTv[:, :, sl], w=[gk])
        P.dma(pt[:, :, :], pTv[:, :, sl], w=[ptk])
        P.add("pool", lambda e, x=x, xb=xb: e.tensor_copy(out=xb[:, :, :], in_=x[:, :, :]), r=[xk], w=[xbk])
        for k in range(8):
            P.add("pool", lambda e, g=g, gb=gb, k=k: e.tensor_copy(out=sseq(gb[:, k, :]), in_=gseq(g[:, k, :])), r=[gk], w=[gbk])
        P.add("pool", lambda e, pt=pt, pb=pb: e.tensor_copy(out=pb[:, :, :], in_=pt[:, :, :]), r=[ptk], w=[pbk])
        m, mk = P.nxt("m")
        for et in range(8):
            sp_, spk = P.nxt("ps")
            P.add("pe", lambda e, sp_=sp_, et=et, gb=gb: [e.matmul(sp_[:, :TT], lhsT=w_glu[:, k, et * 128:(et + 1) * 128], rhs=gb[:, k, :],
                                                                  start=(k == 0), stop=(k == 7)) for k in range(8)][-1],
                  r=[gbk] + [("w_glu", k) for k in range(8)], w=[spk])
            zp, zpk = P.nxt("ps")
            P.add("pe", lambda e, zp=zp, et=et, xb=xb: [e.matmul(zp[:, :TT], lhsT=w_z[:, k, et * 128:(et + 1) * 128], rhs=xb[:, k, :],
                                                                start=(k == 0), stop=(k == 7)) for k in range(8)][-1],
                  r=[xbk] + [("w_z", k) for k in range(8)], w=[zpk])
            sg, sgk = P.nxt("tmp")
            P.add("act", lambda e, sg=sg, sp_=sp_, et=et: e.activation(out=sg[:, :TT], in_=sp_[:, :TT], func=AF.Sigmoid, bias=bglu[:, et:et + 1]),
                  r=[spk, "bglu"], w=[sgk])
            sz, szk = P.nxt("tmp")
            P.add("act", lambda e, sz=sz, zp=zp: e.activation(out=sz[:, :TT], in_=zp[:, :TT], func=AF.Silu), r=[zpk], w=[szk])
            P.add("dve", lambda e, sg=sg, g=g, et=et: e.tensor_tensor(out=sseq(sg[:, :TT]), in0=gseq(g[:, et, :]), in1=sseq(sg[:, :TT]), op=ALU.mult),
                  r=[gk, sgk], w=[sgk])
            P.add("dve", lambda e, sg=sg, sz=sz, m=m, et=et: e.tensor_tensor(out=m[:, et, :], in0=sg[:, :TT], in1=sz[:, :TT], op=ALU.mult),
                  r=[sgk, szk], w=[(mk, et)])

        def y_ps(d, m=m, mk=mk):
            yp, yk = P.nxt("ps")
            P.add("pe", lambda e: [e.matmul(yp[:, :TT], lhsT=w_out[:, k, d * 128:(d + 1) * 128], rhs=m[:, k, :],
                                            start=(k == 0), stop=(k == 7)) for k in range(8)][-1],
                  r=[(mk, k) for k in range(8)] + [("w_out", k) for k in range(8)], w=[yk])
            return yp[:, :TT], yk

        o, ok = P.nxt("o")
        tail(P, TT, x, xk, y_ps, pb, pbk, w_pg, w_pe, ones, lng, lnb, o, ok, "s")
        P.dma(oTv[:, :, sl], o[:, :, :], r=[(ok, d) for d in range(8)], w=[("out", j)])
        outkeys.append(("out", j))
    P.add("sp", None, r=outkeys)
    return P.finalize() if own else None


def s5b_layer(xs, gs, p_l, w_in, w_glu, b_glu, w_out, w_pg, w_pe, ln_g, ln_b):
    in_maps = []
    w_z = np.ascontiguousarray(w_in[:, D:])
    for c in range(NCORES):
        b, h = c // 2, c % 2
        in_maps.append(dict(xT=xs[c], gT=gs[c], pT=np.ascontiguousarray(p_l[b, h * 2048:(h + 1) * 2048].T),
                            w_z=w_z, w_glu=w_glu, b_glu=pp(b_glu), w_out=w_out, w_pg=w_pg, w_pe=w_pe,
                            ln_g=pp(ln_g), ln_b=pp(ln_b)))
    res = run("s5b", build_s5b, in_maps)
    return [r["oT"] for r in res]


PI = float(np.pi)


def _tt(P, eng, o, a, b, op):
    P.add(eng, lambda e: e.tensor_tensor(out=o[0], in0=a[0], in1=b[0], op=op), r=[a[1], b[1]], w=[o[1]])


def _ts(P, eng, o, a, s1, op0, s2=None, op1=None, extra=()):
    if op1 is None:
        P.add(eng, lambda e: e.tensor_scalar(out=o[0], in0=a[0], scalar1=s1, scalar2=None, op0=op0), r=[a[1]] + list(extra), w=[o[1]])
    else:
        P.add(eng, lambda e: e.tensor_scalar(out=o[0], in0=a[0], scalar1=s1, scalar2=s2, op0=op0, op1=op1), r=[a[1]] + list(extra), w=[o[1]])


def _stt(P, o, a, sc, b, op0, op1, extra=()):
    P.add("dve", lambda e: e.scalar_tensor_tensor(out=o[0], in0=a[0], scalar=sc, in1=b[0], op0=op0, op1=op1),
          r=[a[1], b[1]] + list(extra), w=[o[1]])


def _cmul(P, orr, oi, ar, ai, br, bi, t):
    _tt(P, "dve", orr, ar, br, ALU.mult)
    _tt(P, "dve", t, ai, bi, ALU.mult)
    _tt(P, "dve", orr, orr, t, ALU.subtract)
    _tt(P, "dve", oi, ar, bi, ALU.mult)
    _tt(P, "dve", t, ai, br, ALU.mult)
    _tt(P, "dve", oi, oi, t, ALU.add)


def _cplx_setup(P, tp, shape, lamre, lamim, ldt):
    nel = int(np.prod(shape))
    def T(n):
        if n not in tp:
            tp[n] = P.sb("tp_" + n, [128, 512])
        return (tp[n][:, :nel].rearrange("p (a b) -> p a b", b=shape[-1]), "tp_" + n)
    dt, thr, thi, tmp, x, x2, acc, sn, cs, t1, t2 = [T(n) for n in ("dt", "thr", "thi", "tmp", "x", "x2", "acc", "sn", "cs", "t1", "t2")]
    er, ar, ai, nr, den, cr, ci = [T(n) for n in ("er", "ar", "ai", "nr", "den", "cr", "ci")]
    P.add("act", lambda e: e.activation(out=dt[0], in_=ldt[0], func=AF.Exp), r=[ldt[1]], w=[dt[1]])
    _tt(P, "dve", thr, lamre, dt, ALU.mult)
    _tt(P, "dve", thi, lamim, dt, ALU.mult)
    for _ in range(4):
        _ts(P, "dve", tmp, thi, PI, ALU.is_gt, 2 * PI, ALU.mult)
        _tt(P, "dve", thi, thi, tmp, ALU.subtract)
    _ts(P, "dve", x, thi, 0.125, ALU.mult)
    _tt(P, "dve", x2, x, x, ALU.mult)
    _ts(P, "dve", acc, x2, 1.0 / 362880, ALU.mult)
    for c in (-1.0 / 5040, 1.0 / 120, -1.0 / 6):
        _stt(P, acc, acc, c, x2, ALU.add, ALU.mult)
    _stt(P, sn, acc, 1.0, x, ALU.add, ALU.mult)
    _ts(P, "dve", acc, x2, -1.0 / 3628800, ALU.mult)
    for c in (1.0 / 40320, -1.0 / 720, 1.0 / 24, -0.5):
        _stt(P, acc, acc, c, x2, ALU.add, ALU.mult)
    _ts(P, "dve", cs, acc, 1.0, ALU.add)
    for _ in range(3):
        _tt(P, "dve", t1, cs, cs, ALU.mult)
        _tt(P, "dve", t2, sn, sn, ALU.mult)
        _stt(P, sn, cs, 2.0, sn, ALU.mult, ALU.mult)
        _tt(P, "dve", cs, t1, t2, ALU.subtract)
    _ts(P, "dve", acc, thr, 1.0 / 120, ALU.mult)
    for c in (1.0 / 24, 1.0 / 6, 0.5, 1.0):
        _stt(P, acc, acc, c, thr, ALU.add, ALU.mult)
    _ts(P, "dve", er, acc, 1.0, ALU.add)
    _tt(P, "dve", ar, er, cs, ALU.mult)
    _tt(P, "dve", ai, er, sn, ALU.mult)
    _ts(P, "dve", nr, ar, -1.0, ALU.add)
    _tt(P, "dve", den, lamre, lamre, ALU.mult)
    _tt(P, "dve", t1, lamim, lamim, ALU.mult)
    _tt(P, "dve", den, den, t1, ALU.add)
    P.add("dve", lambda e: e.reciprocal(out=den[0], in_=den[0]), r=[den[1]], w=[den[1]])
    _tt(P, "dve", cr, nr, lamre, ALU.mult)
    _tt(P, "dve", t1, ai, lamim, ALU.mult)
    _tt(P, "dve", cr, cr, t1, ALU.add)
    _tt(P, "dve", cr, cr, den, ALU.mult)
    _tt(P, "dve", ci, ai, lamre, ALU.mult)
    _tt(P, "dve", t1, nr, lamim, ALU.mult)
    _tt(P, "dve", ci, ci, t1, ALU.subtract)
    _tt(P, "dve", ci, ci, den, ALU.mult)
    return ar, ai, cr, ci, t1, t2


def build_s5a(NG=32, NT=4096, TT=256, ngrun=None, stage=9, P=None):
    MT = NG // 8
    NCH = NT // 8
    assert NCH == 512
    own = P is None
    P = P or Prog()
    xT = P.dram("xT", [D, NT])
    w_u_d = P.dram("w_u", [D, MT * 128])
    cl_d = {n: P.dram(n, [128, MT, 64]) for n in ("lamre_cl", "lamim_cl", "ldt_cl", "bre_cl", "bim_cl")}
    pl_d = {n: P.dram(n, [128, NG, 16]) for n in ("lamre_pl", "lamim_pl", "ldt_pl", "bre_pl", "bim_pl", "cre_pl", "cim_pl")}
    dsk_d = P.dram("dsk", [128, MT])
    ident_d = P.dram("ident", [128, 128])
    j2_d = P.dram("j2", [128, 128])
    msk_d = P.dram("msk", [128, 2 + 8])
    identg_d = P.dram("identg", [128, 8, 16])
    gTp = P.dram("gTp", [MT * 128, 8, NCH], out=True)

    P.pool("wst", 1, [128, 1024], F32)
    P.pool("ps", 4, [128, 512], F32, psum=True)
    P.pool("psk", 2, [128, 128], F32, psum=True)
    P.pool("psr", 2, [128, 512], F32, psum=True)
    w_u = load_weight_bf16(P, "w_u", w_u_d, D, MT * 128, "wst")

    def ld(name, shape, src):
        t = P.sb(name, shape)
        P.dma(t[:], src, w=[name])
        return (t[:], name), t
    cl = {n: ld(n, [128, MT, 64], cl_d[n])[0] for n in cl_d}
    pl = {n: ld(n, [128, NG, 16], pl_d[n])[0] for n in pl_d}
    _, dsk = ld("dsk", [128, MT], dsk_d)
    _, ident = ld("ident", [128, 128], ident_d)
    _, j2 = ld("j2", [128, 128], j2_d)
    _, msk = ld("msk", [128, 10], msk_d)
    _, identg = ld("identg", [128, 8, 16], identg_d)
    mlo, mhi = msk[:, 0:1], msk[:, 1:2]

    u_t = P.sb("u", [128, MT, NT], BF16)
    P.pool("x", 2, [128, 8, TT], F32)
    P.pool("xb", 2, [128, 8, TT], BF16)
    xTv = xT.rearrange("(k p) t -> p k t", p=128)
    for j in range(NT // TT):
        x, xk = P.nxt("x"); xb, xbk = P.nxt("xb")
        P.dma(x[:, :, :], xTv[:, :, j * TT:(j + 1) * TT], w=[xk])
        P.add("pool", lambda e, x=x, xb=xb: e.tensor_copy(out=xb[:, :, :], in_=x[:, :, :]), r=[xk], w=[xbk])
        for m in range(MT):
            up, upk = P.nxt("ps")
            P.add("pe", lambda e, up=up, m=m, xb=xb: [e.matmul(up[:, :TT], lhsT=w_u[:, k, m * 128:(m + 1) * 128], rhs=xb[:, k, :],
                                                             start=(k == 0), stop=(k == 7)) for k in range(8)][-1],
                  r=[xbk] + [("w_u", k) for k in range(8)], w=[upk])
            P.add("act", lambda e, up=up, m=m, j=j: e.activation(out=u_t[:, m, j * TT:(j + 1) * TT], in_=up[:, :TT], func=AF.Copy),
                  r=[upk], w=[("u", m, j)])
    ukeys = lambda m: [("u", m, j) for j in range(NT // TT)]

    tp = {}
    ar, ai, cr, ci, t1, t2 = _cplx_setup(P, tp, [MT, 64], cl["lamre_cl"], cl["lamim_cl"], cl["ldt_cl"])
    EB = P.sb("EB", [128, 8, MT, 2, 64])
    ebr = lambda j: (EB[:, j, :, 0, :], ("EB", j))
    ebi = lambda j: (EB[:, j, :, 1, :], ("EB", j))
    _cmul(P, ebr(7), ebi(7), cr, ci, cl["bre_cl"], cl["bim_cl"], t1)
    for j in range(6, -1, -1):
        _cmul(P, ebr(j), ebi(j), ebr(j + 1), ebi(j + 1), ar, ai, t1)

    par, pai, pcr, pci, pt1, pt2 = _cplx_setup(P, tp, [NG, 16], pl["lamre_pl"], pl["lamim_pl"], pl["ldt_pl"])
    def T(n, old=None):
        if old is None:
            t = P.sb(n, [128, NG, 16])
            return (t[:], n)
        return (tp[old][:, :NG * 16].rearrange("p (a b) -> p a b", b=16), "tp_" + old)
    qr, qi, qr2, qi2 = T("qr", "dt"), T("qi", "thr"), T("qr2", "thi"), T("qi2", "tmp")
    car, cai, car2, cai2 = T("car", "x"), T("cai", "x2"), T("car2", "acc"), T("cai2", "sn")
    pr, pi_, pr2, pi2 = T("pr", "cs"), T("pi", "er"), T("pr2", "nr"), T("pi2", "den")
    Cc = T("Cc")
    Xs = P.sb("Xs", [128, 8, NG, 16])
    CS = P.sb("CS", [128, NG, 8, 16])
    AKr = P.sb("AKr", [128, 9, NG]); AKi = P.sb("AKi", [128, 9, NG])
    _ts(P, "dve", pt1, pl["cim_pl"], mhi, ALU.mult, extra=["msk"])
    _stt(P, Cc, pl["cre_pl"], mlo, pt1, ALU.mult, ALU.subtract, extra=["msk"])
    _cmul(P, qr, qi, pcr, pci, pl["bre_pl"], pl["bim_pl"], pt1)

    def stack(o, re, im, sub):
        _ts(P, "dve", pt2, im, mhi, ALU.mult, extra=["msk"])
        _stt(P, o, re, mlo, pt2, ALU.mult, ALU.subtract if sub else ALU.add, extra=["msk"])
    stack((Xs[:, 0, :, :], ("Xs", 0)), qr, qi, False)
    cur_q, nxt_q = (qr, qi), (qr2, qi2)
    cur_c, nxt_c = (pl["cre_pl"], pl["cim_pl"]), (car, cai)
    alt_c = (car2, cai2)
    cur_p, nxt_p = None, (pr, pi_)
    for tau in range(1, 9):
        if tau <= 7:
            _cmul(P, nxt_q[0], nxt_q[1], cur_q[0], cur_q[1], par, pai, pt1)
            cur_q, nxt_q = nxt_q, cur_q
            stack((Xs[:, tau, :, :], ("Xs", tau)), cur_q[0], cur_q[1], False)
        _cmul(P, nxt_c[0], nxt_c[1], cur_c[0], cur_c[1], par, pai, pt1)
        cur_c = nxt_c
        nxt_c = alt_c if cur_c[0][1] == car[1] else (car, cai)
        stack((CS[:, :, tau - 1, :], ("CS", tau - 1)), cur_c[0], cur_c[1], True)
        if cur_p is None:
            cur_p = (par, pai)
        else:
            _cmul(P, nxt_p[0], nxt_p[1], cur_p[0], cur_p[1], par, pai, pt1)
            cur_p = nxt_p
            nxt_p = (pr2, pi2) if cur_p[0][1] == pr[1] else (pr, pi_)
    akr = lambda k: (AKr[:, k, :], ("AK", k))
    aki = lambda k: (AKi[:, k, :], ("AK", k))
    P.add("dve", lambda e: e.tensor_copy(out=AKr[:, 0, :], in_=cur_p[0][0][:, :, 0]), r=[cur_p[0][1]], w=[("AK", 0)])
    P.add("dve", lambda e: e.tensor_copy(out=AKi[:, 0, :], in_=cur_p[1][0][:, :, 0]), r=[cur_p[1][1]], w=[("AK", 0)])
    s1 = P.sb("aks1", [128, NG]); s2 = P.sb("aks2", [128, NG])
    s1 = (s1[:], "aks1"); s2 = (s2[:], "aks2")
    for k in range(1, 9):
        _tt(P, "dve", s1, akr(k - 1), akr(k - 1), ALU.mult)
        _tt(P, "dve", s2, aki(k - 1), aki(k - 1), ALU.mult)
        _tt(P, "dve", akr(k), s1, s2, ALU.subtract)
        _stt(P, aki(k), akr(k - 1), 2.0, aki(k - 1), ALU.mult, ALU.mult)

    P.pool("LE", 2, [128, 8, 128], BF16)
    lm_tiles = P.pool("LM", 2, [128, 8, 128], BF16)
    for t, k_ in lm_tiles:
        P.add("pool", lambda e, t=t: e.memset(t[:], 0.0), w=[k_])
    xp_tiles = P.pool("XP", 8, [128, 128], F32)
    for t, k_ in xp_tiles:
        P.add("pool", lambda e, t=t: e.memset(t[:], 0.0), w=[k_])
    P.pool("ROT", 4, [128, 128], F32)
    P.pool("KS", 2, [128, 128], F32)
    P.pool("X", 2, [128, NCH], F32)
    P.pool("ys", 2, [128, NCH], F32)
    P.pool("gw", 1, [128, NCH], F32)
    P.pool("gy", 2, [128, NCH], F32)
    outkeys = []
    for g in range(NG if ngrun is None else ngrun):
        m, gl = g // 8, g % 8
        gm = msk[:, 2 + gl:3 + gl]
        le, lek = P.nxt("LE")
        for j in range(8):
            P.add("dve", lambda e, le=le, j=j, m=m, gm=gm: e.tensor_scalar(out=le[:, j, :], in0=EB[:, j, m, :, :].rearrange("p a b -> p (a b)"),
                                                                         scalar1=gm, scalar2=None, op0=ALU.mult),
                  r=[("EB", j), "msk"], w=[(lek, j)])
        kp, kpk = P.nxt("psk")
        for tau in range(8):
            xp, xpk = xp_tiles[gl]
            P.add("dve", lambda e, xp=xp, tau=tau, g=g, gl=gl: e.tensor_copy(out=xp[:, 16 * gl:16 * gl + 16], in_=Xs[:, tau, g, :]),
                  r=[("Xs", tau)], w=[xpk])
            P.add("pe", lambda e, xp=xp, tau=tau, g=g, kp=kp: e.matmul(kp[:, 16 * tau:16 * tau + 16], lhsT=xp[:, :], rhs=Cc[0][:, g, :],
                                                                     start=True, stop=True),
                  r=[xpk, "Cc"], w=[kpk])
        ks, ksk = P.nxt("KS")
        P.add("dve", lambda e, ks=ks, kp=kp, m=m, gl=gl: e.scalar_tensor_tensor(out=ks[:, 0:16], in0=identg[:, gl, :], scalar=dsk[:, m:m + 1],
                                                                             in1=kp[:, 0:16], op0=ALU.mult, op1=ALU.add),
              r=["identg", "dsk", kpk], w=[(ksk, 0)])
        P.add("dve", lambda e, ks=ks, kp=kp: e.tensor_copy(out=ks[:, 16:128], in_=kp[:, 16:128]),
              r=[kpk], w=[(ksk, 1)])
        lm, lmk = P.nxt("LM")
        for j in range(8):
            P.add("act", lambda e, lm=lm, ks=ks, j=j: e.activation(out=lm[:, j, 16 * j:128], in_=ks[:, 0:128 - 16 * j], func=AF.Copy),
                  r=[(ksk, 0), (ksk, 1)], w=[(lmk, j)])
        if stage < 2:
            continue
        ep, epk = P.nxt("ps")
        P.add("pe", lambda e, ep=ep, le=le, m=m: [e.matmul(ep[:, :], lhsT=le[:, j, :], rhs=u_t[:, m, j::8], start=(j == 0), stop=(j == 7))
                                                  for j in range(8)][-1],
              r=[(lek, j) for j in range(8)] + ukeys(m), w=[epk])
        X, Xk = P.nxt("X")
        P.add("act", lambda e, X=X, ep=ep: e.activation(out=X[:, :], in_=ep[:, :], func=AF.Copy), r=[epk], w=[Xk])
        for k in range(9 if stage >= 3 else 0):
            d = 1 << k
            rot, rotk = P.nxt("ROT")
            P.add("dve", lambda e, rot=rot, k=k, g=g: e.tensor_scalar(out=rot[:, :], in0=ident[:, :], scalar1=AKr[:, k, g:g + 1], scalar2=None, op0=ALU.mult),
                  r=["ident", ("AK", k)], w=[rotk])
            P.add("dve", lambda e, rot=rot, k=k, g=g: e.scalar_tensor_tensor(out=rot[:, :], in0=j2[:, :], scalar=AKi[:, k, g:g + 1], in1=rot[:, :],
                                                                          op0=ALU.mult, op1=ALU.add),
                  r=["j2", ("AK", k), rotk], w=[rotk])
            rp, rpk = P.nxt("psr")
            P.add("pe", lambda e, rp=rp, rot=rot, X=X, d=d: e.matmul(rp[:, :NCH - d], lhsT=rot[:, :], rhs=X[:, :NCH - d], start=True, stop=True),
                  r=[rotk, Xk], w=[rpk])
            P.add("dve", lambda e, rp=rp, X=X, d=d: e.tensor_tensor(out=X[:, d:], in0=rp[:, :NCH - d], in1=X[:, d:], op=ALU.add),
                  r=[rpk, Xk], w=[Xk])
        if stage < 4:
            continue
        yp, ypk = P.nxt("ps")
        P.add("pe", lambda e, yp=yp, lm=lm, m=m, X=X, g=g: [e.matmul(yp[:, :], lhsT=lm[:, j, :], rhs=u_t[:, m, j::8], start=(j == 0), stop=False)
                                                           for j in range(8)] and
              e.matmul(yp[:, 1:NCH], lhsT=CS[:, g, :, :].rearrange("p a b -> p (a b)"), rhs=X[:, 0:NCH - 1], start=False, stop=True),
              r=[(lmk, j) for j in range(8)] + ukeys(m) + [Xk] + [("CS", t_) for t_ in range(8)], w=[ypk])
        ys, ysk = P.nxt("ys"); gw, gwk = P.nxt("gw"); gy, gyk = P.nxt("gy")
        P.add("act", lambda e, ys=ys, yp=yp: e.activation(out=ys[:, :], in_=yp[:, :], func=AF.Copy), r=[ypk], w=[ysk])
        P.add("dve", lambda e, ys=ys, gw=gw: e.tensor_tensor(out=gw[:, :], in0=ys[:, :], in1=ys[:, :], op=ALU.mult), r=[ysk], w=[gwk])
        P.add("dve", lambda e, gw=gw: e.tensor_scalar(out=gw[:, :], in0=gw[:, :], scalar1=0.044715, scalar2=1.0, op0=ALU.mult, op1=ALU.add),
              r=[gwk], w=[gwk])
        P.add("dve", lambda e, ys=ys, gw=gw: e.tensor_tensor(out=gw[:, :], in0=gw[:, :], in1=ys[:, :], op=ALU.mult), r=[ysk, gwk], w=[gwk])
        P.add("act", lambda e, gw=gw: e.activation(out=gw[:, :], in_=gw[:, :], func=AF.Sigmoid, scale=1.5957691216057308), r=[gwk], w=[gwk])
        P.add("dve", lambda e, ys=ys, gw=gw, gy=gy: e.tensor_tensor(out=gy[:, :], in0=ys[:, :], in1=gw[:, :], op=ALU.mult), r=[ysk, gwk], w=[gyk])
        for t_ in range(8):
            r0 = m * 128 + gl * 16
            P.dma(gTp[r0:r0 + 16, t_, :], gy[16 * t_:16 * t_ + 16, :], r=[gyk], w=[("out", g, t_)])
            outkeys.append(("out", g, t_))
    P.add("sp", None, r=outkeys)
    return P.finalize() if own else None


def s5a_layer(xfull, prm):
    NG = 32
    ident = np.eye(128, dtype=np.float32)
    j2 = np.zeros((128, 128), np.float32)
    j2[np.arange(64), np.arange(64) + 64] = 1.0
    j2[np.arange(64) + 64, np.arange(64)] = -1.0
    msk = np.zeros((128, 10), np.float32)
    msk[:64, 0] = 1.0; msk[64:, 1] = 1.0
    for gl in range(8):
        msk[16 * gl:16 * gl + 16, 2 + gl] = 1.0
    identg = np.zeros((128, 8, 16), np.float32)
    for gl in range(8):
        for h in range(16):
            identg[16 * gl + h, gl, h] = 1.0
    in_maps = []
    for c in range(NCORES):
        b, gh = c // 2, c % 2
        gs = slice(NG * gh, NG * gh + NG)
        def cl(a):
            a = np.broadcast_to(a.reshape(NG // 8, 8, 1, 64), (NG // 8, 8, 16, 64))
            return np.ascontiguousarray(a.transpose(1, 2, 0, 3).reshape(128, NG // 8, 64))
        def clb(a):
            a = a.reshape(NG // 8, 8, 64, 16)
            return np.ascontiguousarray(a.transpose(1, 3, 0, 2).reshape(128, NG // 8, 64))
        def plv(a):
            a = np.broadcast_to(a.T.reshape(1, 64, NG, 1), (2, 64, NG, 16))
            return np.ascontiguousarray(a.reshape(128, NG, 16))
        def plb(a):
            a = np.broadcast_to(a.transpose(1, 0, 2)[None], (2, 64, NG, 16))
            return np.ascontiguousarray(a.reshape(128, NG, 16))
        def plc(a):
            a = np.broadcast_to(a.transpose(2, 0, 1)[None], (2, 64, NG, 16))
            return np.ascontiguousarray(a.reshape(128, NG, 16))
        ldt = np.broadcast_to(prm["log_dt"][gs][:, None], (NG, 64))
        in_maps.append(dict(
            xT=xfull[b], w_u=np.ascontiguousarray(prm["w_in"][:, 512 * gh:512 * gh + 512]),
            lamre_cl=cl(prm["lam_re"][gs]), lamim_cl=cl(prm["lam_im"][gs]), ldt_cl=cl(ldt),
            bre_cl=clb(prm["b_re"][gs]), bim_cl=clb(prm["b_im"][gs]),
            lamre_pl=plv(prm["lam_re"][gs]), lamim_pl=plv(prm["lam_im"][gs]), ldt_pl=plv(ldt),
            bre_pl=plb(prm["b_re"][gs]), bim_pl=plb(prm["b_im"][gs]),
            cre_pl=plc(prm["c_re"][gs]), cim_pl=plc(prm["c_im"][gs]),
            dsk=pp(prm["d"][512 * gh:512 * gh + 512]), ident=ident, j2=j2, msk=msk, identg=identg))
    res = run("s5a", build_s5a, in_maps)
    out = []
    for b in range(4):
        halves = [res[2 * b + gh]["gTp"].transpose(0, 2, 1).reshape(512, 4096) for gh in range(2)]
        out.append(np.ascontiguousarray(np.concatenate(halves, axis=0)))
    return out


RW_GN_EPS = 64e-5


def build_rwa(NT=4096, TT=256, ntiles=None, pipeline=True, P=None):
    C = 64
    NCK = TT // C
    MT = 4
    own = P is None
    P = P or Prog()
    xT = P.dram("xT", [D, NT + 1])
    mu_d = P.dram("mu", [128, 6, 8])
    wq_d = {q: P.dram("w_" + q, [D, 512]) for q in "rkvz"}
    w1_d = P.dram("w1", [D, 64]); a1_d = P.dram("a1", [D, 64])
    w2_d = P.dram("w2", [64, 512]); a2_d = P.dram("a2", [64, 512])
    prm_d = P.dram("prm", [128, 7, 4])
    ident_d = P.dram("ident", [128, 128])
    onesb_d = P.dram("onesb", [128, 128])
    msk_d = P.dram("msk", [64, 4, 512])
    rst_d = P.dram("rst", [128, TT])
    mT = P.dram("mT", [512, NT], out=True)

    P.pool("wst", 1, [128, 512], F32)
    P.pool("ps", 4, [128, 512], F32, psum=True)
    P.pool("ps2", 4, [128, 512], F32, psum=True)
    wq = {q: load_weight_bf16(P, "w_" + q, wq_d[q], D, 512, "wst") for q in "rkvz"}
    w1 = load_weight_bf16(P, "w1", w1_d, D, 64, "wst")
    a1 = load_weight_bf16(P, "a1", a1_d, D, 64, "wst")

    def ld(name, shape, src, dtype=F32):
        t = P.sb(name, shape, dtype)
        P.dma(t[:], src, w=[name])
        return t
    w2f = ld("w2f", [64, 512], w2_d); a2f = ld("a2f", [64, 512], a2_d)
    w2 = P.sb("w2", [64, 512], BF16); a2 = P.sb("a2", [64, 512], BF16)
    P.add("pool", lambda e: e.tensor_copy(out=w2[:, :], in_=w2f[:, :]), r=["w2f"], w=["w2"])
    P.add("pool", lambda e: e.tensor_copy(out=a2[:, :], in_=a2f[:, :]), r=["a2f"], w=["a2"])
    mu = ld("mu", [128, 6, 8], mu_d)
    prm = ld("prm", [128, 7, 4], prm_d)
    ident = ld("ident", [128, 128], ident_d)
    onesb = ld("onesb", [128, 128], onesb_d)
    msk = ld("msk", [64, 4, 512], msk_d)
    rst = ld("rst", [128, TT], rst_d)
    maskU, maskUi, maskL, I8 = msk[:, 0, :], msk[:, 1, :], msk[:, 2, :], msk[:, 3, :]
    W0, A0, KK, KA, RK, LG, LB = range(7)

    ST = P.sb("ST", [128, MT, 64])
    P.add("pool", lambda e: e.memset(ST[:], 0.0), w=["ST"])

    P.pool("xin", 1, [128, 8, TT + 1], F32)
    dx = P.sb("dx", [128, 8, TT]); tmpx = P.sb("tmpx", [128, 8, TT])
    xs = [P.sb(f"xs{i}", [128, 8, TT], BF16) for i in range(6)]
    F = {n: P.sb("F_" + n, [128, MT, TT]) for n in ("r", "k", "v", "sz", "sw", "a", "f6", "f7", "f8", "f9", "f10")}
    t1b = P.sb("t1b", [64, TT], BF16); a1b = P.sb("a1b", [64, TT], BF16)
    P.pool("tm", 2, [64, 3, 512], F32)
    cs = {n: P.sb("c_" + n, [64, 512]) for n in ("N", "NT", "Aak", "Arb", "Ark", "T", "Xa", "XTa", "Xb", "XTb", "WT", "UT", "o", "sq", "on")}
    cs_odd = {n: P.sb("co_" + n, [64, 512]) for n in ("T", "Aak", "Arb", "Ark")}
    st = {n: P.sb("st_" + n, [64, 8]) for n in ("s1", "s2", "mean", "var", "rstd")}
    P.pool("mo", 1, [128, MT, TT], F32)
    xTv = xT.rearrange("(k p) t -> p k t", p=128)
    mTv = mT.rearrange("(m p) t -> p m t", p=128)
    outkeys = []

    def A(eng, fn, r, w):
        P.add(eng, fn, r=r, w=w)

    def do_tile(j):
            xin, xk = P.nxt("xin")
            P.dma(xin[:, :, :], xTv[:, :, j * TT:(j + 1) * TT + 1], w=[xk])
            A("pool", lambda e, xin=xin: e.tensor_tensor(out=dx[:, :, :], in0=xin[:, :, 0:TT], in1=xin[:, :, 1:TT + 1], op=ALU.subtract), [xk], ["dx"])
            for i in range(6):
                for k in range(8):
                    A("dve", lambda e, i=i, k=k: e.scalar_tensor_tensor(out=xs[i][:, k, :], in0=dx[:, k, :], scalar=mu[:, i, k:k + 1], in1=xin[:, k, 1:TT + 1],
                                                                        op0=ALU.mult, op1=ALU.add),
                      ["dx", "mu", xk], [(f"xs{i}", k)])
            for qi, q in enumerate("rkvz"):
                for m in range(MT):
                    pz, pzk = P.nxt("ps")
                    A("pe", lambda e, pz=pz, q=q, qi=qi, m=m: [e.matmul(pz[:, :TT], lhsT=wq[q][:, k, m * 128:(m + 1) * 128], rhs=xs[qi][:, k, :],
                                                                       start=(k == 0), stop=(k == 7)) for k in range(8)][-1],
                      [(f"xs{qi}", k) for k in range(8)] + [("w_" + q, k) for k in range(8)], [pzk])
                    dst = F[{"r": "r", "k": "k", "v": "v", "z": "sz"}[q]]
                    A("act", lambda e, pz=pz, dst=dst, m=m, q=q: e.activation(out=dst[:, m, :], in_=pz[:, :TT], func=AF.Silu if q == "z" else AF.Copy),
                      [pzk], [("F", {"r": "r", "k": "k", "v": "v", "z": "sz"}[q], m)])
            for (wa, wb, xi, tb, tbk, dstn, bias_i, fn1) in ((w1, w2, 4, t1b, "t1b", "sw", W0, AF.Tanh), (a1, a2, 5, a1b, "a1b", "a", A0, AF.Copy)):
                pz, pzk = P.nxt("ps")
                A("pe", lambda e, pz=pz, wa=wa, xi=xi: [e.matmul(pz[:64, :TT], lhsT=wa[:, k, :], rhs=xs[xi][:, k, :], start=(k == 0), stop=(k == 7))
                                                       for k in range(8)][-1],
                  [(f"xs{xi}", k) for k in range(8)] + [("w1" if wa is w1 else "a1", k) for k in range(8)], [pzk])
                A("act", lambda e, pz=pz, tb=tb, fn1=fn1: e.activation(out=tb[:, :], in_=pz[:64, :TT], func=fn1), [pzk], [tbk])
                for m in range(MT):
                    p2, p2k = P.nxt("ps")
                    A("pe", lambda e, p2=p2, wb=wb, tb=tb, m=m: e.matmul(p2[:, :TT], lhsT=wb[:, m * 128:(m + 1) * 128], rhs=tb[:, :], start=True, stop=True),
                      [tbk, "w2" if wb is w2 else "a2"], [p2k])
                    A("act", lambda e, p2=p2, dstn=dstn, m=m, bias_i=bias_i: e.activation(out=F[dstn][:, m, :], in_=p2[:, :TT], func=AF.Sigmoid,
                                                                                     bias=prm[:, bias_i, m:m + 1]),
                      [p2k, "prm"], [("F", dstn, m)])
            for m in range(MT):
                fk = lambda n: ("F", n, m)
                sl = lambda n: F[n][:, m, :]
                A("dve", lambda e, m=m: e.tensor_scalar(out=F["f6"][:, m, :], in0=F["k"][:, m, :], scalar1=prm[:, KK, m:m + 1], scalar2=None, op0=ALU.mult),
                  [fk("k"), "prm"], [fk("f6")])
                A("dve", lambda e, m=m: e.tensor_tensor(out=F["f7"][:, m, :], in0=F["f6"][:, m, :], in1=F["f6"][:, m, :], op=ALU.mult), [fk("f6")], [fk("f7")])
                pz, pzk = P.nxt("ps")
                A("pe", lambda e, pz=pz, m=m: e.matmul(pz[:, :TT], lhsT=onesb[:, :], rhs=F["f7"][:, m, :], start=True, stop=True), [fk("f7"), "onesb"], [pzk])
                A("act", lambda e, pz=pz, m=m: e.activation(out=F["f7"][:, m, :], in_=pz[:, :TT], func=AF.Sqrt), [pzk], [fk("f7")])
                A("dve", lambda e, m=m: e.tensor_scalar(out=F["f7"][:, m, :], in0=F["f7"][:, m, :], scalar1=1e-12, scalar2=None, op0=ALU.max), [fk("f7")], [fk("f7")])
                A("dve", lambda e, m=m: e.reciprocal(out=F["f7"][:, m, :], in_=F["f7"][:, m, :]), [fk("f7")], [fk("f7")])
                A("dve", lambda e, m=m: e.tensor_tensor(out=F["f6"][:, m, :], in0=F["f6"][:, m, :], in1=F["f7"][:, m, :], op=ALU.mult), [fk("f6"), fk("f7")], [fk("f6")])
                A("dve", lambda e, m=m: e.tensor_scalar(out=F["f7"][:, m, :], in0=F["a"][:, m, :], scalar1=-1.0, scalar2=prm[:, KA, m:m + 1], op0=ALU.add, op1=ALU.mult),
                  [fk("a"), fk("f7"), "prm"], [fk("f7")])
                A("dve", lambda e, m=m: e.tensor_tensor(out=F["f7"][:, m, :], in0=F["f7"][:, m, :], in1=F["k"][:, m, :], op=ALU.mult), [fk("f7"), fk("k")], [fk("f7")])
                A("dve", lambda e, m=m: e.tensor_tensor(out=F["f8"][:, m, :], in0=F["f7"][:, m, :], in1=F["k"][:, m, :], op=ALU.add), [fk("f7"), fk("k")], [fk("f8")])
                A("dve", lambda e, m=m: e.tensor_tensor(out=F["f7"][:, m, :], in0=F["f6"][:, m, :], in1=F["a"][:, m, :], op=ALU.mult), [fk("f6"), fk("a"), fk("f7")], [fk("f7")])
                A("dve", lambda e, m=m: e.tensor_scalar(out=F["sw"][:, m, :], in0=F["sw"][:, m, :], scalar1=-0.6065306597126334, scalar2=None, op0=ALU.mult),
                  [fk("sw")], [fk("sw")])
                A("dve", lambda e, m=m: e.tensor_tensor_scan(out=F["k"][:, m, :], data0=rst[:, :], data1=F["sw"][:, m, :], initial=0.0, op0=ALU.mult, op1=ALU.add),
                  [fk("sw"), "rst", fk("k"), fk("f8"), fk("f7")], [fk("k")])
                A("act", lambda e, m=m: e.activation(out=F["a"][:, m, :], in_=F["k"][:, m, :], func=AF.Exp), [fk("k"), fk("a"), fk("f7")], [fk("a")])
                A("act", lambda e, m=m: e.activation(out=F["f9"][:, m, :], in_=F["k"][:, m, :], func=AF.Exp, scale=-1.0), [fk("k")], [fk("f9")])
                A("dve", lambda e, m=m: e.tensor_tensor(out=F["k"][:, m, :], in0=F["k"][:, m, :], in1=F["sw"][:, m, :], op=ALU.subtract), [fk("k"), fk("sw"), fk("a"), fk("f9")], [fk("k")])
                A("act", lambda e, m=m: e.activation(out=F["k"][:, m, :], in_=F["k"][:, m, :], func=AF.Exp), [fk("k")], [fk("k")])
                A("dve", lambda e, m=m: e.scalar_tensor_tensor(out=F["f10"][:, m, :], in0=F["r"][:, m, :], scalar=prm[:, RK, m:m + 1], in1=F["f8"][:, m, :],
                                                               op0=ALU.mult, op1=ALU.mult), [fk("r"), fk("f8"), "prm"], [fk("f10")])
                pz, pzk = P.nxt("ps")
                A("pe", lambda e, pz=pz, m=m: e.matmul(pz[:, :TT], lhsT=onesb[:, :], rhs=F["f10"][:, m, :], start=True, stop=True), [fk("f10"), "onesb"], [pzk])
                A("dve", lambda e, pz=pz, m=m: e.tensor_tensor(out=F["f10"][:, m, :], in0=pz[:, :TT], in1=F["v"][:, m, :], op=ALU.mult), [pzk, fk("v")], [fk("f10")])
                A("dve", lambda e, m=m: e.tensor_tensor(out=F["r"][:, m, :], in0=F["r"][:, m, :], in1=F["a"][:, m, :], op=ALU.mult), [fk("r"), fk("a"), fk("f10")], [fk("r")])
                A("dve", lambda e, m=m: e.scalar_tensor_tensor(out=F["k"][:, m, :], in0=F["f6"][:, m, :], scalar=-1.0, in1=F["k"][:, m, :], op0=ALU.mult, op1=ALU.mult),
                  [fk("f6"), fk("k")], [fk("k")])
                A("dve", lambda e, m=m: e.tensor_tensor(out=F["f8"][:, m, :], in0=F["f8"][:, m, :], in1=F["f9"][:, m, :], op=ALU.mult), [fk("f8"), fk("f9"), fk("f10")], [fk("f8")])
                A("dve", lambda e, m=m: e.tensor_tensor(out=F["f7"][:, m, :], in0=F["f7"][:, m, :], in1=F["f9"][:, m, :], op=ALU.mult), [fk("f7"), fk("f9")], [fk("f7")])
                pcb = F["a"][:, m, C - 1::C].unsqueeze(2).broadcast_to([128, NCK, C])
                A("dve", lambda e, m=m, pcb=pcb: e.tensor_tensor(out=F["f9"][:, m, :].rearrange("p (c t) -> p c t", t=C),
                                                                 in0=F["f8"][:, m, :].rearrange("p (c t) -> p c t", t=C), in1=pcb, op=ALU.mult),
                  [fk("f8"), fk("a"), fk("f9"), fk("f7")], [fk("f9")])
                A("dve", lambda e, m=m, pcb=pcb: e.tensor_tensor(out=F["sw"][:, m, :].rearrange("p (c t) -> p c t", t=C),
                                                                 in0=F["f7"][:, m, :].rearrange("p (c t) -> p c t", t=C), in1=pcb, op=ALU.mult),
                  [fk("f7"), fk("a"), fk("sw"), fk("k")], [fk("sw")])
            RT, AT, KT, BT, VV, KH, BH, PP, BON, SZ = "r", "k", "f8", "f7", "v", "f9", "sw", "a", "f10", "sz"
            allm = lambda n: [("F", n, m) for m in range(MT)]
            mo, mok = P.nxt("mo")
            def do_chunk(c):
                    csl = slice(c * C, (c + 1) * C)
                    DBL = ("T", "Aak", "Arb", "Ark")
                    csx = dict(cs)
                    if c % 2 == 1:
                        csx.update({n: cs_odd[n] for n in DBL})
                    kname = lambda n: "c_" + n + ("_o" if (c % 2 == 1 and n in DBL) else "")
                    pspool = ["ps"]
                    nps = lambda: P.nxt(pspool[0])
                    tm, tmk = P.nxt("tm")
                    for qi, n in enumerate((VV, KH, BH)):
                        pz, pzk = nps()
                        A("pe", lambda e, pz=pz, n=n, csl=csl: [e.transpose(out=pz[:64, m * 128:(m + 1) * 128], in_=F[n][:, m, csl], identity=ident[:, :])
                                                               for m in range(MT)][-1], allm(n) + ["ident"], [pzk])
                        A("act", lambda e, pz=pz, tm=tm, qi=qi: e.activation(out=tm[:, qi, :], in_=pz[:64, :], func=AF.Copy), [pzk], [(tmk, qi)])
                    Vtm, Khtm, Bhtm = tm[:, 0, :], tm[:, 1, :], tm[:, 2, :]

                    def heads():
                        for h in range(8):
                            m, hl = h // 2, h % 2
                            yield h, m, hl, 64 * hl, slice(h * 64, (h + 1) * 64), slice((hl * 4 + m) * 64, (hl * 4 + m + 1) * 64)

                    def fm(n, m, bp):
                        return F[n][bp:bp + 64, m, csl]

                    for (name, ln, rn_, mask) in (("N", BT, AT, maskU), ("NT", AT, BT, maskL), ("Aak", KT, AT, maskU), ("Arb", BT, RT, maskUi), ("Ark", KT, RT, maskUi)):
                        pa, pak = nps()
                        pb, pbk = nps()

                        def f(e, pa=pa, pb=pb, ln=ln, rn_=rn_):
                            last = None
                            for h, m, hl, bp, hsn, hsb in heads():
                                dst = pa if hl == 0 else pb
                                last = e.matmul(dst[:64, m * 64:(m + 1) * 64], lhsT=fm(ln, m, bp), rhs=fm(rn_, m, bp), start=True, stop=True)
                            return last
                        A("pe", f, allm(ln) + allm(rn_), [pak, pbk])
                        A("dve", lambda e, pa=pa, name=name, mask=mask: e.tensor_tensor(out=csx[name][:, 0:256], in0=pa[:64, 0:256], in1=mask[:, 0:256], op=ALU.mult),
                          [pak, "msk"], [(kname(name), 0)])
                        A("dve", lambda e, pb=pb, name=name, mask=mask: e.tensor_tensor(out=csx[name][:, 256:512], in0=pb[:64, 0:256], in1=mask[:, 0:256], op=ALU.mult),
                          [pbk, "msk"], [(kname(name), 1)])
                    ck = lambda n: [(kname(n), 0), (kname(n), 1)]
                    A("dve", lambda e: e.tensor_tensor(out=csx["T"][:, :], in0=csx["N"][:, :], in1=I8, op=ALU.add), ck("N") + ["msk"], ck("T"))

                    def allheads(mmf, r, w):
                        def f(e):
                            last = None
                            for hh in heads():
                                last = mmf(e, *hh)
                            return last
                        A("pe", f, r, w)
                    X, XT = "N", "NT"
                    for lvl in range(5):
                        X2, X2T = ("Xa", "XTa") if lvl % 2 == 0 else ("Xb", "XTb")
                        if lvl < 4:
                            pz, pzk = nps()
                            allheads(lambda e, h, m, hl, bp, hsn, hsb, pz=pz, X=X, XT=XT: e.matmul(pz[:64, hsb], lhsT=csx[XT][:, hsb], rhs=csx[X][:, hsb], start=True, stop=True),
                                     ck(X) + ck(XT), [pzk])
                            A("act", lambda e, pz=pz, X2=X2: e.activation(out=csx[X2][:, :], in_=pz[:64, :], func=AF.Copy), [pzk], ck(X2))
                        pz, pzk = nps()
                        allheads(lambda e, h, m, hl, bp, hsn, hsb, pz=pz, X=X, XT=XT: e.matmul(pz[:64, hsb], lhsT=csx[X][:, hsb], rhs=csx[XT][:, hsb], start=True, stop=True),
                                 ck(X) + ck(XT), [pzk])
                        A("act", lambda e, pz=pz, X2T=X2T: e.activation(out=csx[X2T][:, :], in_=pz[:64, :], func=AF.Copy), [pzk], ck(X2T))
                        pz, pzk = nps()
                        allheads(lambda e, h, m, hl, bp, hsn, hsb, pz=pz, X2T=X2T: e.matmul(pz[:64, hsb], lhsT=csx[X2T][:, hsb], rhs=csx["T"][:, hsb], start=True, stop=True),
                                 ck(X2T) + ck("T"), [pzk])
                        A("dve", lambda e, pz=pz: e.tensor_tensor(out=csx["T"][:, :], in0=pz[:64, :], in1=csx["T"][:, :], op=ALU.add), [pzk] + ck("T"), ck("T"))
                        X, XT = X2, X2T

                    def state_mm(lhs_n, extra, out_name, outv_lo, outv_hi, rkeys):
                        pa, pak = nps()
                        pb, pbk = nps()

                        def f(e):
                            last = None
                            for h, m, hl, bp, hsn, hsb in heads():
                                if hl == 0:
                                    e.matmul(pa[:64, hsb], lhsT=fm(lhs_n, m, 0), rhs=ST[0:64, m, :], start=True, stop=False)
                                    last = extra(e, pa, hsn, hsb, False)
                                else:
                                    last = extra(e, pa, hsn, hsb, True)
                            for h, m, hl, bp, hsn, hsb in heads():
                                if hl == 1:
                                    last = e.matmul(pb[:64, m * 64:(m + 1) * 64], lhsT=fm(lhs_n, m, 64), rhs=ST[64:128, m, :], start=True, stop=True)
                            return last
                        A("pe", f, allm(lhs_n) + ["ST"] + rkeys, [pak, pbk])
                        A("act", lambda e: e.activation(out=csx["sq"][:, 0:256], in_=pb[:64, 0:256], func=AF.Copy), [pbk], ["c_sq"])
                        A("act", lambda e: e.activation(out=outv_lo, in_=pa[:64, 0:256].rearrange("p (m v) -> p m v", v=64), func=AF.Copy), [pak], [(out_name, 0)])
                        A("dve", lambda e: e.tensor_tensor(out=outv_hi, in0=pa[:64, 256:512].rearrange("p (m v) -> p m v", v=64),
                                                           in1=csx["sq"][:, 0:256].rearrange("p (m v) -> p m v", v=64), op=ALU.add), [pak, "c_sq"], [(out_name, 1)])

                    splits.append(len(P.ops))
                    pspool[0] = "ps2"
                    wt3 = csx["WT"][:, :].rearrange("p (b v) -> p b v", v=64)
                    state_mm(AT, lambda e, pa, hsn, hsb, first: e.matmul(pa[:64, hsb], lhsT=csx["Aak"][:, hsb], rhs=Vtm[:, hsn], start=first, stop=True),
                             "c_WT", wt3[:, 0:4, :], wt3[:, 4:8, :], ck("Aak") + [(tmk, 0)])
                    pz, pzk = nps()
                    allheads(lambda e, h, m, hl, bp, hsn, hsb, pz=pz: e.matmul(pz[:64, hsb], lhsT=csx["T"][:, hsb], rhs=csx["WT"][:, hsb], start=True, stop=True),
                             ck("T") + ck("WT"), [pzk])
                    A("act", lambda e, pz=pz: e.activation(out=csx["UT"][:, :], in_=pz[:64, :], func=AF.Copy), [pzk], ck("UT"))
                    o4 = csx["o"][:, :].rearrange("p (m hl v) -> p m hl v", hl=2, v=64)
                    state_mm(RT, lambda e, pa, hsn, hsb, first: [e.matmul(pa[:64, hsb], lhsT=csx["Arb"][:, hsb], rhs=csx["UT"][:, hsb], start=first, stop=False),
                                                                e.matmul(pa[:64, hsb], lhsT=csx["Ark"][:, hsb], rhs=Vtm[:, hsn], start=False, stop=True)][-1],
                             "c_o", o4[:, :, 0, :], o4[:, :, 1, :], ck("Arb") + ck("UT") + ck("Ark") + [(tmk, 0)])
                    pz, pzk = nps()
                    allheads(lambda e, h, m, hl, bp, hsn, hsb, pz=pz: [e.matmul(pz[bp:bp + 64, m * 64:(m + 1) * 64], lhsT=Bhtm[:, hsn], rhs=csx["UT"][:, hsb], start=True, stop=False),
                                                                       e.matmul(pz[bp:bp + 64, m * 64:(m + 1) * 64], lhsT=Khtm[:, hsn], rhs=Vtm[:, hsn], start=False, stop=True)][-1],
                             ck("UT") + [(tmk, 0), (tmk, 1), (tmk, 2)], [pzk])
                    for m in range(MT):
                        A("dve", lambda e, pz=pz, m=m, c=c: e.scalar_tensor_tensor(out=ST[:, m, :], in0=ST[:, m, :], scalar=F[PP][:, m, c * C + C - 1:c * C + C],
                                                                                 in1=pz[:, m * 64:(m + 1) * 64], op0=ALU.mult, op1=ALU.add),
                          ["ST", pzk, ("F", PP, m)], ["ST"])
                    o3 = csx["o"][:, :].rearrange("p (h v) -> p h v", v=64)
                    cok = [("c_o", 0), ("c_o", 1)]
                    A("dve", lambda e, o3=o3: e.tensor_reduce(out=st["s1"][:, :], in_=o3, axis=AX.X, op=ALU.add), cok, ["st_s1"])
                    A("dve", lambda e: e.tensor_tensor(out=csx["sq"][:, :], in0=csx["o"][:, :], in1=csx["o"][:, :], op=ALU.mult), cok, ["c_sq"])
                    A("dve", lambda e: e.tensor_reduce(out=st["s2"][:, :], in_=csx["sq"][:, :].rearrange("p (h v) -> p h v", v=64), axis=AX.X, op=ALU.add), ["c_sq"], ["st_s2"])
                    A("dve", lambda e: e.tensor_scalar(out=st["mean"][:, :], in0=st["s1"][:, :], scalar1=1.0 / 64, scalar2=None, op0=ALU.mult), ["st_s1"], ["st_mean"])
                    A("dve", lambda e: e.tensor_tensor(out=st["var"][:, :], in0=st["mean"][:, :], in1=st["mean"][:, :], op=ALU.mult), ["st_mean"], ["st_var"])
                    A("dve", lambda e: e.scalar_tensor_tensor(out=st["var"][:, :], in0=st["s2"][:, :], scalar=1.0 / 64, in1=st["var"][:, :], op0=ALU.mult, op1=ALU.subtract),
                      ["st_s2", "st_var"], ["st_var"])
                    A("dve", lambda e: e.tensor_scalar(out=st["var"][:, :], in0=st["var"][:, :], scalar1=RW_GN_EPS, scalar2=None, op0=ALU.add), ["st_var"], ["st_var"])
                    A("act", lambda e: e.activation(out=st["rstd"][:, :], in_=st["var"][:, :], func=AF.Sqrt), ["st_var"], ["st_rstd"])
                    A("dve", lambda e: e.reciprocal(out=st["rstd"][:, :], in_=st["rstd"][:, :]), ["st_rstd"], ["st_rstd"])
                    on3 = csx["on"][:, :].rearrange("p (h v) -> p h v", v=64)
                    A("dve", lambda e, o3=o3, on3=on3: e.tensor_tensor(out=on3, in0=o3, in1=st["mean"][:, :].unsqueeze(2).broadcast_to([64, 8, 64]), op=ALU.subtract),
                      cok + ["st_mean"], ["c_on"])
                    A("dve", lambda e, on3=on3: e.tensor_tensor(out=on3, in0=on3, in1=st["rstd"][:, :].unsqueeze(2).broadcast_to([64, 8, 64]), op=ALU.mult),
                      ["c_on", "st_rstd"], ["c_on"])
                    pz, pzk = nps()
                    A("pe", lambda e, pz=pz: [e.transpose(out=pz[:, m * 64:(m + 1) * 64], in_=csx["on"][:, m * 128:(m + 1) * 128], identity=ident[0:64, 0:64])
                                              for m in range(MT)][-1], ["c_on", "ident"], [pzk])
                    for m in range(MT):
                        A("dve", lambda e, pz=pz, m=m, csl=csl, mo=mo: e.tensor_scalar(out=mo[:, m, csl], in0=pz[:, m * 64:(m + 1) * 64], scalar1=prm[:, LG, m:m + 1],
                                                                                      scalar2=prm[:, LB, m:m + 1], op0=ALU.mult, op1=ALU.add),
                          [pzk, "prm"], [(mok, m)])
                        A("pool", lambda e, m=m, csl=csl, mo=mo: e.tensor_tensor(out=mo[:, m, csl], in0=mo[:, m, csl], in1=F[BON][:, m, csl], op=ALU.add),
                          [(mok, m), ("F", BON, m)], [(mok, m)])
                        A("pool", lambda e, m=m, csl=csl, mo=mo: e.tensor_tensor(out=mo[:, m, csl], in0=mo[:, m, csl], in1=F[SZ][:, m, csl], op=ALU.mult),
                          [(mok, m), ("F", SZ, m)], [(mok, m)])
            start = len(P.ops)
            splits, bounds = [], []
            for c in range(NCK):
                b0 = len(P.ops)
                do_chunk(c)
                bounds.append((b0, splits[-1], len(P.ops)))
            if pipeline:
                rec = P.ops[start:]
                del P.ops[start:]
                ph1 = [rec[b0 - start:sp_ - start] for (b0, sp_, b1) in bounds]
                ph2 = [rec[sp_ - start:b1 - start] for (b0, sp_, b1) in bounds]
                P.ops.extend(ph1[0])
                for c in range(NCK):
                    a_, b_ = ph2[c], (ph1[c + 1] if c + 1 < NCK else [])
                    ia = ib = 0
                    while ia < len(a_) or ib < len(b_):
                        if ib >= len(b_) or (ia < len(a_) and ia * len(b_) <= ib * len(a_)):
                            P.ops.append(a_[ia]); ia += 1
                        else:
                            P.ops.append(b_[ib]); ib += 1
            P.dma(mTv[:, :, j * TT:(j + 1) * TT], mo[:, :, :], r=[(mok, m) for m in range(MT)], w=[("out", j)])
            outkeys.append(("out", j))

    for j in range(NT // TT if ntiles is None else ntiles):
        do_tile(j)
    P.add("sp", None, r=outkeys)
    return P.finalize() if own else None


def rwa_layer(xfull, prm):
    TT = 256
    ident = np.eye(128, dtype=np.float32)
    onesb = np.zeros((128, 128), np.float32)
    onesb[:64, :64] = 1.0; onesb[64:, 64:] = 1.0
    i_, t_ = np.meshgrid(np.arange(64), np.arange(64), indexing="ij")
    msk = np.stack([np.tile((i_ < t_), (1, 8)), np.tile((i_ <= t_), (1, 8)), np.tile((i_ > t_), (1, 8)),
                    np.tile(np.eye(64), (1, 8))], axis=1).astype(np.float32)
    rst = np.ones((128, TT), np.float32); rst[:, ::64] = 0.0
    mu = np.stack([pp(prm["mu"][i]) for i in range(6)], axis=1)
    in_maps = []
    for c in range(NCORES):
        b, hh = c // 2, c % 2
        cs_ = slice(512 * hh, 512 * hh + 512)
        pv = [prm["w0"], prm["a0"], prm["k_k"], prm["k_a"], prm["r_k"].reshape(-1), prm["lnx_g"], prm["lnx_b"]]
        prmv = np.stack([pp(v_[cs_]) for v_ in pv], axis=1)
        d = dict(xT=np.ascontiguousarray(np.concatenate([np.zeros((D, 1), np.float32), xfull[b]], axis=1)), mu=np.ascontiguousarray(mu),
                 w1=prm["w1"], a1=prm["a1"], w2=np.ascontiguousarray(prm["w2"][:, cs_]), a2=np.ascontiguousarray(prm["a2"][:, cs_]),
                 prm=np.ascontiguousarray(prmv), ident=ident, onesb=onesb, msk=np.ascontiguousarray(msk), rst=rst)
        for qi, q in enumerate("rkvz"):
            d["w_" + q] = np.ascontiguousarray(prm["w_rkvz"][qi][:, cs_])
        in_maps.append(d)
    res = run("rwa", build_rwa, in_maps)
    return [np.ascontiguousarray(np.concatenate([res[2 * b]["mT"], res[2 * b + 1]["mT"]], axis=0)) for b in range(4)]


def build_outb(NT=2048, TT=256, sel=False, P=None):
    own = P is None
    P = P or Prog()
    xT = P.dram("xT", [D, NT])
    mT_d = P.dram("mT", [D, NT * (2 if sel else 1)])
    pT = P.dram("pT", [256, NT])
    w_out_d = P.dram("w_out", [D, D])
    w_pg_d = P.dram("w_pg", [D, D])
    w_pe_d = P.dram("w_pe", [256, D])
    lng_d = P.dram("ln_g", [128, 8])
    lnb_d = P.dram("ln_b", [128, 8])
    oT = P.dram("oT", [D, NT], out=True)
    P.pool("wst", 2, [128, 1024], F32)
    P.pool("ps", 6, [128, 512], F32, psum=True)
    P.pool("psA", 2, [128, 512], F32, psum=True)
    w_out = load_weight_bf16(P, "w_out", w_out_d, D, D, "wst")
    w_pg, w_pe, ones, lng, lnb = tail_setup(P, TT, w_pg_d, w_pe_d, lng_d, lnb_d)
    if sel:
        P.pool("g2", 1, [128, 8, TT], F32)
        selt = P.sb("sel", [128, 2], F32)
        P.dma(selt[:, :], P.dram("sel", [128, 2]), w=["sel"])
    P.pool("x", 2, [128, 8, TT], F32)
    P.pool("g", 2, [128, 8, TT], F32)
    P.pool("p", 2, [128, 2, TT], F32)
    P.pool("pb", 2, [128, 2, TT], BF16)
    P.pool("m", 2, [128, 8, TT], BF16)
    P.pool("o", 2, [128, 8, TT], F32)
    xTv = xT.rearrange("(k p) t -> p k t", p=128)
    mTv = mT_d.rearrange("(k p) t -> p k t", p=128)
    pTv = pT.rearrange("(k p) t -> p k t", p=128)
    oTv = oT.rearrange("(k p) t -> p k t", p=128)
    outkeys = []

    def do_tile(j):
        sl = slice(j * TT, (j + 1) * TT)
        x, xk = P.nxt("x"); g, gk = P.nxt("g"); pt, ptk = P.nxt("p"); pb, pbk = P.nxt("pb"); m, mk = P.nxt("m")
        P.dma(x[:, :, :], xTv[:, :, sl], w=[xk])
        P.dma(g[:, :, :], mTv[:, :, sl], w=[gk])
        if sel:
            g2, g2k = P.nxt("g2")
            P.dma(g2[:, :, :], mTv[:, :, NT + j * TT:NT + (j + 1) * TT], w=[g2k])
            P.add("pool", lambda e: e.tensor_scalar(out=g[:, :, :], in0=g[:, :, :], scalar1=selt[:, 0:1], scalar2=None, op0=ALU.mult),
                  r=[gk, "sel"], w=[gk])
            P.add("dve", lambda e: e.scalar_tensor_tensor(out=g[:, :, :], in0=g2[:, :, :], scalar=selt[:, 1:2], in1=g[:, :, :],
                                                          op0=ALU.mult, op1=ALU.add), r=[gk, g2k, "sel"], w=[gk])
        P.dma(pt[:, :, :], pTv[:, :, sl], w=[ptk])
        P.add("pool", lambda e: e.tensor_copy(out=pb[:, :, :], in_=pt[:, :, :]), r=[ptk], w=[pbk])
        for k in range(8):
            P.add("pool", lambda e, k=k: e.tensor_copy(out=m[:, k, :], in_=g[:, k, :]), r=[gk], w=[(mk, k)])

        def y_ps(d):
            yp, yk = P.nxt("ps")
            P.add("pe", lambda e: [e.matmul(yp[:, :TT], lhsT=w_out[:, k, d * 128:(d + 1) * 128], rhs=m[:, k, :],
                                            start=(k == 0), stop=(k == 7)) for k in range(8)][-1],
                  r=[(mk, k) for k in range(8)] + [("w_out", k) for k in range(8)], w=[yk])
            return yp[:, :TT], yk

        o, ok = P.nxt("o")
        tail(P, TT, x, xk, y_ps, pb, pbk, w_pg, w_pe, ones, lng, lnb, o, ok, "o")
        P.dma(oTv[:, :, sl], o[:, :, :], r=[(ok, d) for d in range(8)], w=[("out", j)])
        outkeys.append(("out", j))
    for j in range(NT // TT):
        do_tile(j)
    P.add("sp", None, r=outkeys)
    return P.finalize() if own else None


def outb_layer(xs, ms, p_l, w_out, w_pg, w_pe, ln_g, ln_b):
    in_maps = []
    for c in range(NCORES):
        b, h = c // 2, c % 2
        in_maps.append(dict(xT=xs[c], mT=ms[c], pT=np.ascontiguousarray(p_l[b, h * 2048:(h + 1) * 2048].T),
                            w_out=w_out, w_pg=w_pg, w_pe=w_pe, ln_g=pp(ln_g), ln_b=pp(ln_b)))
    res = run("outb", build_outb, in_maps)
    return [r["oT"] for r in res]


_NC_CACHE = {}


def run(name, builder, in_maps):
    if name not in _NC_CACHE:
        _NC_CACHE[name] = builder()
    res = run_bass_kernel_spmd(_NC_CACHE[name], in_maps, core_ids=list(range(NCORES)))
    return res.results


def conv_layer(xs, p_l, w_in, conv_k, w_out, w_pg, w_pe, ln_g, ln_b):
    ck = np.concatenate([pp(conv_k[j]) for j in range(3)], axis=1)
    in_maps = []
    for c in range(NCORES):
        b, h = c // 2, c % 2
        halo = np.zeros((D, 2), np.float32) if h == 0 else xs[c - 1][:, -2:]
        in_maps.append(dict(xT=np.ascontiguousarray(np.concatenate([halo, xs[c]], axis=1)),
                            pT=np.ascontiguousarray(p_l[b, h * 2048:(h + 1) * 2048].T),
                            w_in=w_in, conv_k=ck, w_out=w_out, w_pg=w_pg, w_pe=w_pe, ln_g=pp(ln_g), ln_b=pp(ln_b)))
    res = run("conv", build_conv, in_maps)
    return [r["oT"] for r in res]


NTF = 4096


def build_fused():
    NT = NTF
    P = Prog()
    x1s = P.scratch("x1s", [D, NT])
    gyp = P.scratch("gyp", [D, 8, NT // 8])
    x2s = P.scratch("x2s", [D, NT + 1])
    ms = P.scratch("ms", [D, NT])
    x3s = P.scratch("x3s", [D, NT + 2])
    with P.stage("Z_"):
        z = P.sb("z", [128, 2])
        P.add("pool", lambda e: e.memset(z[:, :], 0.0), w=["z"])
        for k in range(8):
            P.dma(x2s[k * 128:(k + 1) * 128, 0:1], z[:, 0:1], r=["z"], w=[("zz", k)], slow=True)
            P.dma(x3s[k * 128:(k + 1) * 128, 0:2], z[:, 0:2], r=["z"], w=[("zz", k)], slow=True)
    with P.stage("L0_", {"oT": x1s}):
        build_conv(NT=NT, P=P)
    for hf in range(2):
        with P.stage(f"L1a{hf}_", {"xT": x1s, "gTp": gyp[512 * hf:512 * hf + 512]}):
            build_s5a(P=P)
    with P.stage("L1b_", {"xT": x1s, "gT": gyp, "oT": x2s[:, 1:NT + 1]}):
        build_s5b(NT=NT, perm_g=True, P=P)
    for hf in range(2):
        with P.stage(f"L2a{hf}_", {"xT": x2s, "mT": ms[512 * hf:512 * hf + 512]}):
            build_rwa(NT=NT, P=P)
    with P.stage("L2b_", {"xT": x2s[:, 1:NT + 1], "mT": ms, "oT": x3s[:, 2:NT + 2]}):
        build_outb(NT=NT, P=P)
    with P.stage("L3_", {"xT": x3s}):
        build_conv(NT=NT, P=P)
    return P.finalize()


PAIRS = [[0, 1], [2, 3], [4, 5], [6, 7]]


def build_fused_pair():
    NT, NH = 4096, 2048
    P = Prog()
    x1h = P.scratch("x1h", [D, NH]); x1f = P.scratch("x1f", [D, NT])
    gym = P.scratch("gym", [512, 8, NT // 8]); gyg = P.scratch("gyg", [D, 8, NT // 8])
    x2h = P.scratch("x2h", [D, NH]); x2f = P.scratch("x2f", [D, NT + 1])
    msm = P.scratch("msm", [512, NT]); msg = P.scratch("msg", [D, NT])
    x3p = P.scratch("x3p", [D, NH + 2]); hlm = P.scratch("hlm", [D, 2]); hlg = P.scratch("hlg", [2 * D, 2])
    def gather(name, src, rows, place):
        R, C = src.shape
        for q in range(R // rows):
            gq = P.scratch(f"{name}_g{q}", [2 * rows, C])
            P.cc_allgather(src[q * rows:(q + 1) * rows, :], gq, PAIRS, w=[(name, q)])
            for rk in range(2):
                P.dma(place(q, rk), gq[rk * rows:(rk + 1) * rows, :], r=[(name, q)], w=[(name, q, rk)])

    with P.stage("Z_"):
        z = P.sb("z", [128, 2])
        P.add("pool", lambda e: e.memset(z[:, :], 0.0), w=["z"])
        for k in range(8):
            P.dma(x2f[k * 128:(k + 1) * 128, 0:1], z[:, 0:1], r=["z"], w=[("zz", k)], slow=True)
    with P.stage("L0_", {"oT": x1h}):
        build_conv(NT=NH, P=P)
    with P.stage("X1_"):
        gather("x1", x1h, 256, lambda q, rk: x1f[q * 256:(q + 1) * 256, rk * NH:(rk + 1) * NH])
    with P.stage("L1a_", {"xT": x1f, "gTp": gym}):
        build_s5a(P=P)
    with P.stage("X2_"):
        gyg2 = gyg.rearrange("c t n -> c (t n)")
        gather("gy", gym.rearrange("c t n -> c (t n)"), 128, lambda q, rk: gyg2[512 * rk + q * 128:512 * rk + (q + 1) * 128, :])
    with P.stage("L1b_", {"xT": x1h, "gT": gyg, "oT": x2h}):
        build_s5b(NT=NH, perm_g=True, sel=True, P=P)
    with P.stage("X3_"):
        gather("x2", x2h, 256, lambda q, rk: x2f[q * 256:(q + 1) * 256, 1 + rk * NH:1 + (rk + 1) * NH])
    with P.stage("L2a_", {"xT": x2f, "mT": msm}):
        build_rwa(NT=NT, P=P)
    with P.stage("X4_"):
        gather("ms", msm, 128, lambda q, rk: msg[512 * rk + q * 128:512 * rk + (q + 1) * 128, :])
    with P.stage("L2b_", {"xT": x2h, "mT": msg, "oT": x3p[:, 2:NH + 2]}):
        build_outb(NT=NH, sel=True, P=P)
    with P.stage("X5_"):
        P.dma(hlm[:, :], x3p[:, NH:NH + 2], w=["hlm"])
        P.cc_allgather(hlm, hlg, PAIRS, r=["hlm"], w=["hlg"])
        ht = P.sb("ht", [128, 8, 2]); selt = P.sb("sel", [128, 2])
        P.dma(selt[:, :], P.dram("sel", [128, 2]), w=["sel"])
        P.dma(ht[:, :, :], hlg[0:D, :].rearrange("(k p) t -> p k t", p=128), r=["hlg"], w=["ht"])
        P.add("dve", lambda e: e.tensor_scalar(out=ht[:, :, :], in0=ht[:, :, :], scalar1=selt[:, 1:2], scalar2=None, op0=ALU.mult),
              r=["ht", "sel"], w=["ht"])
        P.dma(x3p[:, 0:2].rearrange("(k p) t -> p k t", p=128), ht[:, :, :], r=["ht"], w=["x3halo"])
    with P.stage("L3_", {"xT": x3p}):
        build_conv(NT=NH, P=P)
    return P.finalize()


def kernel(**inp):
    inp = {k: np.asarray(v, np.float32) for k, v in inp.items()}
    x = inp["x"]
    sp = {k[4:]: inp[k][0] for k in inp if k.startswith("ssm_")}
    rp = {k[5:]: inp[k][0] for k in inp if k.startswith("rwkv_")}
    shared = {}

    def put(dst, prefix, d):
        for k, v in d.items():
            dst[prefix + k] = np.ascontiguousarray(v, dtype=np.float32)
    for (pre, i, j) in (("L0_", 0, 0), ("L3_", 3, 1)):
        put(shared, pre, dict(w_in=inp["conv_w_in"][j], conv_k=np.concatenate([pp(inp["conv_k"][j][t]) for t in range(3)], axis=1),
                              w_out=inp["conv_w_out"][j], **_tail_in(inp, i)))
    put(shared, "L1b_", dict(w_z=sp["w_in"][:, D:], w_glu=sp["w_glu"], b_glu=pp(sp["b_glu"]), w_out=sp["w_out"], **_tail_in(inp, 1)))
    put(shared, "L2b_", dict(w_out=rp["w_out"], **_tail_in(inp, 2)))
    halfp = [{}, {}]
    for hf in range(2):
        put(halfp[hf], "L1a_", _s5a_in(sp, hf))
        put(halfp[hf], "L2a_", _rwa_in(rp, hf))
    in_maps = []
    for c in range(NCORES):
        b, h = c // 2, c % 2
        ts = slice(2048 * h, 2048 * h + 2048)
        d = dict(shared)
        d.update(halfp[h])
        halo = np.zeros((D, 2), np.float32) if h == 0 else x[b, 2046:2048].T
        d["L0_xT"] = np.ascontiguousarray(np.concatenate([halo, x[b, ts].T], axis=1))
        for (pre, i) in (("L0_", 0), ("L1b_", 1), ("L2b_", 2), ("L3_", 3)):
            d[pre + "pT"] = np.ascontiguousarray(inp["p"][i][b, ts].T)
        selv = np.zeros((128, 2), np.float32); selv[:, h] = 1.0
        for pre in ("L1b_", "L2b_", "X5_"):
            d[pre + "sel"] = selv
        in_maps.append(d)
    res = run("fusedp", build_fused_pair, in_maps)
    out = np.zeros_like(x)
    for c in range(NCORES):
        out[c // 2, (c % 2) * 2048:(c % 2 + 1) * 2048] = res[c]["L3_oT"].T
    return out


def _tail_in(inp, i):
    return dict(w_pg=inp["ple_gate"][i], w_pe=inp["ple_proj"][i], ln_g=pp(inp["ln_g"][i]), ln_b=pp(inp["ln_b"][i]))


def _s5a_in(prm, gh):
    NG = 32
    ident = np.eye(128, dtype=np.float32)
    j2 = np.zeros((128, 128), np.float32)
    j2[np.arange(64), np.arange(64) + 64] = 1.0
    j2[np.arange(64) + 64, np.arange(64)] = -1.0
    msk = np.zeros((128, 10), np.float32)
    msk[:64, 0] = 1.0; msk[64:, 1] = 1.0
    identg = np.zeros((128, 8, 16), np.float32)
    for gl in range(8):
        msk[16 * gl:16 * gl + 16, 2 + gl] = 1.0
        for h in range(16):
            identg[16 * gl + h, gl, h] = 1.0
    gs = slice(NG * gh, NG * gh + NG)

    def cl(a):
        a = np.broadcast_to(a.reshape(NG // 8, 8, 1, 64), (NG // 8, 8, 16, 64))
        return np.ascontiguousarray(a.transpose(1, 2, 0, 3).reshape(128, NG // 8, 64))

    def clb(a):
        a = a.reshape(NG // 8, 8, 64, 16)
        return np.ascontiguousarray(a.transpose(1, 3, 0, 2).reshape(128, NG // 8, 64))

    def plv(a):
        a = np.broadcast_to(a.T.reshape(1, 64, NG, 1), (2, 64, NG, 16))
        return np.ascontiguousarray(a.reshape(128, NG, 16))

    def plb(a):
        a = np.broadcast_to(a.transpose(1, 0, 2)[None], (2, 64, NG, 16))
        return np.ascontiguousarray(a.reshape(128, NG, 16))

    def plc(a):
        a = np.broadcast_to(a.transpose(2, 0, 1)[None], (2, 64, NG, 16))
        return np.ascontiguousarray(a.reshape(128, NG, 16))
    ldt = np.broadcast_to(prm["log_dt"][gs][:, None], (NG, 64))
    return dict(
        w_u=np.ascontiguousarray(prm["w_in"][:, 512 * gh:512 * gh + 512]),
        lamre_cl=cl(prm["lam_re"][gs]), lamim_cl=cl(prm["lam_im"][gs]), ldt_cl=cl(ldt),
        bre_cl=clb(prm["b_re"][gs]), bim_cl=clb(prm["b_im"][gs]),
        lamre_pl=plv(prm["lam_re"][gs]), lamim_pl=plv(prm["lam_im"][gs]), ldt_pl=plv(ldt),
        bre_pl=plb(prm["b_re"][gs]), bim_pl=plb(prm["b_im"][gs]),
        cre_pl=plc(prm["c_re"][gs]), cim_pl=plc(prm["c_im"][gs]),
        dsk=pp(prm["d"][512 * gh:512 * gh + 512]), ident=ident, j2=j2, msk=msk, identg=identg)


def _rwa_in(prm, hh):
    TT = 256
    ident = np.eye(128, dtype=np.float32)
    onesb = np.zeros((128, 128), np.float32)
    onesb[:64, :64] = 1.0; onesb[64:, 64:] = 1.0
    i_, t_ = np.meshgrid(np.arange(64), np.arange(64), indexing="ij")
    msk = np.stack([np.tile((i_ < t_), (1, 8)), np.tile((i_ <= t_), (1, 8)), np.tile((i_ > t_), (1, 8)),
                    np.tile(np.eye(64), (1, 8))], axis=1).astype(np.float32)
    rst = np.ones((128, TT), np.float32); rst[:, ::64] = 0.0
    mu = np.stack([pp(prm["mu"][i]) for i in range(6)], axis=1)
    cs_ = slice(512 * hh, 512 * hh + 512)
    pv = [prm["w0"], prm["a0"], prm["k_k"], prm["k_a"], prm["r_k"].reshape(-1), prm["lnx_g"], prm["lnx_b"]]
    prmv = np.stack([pp(v_[cs_]) for v_ in pv], axis=1)
    d = dict(mu=np.ascontiguousarray(mu), w1=prm["w1"], a1=prm["a1"], w2=np.ascontiguousarray(prm["w2"][:, cs_]),
             a2=np.ascontiguousarray(prm["a2"][:, cs_]), prm=np.ascontiguousarray(prmv), ident=ident, onesb=onesb,
             msk=np.ascontiguousarray(msk), rst=rst)
    for qi, q in enumerate("rkvz"):
        d["w_" + q] = np.ascontiguousarray(prm["w_rkvz"][qi][:, cs_])
    return d


def kernel_fused_b(**inp):
    inp = {k: np.asarray(v, np.float32) for k, v in inp.items()}
    x = inp["x"]
    sp = {k[4:]: inp[k][0] for k in inp if k.startswith("ssm_")}
    rp = {k[5:]: inp[k][0] for k in inp if k.startswith("rwkv_")}
    shared = {}

    def put(prefix, d):
        for k, v in d.items():
            shared[prefix + k] = np.ascontiguousarray(v, dtype=np.float32)
    for (pre, i, j) in (("L0_", 0, 0), ("L3_", 3, 1)):
        put(pre, dict(w_in=inp["conv_w_in"][j], conv_k=np.concatenate([pp(inp["conv_k"][j][t]) for t in range(3)], axis=1),
                      w_out=inp["conv_w_out"][j], **_tail_in(inp, i)))
    for hf in range(2):
        put(f"L1a{hf}_", _s5a_in(sp, hf))
        put(f"L2a{hf}_", _rwa_in(rp, hf))
    put("L1b_", dict(w_z=sp["w_in"][:, D:], w_glu=sp["w_glu"], b_glu=pp(sp["b_glu"]), w_out=sp["w_out"], **_tail_in(inp, 1)))
    put("L2b_", dict(w_out=rp["w_out"], **_tail_in(inp, 2)))
    in_maps = []
    for c in range(NCORES):
        b = c % 4
        d = dict(shared)
        d["L0_xT"] = np.ascontiguousarray(np.concatenate([np.zeros((D, 2), np.float32), x[b].T], axis=1))
        for (pre, i) in (("L0_", 0), ("L1b_", 1), ("L2b_", 2), ("L3_", 3)):
            d[pre + "pT"] = np.ascontiguousarray(inp["p"][i][b].T)
        in_maps.append(d)
    res = run("fused", build_fused, in_maps)
    return np.ascontiguousarray(np.stack([res[b]["L3_oT"].T for b in range(4)]))


def _halves(full):
    return [np.ascontiguousarray(full[c // 2][:, (c % 2) * 2048:(c % 2 + 1) * 2048]) for c in range(NCORES)]


def _fulls(halves):
    return [np.ascontiguousarray(np.concatenate([halves[2 * b], halves[2 * b + 1]], axis=1)) for b in range(4)]


def kernel_unfused(**inp):
    inp = {k: np.asarray(v, np.float32) for k, v in inp.items()}
    x = inp["x"]
    xs = [np.ascontiguousarray(x[c // 2, (c % 2) * 2048:(c % 2 + 1) * 2048].T) for c in range(NCORES)]
    for i in range(DEPTH):
        kind, j = i % 3, i // 3
        tailw = (inp["ple_gate"][i], inp["ple_proj"][i], inp["ln_g"][i], inp["ln_b"][i])
        if kind == 0:
            xs = conv_layer(xs, inp["p"][i], inp["conv_w_in"][j], inp["conv_k"][j], inp["conv_w_out"][j], *tailw)
        elif kind == 1:
            prm = {k[4:]: inp[k][j] for k in inp if k.startswith("ssm_")}
            gy = s5a_layer(_fulls(xs), prm)
            xs = s5b_layer(xs, _halves(gy), inp["p"][i], prm["w_in"], prm["w_glu"], prm["b_glu"], prm["w_out"], *tailw)
        else:
            prm = {k[5:]: inp[k][j] for k in inp if k.startswith("rwkv_")}
            ms = rwa_layer(_fulls(xs), prm)
            xs = outb_layer(xs, _halves(ms), inp["p"][i], prm["w_out"], *tailw)
    out = np.zeros_like(x)
    for c in range(NCORES):
        out[c // 2, (c % 2) * 2048:(c % 2 + 1) * 2048] = xs[c].T
    return out
```

```python
from contextlib import ExitStack

import numpy as np
import concourse.bass as bass
import concourse.mybir as mybir
from concourse.bass_utils import run_bass_kernel_spmd

F32 = mybir.dt.float32
BF16 = mybir.dt.bfloat16
AF = mybir.ActivationFunctionType
ALU = mybir.AluOpType
AX = mybir.AxisListType

D = 1024
DEPTH = 4
DN_ALPHA = (2 * DEPTH) ** 0.25
LN_EPS = 1e-5
NCORES = 8


class Prog:
    COMPUTE = ("pe", "act", "dve", "pool")
    BLOCKNAME = {"pe": "tensor", "act": "scalar", "dve": "vector", "pool": "gpsimd", "sp": "sync"}
    NDMASEM = 12

    def __init__(self):
        self.nc = bass.Bass("TRN2", target_bir_lowering=False)
        self.ops = []
        self.stack = ExitStack()
        self._rr = {}
        self.prefix = ""
        self.bound = {}
        self.psum_keys = set()

    def dram(self, name, shape, dtype=F32, out=False):
        name = self.prefix + name
        if name in self.bound:
            ap = self.bound[name]
            assert list(ap.shape) == list(shape), (name, ap.shape, shape)
            return ap
        return self.nc.dram_tensor(name, list(shape), dtype,
                                   kind="ExternalOutput" if out else "ExternalInput").ap()

    def scratch(self, name, shape, dtype=F32):
        return self.nc.dram_tensor(name, list(shape), dtype).ap()

    def stage(self, prefix, bind=None):
        prog = self

        class _S:
            def __enter__(s_):
                s_.old = (prog.stack, prog.prefix)
                prog.stack = ExitStack()
                prog.prefix = prefix
                for k, v in (bind or {}).items():
                    prog.bound[prefix + k] = v

            def __exit__(s_, *a):
                prog.stack.close()
                prog.stack, prog.prefix = s_.old
                prog.ops.append(dict(barrier=True, eng=None, fn=None, r=[], w=[], dma=False))
                return False
        return _S()

    def sb(self, name, shape, dtype=F32):
        return self.stack.enter_context(self.nc.sbuf_tensor("s_" + self.prefix + name, list(shape), dtype))

    def ps(self, name, shape, dtype=F32):
        return self.stack.enter_context(self.nc.psum_tensor("p_" + self.prefix + name, list(shape), dtype))

    def pool(self, name, n, shape, dtype=F32, psum=False):
        mk = self.ps if psum else self.sb
        if psum:
            shape = [128, 512]
            self.psum_keys |= {f"{self.prefix}{name}{i}" for i in range(n)}
        tiles = [(mk(f"{name}{i}", shape, dtype), f"{self.prefix}{name}{i}") for i in range(n)]
        self._rr[name] = [tiles, 0]
        return tiles

    def nxt(self, name):
        ent = self._rr[name]
        t = ent[0][ent[1] % len(ent[0])]
        ent[1] += 1
        return t

    def add(self, eng, fn, r=(), w=(), dma=False):
        self.ops.append(dict(eng=eng, fn=fn, r=list(r), w=list(w), dma=dma))

    def dma(self, out, in_, r=(), w=(), q="sp", slow=False):
        if slow:
            self.add(q, lambda e: e.dma_start(out=out, in_=in_, allow_slow_non_contiguous=True), r=r, w=w, dma=True)
        else:
            self.add(q, lambda e: e.dma_start(out=out, in_=in_), r=r, w=w, dma=True)

    def cc_allgather(self, in_ap, out_ap, groups, r=(), w=()):
        self.ops.append(dict(eng="pool", fn=lambda e: e.collective_compute("AllGather", ALU.bypass, replica_groups=groups,
                                                                            ins=[in_ap.opt()], outs=[out_ap.opt()]),
                             r=list(r), w=list(w), dma=False, cc=True))

    def finalize(self):
        nc, ops = self.nc, self.ops
        last_w, readers = {}, {}
        bar = set()
        ccs = []
        last_c, last_d, dslot = {}, {}, {}
        for i, op in enumerate(ops):
            if op.get("barrier"):
                bar = set(last_c.values()) | set(last_d.values()) | set(ccs)
                ccs = []
                op["deps"] = set()
                continue
            if op.get("cc"):
                ccs.append(i)
            elif op["dma"]:
                sl_ = dslot.get(op["eng"], 0)
                dslot[op["eng"]] = sl_ + 1
                last_d[(op["eng"], sl_ % self.NDMASEM)] = i
            elif op["fn"] is not None:
                last_c[op["eng"]] = i
            deps = set(bar)
            raw = set()
            pk = self.psum_keys
            for k in op["r"]:
                if k in last_w:
                    deps.add(last_w[k]); raw.add(last_w[k])
                if k in pk:
                    for j in readers.get(k, ()):
                        if ops[j]["eng"] != op["eng"]:
                            deps.add(j)
            for k in op["w"]:
                if k in last_w:
                    deps.add(last_w[k])
                for j in readers.get(k, ()):
                    deps.add(j)
            deps.discard(i)
            op["deps"] = set(deps)
            if op["fn"] is not None:
                for k in op["r"]:
                    readers.setdefault(k, []).append(i)
            for k in op["w"]:
                last_w[k] = i
                readers[k] = []
        needed = set()
        for op in ops:
            needed |= op["deps"]
        engines = list(self.COMPUTE) + ["sp"]
        SEMMAX = 4000
        ccount = {e: 0 for e in engines}
        dcount = {}
        dn = {e: 0 for e in engines}
        semkeys = set()
        for i, op in enumerate(ops):
            e = op["eng"]
            if op.get("barrier"):
                op["sig"] = None
                continue
            if op.get("cc"):
                op["sig"] = (("cc", i), 1)
                semkeys.add(op["sig"][0])
                continue
            if op["dma"]:
                s = dn[e] % self.NDMASEM
                dn[e] += 1
                c = dcount.get((e, s), 0)
                per = SEMMAX // 16
                op["dprev"] = (("d", e, s, (c - 1) // per), 16 * ((c - 1) % per + 1)) if c > 0 else None
                dcount[(e, s)] = c + 1
                op["sig"] = (("d", e, s, c // per), 16 * (c % per + 1))
                semkeys.add(op["sig"][0])
            elif i in needed:
                c = ccount[e]
                ccount[e] += 1
                op["sig"] = (("c", e, c // SEMMAX), c % SEMMAX + 1)
                semkeys.add(op["sig"][0])
            else:
                op["sig"] = None
        self.nwait = {}
        with ExitStack() as st:
            sems = {k: st.enter_context(nc.semaphore("s_" + "_".join(str(x) for x in k))) for k in sorted(semkeys)}
            block = st.enter_context(nc.Block())
            for e in engines:
                mine = [op for op in ops if op["eng"] == e and not op.get("barrier")]
                if not mine:
                    continue

                def body(eng, mine=mine, eng_name=e):
                    known = {}

                    def wait(sig):
                        if sig is None:
                            return
                        key, val = sig
                        if known.get(key, 0) < val:
                            eng.wait_ge(sems[key], val)
                            known[key] = val
                            self.nwait[eng_name] = self.nwait.get(eng_name, 0) + 1

                    for op in mine:
                        need = {}
                        for j in op["deps"]:
                            sg = ops[j]["sig"]
                            if sg is not None and need.get(sg[0], 0) < sg[1]:
                                need[sg[0]] = sg[1]
                        if op["dma"] and op["dprev"] is not None:
                            sg = op["dprev"]
                            if need.get(sg[0], 0) < sg[1]:
                                need[sg[0]] = sg[1]
                        for kk_, vv_ in sorted(need.items(), key=lambda t: str(t[0])):
                            wait((kk_, vv_))
                        if op["fn"] is None:
                            continue
                        ins = op["fn"](eng)
                        if op["sig"] is not None:
                            ins.then_inc(sems[op["sig"][0]], 16 if op["dma"] else 1)

                getattr(block, self.BLOCKNAME[e])(body)
        self.counts = dict(ccount)
        self.stack.close()
        return nc


def pp(v):
    v = np.asarray(v, np.float32).reshape(-1, 128)
    return np.ascontiguousarray(v.T)


def load_weight_bf16(P, name, w_dram, K, N, stage_pool):
    kt = K // 128
    wt = P.sb(name, [128, kt, N], BF16)
    for k in range(kt):
        for c0 in range(0, N, 1024):
            c1 = min(N, c0 + 1024)
            st, sk = P.nxt(stage_pool)
            P.dma(st[:, : c1 - c0], w_dram[k * 128:(k + 1) * 128, c0:c1], w=[sk])
            P.add("pool", lambda e, st=st, k=k, c0=c0, c1=c1: e.tensor_copy(out=wt[:, k, c0:c1], in_=st[:, : c1 - c0]),
                  r=[sk], w=[(name, k)])
    return wt


def tail(P, TT, x_res, xk, y_ps_fn, pT_bf, pk, w_pg, w_pe, ones, lng, lnb, out_tile, ok, tag):
    r, rk = P.nxt("r")
    rb, rbk = P.nxt("rb")
    for d in range(8):
        yp, yk = y_ps_fn(d)
        P.add("dve", lambda e, d=d, yp=yp: e.scalar_tensor_tensor(out=r[:, d, :], in0=x_res[:, d, :], scalar=float(DN_ALPHA),
                                                                 in1=yp, op0=ALU.mult, op1=ALU.add),
              r=[xk, yk], w=[(rk, d)])
        P.add("pool", lambda e, d=d: e.tensor_copy(out=rb[:, d, :], in_=r[:, d, :]), r=[(rk, d)], w=[(rbk, d)])
    sq, sqk = P.nxt("sq")
    mean_ps, mk = P.nxt("psA")
    msq_ps, qk = P.nxt("psA")
    for d in range(8):
        gp, gk = P.nxt("ps")
        P.add("pe", lambda e, d=d, gp=gp: [e.matmul(gp[:, :TT], lhsT=w_pg[:, k, d * 128:(d + 1) * 128], rhs=rb[:, k, :],
                                                    start=(k == 0), stop=(k == 7)) for k in range(8)][-1],
              r=[(rbk, k) for k in range(8)] + [("w_pg", k) for k in range(8)], w=[gk])
        sg, sgk = P.nxt("tmp")
        P.add("act", lambda e, gp=gp, sg=sg: e.activation(out=sg[:, :TT], in_=gp[:, :TT], func=AF.Sigmoid), r=[gk], w=[sgk])
        ep, ek = P.nxt("ps")
        P.add("pe", lambda e, d=d, ep=ep: [e.matmul(ep[:, :TT], lhsT=w_pe[:, k, d * 128:(d + 1) * 128], rhs=pT_bf[:, k, :],
                                                    start=(k == 0), stop=(k == 1)) for k in range(2)][-1],
              r=[pk] + [("w_pe", k) for k in range(2)], w=[ek])
        P.add("dve", lambda e, ep=ep, sg=sg: e.tensor_tensor(out=sg[:, :TT], in0=ep[:, :TT], in1=sg[:, :TT], op=ALU.mult),
              r=[ek, sgk], w=[sgk])
        P.add("dve", lambda e, d=d, sg=sg: e.tensor_tensor(out=r[:, d, :], in0=r[:, d, :], in1=sg[:, :TT], op=ALU.add),
              r=[sgk, (rk, d)], w=[(rk, d)])
        P.add("act", lambda e, d=d: e.activation(out=sq[:, d, :], in_=r[:, d, :], func=AF.Square), r=[(rk, d)], w=[(sqk, d)])
    P.add("pe", lambda e: [e.matmul(mean_ps[:, :TT], lhsT=ones[:, :], rhs=r[:, k, :], start=(k == 0), stop=(k == 7))
                           for k in range(8)][-1], r=[(rk, k) for k in range(8)] + ["ones"], w=[mk])
    P.add("pe", lambda e: [e.matmul(msq_ps[:, :TT], lhsT=ones[:, :], rhs=sq[:, k, :], start=(k == 0), stop=(k == 7))
                           for k in range(8)][-1], r=[(sqk, k) for k in range(8)] + ["ones"], w=[qk])
    mean, mnk = P.nxt("st")
    rstd, rsk = P.nxt("st")
    P.add("act", lambda e: e.activation(out=mean[:, :TT], in_=mean_ps[:, :TT], func=AF.Copy), r=[mk], w=[mnk])
    P.add("dve", lambda e: e.tensor_tensor(out=rstd[:, :TT], in0=mean[:, :TT], in1=mean[:, :TT], op=ALU.mult), r=[mnk], w=[rsk])
    P.add("dve", lambda e: e.tensor_tensor(out=rstd[:, :TT], in0=msq_ps[:, :TT], in1=rstd[:, :TT], op=ALU.subtract),
          r=[qk, rsk], w=[rsk])
    P.add("dve", lambda e: e.tensor_scalar(out=rstd[:, :TT], in0=rstd[:, :TT], scalar1=float(LN_EPS), scalar2=None, op0=ALU.add),
          r=[rsk], w=[rsk])
    P.add("act", lambda e: e.activation(out=rstd[:, :TT], in_=rstd[:, :TT], func=AF.Sqrt), r=[rsk], w=[rsk])
    P.add("dve", lambda e: e.reciprocal(out=rstd[:, :TT], in_=rstd[:, :TT]), r=[rsk], w=[rsk])
    for d in range(8):
        P.add("dve", lambda e, d=d: e.tensor_tensor(out=r[:, d, :], in0=r[:, d, :], in1=mean[:, :TT], op=ALU.subtract),
              r=[(rk, d), mnk], w=[(rk, d)])
        P.add("dve", lambda e, d=d: e.tensor_tensor(out=r[:, d, :], in0=r[:, d, :], in1=rstd[:, :TT], op=ALU.mult),
              r=[(rk, d), rsk], w=[(rk, d)])
        P.add("dve", lambda e, d=d: e.tensor_scalar(out=out_tile[:, d, :], in0=r[:, d, :], scalar1=lng[:, d:d + 1],
                                                    scalar2=lnb[:, d:d + 1], op0=ALU.mult, op1=ALU.add),
              r=[(rk, d), "lnp"], w=[(ok, d)])


def tail_setup(P, TT, w_pg_d, w_pe_d, lng_d, lnb_d):
    w_pg = load_weight_bf16(P, "w_pg", w_pg_d, 1024, 1024, "wst")
    w_pe = load_weight_bf16(P, "w_pe", w_pe_d, 256, 1024, "wst")
    ones = P.sb("ones", [128, 128], F32)
    P.add("pool", lambda e: e.memset(ones[:, :], 1.0 / D), w=["ones"])
    lng = P.sb("lng", [128, 8], F32)
    lnb = P.sb("lnb", [128, 8], F32)
    P.dma(lng[:, :], lng_d, w=["lnp"])
    P.dma(lnb[:, :], lnb_d, w=["lnp"])
    P.pool("r", 1, [128, 8, TT], F32)
    P.pool("rb", 1, [128, 8, TT], BF16)
    P.pool("sq", 1, [128, 8, TT], F32)
    P.pool("st", 2, [128, TT], F32)
    P.pool("tmp", 6, [128, TT + 2], F32)
    return w_pg, w_pe, ones, lng, lnb


def build_conv(NT=2048, TT=256, P=None):
    own = P is None
    P = P or Prog()
    xT = P.dram("xT", [D, NT + 2])
    pT = P.dram("pT", [256, NT])
    w_in_d = P.dram("w_in", [D, 4 * D])
    ck_d = P.dram("conv_k", [128, 24])
    w_out_d = P.dram("w_out", [D, D])
    w_pg_d = P.dram("w_pg", [D, D])
    w_pe_d = P.dram("w_pe", [256, D])
    lng_d = P.dram("ln_g", [128, 8])
    lnb_d = P.dram("ln_b", [128, 8])
    oT = P.dram("oT", [D, NT], out=True)

    P.pool("wst", 2, [128, 1024], F32)
    P.pool("ps", 6, [128, 512], F32, psum=True)
    P.pool("psA", 2, [128, 512], F32, psum=True)
    w_in = load_weight_bf16(P, "w_in", w_in_d, D, 4 * D, "wst")
    w_out = load_weight_bf16(P, "w_out", w_out_d, D, D, "wst")
    w_pg, w_pe, ones, lng, lnb = tail_setup(P, TT, w_pg_d, w_pe_d, lng_d, lnb_d)
    ck = P.sb("ck", [128, 24], F32)
    P.dma(ck[:, :], ck_d, w=["ck"])
    P.pool("x", 2, [128, 8, TT + 2], F32)
    P.pool("xb", 2, [128, 8, TT + 2], BF16)
    P.pool("p", 2, [128, 2, TT], F32)
    P.pool("pb", 2, [128, 2, TT], BF16)
    P.pool("m", 1, [128, 8, TT], BF16)
    P.pool("o", 2, [128, 8, TT], F32)
    xTv = xT.rearrange("(k p) t -> p k t", p=128)
    pTv = pT.rearrange("(k p) t -> p k t", p=128)
    oTv = oT.rearrange("(k p) t -> p k t", p=128)
    W = TT + 2
    outkeys = []
    for j in range(NT // TT):
        x, xk = P.nxt("x")
        xb, xbk = P.nxt("xb")
        pt, ptk = P.nxt("p")
        pb, pbk = P.nxt("pb")
        P.dma(x[:, :, :], xTv[:, :, j * TT: j * TT + W], w=[xk])
        P.dma(pt[:, :, :], pTv[:, :, j * TT:(j + 1) * TT], w=[ptk])
        P.add("pool", lambda e, x=x, xb=xb: e.tensor_copy(out=xb[:, :, :], in_=x[:, :, :]), r=[xk], w=[xbk])
        P.add("pool", lambda e, pt=pt, pb=pb: e.tensor_copy(out=pb[:, :, :], in_=pt[:, :, :]), r=[ptk], w=[pbk])
        m, mk = P.nxt("m")
        for et in range(8):
            pss = {}
            for qi, q in enumerate(("bg", "cg", "h", "z")):
                pz, pzk = P.nxt("ps")
                c0 = qi * D + et * 128
                lo = 0 if q in ("cg", "h") else 2
                P.add("pe", lambda e, pz=pz, c0=c0, lo=lo, xb=xb: [
                    e.matmul(pz[:, : W - lo], lhsT=w_in[:, k, c0:c0 + 128], rhs=xb[:, k, lo:W], start=(k == 0), stop=(k == 7))
                    for k in range(8)][-1], r=[xbk] + [("w_in", k) for k in range(8)], w=[pzk])
                pss[q] = (pz, pzk)
            cgs, cgk = P.nxt("tmp")
            P.add("act", lambda e, cgs=cgs, a=pss["cg"][0]: e.activation(out=cgs[:, :W], in_=a[:, :W], func=AF.Copy),
                  r=[pss["cg"][1]], w=[cgk])
            P.add("dve", lambda e, cgs=cgs, a=pss["h"][0]: e.tensor_tensor(out=cgs[:, :W], in0=a[:, :W], in1=cgs[:, :W], op=ALU.mult),
                  r=[pss["h"][1], cgk], w=[cgk])
            sz, szk = P.nxt("tmp")
            P.add("act", lambda e, sz=sz, a=pss["z"][0]: e.activation(out=sz[:, :TT], in_=a[:, :TT], func=AF.Silu),
                  r=[pss["z"][1]], w=[szk])
            P.add("dve", lambda e, sz=sz, a=pss["bg"][0]: e.tensor_tensor(out=sz[:, :TT], in0=a[:, :TT], in1=sz[:, :TT], op=ALU.mult),
                  r=[pss["bg"][1], szk], w=[szk])
            cv, cvk = P.nxt("tmp")
            P.add("dve", lambda e, cv=cv, cgs=cgs, et=et: e.tensor_scalar(out=cv[:, :TT], in0=cgs[:, 2:W], scalar1=ck[:, 16 + et:17 + et],
                                                                         scalar2=None, op0=ALU.mult), r=[cgk, "ck"], w=[cvk])
            P.add("dve", lambda e, cv=cv, cgs=cgs, et=et: e.scalar_tensor_tensor(out=cv[:, :TT], in0=cgs[:, 1:W - 1], scalar=ck[:, 8 + et:9 + et],
                                                                                in1=cv[:, :TT], op0=ALU.mult, op1=ALU.add),
                  r=[cgk, cvk, "ck"], w=[cvk])
            P.add("dve", lambda e, cv=cv, cgs=cgs, et=et: e.scalar_tensor_tensor(out=cv[:, :TT], in0=cgs[:, 0:TT], scalar=ck[:, et:et + 1],
                                                                                in1=cv[:, :TT], op0=ALU.mult, op1=ALU.add),
                  r=[cgk, cvk, "ck"], w=[cvk])
            P.add("dve", lambda e, cv=cv, sz=sz, et=et, m=m: e.tensor_tensor(out=m[:, et, :], in0=cv[:, :TT], in1=sz[:, :TT], op=ALU.mult),
                  r=[cvk, szk], w=[(mk, et)])

        def y_ps(d, m=m, mk=mk):
            yp, yk = P.nxt("ps")
            P.add("pe", lambda e: [e.matmul(yp[:, :TT], lhsT=w_out[:, k, d * 128:(d + 1) * 128], rhs=m[:, k, :],
                                            start=(k == 0), stop=(k == 7)) for k in range(8)][-1],
                  r=[(mk, k) for k in range(8)] + [("w_out", k) for k in range(8)], w=[yk])
            return yp[:, :TT], yk

        o, ok = P.nxt("o")
        tail(P, TT, x[:, :, 2:W], xk, y_ps, pb, pbk, w_pg, w_pe, ones, lng, lnb, o, ok, "c")
        P.dma(oTv[:, :, j * TT:(j + 1) * TT], o[:, :, :], r=[(ok, d) for d in range(8)], w=[("out", j)])
        outkeys.append(("out", j))
    P.add("sp", None, r=outkeys)
    return P.finalize() if own else None


def build_s5b(NT=2048, TT=256, perm_g=False, sel=False, P=None):
    own = P is None
    P = P or Prog()
    xT = P.dram("xT", [D, NT])
    gcols = NT * (2 if sel else 1)
    gT = P.dram("gT", [D, 8, gcols // 8] if perm_g else [D, gcols])
    pT = P.dram("pT", [256, NT])
    w_z_d = P.dram("w_z", [D, D])
    w_glu_d = P.dram("w_glu", [D, D])
    bglu_d = P.dram("b_glu", [128, 8])
    w_out_d = P.dram("w_out", [D, D])
    w_pg_d = P.dram("w_pg", [D, D])
    w_pe_d = P.dram("w_pe", [256, D])
    lng_d = P.dram("ln_g", [128, 8])
    lnb_d = P.dram("ln_b", [128, 8])
    oT = P.dram("oT", [D, NT], out=True)
    P.pool("wst", 2, [128, 1024], F32)
    P.pool("ps", 6, [128, 512], F32, psum=True)
    P.pool("psA", 2, [128, 512], F32, psum=True)
    w_z = load_weight_bf16(P, "w_z", w_z_d, D, D, "wst")
    w_glu = load_weight_bf16(P, "w_glu", w_glu_d, D, D, "wst")
    w_out = load_weight_bf16(P, "w_out", w_out_d, D, D, "wst")
    w_pg, w_pe, ones, lng, lnb = tail_setup(P, TT, w_pg_d, w_pe_d, lng_d, lnb_d)
    bglu = P.sb("bglu", [128, 8], F32)
    P.dma(bglu[:, :], bglu_d, w=["bglu"])
    P.pool("x", 2, [128, 8, TT], F32)
    P.pool("xb", 2, [128, 8, TT], BF16)
    P.pool("g", 2, [128, 8, TT], F32)
    P.pool("gb", 2, [128, 8, TT], BF16)
    P.pool("p", 2, [128, 2, TT], F32)
    P.pool("pb", 2, [128, 2, TT], BF16)
    P.pool("m", 1, [128, 8, TT], BF16)
    P.pool("o", 2, [128, 8, TT], F32)
    if sel:
        P.pool("g2", 1, [128, 8, TT], F32)
        selt = P.sb("sel", [128, 2], F32)
        P.dma(selt[:, :], P.dram("sel", [128, 2]), w=["sel"])
    xTv = xT.rearrange("(k p) t -> p k t", p=128)
    gTv = gT.rearrange("(k p) t n -> p k t n", p=128) if perm_g else gT.rearrange("(k p) t -> p k t", p=128)
    pTv = pT.rearrange("(k p) t -> p k t", p=128)
    oTv = oT.rearrange("(k p) t -> p k t", p=128)
    outkeys = []
    CN = TT // 8
    gseq = (lambda ap: ap.rearrange("p (t n) -> p n t", t=8)) if perm_g else (lambda ap: ap)
    sseq = (lambda ap: ap.rearrange("p (n t) -> p n t", t=8)) if perm_g else (lambda ap: ap)
    for j in range(NT // TT):
        sl = slice(j * TT, (j + 1) * TT)
        x, xk = P.nxt("x"); xb, xbk = P.nxt("xb")
        g, gk = P.nxt("g"); gb, gbk = P.nxt("gb")
        pt, ptk = P.nxt("p"); pb, pbk = P.nxt("pb")
        P.dma(x[:, :, :], xTv[:, :, sl], w=[xk])
        if perm_g:
            for k in range(8):
                P.dma(g[:, k, :].rearrange("p (t n) -> p t n", t=8), gTv[:, k, :, j * CN:(j + 1) * CN], w=[gk])
            if sel:
                g2, g2k = P.nxt("g2")
                for k in range(8):
                    P.dma(g2[:, k, :].rearrange("p (t n) -> p t n", t=8), gTv[:, k, :, NT // 8 + j * CN:NT // 8 + (j + 1) * CN], w=[g2k])
                P.add("pool", lambda e, g=g: e.tensor_scalar(out=g[:, :, :], in0=g[:, :, :], scalar1=selt[:, 0:1], scalar2=None, op0=ALU.mult),
                      r=[gk, "sel"], w=[gk])
                P.add("dve", lambda e, g=g, g2=g2: e.scalar_tensor_tensor(out=g[:, :, :], in0=g2[:, :, :], scalar=selt[:, 1:2], in1=g[:, :, :],
                                                                          op0=ALU.mult, op1=ALU.add), r=[gk, g2k, "sel"], w=[gk])
        else:
            P.dma(g[:, :, :], gTv[:, :, sl], w=[gk])
        P.dma(pt[:, :, :], pTv[:, :, sl], w=[ptk])
        P.add("pool", lambda e, x=x, xb=xb: e.tensor_copy(out=xb[:, :, :], in_=x[:, :, :]), r=[xk], w=[xbk])
        for k in range(8):
            P.add("pool", lambda e, g=g, gb=gb, k=k: e.tensor_copy(out=sseq(gb[:, k, :]), in_=gseq(g[:, k, :])), r=[gk], w=[gbk])
        P.add("pool", lambda e, pt=pt, pb=pb: e.tensor_copy(out=pb[:, :, :], in_=pt[:, :, :]), r=[ptk], w=[pbk])
        m, mk = P.nxt("m")
        for et in range(8):
            sp_, spk = P.nxt("ps")
            P.add("pe", lambda e, sp_=sp_, et=et, gb=gb: [e.matmul(sp_[:, :TT], lhsT=w_glu[:, k, et * 128:(et + 1) * 128], rhs=gb[:, k, :],
                                                                  start=(k == 0), stop=(k == 7)) for k in range(8)][-1],
                  r=[gbk] + [("w_glu", k) for k in range(8)], w=[spk])
            zp, zpk = P.nxt("ps")
            P.add("pe", lambda e, zp=zp, et=et, xb=xb: [e.matmul(zp[:, :TT], lhsT=w_z[:, k, et * 128:(et + 1) * 128], rhs=xb[:, k, :],
                                                                start=(k == 0), stop=(k == 7)) for k in range(8)][-1],
                  r=[xbk] + [("w_z", k) for k in range(8)], w=[zpk])
            sg, sgk = P.nxt("tmp")
            P.add("act", lambda e, sg=sg, sp_=sp_, et=et: e.activation(out=sg[:, :TT], in_=sp_[:, :TT], func=AF.Sigmoid, bias=bglu[:, et:et + 1]),
                  r=[spk, "bglu"], w=[sgk])
            sz, szk = P.nxt("tmp")
            P.add("act", lambda e, sz=sz, zp=zp: e.activation(out=sz[:, :TT], in_=zp[:, :TT], func=AF.Silu), r=[zpk], w=[szk])
            P.add("dve", lambda e, sg=sg, g=g, et=et: e.tensor_tensor(out=sseq(sg[:, :TT]), in0=gseq(g[:, et, :]), in1=sseq(sg[:, :TT]), op=ALU.mult),
                  r=[gk, sgk], w=[sgk])
            P.add("dve", lambda e, sg=sg, sz=sz, m=m, et=et: e.tensor_tensor(out=m[:, et, :], in0=sg[:, :TT], in1=sz[:, :TT], op=ALU.mult),
                  r=[sgk, szk], w=[(mk, et)])

        def y_ps(d, m=m, mk=mk):
            yp, yk = P.nxt("ps")
            P.add("pe", lambda e: [e.matmul(yp[:, :TT], lhsT=w_out[:, k, d * 128:(d + 1) * 128], rhs=m[:, k, :],
                                            start=(k == 0), stop=(k == 7)) for k in range(8)][-1],
                  r=[(mk, k) for k in range(8)] + [("w_out", k) for k in range(8)], w=[yk])
            return yp[:, :TT], yk

        o, ok = P.nxt("o")
        tail(P, TT, x, xk, y_ps, pb, pbk, w_pg, w_pe, ones, lng, lnb, o, ok, "s")
        P.dma(oTv[:, :, sl], o[:, :, :], r=[(ok, d) for d in range(8)], w=[("out", j)])
        outkeys.append(("out", j))
    P.add("sp", None, r=outkeys)
    return P.finalize() if own else None


def s5b_layer(xs, gs, p_l, w_in, w_glu, b_glu, w_out, w_pg, w_pe, ln_g, ln_b):
    in_maps = []
    w_z = np.ascontiguousarray(w_in[:, D:])
    for c in range(NCORES):
        b, h = c // 2, c % 2
        in_maps.append(dict(xT=xs[c], gT=gs[c], pT=np.ascontiguousarray(p_l[b, h * 2048:(h + 1) * 2048].T),
                            w_z=w_z, w_glu=w_glu, b_glu=pp(b_glu), w_out=w_out, w_pg=w_pg, w_pe=w_pe,
                            ln_g=pp(ln_g), ln_b=pp(ln_b)))
    res = run("s5b", build_s5b, in_maps)
    return [r["oT"] for r in res]


PI = float(np.pi)


def _tt(P, eng, o, a, b, op):
    P.add(eng, lambda e: e.tensor_tensor(out=o[0], in0=a[0], in1=b[0], op=op), r=[a[1], b[1]], w=[o[1]])


def _ts(P, eng, o, a, s1, op0, s2=None, op1=None, extra=()):
    if op1 is None:
        P.add(eng, lambda e: e.tensor_scalar(out=o[0], in0=a[0], scalar1=s1, scalar2=None, op0=op0), r=[a[1]] + list(extra), w=[o[1]])
    else:
        P.add(eng, lambda e: e.tensor_scalar(out=o[0], in0=a[0], scalar1=s1, scalar2=s2, op0=op0, op1=op1), r=[a[1]] + list(extra), w=[o[1]])


def _stt(P, o, a, sc, b, op0, op1, extra=()):
    P.add("dve", lambda e: e.scalar_tensor_tensor(out=o[0], in0=a[0], scalar=sc, in1=b[0], op0=op0, op1=op1),
          r=[a[1], b[1]] + list(extra), w=[o[1]])


def _cmul(P, orr, oi, ar, ai, br, bi, t):
    _tt(P, "dve", orr, ar, br, ALU.mult)
    _tt(P, "dve", t, ai, bi, ALU.mult)
    _tt(P, "dve", orr, orr, t, ALU.subtract)
    _tt(P, "dve", oi, ar, bi, ALU.mult)
    _tt(P, "dve", t, ai, br, ALU.mult)
    _tt(P, "dve", oi, oi, t, ALU.add)


def _cplx_setup(P, tp, shape, lamre, lamim, ldt):
    nel = int(np.prod(shape))
    def T(n):
        if n not in tp:
            tp[n] = P.sb("tp_" + n, [128, 512])
        return (tp[n][:, :nel].rearrange("p (a b) -> p a b", b=shape[-1]), "tp_" + n)
    dt, thr, thi, tmp, x, x2, acc, sn, cs, t1, t2 = [T(n) for n in ("dt", "thr", "thi", "tmp", "x", "x2", "acc", "sn", "cs", "t1", "t2")]
    er, ar, ai, nr, den, cr, ci = [T(n) for n in ("er", "ar", "ai", "nr", "den", "cr", "ci")]
    P.add("act", lambda e: e.activation(out=dt[0], in_=ldt[0], func=AF.Exp), r=[ldt[1]], w=[dt[1]])
    _tt(P, "dve", thr, lamre, dt, ALU.mult)
    _tt(P, "dve", thi, lamim, dt, ALU.mult)
    for _ in range(4):
        _ts(P, "dve", tmp, thi, PI, ALU.is_gt, 2 * PI, ALU.mult)
        _tt(P, "dve", thi, thi, tmp, ALU.subtract)
    _ts(P, "dve", x, thi, 0.125, ALU.mult)
    _tt(P, "dve", x2, x, x, ALU.mult)
    _ts(P, "dve", acc, x2, 1.0 / 362880, ALU.mult)
    for c in (-1.0 / 5040, 1.0 / 120, -1.0 / 6):
        _stt(P, acc, acc, c, x2, ALU.add, ALU.mult)
    _stt(P, sn, acc, 1.0, x, ALU.add, ALU.mult)
    _ts(P, "dve", acc, x2, -1.0 / 3628800, ALU.mult)
    for c in (1.0 / 40320, -1.0 / 720, 1.0 / 24, -0.5):
        _stt(P, acc, acc, c, x2, ALU.add, ALU.mult)
    _ts(P, "dve", cs, acc, 1.0, ALU.add)
    for _ in range(3):
        _tt(P, "dve", t1, cs, cs, ALU.mult)
        _tt(P, "dve", t2, sn, sn, ALU.mult)
        _stt(P, sn, cs, 2.0, sn, ALU.mult, ALU.mult)
        _tt(P, "dve", cs, t1, t2, ALU.subtract)
    _ts(P, "dve", acc, thr, 1.0 / 120, ALU.mult)
    for c in (1.0 / 24, 1.0 / 6, 0.5, 1.0):
        _stt(P, acc, acc, c, thr, ALU.add, ALU.mult)
    _ts(P, "dve", er, acc, 1.0, ALU.add)
    _tt(P, "dve", ar, er, cs, ALU.mult)
    _tt(P, "dve", ai, er, sn, ALU.mult)
    _ts(P, "dve", nr, ar, -1.0, ALU.add)
    _tt(P, "dve", den, lamre, lamre, ALU.mult)
    _tt(P, "dve", t1, lamim, lamim, ALU.mult)
    _tt(P, "dve", den, den, t1, ALU.add)
    P.add("dve", lambda e: e.reciprocal(out=den[0], in_=den[0]), r=[den[1]], w=[den[1]])
    _tt(P, "dve", cr, nr, lamre, ALU.mult)
    _tt(P, "dve", t1, ai, lamim, ALU.mult)
    _tt(P, "dve", cr, cr, t1, ALU.add)
    _tt(P, "dve", cr, cr, den, ALU.mult)
    _tt(P, "dve", ci, ai, lamre, ALU.mult)
    _tt(P, "dve", t1, nr, lamim, ALU.mult)
    _tt(P, "dve", ci, ci, t1, ALU.subtract)
    _tt(P, "dve", ci, ci, den, ALU.mult)
    return ar, ai, cr, ci, t1, t2


def build_s5a(NG=32, NT=4096, TT=256, ngrun=None, stage=9, P=None):
    MT = NG // 8
    NCH = NT // 8
    assert NCH == 512
    own = P is None
    P = P or Prog()
    xT = P.dram("xT", [D, NT])
    w_u_d = P.dram("w_u", [D, MT * 128])
    cl_d = {n: P.dram(n, [128, MT, 64]) for n in ("lamre_cl", "lamim_cl", "ldt_cl", "bre_cl", "bim_cl")}
    pl_d = {n: P.dram(n, [128, NG, 16]) for n in ("lamre_pl", "lamim_pl", "ldt_pl", "bre_pl", "bim_pl", "cre_pl", "cim_pl")}
    dsk_d = P.dram("dsk", [128, MT])
    ident_d = P.dram("ident", [128, 128])
    j2_d = P.dram("j2", [128, 128])
    msk_d = P.dram("msk", [128, 2 + 8])
    identg_d = P.dram("identg", [128, 8, 16])
    gTp = P.dram("gTp", [MT * 128, 8, NCH], out=True)

    P.pool("wst", 1, [128, 1024], F32)
    P.pool("ps", 4, [128, 512], F32, psum=True)
    P.pool("psk", 2, [128, 128], F32, psum=True)
    P.pool("psr", 2, [128, 512], F32, psum=True)
    w_u = load_weight_bf16(P, "w_u", w_u_d, D, MT * 128, "wst")

    def ld(name, shape, src):
        t = P.sb(name, shape)
        P.dma(t[:], src, w=[name])
        return (t[:], name), t
    cl = {n: ld(n, [128, MT, 64], cl_d[n])[0] for n in cl_d}
    pl = {n: ld(n, [128, NG, 16], pl_d[n])[0] for n in pl_d}
    _, dsk = ld("dsk", [128, MT], dsk_d)
    _, ident = ld("ident", [128, 128], ident_d)
    _, j2 = ld("j2", [128, 128], j2_d)
    _, msk = ld("msk", [128, 10], msk_d)
    _, identg = ld("identg", [128, 8, 16], identg_d)
    mlo, mhi = msk[:, 0:1], msk[:, 1:2]

    u_t = P.sb("u", [128, MT, NT], BF16)
    P.pool("x", 1, [128, 8, TT], F32)
    P.pool("xb", 2, [128, 8, TT], BF16)
    xTv = xT.rearrange("(k p) t -> p k t", p=128)
    for j in range(NT // TT):
        x, xk = P.nxt("x"); xb, xbk = P.nxt("xb")
        P.dma(x[:, :, :], xTv[:, :, j * TT:(j + 1) * TT], w=[xk])
        P.add("pool", lambda e, x=x, xb=xb: e.tensor_copy(out=xb[:, :, :], in_=x[:, :, :]), r=[xk], w=[xbk])
        for m in range(MT):
            up, upk = P.nxt("ps")
            P.add("pe", lambda e, up=up, m=m, xb=xb: [e.matmul(up[:, :TT], lhsT=w_u[:, k, m * 128:(m + 1) * 128], rhs=xb[:, k, :],
                                                             start=(k == 0), stop=(k == 7)) for k in range(8)][-1],
                  r=[xbk] + [("w_u", k) for k in range(8)], w=[upk])
            P.add("act", lambda e, up=up, m=m, j=j: e.activation(out=u_t[:, m, j * TT:(j + 1) * TT], in_=up[:, :TT], func=AF.Copy),
                  r=[upk], w=[("u", m, j)])
    ukeys = lambda m: [("u", m, j) for j in range(NT // TT)]

    tp = {}
    ar, ai, cr, ci, t1, t2 = _cplx_setup(P, tp, [MT, 64], cl["lamre_cl"], cl["lamim_cl"], cl["ldt_cl"])
    EB = P.sb("EB", [128, 8, MT, 2, 64])
    ebr = lambda j: (EB[:, j, :, 0, :], ("EB", j))
    ebi = lambda j: (EB[:, j, :, 1, :], ("EB", j))
    _cmul(P, ebr(7), ebi(7), cr, ci, cl["bre_cl"], cl["bim_cl"], t1)
    for j in range(6, -1, -1):
        _cmul(P, ebr(j), ebi(j), ebr(j + 1), ebi(j + 1), ar, ai, t1)

    par, pai, pcr, pci, pt1, pt2 = _cplx_setup(P, tp, [NG, 16], pl["lamre_pl"], pl["lamim_pl"], pl["ldt_pl"])
    def T(n, old=None):
        if old is None:
            t = P.sb(n, [128, NG, 16])
            return (t[:], n)
        return (tp[old][:, :NG * 16].rearrange("p (a b) -> p a b", b=16), "tp_" + old)
    qr, qi, qr2, qi2 = T("qr", "dt"), T("qi", "thr"), T("qr2", "thi"), T("qi2", "tmp")
    car, cai, car2, cai2 = T("car", "x"), T("cai", "x2"), T("car2", "acc"), T("cai2", "sn")
    pr, pi_, pr2, pi2 = T("pr", "cs"), T("pi", "er"), T("pr2", "nr"), T("pi2", "den")
    Cc = T("Cc")
    Xs = P.sb("Xs", [128, 8, NG, 16])
    CS = P.sb("CS", [128, NG, 8, 16])
    AKr = P.sb("AKr", [128, 9, NG]); AKi = P.sb("AKi", [128, 9, NG])
    _ts(P, "dve", pt1, pl["cim_pl"], mhi, ALU.mult, extra=["msk"])
    _stt(P, Cc, pl["cre_pl"], mlo, pt1, ALU.mult, ALU.subtract, extra=["msk"])
    _cmul(P, qr, qi, pcr, pci, pl["bre_pl"], pl["bim_pl"], pt1)

    def stack(o, re, im, sub):
        _ts(P, "dve", pt2, im, mhi, ALU.mult, extra=["msk"])
        _stt(P, o, re, mlo, pt2, ALU.mult, ALU.subtract if sub else ALU.add, extra=["msk"])
    stack((Xs[:, 0, :, :], ("Xs", 0)), qr, qi, False)
    cur_q, nxt_q = (qr, qi), (qr2, qi2)
    cur_c, nxt_c = (pl["cre_pl"], pl["cim_pl"]), (car, cai)
    alt_c = (car2, cai2)
    cur_p, nxt_p = None, (pr, pi_)
    for tau in range(1, 9):
        if tau <= 7:
            _cmul(P, nxt_q[0], nxt_q[1], cur_q[0], cur_q[1], par, pai, pt1)
            cur_q, nxt_q = nxt_q, cur_q
            stack((Xs[:, tau, :, :], ("Xs", tau)), cur_q[0], cur_q[1], False)
        _cmul(P, nxt_c[0], nxt_c[1], cur_c[0], cur_c[1], par, pai, pt1)
        cur_c = nxt_c
        nxt_c = alt_c if cur_c[0][1] == car[1] else (car, cai)
        stack((CS[:, :, tau - 1, :], ("CS", tau - 1)), cur_c[0], cur_c[1], True)
        if cur_p is None:
            cur_p = (par, pai)
        else:
            _cmul(P, nxt_p[0], nxt_p[1], cur_p[0], cur_p[1], par, pai, pt1)
            cur_p = nxt_p
            nxt_p = (pr2, pi2) if cur_p[0][1] == pr[1] else (pr, pi_)
    akr = lambda k: (AKr[:, k, :], ("AK", k))
    aki = lambda k: (AKi[:, k, :], ("AK", k))
    P.add("dve", lambda e: e.tensor_copy(out=AKr[:, 0, :], in_=cur_p[0][0][:, :, 0]), r=[cur_p[0][1]], w=[("AK", 0)])
    P.add("dve", lambda e: e.tensor_copy(out=AKi[:, 0, :], in_=cur_p[1][0][:, :, 0]), r=[cur_p[1][1]], w=[("AK", 0)])
    s1 = P.sb("aks1", [128, NG]); s2 = P.sb("aks2", [128, NG])
    s1 = (s1[:], "aks1"); s2 = (s2[:], "aks2")
    for k in range(1, 9):
        _tt(P, "dve", s1, akr(k - 1), akr(k - 1), ALU.mult)
        _tt(P, "dve", s2, aki(k - 1), aki(k - 1), ALU.mult)
        _tt(P, "dve", akr(k), s1, s2, ALU.subtract)
        _stt(P, aki(k), akr(k - 1), 2.0, aki(k - 1), ALU.mult, ALU.mult)

    P.pool("LE", 2, [128, 8, 128], BF16)
    lm_tiles = P.pool("LM", 2, [128, 8, 128], BF16)
    for t, k_ in lm_tiles:
        P.add("pool", lambda e, t=t: e.memset(t[:], 0.0), w=[k_])
    P.pool("XPA", 2, [128, 8, 128], F32)
    P.pool("ROT", 4, [128, 128], F32)
    P.pool("KS", 2, [128, 128], F32)
    P.pool("X", 2, [128, NCH], F32)
    P.pool("ys", 2, [128, NCH], F32)
    P.pool("gw", 1, [128, NCH], F32)
    P.pool("gy", 2, [128, NCH], F32)
    outkeys = []
    for g in range(NG if ngrun is None else ngrun):
        m, gl = g // 8, g % 8
        gm = msk[:, 2 + gl:3 + gl]
        le, lek = P.nxt("LE")
        for j in range(8):
            P.add("dve", lambda e, le=le, j=j, m=m, gm=gm: e.tensor_scalar(out=le[:, j, :], in0=EB[:, j, m, :, :].rearrange("p a b -> p (a b)"),
                                                                         scalar1=gm, scalar2=None, op0=ALU.mult),
                  r=[("EB", j), "msk"], w=[(lek, j)])
        kp, kpk = P.nxt("psk")
        xp, xpk = P.nxt("XPA")
        P.add("pool", lambda e, xp=xp: e.memset(xp[:, :, :], 0.0), w=[xpk])
        P.add("dve", lambda e, xp=xp, g=g, gl=gl: e.tensor_copy(out=xp[:, :, 16 * gl:16 * gl + 16], in_=Xs[:, :, g, :]),
              r=[("Xs", t_) for t_ in range(8)] + [xpk], w=[xpk])
        P.add("pe", lambda e, xp=xp, g=g, kp=kp: [e.matmul(kp[:, 16 * tau:16 * tau + 16], lhsT=xp[:, tau, :], rhs=Cc[0][:, g, :],
                                                          start=True, stop=True) for tau in range(8)][-1],
              r=[xpk, "Cc"], w=[kpk])
        ks, ksk = P.nxt("KS")
        P.add("dve", lambda e, ks=ks, kp=kp, m=m, gl=gl: e.scalar_tensor_tensor(out=ks[:, 0:16], in0=identg[:, gl, :], scalar=dsk[:, m:m + 1],
                                                                             in1=kp[:, 0:16], op0=ALU.mult, op1=ALU.add),
              r=["identg", "dsk", kpk], w=[(ksk, 0)])
        P.add("dve", lambda e, ks=ks, kp=kp: e.tensor_copy(out=ks[:, 16:128], in_=kp[:, 16:128]),
              r=[kpk], w=[(ksk, 1)])
        lm, lmk = P.nxt("LM")
        for j in range(8):
            P.add("pool", lambda e, lm=lm, ks=ks, j=j: e.tensor_copy(out=lm[:, j, 16 * j:128], in_=ks[:, 0:128 - 16 * j]),
                  r=[(ksk, 0), (ksk, 1)], w=[(lmk, j)])
        if stage < 2:
            continue
        ep, epk = P.nxt("ps")
        P.add("pe", lambda e, ep=ep, le=le, m=m: [e.matmul(ep[:, :], lhsT=le[:, j, :], rhs=u_t[:, m, j::8], start=(j == 0), stop=(j == 7))
                                                  for j in range(8)][-1],
              r=[(lek, j) for j in range(8)] + ukeys(m), w=[epk])
        X, Xk = P.nxt("X")
        P.add("act", lambda e, X=X, ep=ep: e.activation(out=X[:, :], in_=ep[:, :], func=AF.Copy), r=[epk], w=[Xk])
        for k in range(9 if stage >= 3 else 0):
            d = 1 << k
            rot, rotk = P.nxt("ROT")
            P.add("pool", lambda e, rot=rot, k=k, g=g: e.tensor_scalar(out=rot[:, :], in0=ident[:, :], scalar1=AKr[:, k, g:g + 1], scalar2=None, op0=ALU.mult),
                  r=["ident", ("AK", k)], w=[rotk])
            P.add("dve", lambda e, rot=rot, k=k, g=g: e.scalar_tensor_tensor(out=rot[:, :], in0=j2[:, :], scalar=AKi[:, k, g:g + 1], in1=rot[:, :],
                                                                          op0=ALU.mult, op1=ALU.add),
                  r=["j2", ("AK", k), rotk], w=[rotk])
            rp, rpk = P.nxt("psr")
            P.add("pe", lambda e, rp=rp, rot=rot, X=X, d=d: e.matmul(rp[:, :NCH - d], lhsT=rot[:, :], rhs=X[:, :NCH - d], start=True, stop=True),
                  r=[rotk, Xk], w=[rpk])
            P.add("dve", lambda e, rp=rp, X=X, d=d: e.tensor_tensor(out=X[:, d:], in0=rp[:, :NCH - d], in1=X[:, d:], op=ALU.add),
                  r=[rpk, Xk], w=[Xk])
        if stage < 4:
            continue
        yp, ypk = P.nxt("ps")
        P.add("pe", lambda e, yp=yp, lm=lm, m=m, X=X, g=g: [e.matmul(yp[:, :], lhsT=lm[:, j, :], rhs=u_t[:, m, j::8], start=(j == 0), stop=False)
                                                           for j in range(8)] and
              e.matmul(yp[:, 1:NCH], lhsT=CS[:, g, :, :].rearrange("p a b -> p (a b)"), rhs=X[:, 0:NCH - 1], start=False, stop=True),
              r=[(lmk, j) for j in range(8)] + ukeys(m) + [Xk] + [("CS", t_) for t_ in range(8)], w=[ypk])
        ys, ysk = P.nxt("ys"); gw, gwk = P.nxt("gw"); gy, gyk = P.nxt("gy")
        P.add("act", lambda e, ys=ys, yp=yp: e.activation(out=ys[:, :], in_=yp[:, :], func=AF.Copy), r=[ypk], w=[ysk])
        P.add("dve", lambda e, ys=ys, gw=gw: e.tensor_tensor(out=gw[:, :], in0=ys[:, :], in1=ys[:, :], op=ALU.mult), r=[ysk], w=[gwk])
        P.add("dve", lambda e, gw=gw: e.tensor_scalar(out=gw[:, :], in0=gw[:, :], scalar1=0.044715, scalar2=1.0, op0=ALU.mult, op1=ALU.add),
              r=[gwk], w=[gwk])
        P.add("dve", lambda e, ys=ys, gw=gw: e.tensor_tensor(out=gw[:, :], in0=gw[:, :], in1=ys[:, :], op=ALU.mult), r=[ysk, gwk], w=[gwk])
        P.add("act", lambda e, gw=gw: e.activation(out=gw[:, :], in_=gw[:, :], func=AF.Sigmoid, scale=1.5957691216057308), r=[gwk], w=[gwk])
        P.add("dve", lambda e, ys=ys, gw=gw, gy=gy: e.tensor_tensor(out=gy[:, :], in0=ys[:, :], in1=gw[:, :], op=ALU.mult), r=[ysk, gwk], w=[gyk])
        for t_ in range(8):
            r0 = m * 128 + gl * 16
            P.dma(gTp[r0:r0 + 16, t_, :], gy[16 * t_:16 * t_ + 16, :], r=[gyk], w=[("out", g, t_)])
            outkeys.append(("out", g, t_))
    P.add("sp", None, r=outkeys)
    return P.finalize() if own else None


def s5a_layer(xfull, prm):
    NG = 32
    ident = np.eye(128, dtype=np.float32)
    j2 = np.zeros((128, 128), np.float32)
    j2[np.arange(64), np.arange(64) + 64] = 1.0
    j2[np.arange(64) + 64, np.arange(64)] = -1.0
    msk = np.zeros((128, 10), np.float32)
    msk[:64, 0] = 1.0; msk[64:, 1] = 1.0
    for gl in range(8):
        msk[16 * gl:16 * gl + 16, 2 + gl] = 1.0
    identg = np.zeros((128, 8, 16), np.float32)
    for gl in range(8):
        for h in range(16):
            identg[16 * gl + h, gl, h] = 1.0
    in_maps = []
    for c in range(NCORES):
        b, gh = c // 2, c % 2
        gs = slice(NG * gh, NG * gh + NG)
        def cl(a):
            a = np.broadcast_to(a.reshape(NG // 8, 8, 1, 64), (NG // 8, 8, 16, 64))
            return np.ascontiguousarray(a.transpose(1, 2, 0, 3).reshape(128, NG // 8, 64))
        def clb(a):
            a = a.reshape(NG // 8, 8, 64, 16)
            return np.ascontiguousarray(a.transpose(1, 3, 0, 2).reshape(128, NG // 8, 64))
        def plv(a):
            a = np.broadcast_to(a.T.reshape(1, 64, NG, 1), (2, 64, NG, 16))
            return np.ascontiguousarray(a.reshape(128, NG, 16))
        def plb(a):
            a = np.broadcast_to(a.transpose(1, 0, 2)[None], (2, 64, NG, 16))
            return np.ascontiguousarray(a.reshape(128, NG, 16))
        def plc(a):
            a = np.broadcast_to(a.transpose(2, 0, 1)[None], (2, 64, NG, 16))
            return np.ascontiguousarray(a.reshape(128, NG, 16))
        ldt = np.broadcast_to(prm["log_dt"][gs][:, None], (NG, 64))
        in_maps.append(dict(
            xT=xfull[b], w_u=np.ascontiguousarray(prm["w_in"][:, 512 * gh:512 * gh + 512]),
            lamre_cl=cl(prm["lam_re"][gs]), lamim_cl=cl(prm["lam_im"][gs]), ldt_cl=cl(ldt),
            bre_cl=clb(prm["b_re"][gs]), bim_cl=clb(prm["b_im"][gs]),
            lamre_pl=plv(prm["lam_re"][gs]), lamim_pl=plv(prm["lam_im"][gs]), ldt_pl=plv(ldt),
            bre_pl=plb(prm["b_re"][gs]), bim_pl=plb(prm["b_im"][gs]),
            cre_pl=plc(prm["c_re"][gs]), cim_pl=plc(prm["c_im"][gs]),
            dsk=pp(prm["d"][512 * gh:512 * gh + 512]), ident=ident, j2=j2, msk=msk, identg=identg))
    res = run("s5a", build_s5a, in_maps)
    out = []
    for b in range(4):
        halves = [res[2 * b + gh]["gTp"].transpose(0, 2, 1).reshape(512, 4096) for gh in range(2)]
        out.append(np.ascontiguousarray(np.concatenate(halves, axis=0)))
    return out


RW_GN_EPS = 64e-5


def build_rwa(NT=4096, TT=256, ntiles=None, pipeline=True, P=None):
    C = 64
    NCK = TT // C
    MT = 4
    own = P is None
    P = P or Prog()
    xT = P.dram("xT", [D, NT + 1])
    mu_d = P.dram("mu", [128, 6, 8])
    wq_d = {q: P.dram("w_" + q, [D, 512]) for q in "rkvz"}
    w1_d = P.dram("w1", [D, 64]); a1_d = P.dram("a1", [D, 64])
    w2_d = P.dram("w2", [64, 512]); a2_d = P.dram("a2", [64, 512])
    prm_d = P.dram("prm", [128, 7, 4])
    ident_d = P.dram("ident", [128, 128])
    onesb_d = P.dram("onesb", [128, 128])
    msk_d = P.dram("msk", [64, 4, 512])
    rst_d = P.dram("rst", [128, TT])
    mT = P.dram("mT", [512, NT], out=True)

    P.pool("wst", 1, [128, 512], F32)
    P.pool("ps", 4, [128, 512], F32, psum=True)
    P.pool("ps2", 4, [128, 512], F32, psum=True)
    wq = {q: load_weight_bf16(P, "w_" + q, wq_d[q], D, 512, "wst") for q in "rkvz"}
    w1 = load_weight_bf16(P, "w1", w1_d, D, 64, "wst")
    a1 = load_weight_bf16(P, "a1", a1_d, D, 64, "wst")

    def ld(name, shape, src, dtype=F32):
        t = P.sb(name, shape, dtype)
        P.dma(t[:], src, w=[name])
        return t
    w2f = ld("w2f", [64, 512], w2_d); a2f = ld("a2f", [64, 512], a2_d)
    w2 = P.sb("w2", [64, 512], BF16); a2 = P.sb("a2", [64, 512], BF16)
    P.add("pool", lambda e: e.tensor_copy(out=w2[:, :], in_=w2f[:, :]), r=["w2f"], w=["w2"])
    P.add("pool", lambda e: e.tensor_copy(out=a2[:, :], in_=a2f[:, :]), r=["a2f"], w=["a2"])
    mu = ld("mu", [128, 6, 8], mu_d)
    prm = ld("prm", [128, 7, 4], prm_d)
    ident = ld("ident", [128, 128], ident_d)
    onesb = ld("onesb", [128, 128], onesb_d)
    msk = ld("msk", [64, 4, 512], msk_d)
    rst = ld("rst", [128, TT], rst_d)
    maskU, maskUi, maskL, I8 = msk[:, 0, :], msk[:, 1, :], msk[:, 2, :], msk[:, 3, :]
    W0, A0, KK, KA, RK, LG, LB = range(7)

    ST = P.sb("ST", [128, MT, 64])
    P.add("pool", lambda e: e.memset(ST[:], 0.0), w=["ST"])

    P.pool("xin", 1, [128, 8, TT + 1], F32)
    dx = P.sb("dx", [128, 8, TT]); tmpx = P.sb("tmpx", [128, 8, TT])
    xs = [P.sb(f"xs{i}", [128, 8, TT], BF16) for i in range(6)]
    F = {n: P.sb("F_" + n, [128, MT, TT]) for n in ("r", "k", "v", "sz", "sw", "a", "f6", "f7", "f8", "f9", "f10")}
    t1b = P.sb("t1b", [64, TT], BF16); a1b = P.sb("a1b", [64, TT], BF16)
    P.pool("tm", 2, [64, 3, 512], F32)
    cs = {n: P.sb("c_" + n, [64, 512]) for n in ("N", "NT", "Aak", "Arb", "Ark", "T", "Xa", "XTa", "Xb", "XTb", "WT", "UT", "o", "sq", "on")}
    cs_odd = {n: P.sb("co_" + n, [64, 512]) for n in ("T", "Aak", "Arb", "Ark")}
    st = {n: P.sb("st_" + n, [64, 8]) for n in ("s1", "s2", "mean", "var", "rstd")}
    P.pool("mo", 1, [128, MT, TT], F32)
    xTv = xT.rearrange("(k p) t -> p k t", p=128)
    mTv = mT.rearrange("(m p) t -> p m t", p=128)
    outkeys = []

    def A(eng, fn, r, w):
        P.add(eng, fn, r=r, w=w)

    def do_tile(j):
            xin, xk = P.nxt("xin")
            P.dma(xin[:, :, :], xTv[:, :, j * TT:(j + 1) * TT + 1], w=[xk])
            A("pool", lambda e, xin=xin: e.tensor_tensor(out=dx[:, :, :], in0=xin[:, :, 0:TT], in1=xin[:, :, 1:TT + 1], op=ALU.subtract), [xk], ["dx"])
            for i in range(6):
                for k in range(8):
                    A("dve", lambda e, i=i, k=k: e.scalar_tensor_tensor(out=xs[i][:, k, :], in0=dx[:, k, :], scalar=mu[:, i, k:k + 1], in1=xin[:, k, 1:TT + 1],
                                                                        op0=ALU.mult, op1=ALU.add),
                      ["dx", "mu", xk], [(f"xs{i}", k)])
            for qi, q in enumerate("rkvz"):
                for m in range(MT):
                    pz, pzk = P.nxt("ps")
                    A("pe", lambda e, pz=pz, q=q, qi=qi, m=m: [e.matmul(pz[:, :TT], lhsT=wq[q][:, k, m * 128:(m + 1) * 128], rhs=xs[qi][:, k, :],
                                                                       start=(k == 0), stop=(k == 7)) for k in range(8)][-1],
                      [(f"xs{qi}", k) for k in range(8)] + [("w_" + q, k) for k in range(8)], [pzk])
                    dst = F[{"r": "r", "k": "k", "v": "v", "z": "sz"}[q]]
                    A("act", lambda e, pz=pz, dst=dst, m=m, q=q: e.activation(out=dst[:, m, :], in_=pz[:, :TT], func=AF.Silu if q == "z" else AF.Copy),
                      [pzk], [("F", {"r": "r", "k": "k", "v": "v", "z": "sz"}[q], m)])
            for (wa, wb, xi, tb, tbk, dstn, bias_i, fn1) in ((w1, w2, 4, t1b, "t1b", "sw", W0, AF.Tanh), (a1, a2, 5, a1b, "a1b", "a", A0, AF.Copy)):
                pz, pzk = P.nxt("ps")
                A("pe", lambda e, pz=pz, wa=wa, xi=xi: [e.matmul(pz[:64, :TT], lhsT=wa[:, k, :], rhs=xs[xi][:, k, :], start=(k == 0), stop=(k == 7))
                                                       for k in range(8)][-1],
                  [(f"xs{xi}", k) for k in range(8)] + [("w1" if wa is w1 else "a1", k) for k in range(8)], [pzk])
                A("act", lambda e, pz=pz, tb=tb, fn1=fn1: e.activation(out=tb[:, :], in_=pz[:64, :TT], func=fn1), [pzk], [tbk])
                for m in range(MT):
                    p2, p2k = P.nxt("ps")
                    A("pe", lambda e, p2=p2, wb=wb, tb=tb, m=m: e.matmul(p2[:, :TT], lhsT=wb[:, m * 128:(m + 1) * 128], rhs=tb[:, :], start=True, stop=True),
                      [tbk, "w2" if wb is w2 else "a2"], [p2k])
                    A("act", lambda e, p2=p2, dstn=dstn, m=m, bias_i=bias_i: e.activation(out=F[dstn][:, m, :], in_=p2[:, :TT], func=AF.Sigmoid,
                                                                                     bias=prm[:, bias_i, m:m + 1]),
                      [p2k, "prm"], [("F", dstn, m)])
            for m in range(MT):
                fk = lambda n: ("F", n, m)
                sl = lambda n: F[n][:, m, :]
                A("dve", lambda e, m=m: e.tensor_scalar(out=F["f6"][:, m, :], in0=F["k"][:, m, :], scalar1=prm[:, KK, m:m + 1], scalar2=None, op0=ALU.mult),
                  [fk("k"), "prm"], [fk("f6")])
                A("dve", lambda e, m=m: e.tensor_tensor(out=F["f7"][:, m, :], in0=F["f6"][:, m, :], in1=F["f6"][:, m, :], op=ALU.mult), [fk("f6")], [fk("f7")])
                pz, pzk = P.nxt("ps")
                A("pe", lambda e, pz=pz, m=m: e.matmul(pz[:, :TT], lhsT=onesb[:, :], rhs=F["f7"][:, m, :], start=True, stop=True), [fk("f7"), "onesb"], [pzk])
                A("act", lambda e, pz=pz, m=m: e.activation(out=F["f7"][:, m, :], in_=pz[:, :TT], func=AF.Sqrt), [pzk], [fk("f7")])
                A("dve", lambda e, m=m: e.tensor_scalar(out=F["f7"][:, m, :], in0=F["f7"][:, m, :], scalar1=1e-12, scalar2=None, op0=ALU.max), [fk("f7")], [fk("f7")])
                A("dve", lambda e, m=m: e.reciprocal(out=F["f7"][:, m, :], in_=F["f7"][:, m, :]), [fk("f7")], [fk("f7")])
                A("dve", lambda e, m=m: e.tensor_tensor(out=F["f6"][:, m, :], in0=F["f6"][:, m, :], in1=F["f7"][:, m, :], op=ALU.mult), [fk("f6"), fk("f7")], [fk("f6")])
                A("dve", lambda e, m=m: e.tensor_scalar(out=F["f7"][:, m, :], in0=F["a"][:, m, :], scalar1=-1.0, scalar2=prm[:, KA, m:m + 1], op0=ALU.add, op1=ALU.mult),
                  [fk("a"), fk("f7"), "prm"], [fk("f7")])
                A("dve", lambda e, m=m: e.tensor_tensor(out=F["f7"][:, m, :], in0=F["f7"][:, m, :], in1=F["k"][:, m, :], op=ALU.mult), [fk("f7"), fk("k")], [fk("f7")])
                A("dve", lambda e, m=m: e.tensor_tensor(out=F["f8"][:, m, :], in0=F["f7"][:, m, :], in1=F["k"][:, m, :], op=ALU.add), [fk("f7"), fk("k")], [fk("f8")])
                A("dve", lambda e, m=m: e.tensor_tensor(out=F["f7"][:, m, :], in0=F["f6"][:, m, :], in1=F["a"][:, m, :], op=ALU.mult), [fk("f6"), fk("a"), fk("f7")], [fk("f7")])
                A("dve", lambda e, m=m: e.tensor_scalar(out=F["sw"][:, m, :], in0=F["sw"][:, m, :], scalar1=-0.6065306597126334, scalar2=None, op0=ALU.mult),
                  [fk("sw")], [fk("sw")])
                A("dve", lambda e, m=m: e.tensor_tensor_scan(out=F["k"][:, m, :], data0=rst[:, :], data1=F["sw"][:, m, :], initial=0.0, op0=ALU.mult, op1=ALU.add),
                  [fk("sw"), "rst", fk("k"), fk("f8"), fk("f7")], [fk("k")])
                A("act", lambda e, m=m: e.activation(out=F["a"][:, m, :], in_=F["k"][:, m, :], func=AF.Exp), [fk("k"), fk("a"), fk("f7")], [fk("a")])
                A("act", lambda e, m=m: e.activation(out=F["f9"][:, m, :], in_=F["k"][:, m, :], func=AF.Exp, scale=-1.0), [fk("k")], [fk("f9")])
                A("dve", lambda e, m=m: e.tensor_tensor(out=F["k"][:, m, :], in0=F["k"][:, m, :], in1=F["sw"][:, m, :], op=ALU.subtract), [fk("k"), fk("sw"), fk("a"), fk("f9")], [fk("k")])
                A("act", lambda e, m=m: e.activation(out=F["k"][:, m, :], in_=F["k"][:, m, :], func=AF.Exp), [fk("k")], [fk("k")])
                A("dve", lambda e, m=m: e.scalar_tensor_tensor(out=F["f10"][:, m, :], in0=F["r"][:, m, :], scalar=prm[:, RK, m:m + 1], in1=F["f8"][:, m, :],
                                                               op0=ALU.mult, op1=ALU.mult), [fk("r"), fk("f8"), "prm"], [fk("f10")])
                pz, pzk = P.nxt("ps")
                A("pe", lambda e, pz=pz, m=m: e.matmul(pz[:, :TT], lhsT=onesb[:, :], rhs=F["f10"][:, m, :], start=True, stop=True), [fk("f10"), "onesb"], [pzk])
                A("dve", lambda e, pz=pz, m=m: e.tensor_tensor(out=F["f10"][:, m, :], in0=pz[:, :TT], in1=F["v"][:, m, :], op=ALU.mult), [pzk, fk("v")], [fk("f10")])
                A("dve", lambda e, m=m: e.tensor_tensor(out=F["r"][:, m, :], in0=F["r"][:, m, :], in1=F["a"][:, m, :], op=ALU.mult), [fk("r"), fk("a"), fk("f10")], [fk("r")])
                A("dve", lambda e, m=m: e.scalar_tensor_tensor(out=F["k"][:, m, :], in0=F["f6"][:, m, :], scalar=-1.0, in1=F["k"][:, m, :], op0=ALU.mult, op1=ALU.mult),
                  [fk("f6"), fk("k")], [fk("k")])
                A("dve", lambda e, m=m: e.tensor_tensor(out=F["f8"][:, m, :], in0=F["f8"][:, m, :], in1=F["f9"][:, m, :], op=ALU.mult), [fk("f8"), fk("f9"), fk("f10")], [fk("f8")])
                A("dve", lambda e, m=m: e.tensor_tensor(out=F["f7"][:, m, :], in0=F["f7"][:, m, :], in1=F["f9"][:, m, :], op=ALU.mult), [fk("f7"), fk("f9")], [fk("f7")])
                pcb = F["a"][:, m, C - 1::C].unsqueeze(2).broadcast_to([128, NCK, C])
                A("dve", lambda e, m=m, pcb=pcb: e.tensor_tensor(out=F["f9"][:, m, :].rearrange("p (c t) -> p c t", t=C),
                                                                 in0=F["f8"][:, m, :].rearrange("p (c t) -> p c t", t=C), in1=pcb, op=ALU.mult),
                  [fk("f8"), fk("a"), fk("f9"), fk("f7")], [fk("f9")])
                A("dve", lambda e, m=m, pcb=pcb: e.tensor_tensor(out=F["sw"][:, m, :].rearrange("p (c t) -> p c t", t=C),
                                                                 in0=F["f7"][:, m, :].rearrange("p (c t) -> p c t", t=C), in1=pcb, op=ALU.mult),
                  [fk("f7"), fk("a"), fk("sw"), fk("k")], [fk("sw")])
            RT, AT, KT, BT, VV, KH, BH, PP, BON, SZ = "r", "k", "f8", "f7", "v", "f9", "sw", "a", "f10", "sz"
            allm = lambda n: [("F", n, m) for m in range(MT)]
            mo, mok = P.nxt("mo")
            def do_chunk(c):
                    csl = slice(c * C, (c + 1) * C)
                    DBL = ("T", "Aak", "Arb", "Ark")
                    csx = dict(cs)
                    if c % 2 == 1:
                        csx.update({n: cs_odd[n] for n in DBL})
                    kname = lambda n: "c_" + n + ("_o" if (c % 2 == 1 and n in DBL) else "")
                    pspool = ["ps"]
                    nps = lambda: P.nxt(pspool[0])
                    tm, tmk = P.nxt("tm")
                    for qi, n in enumerate((VV, KH, BH)):
                        pz, pzk = nps()
                        A("pe", lambda e, pz=pz, n=n, csl=csl: [e.transpose(out=pz[:64, m * 128:(m + 1) * 128], in_=F[n][:, m, csl], identity=ident[:, :])
                                                               for m in range(MT)][-1], allm(n) + ["ident"], [pzk])
                        A("act", lambda e, pz=pz, tm=tm, qi=qi: e.activation(out=tm[:, qi, :], in_=pz[:64, :], func=AF.Copy), [pzk], [(tmk, qi)])
                    Vtm, Khtm, Bhtm = tm[:, 0, :], tm[:, 1, :], tm[:, 2, :]

                    def heads():
                        for h in range(8):
                            m, hl = h // 2, h % 2
                            yield h, m, hl, 64 * hl, slice(h * 64, (h + 1) * 64), slice((hl * 4 + m) * 64, (hl * 4 + m + 1) * 64)

                    def fm(n, m, bp):
                        return F[n][bp:bp + 64, m, csl]

                    for (name, ln, rn_, mask) in (("N", BT, AT, maskU), ("NT", AT, BT, maskL), ("Aak", KT, AT, maskU), ("Arb", BT, RT, maskUi), ("Ark", KT, RT, maskUi)):
                        pa, pak = nps()
                        pb, pbk = nps()

                        def f(e, pa=pa, pb=pb, ln=ln, rn_=rn_):
                            last = None
                            for h, m, hl, bp, hsn, hsb in heads():
                                dst = pa if hl == 0 else pb
                                last = e.matmul(dst[:64, m * 64:(m + 1) * 64], lhsT=fm(ln, m, bp), rhs=fm(rn_, m, bp), start=True, stop=True)
                            return last
                        A("pe", f, allm(ln) + allm(rn_), [pak, pbk])
                        A("dve", lambda e, pa=pa, name=name, mask=mask: e.tensor_tensor(out=csx[name][:, 0:256], in0=pa[:64, 0:256], in1=mask[:, 0:256], op=ALU.mult),
                          [pak, "msk"], [(kname(name), 0)])
                        A("dve", lambda e, pb=pb, name=name, mask=mask: e.tensor_tensor(out=csx[name][:, 256:512], in0=pb[:64, 0:256], in1=mask[:, 0:256], op=ALU.mult),
                          [pbk, "msk"], [(kname(name), 1)])
                    ck = lambda n: [(kname(n), 0), (kname(n), 1)]
                    A("dve", lambda e: e.tensor_tensor(out=csx["T"][:, :], in0=csx["N"][:, :], in1=I8, op=ALU.add), ck("N") + ["msk"], ck("T"))

                    def allheads(mmf, r, w):
                        def f(e):
                            last = None
                            for hh in heads():
                                last = mmf(e, *hh)
                            return last
                        A("pe", f, r, w)
                    X, XT = "N", "NT"
                    for lvl in range(5):
                        X2, X2T = ("Xa", "XTa") if lvl % 2 == 0 else ("Xb", "XTb")
                        if lvl < 4:
                            pz, pzk = nps()
                            allheads(lambda e, h, m, hl, bp, hsn, hsb, pz=pz, X=X, XT=XT: e.matmul(pz[:64, hsb], lhsT=csx[XT][:, hsb], rhs=csx[X][:, hsb], start=True, stop=True),
                                     ck(X) + ck(XT), [pzk])
                            A("act", lambda e, pz=pz, X2=X2: e.activation(out=csx[X2][:, :], in_=pz[:64, :], func=AF.Copy), [pzk], ck(X2))
                        pz, pzk = nps()
                        allheads(lambda e, h, m, hl, bp, hsn, hsb, pz=pz, X=X, XT=XT: e.matmul(pz[:64, hsb], lhsT=csx[X][:, hsb], rhs=csx[XT][:, hsb], start=True, stop=True),
                                 ck(X) + ck(XT), [pzk])
                        A("act", lambda e, pz=pz, X2T=X2T: e.activation(out=csx[X2T][:, :], in_=pz[:64, :], func=AF.Copy), [pzk], ck(X2T))
                        pz, pzk = nps()
                        allheads(lambda e, h, m, hl, bp, hsn, hsb, pz=pz, X2T=X2T: e.matmul(pz[:64, hsb], lhsT=csx[X2T][:, hsb], rhs=csx["T"][:, hsb], start=True, stop=True),
                                 ck(X2T) + ck("T"), [pzk])
                        A("dve", lambda e, pz=pz: e.tensor_tensor(out=csx["T"][:, :], in0=pz[:64, :], in1=csx["T"][:, :], op=ALU.add), [pzk] + ck("T"), ck("T"))
                        X, XT = X2, X2T

                    def state_mm(lhs_n, extra, out_name, outv_lo, outv_hi, rkeys):
                        pa, pak = nps()
                        pb, pbk = nps()

                        def f(e):
                            last = None
                            for h, m, hl, bp, hsn, hsb in heads():
                                if hl == 0:
                                    e.matmul(pa[:64, hsb], lhsT=fm(lhs_n, m, 0), rhs=ST[0:64, m, :], start=True, stop=False)
                                    last = extra(e, pa, hsn, hsb, False)
                                else:
                                    last = extra(e, pa, hsn, hsb, True)
                            for h, m, hl, bp, hsn, hsb in heads():
                                if hl == 1:
                                    last = e.matmul(pb[:64, m * 64:(m + 1) * 64], lhsT=fm(lhs_n, m, 64), rhs=ST[64:128, m, :], start=True, stop=True)
                            return last
                        A("pe", f, allm(lhs_n) + ["ST"] + rkeys, [pak, pbk])
                        A("act", lambda e: e.activation(out=csx["sq"][:, 0:256], in_=pb[:64, 0:256], func=AF.Copy), [pbk], ["c_sq"])
                        A("act", lambda e: e.activation(out=outv_lo, in_=pa[:64, 0:256].rearrange("p (m v) -> p m v", v=64), func=AF.Copy), [pak], [(out_name, 0)])
                        A("dve", lambda e: e.tensor_tensor(out=outv_hi, in0=pa[:64, 256:512].rearrange("p (m v) -> p m v", v=64),
                                                           in1=csx["sq"][:, 0:256].rearrange("p (m v) -> p m v", v=64), op=ALU.add), [pak, "c_sq"], [(out_name, 1)])

                    splits.append(len(P.ops))
                    pspool[0] = "ps2"
                    wt3 = csx["WT"][:, :].rearrange("p (b v) -> p b v", v=64)
                    state_mm(AT, lambda e, pa, hsn, hsb, first: e.matmul(pa[:64, hsb], lhsT=csx["Aak"][:, hsb], rhs=Vtm[:, hsn], start=first, stop=True),
                             "c_WT", wt3[:, 0:4, :], wt3[:, 4:8, :], ck("Aak") + [(tmk, 0)])
                    pz, pzk = nps()
                    allheads(lambda e, h, m, hl, bp, hsn, hsb, pz=pz: e.matmul(pz[:64, hsb], lhsT=csx["T"][:, hsb], rhs=csx["WT"][:, hsb], start=True, stop=True),
                             ck("T") + ck("WT"), [pzk])
                    A("act", lambda e, pz=pz: e.activation(out=csx["UT"][:, :], in_=pz[:64, :], func=AF.Copy), [pzk], ck("UT"))
                    o4 = csx["o"][:, :].rearrange("p (m hl v) -> p m hl v", hl=2, v=64)
                    state_mm(RT, lambda e, pa, hsn, hsb, first: [e.matmul(pa[:64, hsb], lhsT=csx["Arb"][:, hsb], rhs=csx["UT"][:, hsb], start=first, stop=False),
                                                                e.matmul(pa[:64, hsb], lhsT=csx["Ark"][:, hsb], rhs=Vtm[:, hsn], start=False, stop=True)][-1],
                             "c_o", o4[:, :, 0, :], o4[:, :, 1, :], ck("Arb") + ck("UT") + ck("Ark") + [(tmk, 0)])
                    pz, pzk = nps()
                    allheads(lambda e, h, m, hl, bp, hsn, hsb, pz=pz: [e.matmul(pz[bp:bp + 64, m * 64:(m + 1) * 64], lhsT=Bhtm[:, hsn], rhs=csx["UT"][:, hsb], start=True, stop=False),
                                                                       e.matmul(pz[bp:bp + 64, m * 64:(m + 1) * 64], lhsT=Khtm[:, hsn], rhs=Vtm[:, hsn], start=False, stop=True)][-1],
                             ck("UT") + [(tmk, 0), (tmk, 1), (tmk, 2)], [pzk])
                    for m in range(MT):
                        A("dve", lambda e, pz=pz, m=m, c=c: e.scalar_tensor_tensor(out=ST[:, m, :], in0=ST[:, m, :], scalar=F[PP][:, m, c * C + C - 1:c * C + C],
                                                                                 in1=pz[:, m * 64:(m + 1) * 64], op0=ALU.mult, op1=ALU.add),
                          ["ST", pzk, ("F", PP, m)], ["ST"])
                    o3 = csx["o"][:, :].rearrange("p (h v) -> p h v", v=64)
                    cok = [("c_o", 0), ("c_o", 1)]
                    A("dve", lambda e, o3=o3: e.tensor_reduce(out=st["s1"][:, :], in_=o3, axis=AX.X, op=ALU.add), cok, ["st_s1"])
                    A("dve", lambda e: e.tensor_tensor(out=csx["sq"][:, :], in0=csx["o"][:, :], in1=csx["o"][:, :], op=ALU.mult), cok, ["c_sq"])
                    A("dve", lambda e: e.tensor_reduce(out=st["s2"][:, :], in_=csx["sq"][:, :].rearrange("p (h v) -> p h v", v=64), axis=AX.X, op=ALU.add), ["c_sq"], ["st_s2"])
                    A("dve", lambda e: e.tensor_scalar(out=st["mean"][:, :], in0=st["s1"][:, :], scalar1=1.0 / 64, scalar2=None, op0=ALU.mult), ["st_s1"], ["st_mean"])
                    A("dve", lambda e: e.tensor_tensor(out=st["var"][:, :], in0=st["mean"][:, :], in1=st["mean"][:, :], op=ALU.mult), ["st_mean"], ["st_var"])
                    A("dve", lambda e: e.scalar_tensor_tensor(out=st["var"][:, :], in0=st["s2"][:, :], scalar=1.0 / 64, in1=st["var"][:, :], op0=ALU.mult, op1=ALU.subtract),
                      ["st_s2", "st_var"], ["st_var"])
                    A("dve", lambda e: e.tensor_scalar(out=st["var"][:, :], in0=st["var"][:, :], scalar1=RW_GN_EPS, scalar2=None, op0=ALU.add), ["st_var"], ["st_var"])
                    A("act", lambda e: e.activation(out=st["rstd"][:, :], in_=st["var"][:, :], func=AF.Sqrt), ["st_var"], ["st_rstd"])
                    A("dve", lambda e: e.reciprocal(out=st["rstd"][:, :], in_=st["rstd"][:, :]), ["st_rstd"], ["st_rstd"])
                    on3 = csx["on"][:, :].rearrange("p (h v) -> p h v", v=64)
                    A("dve", lambda e, o3=o3, on3=on3: e.tensor_tensor(out=on3, in0=o3, in1=st["mean"][:, :].unsqueeze(2).broadcast_to([64, 8, 64]), op=ALU.subtract),
                      cok + ["st_mean"], ["c_on"])
                    A("dve", lambda e, on3=on3: e.tensor_tensor(out=on3, in0=on3, in1=st["rstd"][:, :].unsqueeze(2).broadcast_to([64, 8, 64]), op=ALU.mult),
                      ["c_on", "st_rstd"], ["c_on"])
                    pz, pzk = nps()
                    A("pe", lambda e, pz=pz: [e.transpose(out=pz[:, m * 64:(m + 1) * 64], in_=csx["on"][:, m * 128:(m + 1) * 128], identity=ident[0:64, 0:64])
                                              for m in range(MT)][-1], ["c_on", "ident"], [pzk])
                    for m in range(MT):
                        A("dve", lambda e, pz=pz, m=m, csl=csl, mo=mo: e.tensor_scalar(out=mo[:, m, csl], in0=pz[:, m * 64:(m + 1) * 64], scalar1=prm[:, LG, m:m + 1],
                                                                                      scalar2=prm[:, LB, m:m + 1], op0=ALU.mult, op1=ALU.add),
                          [pzk, "prm"], [(mok, m)])
                        A("pool", lambda e, m=m, csl=csl, mo=mo: e.tensor_tensor(out=mo[:, m, csl], in0=mo[:, m, csl], in1=F[BON][:, m, csl], op=ALU.add),
                          [(mok, m), ("F", BON, m)], [(mok, m)])
                        A("pool", lambda e, m=m, csl=csl, mo=mo: e.tensor_tensor(out=mo[:, m, csl], in0=mo[:, m, csl], in1=F[SZ][:, m, csl], op=ALU.mult),
                          [(mok, m), ("F", SZ, m)], [(mok, m)])
            start = len(P.ops)
            splits, bounds = [], []
            for c in range(NCK):
                b0 = len(P.ops)
                do_chunk(c)
                bounds.append((b0, splits[-1], len(P.ops)))
            if pipeline:
                rec = P.ops[start:]
                del P.ops[start:]
                ph1 = [rec[b0 - start:sp_ - start] for (b0, sp_, b1) in bounds]
                ph2 = [rec[sp_ - start:b1 - start] for (b0, sp_, b1) in bounds]
                P.ops.extend(ph1[0])
                for c in range(NCK):
                    a_, b_ = ph2[c], (ph1[c + 1] if c + 1 < NCK else [])
                    ia = ib = 0
                    while ia < len(a_) or ib < len(b_):
                        if ib >= len(b_) or (ia < len(a_) and ia * len(b_) <= ib * len(a_)):
                            P.ops.append(a_[ia]); ia += 1
                        else:
                            P.ops.append(b_[ib]); ib += 1
            P.dma(mTv[:, :, j * TT:(j + 1) * TT], mo[:, :, :], r=[(mok, m) for m in range(MT)], w=[("out", j)])
            outkeys.append(("out", j))

    for j in range(NT // TT if ntiles is None else ntiles):
        do_tile(j)
    P.add("sp", None, r=outkeys)
    return P.finalize() if own else None


def rwa_layer(xfull, prm):
    TT = 256
    ident = np.eye(128, dtype=np.float32)
    onesb = np.zeros((128, 128), np.float32)
    onesb[:64, :64] = 1.0; onesb[64:, 64:] = 1.0
    i_, t_ = np.meshgrid(np.arange(64), np.arange(64), indexing="ij")
    msk = np.stack([np.tile((i_ < t_), (1, 8)), np.tile((i_ <= t_), (1, 8)), np.tile((i_ > t_), (1, 8)),
                    np.tile(np.eye(64), (1, 8))], axis=1).astype(np.float32)
    rst = np.ones((128, TT), np.float32); rst[:, ::64] = 0.0
    mu = np.stack([pp(prm["mu"][i]) for i in range(6)], axis=1)
    in_maps = []
    for c in range(NCORES):
        b, hh = c // 2, c % 2
        cs_ = slice(512 * hh, 512 * hh + 512)
        pv = [prm["w0"], prm["a0"], prm["k_k"], prm["k_a"], prm["r_k"].reshape(-1), prm["lnx_g"], prm["lnx_b"]]
        prmv = np.stack([pp(v_[cs_]) for v_ in pv], axis=1)
        d = dict(xT=np.ascontiguousarray(np.concatenate([np.zeros((D, 1), np.float32), xfull[b]], axis=1)), mu=np.ascontiguousarray(mu),
                 w1=prm["w1"], a1=prm["a1"], w2=np.ascontiguousarray(prm["w2"][:, cs_]), a2=np.ascontiguousarray(prm["a2"][:, cs_]),
                 prm=np.ascontiguousarray(prmv), ident=ident, onesb=onesb, msk=np.ascontiguousarray(msk), rst=rst)
        for qi, q in enumerate("rkvz"):
            d["w_" + q] = np.ascontiguousarray(prm["w_rkvz"][qi][:, cs_])
        in_maps.append(d)
    res = run("rwa", build_rwa, in_maps)
    return [np.ascontiguousarray(np.concatenate([res[2 * b]["mT"], res[2 * b + 1]["mT"]], axis=0)) for b in range(4)]


def build_outb(NT=2048, TT=256, sel=False, P=None):
    own = P is None
    P = P or Prog()
    xT = P.dram("xT", [D, NT])
    mT_d = P.dram("mT", [D, NT * (2 if sel else 1)])
    pT = P.dram("pT", [256, NT])
    w_out_d = P.dram("w_out", [D, D])
    w_pg_d = P.dram("w_pg", [D, D])
    w_pe_d = P.dram("w_pe", [256, D])
    lng_d = P.dram("ln_g", [128, 8])
    lnb_d = P.dram("ln_b", [128, 8])
    oT = P.dram("oT", [D, NT], out=True)
    P.pool("wst", 2, [128, 1024], F32)
    P.pool("ps", 6, [128, 512], F32, psum=True)
    P.pool("psA", 2, [128, 512], F32, psum=True)
    w_out = load_weight_bf16(P, "w_out", w_out_d, D, D, "wst")
    w_pg, w_pe, ones, lng, lnb = tail_setup(P, TT, w_pg_d, w_pe_d, lng_d, lnb_d)
    if sel:
        P.pool("g2", 1, [128, 8, TT], F32)
        selt = P.sb("sel", [128, 2], F32)
        P.dma(selt[:, :], P.dram("sel", [128, 2]), w=["sel"])
    P.pool("x", 2, [128, 8, TT], F32)
    P.pool("g", 2, [128, 8, TT], F32)
    P.pool("p", 2, [128, 2, TT], F32)
    P.pool("pb", 2, [128, 2, TT], BF16)
    P.pool("m", 2, [128, 8, TT], BF16)
    P.pool("o", 2, [128, 8, TT], F32)
    xTv = xT.rearrange("(k p) t -> p k t", p=128)
    mTv = mT_d.rearrange("(k p) t -> p k t", p=128)
    pTv = pT.rearrange("(k p) t -> p k t", p=128)
    oTv = oT.rearrange("(k p) t -> p k t", p=128)
    outkeys = []

    def do_tile(j):
        sl = slice(j * TT, (j + 1) * TT)
        x, xk = P.nxt("x"); g, gk = P.nxt("g"); pt, ptk = P.nxt("p"); pb, pbk = P.nxt("pb"); m, mk = P.nxt("m")
        P.dma(x[:, :, :], xTv[:, :, sl], w=[xk])
        P.dma(g[:, :, :], mTv[:, :, sl], w=[gk])
        if sel:
            g2, g2k = P.nxt("g2")
            P.dma(g2[:, :, :], mTv[:, :, NT + j * TT:NT + (j + 1) * TT], w=[g2k])
            P.add("pool", lambda e: e.tensor_scalar(out=g[:, :, :], in0=g[:, :, :], scalar1=selt[:, 0:1], scalar2=None, op0=ALU.mult),
                  r=[gk, "sel"], w=[gk])
            P.add("dve", lambda e: e.scalar_tensor_tensor(out=g[:, :, :], in0=g2[:, :, :], scalar=selt[:, 1:2], in1=g[:, :, :],
                                                          op0=ALU.mult, op1=ALU.add), r=[gk, g2k, "sel"], w=[gk])
        P.dma(pt[:, :, :], pTv[:, :, sl], w=[ptk])
        P.add("pool", lambda e: e.tensor_copy(out=pb[:, :, :], in_=pt[:, :, :]), r=[ptk], w=[pbk])
        for k in range(8):
            P.add("pool", lambda e, k=k: e.tensor_copy(out=m[:, k, :], in_=g[:, k, :]), r=[gk], w=[(mk, k)])

        def y_ps(d):
            yp, yk = P.nxt("ps")
            P.add("pe", lambda e: [e.matmul(yp[:, :TT], lhsT=w_out[:, k, d * 128:(d + 1) * 128], rhs=m[:, k, :],
                                            start=(k == 0), stop=(k == 7)) for k in range(8)][-1],
                  r=[(mk, k) for k in range(8)] + [("w_out", k) for k in range(8)], w=[yk])
            return yp[:, :TT], yk

        o, ok = P.nxt("o")
        tail(P, TT, x, xk, y_ps, pb, pbk, w_pg, w_pe, ones, lng, lnb, o, ok, "o")
        P.dma(oTv[:, :, sl], o[:, :, :], r=[(ok, d) for d in range(8)], w=[("out", j)])
        outkeys.append(("out", j))
    for j in range(NT // TT):
        do_tile(j)
    P.add("sp", None, r=outkeys)
    return P.finalize() if own else None


def outb_layer(xs, ms, p_l, w_out, w_pg, w_pe, ln_g, ln_b):
    in_maps = []
    for c in range(NCORES):
        b, h = c // 2, c % 2
        in_maps.append(dict(xT=xs[c], mT=ms[c], pT=np.ascontiguousarray(p_l[b, h * 2048:(h + 1) * 2048].T),
                            w_out=w_out, w_pg=w_pg, w_pe=w_pe, ln_g=pp(ln_g), ln_b=pp(ln_b)))
    res = run("outb", build_outb, in_maps)
    return [r["oT"] for r in res]


_NC_CACHE = {}


def run(name, builder, in_maps):
    if name not in _NC_CACHE:
        _NC_CACHE[name] = builder()
    res = run_bass_kernel_spmd(_NC_CACHE[name], in_maps, core_ids=list(range(NCORES)))
    return res.results


def conv_layer(xs, p_l, w_in, conv_k, w_out, w_pg, w_pe, ln_g, ln_b):
    ck = np.concatenate([pp(conv_k[j]) for j in range(3)], axis=1)
    in_maps = []
    for c in range(NCORES):
        b, h = c // 2, c % 2
        halo = np.zeros((D, 2), np.float32) if h == 0 else xs[c - 1][:, -2:]
        in_maps.append(dict(xT=np.ascontiguousarray(np.concatenate([halo, xs[c]], axis=1)),
                            pT=np.ascontiguousarray(p_l[b, h * 2048:(h + 1) * 2048].T),
                            w_in=w_in, conv_k=ck, w_out=w_out, w_pg=w_pg, w_pe=w_pe, ln_g=pp(ln_g), ln_b=pp(ln_b)))
    res = run("conv", build_conv, in_maps)
    return [r["oT"] for r in res]


NTF = 4096


def build_fused():
    NT = NTF
    P = Prog()
    x1s = P.scratch("x1s", [D, NT])
    gyp = P.scratch("gyp", [D, 8, NT // 8])
    x2s = P.scratch("x2s", [D, NT + 1])
    ms = P.scratch("ms", [D, NT])
    x3s = P.scratch("x3s", [D, NT + 2])
    with P.stage("Z_"):
        z = P.sb("z", [128, 2])
        P.add("pool", lambda e: e.memset(z[:, :], 0.0), w=["z"])
        for k in range(8):
            P.dma(x2s[k * 128:(k + 1) * 128, 0:1], z[:, 0:1], r=["z"], w=[("zz", k)], slow=True)
            P.dma(x3s[k * 128:(k + 1) * 128, 0:2], z[:, 0:2], r=["z"], w=[("zz", k)], slow=True)
    with P.stage("L0_", {"oT": x1s}):
        build_conv(NT=NT, P=P)
    for hf in range(2):
        with P.stage(f"L1a{hf}_", {"xT": x1s, "gTp": gyp[512 * hf:512 * hf + 512]}):
            build_s5a(P=P)
    with P.stage("L1b_", {"xT": x1s, "gT": gyp, "oT": x2s[:, 1:NT + 1]}):
        build_s5b(NT=NT, perm_g=True, P=P)
    for hf in range(2):
        with P.stage(f"L2a{hf}_", {"xT": x2s, "mT": ms[512 * hf:512 * hf + 512]}):
            build_rwa(NT=NT, P=P)
    with P.stage("L2b_", {"xT": x2s[:, 1:NT + 1], "mT": ms, "oT": x3s[:, 2:NT + 2]}):
        build_outb(NT=NT, P=P)
    with P.stage("L3_", {"xT": x3s}):
        build_conv(NT=NT, P=P)
    return P.finalize()


PAIRS = [[0, 1], [2, 3], [4, 5], [6, 7]]


def build_fused_pair():
    NT, NH = 4096, 2048
    P = Prog()
    x1h = P.scratch("x1h", [D, NH]); x1f = P.scratch("x1f", [D, NT])
    gym = P.scratch("gym", [512, 8, NT // 8]); gyg = P.scratch("gyg", [D, 8, NT // 8])
    x2h = P.scratch("x2h", [D, NH]); x2f = P.scratch("x2f", [D, NT + 1])
    msm = P.scratch("msm", [512, NT]); msg = P.scratch("msg", [D, NT])
    x3p = P.scratch("x3p", [D, NH + 2]); hlm = P.scratch("hlm", [D, 2]); hlg = P.scratch("hlg", [2 * D, 2])
    def gather(name, src, rows, place):
        R, C = src.shape
        for q in range(R // rows):
            gq = P.scratch(f"{name}_g{q}", [2 * rows, C])
            P.cc_allgather(src[q * rows:(q + 1) * rows, :], gq, PAIRS, w=[(name, q)])
            for rk in range(2):
                P.dma(place(q, rk), gq[rk * rows:(rk + 1) * rows, :], r=[(name, q)], w=[(name, q, rk)])

    with P.stage("Z_"):
        z = P.sb("z", [128, 2])
        P.add("pool", lambda e: e.memset(z[:, :], 0.0), w=["z"])
        for k in range(8):
            P.dma(x2f[k * 128:(k + 1) * 128, 0:1], z[:, 0:1], r=["z"], w=[("zz", k)], slow=True)
    with P.stage("L0_", {"oT": x1h}):
        build_conv(NT=NH, P=P)
    with P.stage("X1_"):
        gather("x1", x1h, 256, lambda q, rk: x1f[q * 256:(q + 1) * 256, rk * NH:(rk + 1) * NH])
    with P.stage("L1a_", {"xT": x1f, "gTp": gym}):
        build_s5a(P=P)
    with P.stage("X2_"):
        gyg2 = gyg.rearrange("c t n -> c (t n)")
        gather("gy", gym.rearrange("c t n -> c (t n)"), 128, lambda q, rk: gyg2[512 * rk + q * 128:512 * rk + (q + 1) * 128, :])
    with P.stage("L1b_", {"xT": x1h, "gT": gyg, "oT": x2h}):
        build_s5b(NT=NH, perm_g=True, sel=True, P=P)
    with P.stage("X3_"):
        gather("x2", x2h, 256, lambda q, rk: x2f[q * 256:(q + 1) * 256, 1 + rk * NH:1 + (rk + 1) * NH])
    with P.stage("L2a_", {"xT": x2f, "mT": msm}):
        build_rwa(NT=NT, P=P)
    with P.stage("X4_"):
        gather("ms", msm, 128, lambda q, rk: msg[512 * rk + q * 128:512 * rk + (q + 1) * 128, :])
    with P.stage("L2b_", {"xT": x2h, "mT": msg, "oT": x3p[:, 2:NH + 2]}):
        build_outb(NT=NH, sel=True, P=P)
    with P.stage("X5_"):
        P.dma(hlm[:, :], x3p[:, NH:NH + 2], w=["hlm"])
        P.cc_allgather(hlm, hlg, PAIRS, r=["hlm"], w=["hlg"])
        ht = P.sb("ht", [128, 8, 2]); selt = P.sb("sel", [128, 2])
        P.dma(selt[:, :], P.dram("sel", [128, 2]), w=["sel"])
        P.dma(ht[:, :, :], hlg[0:D, :].rearrange("(k p) t -> p k t", p=128), r=["hlg"], w=["ht"])
        P.add("dve", lambda e: e.tensor_scalar(out=ht[:, :, :], in0=ht[:, :, :], scalar1=selt[:, 1:2], scalar2=None, op0=ALU.mult),
              r=["ht", "sel"], w=["ht"])
        P.dma(x3p[:, 0:2].rearrange("(k p) t -> p k t", p=128), ht[:, :, :], r=["ht"], w=["x3halo"])
    with P.stage("L3_", {"xT": x3p}):
        build_conv(NT=NH, P=P)
    return P.finalize()


def kernel(**inp):
    inp = {k: np.asarray(v, np.float32) for k, v in inp.items()}
    x = inp["x"]
    sp = {k[4:]: inp[k][0] for k in inp if k.startswith("ssm_")}
    rp = {k[5:]: inp[k][0] for k in inp if k.startswith("rwkv_")}
    shared = {}

    def put(dst, prefix, d):
        for k, v in d.items():
            dst[prefix + k] = np.ascontiguousarray(v, dtype=np.float32)
    for (pre, i, j) in (("L0_", 0, 0), ("L3_", 3, 1)):
        put(shared, pre, dict(w_in=inp["conv_w_in"][j], conv_k=np.concatenate([pp(inp["conv_k"][j][t]) for t in range(3)], axis=1),
                              w_out=inp["conv_w_out"][j], **_tail_in(inp, i)))
    put(shared, "L1b_", dict(w_z=sp["w_in"][:, D:], w_glu=sp["w_glu"], b_glu=pp(sp["b_glu"]), w_out=sp["w_out"], **_tail_in(inp, 1)))
    put(shared, "L2b_", dict(w_out=rp["w_out"], **_tail_in(inp, 2)))
    halfp = [{}, {}]
    for hf in range(2):
        put(halfp[hf], "L1a_", _s5a_in(sp, hf))
        put(halfp[hf], "L2a_", _rwa_in(rp, hf))
    in_maps = []
    for c in range(NCORES):
        b, h = c // 2, c % 2
        ts = slice(2048 * h, 2048 * h + 2048)
        d = dict(shared)
        d.update(halfp[h])
        halo = np.zeros((D, 2), np.float32) if h == 0 else x[b, 2046:2048].T
        d["L0_xT"] = np.ascontiguousarray(np.concatenate([halo, x[b, ts].T], axis=1))
        for (pre, i) in (("L0_", 0), ("L1b_", 1), ("L2b_", 2), ("L3_", 3)):
            d[pre + "pT"] = np.ascontiguousarray(inp["p"][i][b, ts].T)
        selv = np.zeros((128, 2), np.float32); selv[:, h] = 1.0
        for pre in ("L1b_", "L2b_", "X5_"):
            d[pre + "sel"] = selv
        in_maps.append(d)
    res = run("fusedp", build_fused_pair, in_maps)
    out = np.zeros_like(x)
    for c in range(NCORES):
        out[c // 2, (c % 2) * 2048:(c % 2 + 1) * 2048] = res[c]["L3_oT"].T
    return out


def _tail_in(inp, i):
    return dict(w_pg=inp["ple_gate"][i], w_pe=inp["ple_proj"][i], ln_g=pp(inp["ln_g"][i]), ln_b=pp(inp["ln_b"][i]))


def _s5a_in(prm, gh):
    NG = 32
    ident = np.eye(128, dtype=np.float32)
    j2 = np.zeros((128, 128), np.float32)
    j2[np.arange(64), np.arange(64) + 64] = 1.0
    j2[np.arange(64) + 64, np.arange(64)] = -1.0
    msk = np.zeros((128, 10), np.float32)
    msk[:64, 0] = 1.0; msk[64:, 1] = 1.0
    identg = np.zeros((128, 8, 16), np.float32)
    for gl in range(8):
        msk[16 * gl:16 * gl + 16, 2 + gl] = 1.0
        for h in range(16):
            identg[16 * gl + h, gl, h] = 1.0
    gs = slice(NG * gh, NG * gh + NG)

    def cl(a):
        a = np.broadcast_to(a.reshape(NG // 8, 8, 1, 64), (NG // 8, 8, 16, 64))
        return np.ascontiguousarray(a.transpose(1, 2, 0, 3).reshape(128, NG // 8, 64))

    def clb(a):
        a = a.reshape(NG // 8, 8, 64, 16)
        return np.ascontiguousarray(a.transpose(1, 3, 0, 2).reshape(128, NG // 8, 64))

    def plv(a):
        a = np.broadcast_to(a.T.reshape(1, 64, NG, 1), (2, 64, NG, 16))
        return np.ascontiguousarray(a.reshape(128, NG, 16))

    def plb(a):
        a = np.broadcast_to(a.transpose(1, 0, 2)[None], (2, 64, NG, 16))
        return np.ascontiguousarray(a.reshape(128, NG, 16))

    def plc(a):
        a = np.broadcast_to(a.transpose(2, 0, 1)[None], (2, 64, NG, 16))
        return np.ascontiguousarray(a.reshape(128, NG, 16))
    ldt = np.broadcast_to(prm["log_dt"][gs][:, None], (NG, 64))
    return dict(
        w_u=np.ascontiguousarray(prm["w_in"][:, 512 * gh:512 * gh + 512]),
        lamre_cl=cl(prm["lam_re"][gs]), lamim_cl=cl(prm["lam_im"][gs]), ldt_cl=cl(ldt),
        bre_cl=clb(prm["b_re"][gs]), bim_cl=clb(prm["b_im"][gs]),
        lamre_pl=plv(prm["lam_re"][gs]), lamim_pl=plv(prm["lam_im"][gs]), ldt_pl=plv(ldt),
        bre_pl=plb(prm["b_re"][gs]), bim_pl=plb(prm["b_im"][gs]),
        cre_pl=plc(prm["c_re"][gs]), cim_pl=plc(prm["c_im"][gs]),
        dsk=pp(prm["d"][512 * gh:512 * gh + 512]), ident=ident, j2=j2, msk=msk, identg=identg)


def _rwa_in(prm, hh):
    TT = 256
    ident = np.eye(128, dtype=np.float32)
    onesb = np.zeros((128, 128), np.float32)
    onesb[:64, :64] = 1.0; onesb[64:, 64:] = 1.0
    i_, t_ = np.meshgrid(np.arange(64), np.arange(64), indexing="ij")
    msk = np.stack([np.tile((i_ < t_), (1, 8)), np.tile((i_ <= t_), (1, 8)), np.tile((i_ > t_), (1, 8)),
                    np.tile(np.eye(64), (1, 8))], axis=1).astype(np.float32)
    rst = np.ones((128, TT), np.float32); rst[:, ::64] = 0.0
    mu = np.stack([pp(prm["mu"][i]) for i in range(6)], axis=1)
    cs_ = slice(512 * hh, 512 * hh + 512)
    pv = [prm["w0"], prm["a0"], prm["k_k"], prm["k_a"], prm["r_k"].reshape(-1), prm["lnx_g"], prm["lnx_b"]]
    prmv = np.stack([pp(v_[cs_]) for v_ in pv], axis=1)
    d = dict(mu=np.ascontiguousarray(mu), w1=prm["w1"], a1=prm["a1"], w2=np.ascontiguousarray(prm["w2"][:, cs_]),
             a2=np.ascontiguousarray(prm["a2"][:, cs_]), prm=np.ascontiguousarray(prmv), ident=ident, onesb=onesb,
             msk=np.ascontiguousarray(msk), rst=rst)
    for qi, q in enumerate("rkvz"):
        d["w_" + q] = np.ascontiguousarray(prm["w_rkvz"][qi][:, cs_])
    return d


def kernel_fused_b(**inp):
    inp = {k: np.asarray(v, np.float32) for k, v in inp.items()}
    x = inp["x"]
    sp = {k[4:]: inp[k][0] for k in inp if k.startswith("ssm_")}
    rp = {k[5:]: inp[k][0] for k in inp if k.startswith("rwkv_")}
    shared = {}

    def put(prefix, d):
        for k, v in d.items():
            shared[prefix + k] = np.ascontiguousarray(v, dtype=np.float32)
    for (pre, i, j) in (("L0_", 0, 0), ("L3_", 3, 1)):
        put(pre, dict(w_in=inp["conv_w_in"][j], conv_k=np.concatenate([pp(inp["conv_k"][j][t]) for t in range(3)], axis=1),
                      w_out=inp["conv_w_out"][j], **_tail_in(inp, i)))
    for hf in range(2):
        put(f"L1a{hf}_", _s5a_in(sp, hf))
        put(f"L2a{hf}_", _rwa_in(rp, hf))
    put("L1b_", dict(w_z=sp["w_in"][:, D:], w_glu=sp["w_glu"], b_glu=pp(sp["b_glu"]), w_out=sp["w_out"], **_tail_in(inp, 1)))
    put("L2b_", dict(w_out=rp["w_out"], **_tail_in(inp, 2)))
    in_maps = []
    for c in range(NCORES):
        b = c % 4
        d = dict(shared)
        d["L0_xT"] = np.ascontiguousarray(np.concatenate([np.zeros((D, 2), np.float32), x[b].T], axis=1))
        for (pre, i) in (("L0_", 0), ("L1b_", 1), ("L2b_", 2), ("L3_", 3)):
            d[pre + "pT"] = np.ascontiguousarray(inp["p"][i][b].T)
        in_maps.append(d)
    res = run("fused", build_fused, in_maps)
    return np.ascontiguousarray(np.stack([res[b]["L3_oT"].T for b in range(4)]))


def _halves(full):
    return [np.ascontiguousarray(full[c // 2][:, (c % 2) * 2048:(c % 2 + 1) * 2048]) for c in range(NCORES)]


def _fulls(halves):
    return [np.ascontiguousarray(np.concatenate([halves[2 * b], halves[2 * b + 1]], axis=1)) for b in range(4)]


def kernel_unfused(**inp):
    inp = {k: np.asarray(v, np.float32) for k, v in inp.items()}
    x = inp["x"]
    xs = [np.ascontiguousarray(x[c // 2, (c % 2) * 2048:(c % 2 + 1) * 2048].T) for c in range(NCORES)]
    for i in range(DEPTH):
        kind, j = i % 3, i // 3
        tailw = (inp["ple_gate"][i], inp["ple_proj"][i], inp["ln_g"][i], inp["ln_b"][i])
        if kind == 0:
            xs = conv_layer(xs, inp["p"][i], inp["conv_w_in"][j], inp["conv_k"][j], inp["conv_w_out"][j], *tailw)
        elif kind == 1:
            prm = {k[4:]: inp[k][j] for k in inp if k.startswith("ssm_")}
            gy = s5a_layer(_fulls(xs), prm)
            xs = s5b_layer(xs, _halves(gy), inp["p"][i], prm["w_in"], prm["w_glu"], prm["b_glu"], prm["w_out"], *tailw)
        else:
            prm = {k[5:]: inp[k][j] for k in inp if k.startswith("rwkv_")}
            ms = rwa_layer(_fulls(xs), prm)
            xs = outb_layer(xs, _halves(ms), inp["p"][i], prm["w_out"], *tailw)
    out = np.zeros_like(x)
    for c in range(NCORES):
        out[c // 2, (c % 2) * 2048:(c % 2 + 1) * 2048] = xs[c].T
    return out
```

```python
from contextlib import ExitStack

import numpy as np
import concourse.bass as bass
import concourse.mybir as mybir
from concourse.bass_utils import run_bass_kernel_spmd

F32 = mybir.dt.float32
BF16 = mybir.dt.bfloat16
AF = mybir.ActivationFunctionType
ALU = mybir.AluOpType
AX = mybir.AxisListType

D = 1024
DEPTH = 4
DN_ALPHA = (2 * DEPTH) ** 0.25
LN_EPS = 1e-5
NCORES = 8


class Prog:
    COMPUTE = ("pe", "act", "dve", "pool")
    BLOCKNAME = {"pe": "tensor", "act": "scalar", "dve": "vector", "pool": "gpsimd", "sp": "sync"}
    NDMASEM = 12

    def __init__(self):
        self.nc = bass.Bass("TRN2", target_bir_lowering=False)
        self.ops = []
        self.stack = ExitStack()
        self._rr = {}
        self.prefix = ""
        self.bound = {}
        self.psum_keys = set()

    def dram(self, name, shape, dtype=F32, out=False):
        name = self.prefix + name
        if name in self.bound:
            ap = self.bound[name]
            assert list(ap.shape) == list(shape), (name, ap.shape, shape)
            return ap
        return self.nc.dram_tensor(name, list(shape), dtype,
                                   kind="ExternalOutput" if out else "ExternalInput").ap()

    def scratch(self, name, shape, dtype=F32):
        return self.nc.dram_tensor(name, list(shape), dtype).ap()

    def stage(self, prefix, bind=None):
        prog = self

        class _S:
            def __enter__(s_):
                s_.old = (prog.stack, prog.prefix)
                prog.stack = ExitStack()
                prog.prefix = prefix
                for k, v in (bind or {}).items():
                    prog.bound[prefix + k] = v

            def __exit__(s_, *a):
                prog.stack.close()
                prog.stack, prog.prefix = s_.old
                prog.ops.append(dict(barrier=True, eng=None, fn=None, r=[], w=[], dma=False))
                return False
        return _S()

    def sb(self, name, shape, dtype=F32):
        return self.stack.enter_context(self.nc.sbuf_tensor("s_" + self.prefix + name, list(shape), dtype))

    def ps(self, name, shape, dtype=F32):
        return self.stack.enter_context(self.nc.psum_tensor("p_" + self.prefix + name, list(shape), dtype))

    def pool(self, name, n, shape, dtype=F32, psum=False):
        mk = self.ps if psum else self.sb
        if psum:
            shape = [128, 512]
            self.psum_keys |= {f"{self.prefix}{name}{i}" for i in range(n)}
        tiles = [(mk(f"{name}{i}", shape, dtype), f"{self.prefix}{name}{i}") for i in range(n)]
        self._rr[name] = [tiles, 0]
        return tiles

    def nxt(self, name):
        ent = self._rr[name]
        t = ent[0][ent[1] % len(ent[0])]
        ent[1] += 1
        return t

    def add(self, eng, fn, r=(), w=(), dma=False):
        self.ops.append(dict(eng=eng, fn=fn, r=list(r), w=list(w), dma=dma))

    def dma(self, out, in_, r=(), w=(), q="sp", slow=False):
        if slow:
            self.add(q, lambda e: e.dma_start(out=out, in_=in_, allow_slow_non_contiguous=True), r=r, w=w, dma=True)
        else:
            self.add(q, lambda e: e.dma_start(out=out, in_=in_), r=r, w=w, dma=True)

    def cc_allgather(self, in_ap, out_ap, groups, r=(), w=()):
        self.ops.append(dict(eng="pool", fn=lambda e: e.collective_compute("AllGather", ALU.bypass, replica_groups=groups,
                                                                            ins=[in_ap.opt()], outs=[out_ap.opt()]),
                             r=list(r), w=list(w), dma=False, cc=True))

    def finalize(self):
        nc, ops = self.nc, self.ops
        last_w, readers = {}, {}
        bar = set()
        ccs = []
        last_c, last_d, dslot = {}, {}, {}
        for i, op in enumerate(ops):
            if op.get("barrier"):
                bar = set(last_c.values()) | set(last_d.values()) | set(ccs)
                ccs = []
                op["deps"] = set()
                continue
            if op.get("cc"):
                ccs.append(i)
            elif op["dma"]:
                sl_ = dslot.get(op["eng"], 0)
                dslot[op["eng"]] = sl_ + 1
                last_d[(op["eng"], sl_ % self.NDMASEM)] = i
            elif op["fn"] is not None:
                last_c[op["eng"]] = i
            deps = set(bar)
            raw = set()
            pk = self.psum_keys
            for k in op["r"]:
                if k in last_w:
                    deps.add(last_w[k]); raw.add(last_w[k])
                if k in pk:
                    for j in readers.get(k, ()):
                        if ops[j]["eng"] != op["eng"]:
                            deps.add(j)
            for k in op["w"]:
                if k in last_w:
                    deps.add(last_w[k])
                for j in readers.get(k, ()):
                    deps.add(j)
            deps.discard(i)
            op["deps"] = set(deps)
            if op["fn"] is not None:
                for k in op["r"]:
                    readers.setdefault(k, []).append(i)
            for k in op["w"]:
                last_w[k] = i
                readers[k] = []
        needed = set()
        for op in ops:
            needed |= op["deps"]
        engines = list(self.COMPUTE) + ["sp"]
        SEMMAX = 4000
        ccount = {e: 0 for e in engines}
        dcount = {}
        dn = {e: 0 for e in engines}
        semkeys = set()
        for i, op in enumerate(ops):
            e = op["eng"]
            if op.get("barrier"):
                op["sig"] = None
                continue
            if op.get("cc"):
                op["sig"] = (("cc", i), 1)
                semkeys.add(op["sig"][0])
                continue
            if op["dma"]:
                s = dn[e] % self.NDMASEM
                dn[e] += 1
                c = dcount.get((e, s), 0)
                per = SEMMAX // 16
                op["dprev"] = (("d", e, s, (c - 1) // per), 16 * ((c - 1) % per + 1)) if c > 0 else None
                dcount[(e, s)] = c + 1
                op["sig"] = (("d", e, s, c // per), 16 * (c % per + 1))
                semkeys.add(op["sig"][0])
            elif i in needed:
                c = ccount[e]
                ccount[e] += 1
                op["sig"] = (("c", e, c // SEMMAX), c % SEMMAX + 1)
                semkeys.add(op["sig"][0])
            else:
                op["sig"] = None
        self.nwait = {}
        with ExitStack() as st:
            sems = {k: st.enter_context(nc.semaphore("s_" + "_".join(str(x) for x in k))) for k in sorted(semkeys)}
            block = st.enter_context(nc.Block())
            for e in engines:
                mine = [op for op in ops if op["eng"] == e and not op.get("barrier")]
                if not mine:
                    continue

                def body(eng, mine=mine, eng_name=e):
                    known = {}

                    def wait(sig):
                        if sig is None:
                            return
                        key, val = sig
                        if known.get(key, 0) < val:
                            eng.wait_ge(sems[key], val)
                            known[key] = val
                            self.nwait[eng_name] = self.nwait.get(eng_name, 0) + 1

                    for op in mine:
                        need = {}
                        for j in op["deps"]:
                            sg = ops[j]["sig"]
                            if sg is not None and need.get(sg[0], 0) < sg[1]:
                                need[sg[0]] = sg[1]
                        if op["dma"] and op["dprev"] is not None:
                            sg = op["dprev"]
                            if need.get(sg[0], 0) < sg[1]:
                                need[sg[0]] = sg[1]
                        for kk_, vv_ in sorted(need.items(), key=lambda t: str(t[0])):
                            wait((kk_, vv_))
                        if op["fn"] is None:
                            continue
                        ins = op["fn"](eng)
                        if op["sig"] is not None:
                            ins.then_inc(sems[op["sig"][0]], 16 if op["dma"] else 1)

                getattr(block, self.BLOCKNAME[e])(body)
        self.counts = dict(ccount)
        self.stack.close()
        return nc


def pp(v):
    v = np.asarray(v, np.float32).reshape(-1, 128)
    return np.ascontiguousarray(v.T)


def load_weight_bf16(P, name, w_dram, K, N, stage_pool):
    kt = K // 128
    wt = P.sb(name, [128, kt, N], BF16)
    for k in range(kt):
        for c0 in range(0, N, 1024):
            c1 = min(N, c0 + 1024)
            st, sk = P.nxt(stage_pool)
            P.dma(st[:, : c1 - c0], w_dram[k * 128:(k + 1) * 128, c0:c1], w=[sk])
            if (k + c0 // 1024) % 2 == 0:
                P.add("dve", lambda e, st=st, k=k, c0=c0, c1=c1: e.tensor_copy(out=wt[:, k, c0:c1], in_=st[:, : c1 - c0]),
                      r=[sk], w=[(name, k)])
            else:
                P.add("act", lambda e, st=st, k=k, c0=c0, c1=c1: e.activation(out=wt[:, k, c0:c1], in_=st[:, : c1 - c0], func=AF.Copy),
                      r=[sk], w=[(name, k)])
    return wt


def tail(P, TT, x_res, xk, y_ps_fn, pT_bf, pk, w_pg, w_pe, ones, lng, lnb, out_tile, ok, tag):
    r, rk = P.nxt("r")
    rb, rbk = P.nxt("rb")
    for d in range(8):
        yp, yk = y_ps_fn(d)
        P.add("dve", lambda e, d=d, yp=yp: e.scalar_tensor_tensor(out=r[:, d, :], in0=x_res[:, d, :], scalar=float(DN_ALPHA),
                                                                 in1=yp, op0=ALU.mult, op1=ALU.add),
              r=[xk, yk], w=[(rk, d)])
        P.add("pool", lambda e, d=d: e.tensor_copy(out=rb[:, d, :], in_=r[:, d, :]), r=[(rk, d)], w=[(rbk, d)])
    sq, sqk = P.nxt("sq")
    mean_ps, mk = P.nxt("psA")
    msq_ps, qk = P.nxt("psA")
    for d in range(8):
        gp, gk = P.nxt("ps")
        P.add("pe", lambda e, d=d, gp=gp: [e.matmul(gp[:, :TT], lhsT=w_pg[:, k, d * 128:(d + 1) * 128], rhs=rb[:, k, :],
                                                    start=(k == 0), stop=(k == 7)) for k in range(8)][-1],
              r=[(rbk, k) for k in range(8)] + [("w_pg", k) for k in range(8)], w=[gk])
        sg, sgk = P.nxt("tmp")
        P.add("act", lambda e, gp=gp, sg=sg: e.activation(out=sg[:, :TT], in_=gp[:, :TT], func=AF.Sigmoid), r=[gk], w=[sgk])
        ep, ek = P.nxt("ps")
        P.add("pe", lambda e, d=d, ep=ep: [e.matmul(ep[:, :TT], lhsT=w_pe[:, k, d * 128:(d + 1) * 128], rhs=pT_bf[:, k, :],
                                                    start=(k == 0), stop=(k == 1)) for k in range(2)][-1],
              r=[pk] + [("w_pe", k) for k in range(2)], w=[ek])
        P.add("dve", lambda e, ep=ep, sg=sg: e.tensor_tensor(out=sg[:, :TT], in0=ep[:, :TT], in1=sg[:, :TT], op=ALU.mult),
              r=[ek, sgk], w=[sgk])
        P.add("dve", lambda e, d=d, sg=sg: e.tensor_tensor(out=r[:, d, :], in0=r[:, d, :], in1=sg[:, :TT], op=ALU.add),
              r=[sgk, (rk, d)], w=[(rk, d)])
        P.add("act", lambda e, d=d: e.activation(out=sq[:, d, :], in_=r[:, d, :], func=AF.Square), r=[(rk, d)], w=[(sqk, d)])
    P.add("pe", lambda e: [e.matmul(mean_ps[:, :TT], lhsT=ones[:, :], rhs=r[:, k, :], start=(k == 0), stop=(k == 7))
                           for k in range(8)][-1], r=[(rk, k) for k in range(8)] + ["ones"], w=[mk])
    P.add("pe", lambda e: [e.matmul(msq_ps[:, :TT], lhsT=ones[:, :], rhs=sq[:, k, :], start=(k == 0), stop=(k == 7))
                           for k in range(8)][-1], r=[(sqk, k) for k in range(8)] + ["ones"], w=[qk])
    mean, mnk = P.nxt("st")
    rstd, rsk = P.nxt("st")
    P.add("act", lambda e: e.activation(out=mean[:, :TT], in_=mean_ps[:, :TT], func=AF.Copy), r=[mk], w=[mnk])
    P.add("dve", lambda e: e.tensor_tensor(out=rstd[:, :TT], in0=mean[:, :TT], in1=mean[:, :TT], op=ALU.mult), r=[mnk], w=[rsk])
    P.add("dve", lambda e: e.tensor_tensor(out=rstd[:, :TT], in0=msq_ps[:, :TT], in1=rstd[:, :TT], op=ALU.subtract),
          r=[qk, rsk], w=[rsk])
    P.add("dve", lambda e: e.tensor_scalar(out=rstd[:, :TT], in0=rstd[:, :TT], scalar1=float(LN_EPS), scalar2=None, op0=ALU.add),
          r=[rsk], w=[rsk])
    P.add("act", lambda e: e.activation(out=rstd[:, :TT], in_=rstd[:, :TT], func=AF.Sqrt), r=[rsk], w=[rsk])
    P.add("dve", lambda e: e.reciprocal(out=rstd[:, :TT], in_=rstd[:, :TT]), r=[rsk], w=[rsk])
    for d in range(8):
        P.add("dve", lambda e, d=d: e.tensor_tensor(out=r[:, d, :], in0=r[:, d, :], in1=mean[:, :TT], op=ALU.subtract),
              r=[(rk, d), mnk], w=[(rk, d)])
        P.add("dve", lambda e, d=d: e.tensor_tensor(out=r[:, d, :], in0=r[:, d, :], in1=rstd[:, :TT], op=ALU.mult),
              r=[(rk, d), rsk], w=[(rk, d)])
        P.add("dve", lambda e, d=d: e.tensor_scalar(out=out_tile[:, d, :], in0=r[:, d, :], scalar1=lng[:, d:d + 1],
                                                    scalar2=lnb[:, d:d + 1], op0=ALU.mult, op1=ALU.add),
              r=[(rk, d), "lnp"], w=[(ok, d)])


def tail_setup(P, TT, w_pg_d, w_pe_d, lng_d, lnb_d):
    w_pg = load_weight_bf16(P, "w_pg", w_pg_d, 1024, 1024, "wst")
    w_pe = load_weight_bf16(P, "w_pe", w_pe_d, 256, 1024, "wst")
    ones = P.sb("ones", [128, 128], F32)
    P.add("pool", lambda e: e.memset(ones[:, :], 1.0 / D), w=["ones"])
    lng = P.sb("lng", [128, 8], F32)
    lnb = P.sb("lnb", [128, 8], F32)
    P.dma(lng[:, :], lng_d, w=["lnp"])
    P.dma(lnb[:, :], lnb_d, w=["lnp"])
    P.pool("r", 1, [128, 8, TT], F32)
    P.pool("rb", 1, [128, 8, TT], BF16)
    P.pool("sq", 1, [128, 8, TT], F32)
    P.pool("st", 2, [128, TT], F32)
    P.pool("tmp", 6, [128, TT + 2], F32)
    return w_pg, w_pe, ones, lng, lnb


def build_conv(NT=2048, TT=256, P=None):
    own = P is None
    P = P or Prog()
    xT = P.dram("xT", [D, NT + 2])
    pT = P.dram("pT", [256, NT])
    w_in_d = P.dram("w_in", [D, 4 * D])
    ck_d = P.dram("conv_k", [128, 24])
    w_out_d = P.dram("w_out", [D, D])
    w_pg_d = P.dram("w_pg", [D, D])
    w_pe_d = P.dram("w_pe", [256, D])
    lng_d = P.dram("ln_g", [128, 8])
    lnb_d = P.dram("ln_b", [128, 8])
    oT = P.dram("oT", [D, NT], out=True)

    P.pool("wst", 2, [128, 1024], F32)
    P.pool("ps", 6, [128, 512], F32, psum=True)
    P.pool("psA", 2, [128, 512], F32, psum=True)
    w_in = load_weight_bf16(P, "w_in", w_in_d, D, 4 * D, "wst")
    w_out = load_weight_bf16(P, "w_out", w_out_d, D, D, "wst")
    w_pg, w_pe, ones, lng, lnb = tail_setup(P, TT, w_pg_d, w_pe_d, lng_d, lnb_d)
    ck = P.sb("ck", [128, 24], F32)
    P.dma(ck[:, :], ck_d, w=["ck"])
    P.pool("x", 2, [128, 8, TT + 2], F32)
    P.pool("xb", 2, [128, 8, TT + 2], BF16)
    P.pool("p", 2, [128, 2, TT], F32)
    P.pool("pb", 2, [128, 2, TT], BF16)
    P.pool("m", 1, [128, 8, TT], BF16)
    P.pool("o", 2, [128, 8, TT], F32)
    xTv = xT.rearrange("(k p) t -> p k t", p=128)
    pTv = pT.rearrange("(k p) t -> p k t", p=128)
    oTv = oT.rearrange("(k p) t -> p k t", p=128)
    W = TT + 2
    outkeys = []
    for j in range(NT // TT):
        x, xk = P.nxt("x")
        xb, xbk = P.nxt("xb")
        pt, ptk = P.nxt("p")
        pb, pbk = P.nxt("pb")
        P.dma(x[:, :, :], xTv[:, :, j * TT: j * TT + W], w=[xk])
        P.dma(pt[:, :, :], pTv[:, :, j * TT:(j + 1) * TT], w=[ptk])
        P.add("pool", lambda e, x=x, xb=xb: e.tensor_copy(out=xb[:, :, :], in_=x[:, :, :]), r=[xk], w=[xbk])
        P.add("pool", lambda e, pt=pt, pb=pb: e.tensor_copy(out=pb[:, :, :], in_=pt[:, :, :]), r=[ptk], w=[pbk])
        m, mk = P.nxt("m")
        for et in range(8):
            pss = {}
            for qi, q in enumerate(("bg", "cg", "h", "z")):
                pz, pzk = P.nxt("ps")
                c0 = qi * D + et * 128
                lo = 0 if q in ("cg", "h") else 2
                P.add("pe", lambda e, pz=pz, c0=c0, lo=lo, xb=xb: [
                    e.matmul(pz[:, : W - lo], lhsT=w_in[:, k, c0:c0 + 128], rhs=xb[:, k, lo:W], start=(k == 0), stop=(k == 7))
                    for k in range(8)][-1], r=[xbk] + [("w_in", k) for k in range(8)], w=[pzk])
                pss[q] = (pz, pzk)
            cgs, cgk = P.nxt("tmp")
            P.add("act", lambda e, cgs=cgs, a=pss["cg"][0]: e.activation(out=cgs[:, :W], in_=a[:, :W], func=AF.Copy),
                  r=[pss["cg"][1]], w=[cgk])
            P.add("dve", lambda e, cgs=cgs, a=pss["h"][0]: e.tensor_tensor(out=cgs[:, :W], in0=a[:, :W], in1=cgs[:, :W], op=ALU.mult),
                  r=[pss["h"][1], cgk], w=[cgk])
            sz, szk = P.nxt("tmp")
            P.add("act", lambda e, sz=sz, a=pss["z"][0]: e.activation(out=sz[:, :TT], in_=a[:, :TT], func=AF.Silu),
                  r=[pss["z"][1]], w=[szk])
            P.add("dve", lambda e, sz=sz, a=pss["bg"][0]: e.tensor_tensor(out=sz[:, :TT], in0=a[:, :TT], in1=sz[:, :TT], op=ALU.mult),
                  r=[pss["bg"][1], szk], w=[szk])
            cv, cvk = P.nxt("tmp")
            P.add("dve", lambda e, cv=cv, cgs=cgs, et=et: e.tensor_scalar(out=cv[:, :TT], in0=cgs[:, 2:W], scalar1=ck[:, 16 + et:17 + et],
                                                                         scalar2=None, op0=ALU.mult), r=[cgk, "ck"], w=[cvk])
            P.add("dve", lambda e, cv=cv, cgs=cgs, et=et: e.scalar_tensor_tensor(out=cv[:, :TT], in0=cgs[:, 1:W - 1], scalar=ck[:, 8 + et:9 + et],
                                                                                in1=cv[:, :TT], op0=ALU.mult, op1=ALU.add),
                  r=[cgk, cvk, "ck"], w=[cvk])
            P.add("dve", lambda e, cv=cv, cgs=cgs, et=et: e.scalar_tensor_tensor(out=cv[:, :TT], in0=cgs[:, 0:TT], scalar=ck[:, et:et + 1],
                                                                                in1=cv[:, :TT], op0=ALU.mult, op1=ALU.add),
                  r=[cgk, cvk, "ck"], w=[cvk])
            P.add("dve", lambda e, cv=cv, sz=sz, et=et, m=m: e.tensor_tensor(out=m[:, et, :], in0=cv[:, :TT], in1=sz[:, :TT], op=ALU.mult),
                  r=[cvk, szk], w=[(mk, et)])

        def y_ps(d, m=m, mk=mk):
            yp, yk = P.nxt("ps")
            P.add("pe", lambda e: [e.matmul(yp[:, :TT], lhsT=w_out[:, k, d * 128:(d + 1) * 128], rhs=m[:, k, :],
                                            start=(k == 0), stop=(k == 7)) for k in range(8)][-1],
                  r=[(mk, k) for k in range(8)] + [("w_out", k) for k in range(8)], w=[yk])
            return yp[:, :TT], yk

        o, ok = P.nxt("o")
        tail(P, TT, x[:, :, 2:W], xk, y_ps, pb, pbk, w_pg, w_pe, ones, lng, lnb, o, ok, "c")
        P.dma(oTv[:, :, j * TT:(j + 1) * TT], o[:, :, :], r=[(ok, d) for d in range(8)], w=[("out", j)])
        outkeys.append(("out", j))
    P.add("sp", None, r=outkeys)
    return P.finalize() if own else None


def build_s5b(NT=2048, TT=256, perm_g=False, sel=False, P=None):
    own = P is None
    P = P or Prog()
    xT = P.dram("xT", [D, NT])
    gcols = NT * (2 if sel else 1)
    gT = P.dram("gT", [D, 8, gcols // 8] if perm_g else [D, gcols])
    pT = P.dram("pT", [256, NT])
    w_z_d = P.dram("w_z", [D, D])
    w_glu_d = P.dram("w_glu", [D, D])
    bglu_d = P.dram("b_glu", [128, 8])
    w_out_d = P.dram("w_out", [D, D])
    w_pg_d = P.dram("w_pg", [D, D])
    w_pe_d = P.dram("w_pe", [256, D])
    lng_d = P.dram("ln_g", [128, 8])
    lnb_d = P.dram("ln_b", [128, 8])
    oT = P.dram("oT", [D, NT], out=True)
    P.pool("wst", 2, [128, 1024], F32)
    P.pool("ps", 6, [128, 512], F32, psum=True)
    P.pool("psA", 2, [128, 512], F32, psum=True)
    w_z = load_weight_bf16(P, "w_z", w_z_d, D, D, "wst")
    w_glu = load_weight_bf16(P, "w_glu", w_glu_d, D, D, "wst")
    w_out = load_weight_bf16(P, "w_out", w_out_d, D, D, "wst")
    w_pg, w_pe, ones, lng, lnb = tail_setup(P, TT, w_pg_d, w_pe_d, lng_d, lnb_d)
    bglu = P.sb("bglu", [128, 8], F32)
    P.dma(bglu[:, :], bglu_d, w=["bglu"])
    P.pool("x", 2, [128, 8, TT], F32)
    P.pool("xb", 2, [128, 8, TT], BF16)
    P.pool("g", 2, [128, 8, TT], F32)
    P.pool("gb", 2, [128, 8, TT], BF16)
    P.pool("p", 2, [128, 2, TT], F32)
    P.pool("pb", 2, [128, 2, TT], BF16)
    P.pool("m", 1, [128, 8, TT], BF16)
    P.pool("o", 2, [128, 8, TT], F32)
    if sel:
        P.pool("g2", 1, [128, 8, TT], F32)
        selt = P.sb("sel", [128, 2], F32)
        P.dma(selt[:, :], P.dram("sel", [128, 2]), w=["sel"])
    xTv = xT.rearrange("(k p) t -> p k t", p=128)
    gTv = gT.rearrange("(k p) t n -> p k t n", p=128) if perm_g else gT.rearrange("(k p) t -> p k t", p=128)
    pTv = pT.rearrange("(k p) t -> p k t", p=128)
    oTv = oT.rearrange("(k p) t -> p k t", p=128)
    outkeys = []
    CN = TT // 8
    gseq = (lambda ap: ap.rearrange("p (t n) -> p n t", t=8)) if perm_g else (lambda ap: ap)
    sseq = (lambda ap: ap.rearrange("p (n t) -> p n t", t=8)) if perm_g else (lambda ap: ap)
    for j in range(NT // TT):
        sl = slice(j * TT, (j + 1) * TT)
        x, xk = P.nxt("x"); xb, xbk = P.nxt("xb")
        g, gk = P.nxt("g"); gb, gbk = P.nxt("gb")
        pt, ptk = P.nxt("p"); pb, pbk = P.nxt("pb")
        P.dma(x[:, :, :], xTv[:, :, sl], w=[xk])
        if perm_g:
            for k in range(8):
                P.dma(g[:, k, :].rearrange("p (t n) -> p t n", t=8), gTv[:, k, :, j * CN:(j + 1) * CN], w=[gk])
            if sel:
                g2, g2k = P.nxt("g2")
                for k in range(8):
                    P.dma(g2[:, k, :].rearrange("p (t n) -> p t n", t=8), gTv[:, k, :, NT // 8 + j * CN:NT // 8 + (j + 1) * CN], w=[g2k])
                P.add("pool", lambda e, g=g: e.tensor_scalar(out=g[:, :, :], in0=g[:, :, :], scalar1=selt[:, 0:1], scalar2=None, op0=ALU.mult),
                      r=[gk, "sel"], w=[gk])
                P.add("dve", lambda e, g=g, g2=g2: e.scalar_tensor_tensor(out=g[:, :, :], in0=g2[:, :, :], scalar=selt[:, 1:2], in1=g[:, :, :],
                                                                          op0=ALU.mult, op1=ALU.add), r=[gk, g2k, "sel"], w=[gk])
        else:
            P.dma(g[:, :, :], gTv[:, :, sl], w=[gk])
        P.dma(pt[:, :, :], pTv[:, :, sl], w=[ptk])
        P.add("pool", lambda e, x=x, xb=xb: e.tensor_copy(out=xb[:, :, :], in_=x[:, :, :]), r=[xk], w=[xbk])
        for k in range(8):
            P.add("pool", lambda e, g=g, gb=gb, k=k: e.tensor_copy(out=sseq(gb[:, k, :]), in_=gseq(g[:, k, :])), r=[gk], w=[gbk])
        P.add("pool", lambda e, pt=pt, pb=pb: e.tensor_copy(out=pb[:, :, :], in_=pt[:, :, :]), r=[ptk], w=[pbk])
        m, mk = P.nxt("m")
        for et in range(8):
            sp_, spk = P.nxt("ps")
            P.add("pe", lambda e, sp_=sp_, et=et, gb=gb: [e.matmul(sp_[:, :TT], lhsT=w_glu[:, k, et * 128:(et + 1) * 128], rhs=gb[:, k, :],
                                                                  start=(k == 0), stop=(k == 7)) for k in range(8)][-1],
                  r=[gbk] + [("w_glu", k) for k in range(8)], w=[spk])
            zp, zpk = P.nxt("ps")
            P.add("pe", lambda e, zp=zp, et=et, xb=xb: [e.matmul(zp[:, :TT], lhsT=w_z[:, k, et * 128:(et + 1) * 128], rhs=xb[:, k, :],
                                                                start=(k == 0), stop=(k == 7)) for k in range(8)][-1],
                  r=[xbk] + [("w_z", k) for k in range(8)], w=[zpk])
            sg, sgk = P.nxt("tmp")
            P.add("act", lambda e, sg=sg, sp_=sp_, et=et: e.activation(out=sg[:, :TT], in_=sp_[:, :TT], func=AF.Sigmoid, bias=bglu[:, et:et + 1]),
                  r=[spk, "bglu"], w=[sgk])
            sz, szk = P.nxt("tmp")
            P.add("act", lambda e, sz=sz, zp=zp: e.activation(out=sz[:, :TT], in_=zp[:, :TT], func=AF.Silu), r=[zpk], w=[szk])
            P.add("dve", lambda e, sg=sg, g=g, et=et: e.tensor_tensor(out=sseq(sg[:, :TT]), in0=gseq(g[:, et, :]), in1=sseq(sg[:, :TT]), op=ALU.mult),
                  r=[gk, sgk], w=[sgk])
            P.add("dve", lambda e, sg=sg, sz=sz, m=m, et=et: e.tensor_tensor(out=m[:, et, :], in0=sg[:, :TT], in1=sz[:, :TT], op=ALU.mult),
                  r=[sgk, szk], w=[(mk, et)])

        def y_ps(d, m=m, mk=mk):
            yp, yk = P.nxt("ps")
            P.add("pe", lambda e: [e.matmul(yp[:, :TT], lhsT=w_out[:, k, d * 128:(d + 1) * 128], rhs=m[:, k, :],
                                            start=(k == 0), stop=(k == 7)) for k in range(8)][-1],
                  r=[(mk, k) for k in range(8)] + [("w_out", k) for k in range(8)], w=[yk])
            return yp[:, :TT], yk

        o, ok = P.nxt("o")
        tail(P, TT, x, xk, y_ps, pb, pbk, w_pg, w_pe, ones, lng, lnb, o, ok, "s")
        P.dma(oTv[:, :, sl], o[:, :, :], r=[(ok, d) for d in range(8)], w=[("out", j)])
        outkeys.append(("out", j))
    P.add("sp", None, r=outkeys)
    return P.finalize() if own else None


def s5b_layer(xs, gs, p_l, w_in, w_glu, b_glu, w_out, w_pg, w_pe, ln_g, ln_b):
    in_maps = []
    w_z = np.ascontiguousarray(w_in[:, D:])
    for c in range(NCORES):
        b, h = c // 2, c % 2
        in_maps.append(dict(xT=xs[c], gT=gs[c], pT=np.ascontiguousarray(p_l[b, h * 2048:(h + 1) * 2048].T),
                            w_z=w_z, w_glu=w_glu, b_glu=pp(b_glu), w_out=w_out, w_pg=w_pg, w_pe=w_pe,
                            ln_g=pp(ln_g), ln_b=pp(ln_b)))
    res = run("s5b", build_s5b, in_maps)
    return [r["oT"] for r in res]


PI = float(np.pi)


def _tt(P, eng, o, a, b, op):
    P.add(eng, lambda e: e.tensor_tensor(out=o[0], in0=a[0], in1=b[0], op=op), r=[a[1], b[1]], w=[o[1]])


def _ts(P, eng, o, a, s1, op0, s2=None, op1=None, extra=()):
    if op1 is None:
        P.add(eng, lambda e: e.tensor_scalar(out=o[0], in0=a[0], scalar1=s1, scalar2=None, op0=op0), r=[a[1]] + list(extra), w=[o[1]])
    else:
        P.add(eng, lambda e: e.tensor_scalar(out=o[0], in0=a[0], scalar1=s1, scalar2=s2, op0=op0, op1=op1), r=[a[1]] + list(extra), w=[o[1]])


def _stt(P, o, a, sc, b, op0, op1, extra=()):
    P.add("dve", lambda e: e.scalar_tensor_tensor(out=o[0], in0=a[0], scalar=sc, in1=b[0], op0=op0, op1=op1),
          r=[a[1], b[1]] + list(extra), w=[o[1]])


def _cmul(P, orr, oi, ar, ai, br, bi, t):
    _tt(P, "dve", orr, ar, br, ALU.mult)
    _tt(P, "dve", t, ai, bi, ALU.mult)
    _tt(P, "dve", orr, orr, t, ALU.subtract)
    _tt(P, "dve", oi, ar, bi, ALU.mult)
    _tt(P, "dve", t, ai, br, ALU.mult)
    _tt(P, "dve", oi, oi, t, ALU.add)


def _cplx_setup(P, tp, shape, lamre, lamim, ldt):
    nel = int(np.prod(shape))
    def T(n):
        if n not in tp:
            tp[n] = P.sb("tp_" + n, [128, 512])
        return (tp[n][:, :nel].rearrange("p (a b) -> p a b", b=shape[-1]), "tp_" + n)
    dt, thr, thi, tmp, x, x2, acc, sn, cs, t1, t2 = [T(n) for n in ("dt", "thr", "thi", "tmp", "x", "x2", "acc", "sn", "cs", "t1", "t2")]
    er, ar, ai, nr, den, cr, ci = [T(n) for n in ("er", "ar", "ai", "nr", "den", "cr", "ci")]
    P.add("act", lambda e: e.activation(out=dt[0], in_=ldt[0], func=AF.Exp), r=[ldt[1]], w=[dt[1]])
    _tt(P, "dve", thr, lamre, dt, ALU.mult)
    _tt(P, "dve", thi, lamim, dt, ALU.mult)
    for _ in range(4):
        _ts(P, "dve", tmp, thi, PI, ALU.is_gt, 2 * PI, ALU.mult)
        _tt(P, "dve", thi, thi, tmp, ALU.subtract)
    _ts(P, "dve", x, thi, 0.125, ALU.mult)
    _tt(P, "dve", x2, x, x, ALU.mult)
    _ts(P, "dve", acc, x2, 1.0 / 362880, ALU.mult)
    for c in (-1.0 / 5040, 1.0 / 120, -1.0 / 6):
        _stt(P, acc, acc, c, x2, ALU.add, ALU.mult)
    _stt(P, sn, acc, 1.0, x, ALU.add, ALU.mult)
    _ts(P, "dve", acc, x2, -1.0 / 3628800, ALU.mult)
    for c in (1.0 / 40320, -1.0 / 720, 1.0 / 24, -0.5):
        _stt(P, acc, acc, c, x2, ALU.add, ALU.mult)
    _ts(P, "dve", cs, acc, 1.0, ALU.add)
    for _ in range(3):
        _tt(P, "dve", t1, cs, cs, ALU.mult)
        _tt(P, "dve", t2, sn, sn, ALU.mult)
        _stt(P, sn, cs, 2.0, sn, ALU.mult, ALU.mult)
        _tt(P, "dve", cs, t1, t2, ALU.subtract)
    _ts(P, "dve", acc, thr, 1.0 / 120, ALU.mult)
    for c in (1.0 / 24, 1.0 / 6, 0.5, 1.0):
        _stt(P, acc, acc, c, thr, ALU.add, ALU.mult)
    _ts(P, "dve", er, acc, 1.0, ALU.add)
    _tt(P, "dve", ar, er, cs, ALU.mult)
    _tt(P, "dve", ai, er, sn, ALU.mult)
    _ts(P, "dve", nr, ar, -1.0, ALU.add)
    _tt(P, "dve", den, lamre, lamre, ALU.mult)
    _tt(P, "dve", t1, lamim, lamim, ALU.mult)
    _tt(P, "dve", den, den, t1, ALU.add)
    P.add("dve", lambda e: e.reciprocal(out=den[0], in_=den[0]), r=[den[1]], w=[den[1]])
    _tt(P, "dve", cr, nr, lamre, ALU.mult)
    _tt(P, "dve", t1, ai, lamim, ALU.mult)
    _tt(P, "dve", cr, cr, t1, ALU.add)
    _tt(P, "dve", cr, cr, den, ALU.mult)
    _tt(P, "dve", ci, ai, lamre, ALU.mult)
    _tt(P, "dve", t1, nr, lamim, ALU.mult)
    _tt(P, "dve", ci, ci, t1, ALU.subtract)
    _tt(P, "dve", ci, ci, den, ALU.mult)
    return ar, ai, cr, ci, t1, t2


def build_s5a(NG=32, NT=4096, TT=256, ngrun=None, stage=9, P=None):
    MT = NG // 8
    NCH = NT // 8
    assert NCH == 512
    own = P is None
    P = P or Prog()
    xT = P.dram("xT", [D, NT])
    w_u_d = P.dram("w_u", [D, MT * 128])
    cl_d = {n: P.dram(n, [128, MT, 64]) for n in ("lamre_cl", "lamim_cl", "ldt_cl", "bre_cl", "bim_cl")}
    pl_d = {n: P.dram(n, [128, NG, 16]) for n in ("lamre_pl", "lamim_pl", "ldt_pl", "bre_pl", "bim_pl", "cre_pl", "cim_pl")}
    dsk_d = P.dram("dsk", [128, MT])
    ident_d = P.dram("ident", [128, 128])
    j2_d = P.dram("j2", [128, 128])
    msk_d = P.dram("msk", [128, 2 + 8])
    identg_d = P.dram("identg", [128, 8, 16])
    gTp = P.dram("gTp", [MT * 128, 8, NCH], out=True)

    P.pool("wst", 1, [128, 1024], F32)
    P.pool("ps", 4, [128, 512], F32, psum=True)
    P.pool("psk", 2, [128, 128], F32, psum=True)
    P.pool("psr", 2, [128, 512], F32, psum=True)
    w_u = load_weight_bf16(P, "w_u", w_u_d, D, MT * 128, "wst")

    def ld(name, shape, src):
        t = P.sb(name, shape)
        P.dma(t[:], src, w=[name])
        return (t[:], name), t
    cl = {n: ld(n, [128, MT, 64], cl_d[n])[0] for n in cl_d}
    pl = {n: ld(n, [128, NG, 16], pl_d[n])[0] for n in pl_d}
    _, dsk = ld("dsk", [128, MT], dsk_d)
    _, ident = ld("ident", [128, 128], ident_d)
    _, j2 = ld("j2", [128, 128], j2_d)
    _, msk = ld("msk", [128, 10], msk_d)
    _, identg = ld("identg", [128, 8, 16], identg_d)
    mlo, mhi = msk[:, 0:1], msk[:, 1:2]

    u_t = P.sb("u", [128, MT, NT], BF16)
    P.pool("x", 1, [128, 8, TT], F32)
    P.pool("xb", 2, [128, 8, TT], BF16)
    xTv = xT.rearrange("(k p) t -> p k t", p=128)
    for j in range(NT // TT):
        x, xk = P.nxt("x"); xb, xbk = P.nxt("xb")
        P.dma(x[:, :, :], xTv[:, :, j * TT:(j + 1) * TT], w=[xk])
        P.add("pool", lambda e, x=x, xb=xb: e.tensor_copy(out=xb[:, :, :], in_=x[:, :, :]), r=[xk], w=[xbk])
        for m in range(MT):
            up, upk = P.nxt("ps")
            P.add("pe", lambda e, up=up, m=m, xb=xb: [e.matmul(up[:, :TT], lhsT=w_u[:, k, m * 128:(m + 1) * 128], rhs=xb[:, k, :],
                                                             start=(k == 0), stop=(k == 7)) for k in range(8)][-1],
                  r=[xbk] + [("w_u", k) for k in range(8)], w=[upk])
            P.add("act", lambda e, up=up, m=m, j=j: e.activation(out=u_t[:, m, j * TT:(j + 1) * TT], in_=up[:, :TT], func=AF.Copy),
                  r=[upk], w=[("u", m, j)])
    ukeys = lambda m: [("u", m, j) for j in range(NT // TT)]

    tp = {}
    ar, ai, cr, ci, t1, t2 = _cplx_setup(P, tp, [MT, 64], cl["lamre_cl"], cl["lamim_cl"], cl["ldt_cl"])
    EB = P.sb("EB", [128, 8, MT, 2, 64])
    ebr = lambda j: (EB[:, j, :, 0, :], ("EB", j))
    ebi = lambda j: (EB[:, j, :, 1, :], ("EB", j))
    _cmul(P, ebr(7), ebi(7), cr, ci, cl["bre_cl"], cl["bim_cl"], t1)
    for j in range(6, -1, -1):
        _cmul(P, ebr(j), ebi(j), ebr(j + 1), ebi(j + 1), ar, ai, t1)

    par, pai, pcr, pci, pt1, pt2 = _cplx_setup(P, tp, [NG, 16], pl["lamre_pl"], pl["lamim_pl"], pl["ldt_pl"])
    def T(n, old=None):
        if old is None:
            t = P.sb(n, [128, NG, 16])
            return (t[:], n)
        return (tp[old][:, :NG * 16].rearrange("p (a b) -> p a b", b=16), "tp_" + old)
    qr, qi, qr2, qi2 = T("qr", "dt"), T("qi", "thr"), T("qr2", "thi"), T("qi2", "tmp")
    car, cai, car2, cai2 = T("car", "x"), T("cai", "x2"), T("car2", "acc"), T("cai2", "sn")
    pr, pi_, pr2, pi2 = T("pr", "cs"), T("pi", "er"), T("pr2", "nr"), T("pi2", "den")
    Cc = T("Cc")
    Xs = P.sb("Xs", [128, 8, NG, 16])
    CS = P.sb("CS", [128, NG, 8, 16])
    AKr = P.sb("AKr", [128, 9, NG]); AKi = P.sb("AKi", [128, 9, NG])
    _ts(P, "dve", pt1, pl["cim_pl"], mhi, ALU.mult, extra=["msk"])
    _stt(P, Cc, pl["cre_pl"], mlo, pt1, ALU.mult, ALU.subtract, extra=["msk"])
    _cmul(P, qr, qi, pcr, pci, pl["bre_pl"], pl["bim_pl"], pt1)

    def stack(o, re, im, sub):
        _ts(P, "dve", pt2, im, mhi, ALU.mult, extra=["msk"])
        _stt(P, o, re, mlo, pt2, ALU.mult, ALU.subtract if sub else ALU.add, extra=["msk"])
    stack((Xs[:, 0, :, :], ("Xs", 0)), qr, qi, False)
    cur_q, nxt_q = (qr, qi), (qr2, qi2)
    cur_c, nxt_c = (pl["cre_pl"], pl["cim_pl"]), (car, cai)
    alt_c = (car2, cai2)
    cur_p, nxt_p = None, (pr, pi_)
    for tau in range(1, 9):
        if tau <= 7:
            _cmul(P, nxt_q[0], nxt_q[1], cur_q[0], cur_q[1], par, pai, pt1)
            cur_q, nxt_q = nxt_q, cur_q
            stack((Xs[:, tau, :, :], ("Xs", tau)), cur_q[0], cur_q[1], False)
        _cmul(P, nxt_c[0], nxt_c[1], cur_c[0], cur_c[1], par, pai, pt1)
        cur_c = nxt_c
        nxt_c = alt_c if cur_c[0][1] == car[1] else (car, cai)
        stack((CS[:, :, tau - 1, :], ("CS", tau - 1)), cur_c[0], cur_c[1], True)
        if cur_p is None:
            cur_p = (par, pai)
        else:
            _cmul(P, nxt_p[0], nxt_p[1], cur_p[0], cur_p[1], par, pai, pt1)
            cur_p = nxt_p
            nxt_p = (pr2, pi2) if cur_p[0][1] == pr[1] else (pr, pi_)
    akr = lambda k: (AKr[:, k, :], ("AK", k))
    aki = lambda k: (AKi[:, k, :], ("AK", k))
    P.add("dve", lambda e: e.tensor_copy(out=AKr[:, 0, :], in_=cur_p[0][0][:, :, 0]), r=[cur_p[0][1]], w=[("AK", 0)])
    P.add("dve", lambda e: e.tensor_copy(out=AKi[:, 0, :], in_=cur_p[1][0][:, :, 0]), r=[cur_p[1][1]], w=[("AK", 0)])
    s1 = P.sb("aks1", [128, NG]); s2 = P.sb("aks2", [128, NG])
    s1 = (s1[:], "aks1"); s2 = (s2[:], "aks2")
    for k in range(1, 9):
        _tt(P, "dve", s1, akr(k - 1), akr(k - 1), ALU.mult)
        _tt(P, "dve", s2, aki(k - 1), aki(k - 1), ALU.mult)
        _tt(P, "dve", akr(k), s1, s2, ALU.subtract)
        _stt(P, aki(k), akr(k - 1), 2.0, aki(k - 1), ALU.mult, ALU.mult)

    P.pool("LE", 2, [128, 8, 128], BF16)
    lm_tiles = P.pool("LM", 2, [128, 8, 128], BF16)
    for t, k_ in lm_tiles:
        P.add("pool", lambda e, t=t: e.memset(t[:], 0.0), w=[k_])
    P.pool("XPA", 2, [128, 8, 128], F32)
    P.pool("ROT", 4, [128, 128], F32)
    P.pool("KS", 2, [128, 128], F32)
    P.pool("X", 2, [128, NCH], F32)
    P.pool("ys", 2, [128, NCH], F32)
    P.pool("gw", 1, [128, NCH], F32)
    P.pool("gy", 2, [128, NCH], F32)
    outkeys = []
    for g in range(NG if ngrun is None else ngrun):
        m, gl = g // 8, g % 8
        gm = msk[:, 2 + gl:3 + gl]
        le, lek = P.nxt("LE")
        for j in range(8):
            P.add("dve", lambda e, le=le, j=j, m=m, gm=gm: e.tensor_scalar(out=le[:, j, :], in0=EB[:, j, m, :, :].rearrange("p a b -> p (a b)"),
                                                                         scalar1=gm, scalar2=None, op0=ALU.mult),
                  r=[("EB", j), "msk"], w=[(lek, j)])
        kp, kpk = P.nxt("psk")
        xp, xpk = P.nxt("XPA")
        P.add("pool", lambda e, xp=xp: e.memset(xp[:, :, :], 0.0), w=[xpk])
        P.add("dve", lambda e, xp=xp, g=g, gl=gl: e.tensor_copy(out=xp[:, :, 16 * gl:16 * gl + 16], in_=Xs[:, :, g, :]),
              r=[("Xs", t_) for t_ in range(8)] + [xpk], w=[xpk])
        P.add("pe", lambda e, xp=xp, g=g, kp=kp: [e.matmul(kp[:, 16 * tau:16 * tau + 16], lhsT=xp[:, tau, :], rhs=Cc[0][:, g, :],
                                                          start=True, stop=True) for tau in range(8)][-1],
              r=[xpk, "Cc"], w=[kpk])
        ks, ksk = P.nxt("KS")
        P.add("dve", lambda e, ks=ks, kp=kp, m=m, gl=gl: e.scalar_tensor_tensor(out=ks[:, 0:16], in0=identg[:, gl, :], scalar=dsk[:, m:m + 1],
                                                                             in1=kp[:, 0:16], op0=ALU.mult, op1=ALU.add),
              r=["identg", "dsk", kpk], w=[(ksk, 0)])
        P.add("dve", lambda e, ks=ks, kp=kp: e.tensor_copy(out=ks[:, 16:128], in_=kp[:, 16:128]),
              r=[kpk], w=[(ksk, 1)])
        lm, lmk = P.nxt("LM")
        for j in range(8):
            P.add("pool", lambda e, lm=lm, ks=ks, j=j: e.tensor_copy(out=lm[:, j, 16 * j:128], in_=ks[:, 0:128 - 16 * j]),
                  r=[(ksk, 0), (ksk, 1)], w=[(lmk, j)])
        if stage < 2:
            continue
        ep, epk = P.nxt("ps")
        P.add("pe", lambda e, ep=ep, le=le, m=m: [e.matmul(ep[:, :], lhsT=le[:, j, :], rhs=u_t[:, m, j::8], start=(j == 0), stop=(j == 7))
                                                  for j in range(8)][-1],
              r=[(lek, j) for j in range(8)] + ukeys(m), w=[epk])
        X, Xk = P.nxt("X")
        P.add("act", lambda e, X=X, ep=ep: e.activation(out=X[:, :], in_=ep[:, :], func=AF.Copy), r=[epk], w=[Xk])
        for k in range(9 if stage >= 3 else 0):
            d = 1 << k
            rot, rotk = P.nxt("ROT")
            P.add("pool", lambda e, rot=rot, k=k, g=g: e.tensor_scalar(out=rot[:, :], in0=ident[:, :], scalar1=AKr[:, k, g:g + 1], scalar2=None, op0=ALU.mult),
                  r=["ident", ("AK", k)], w=[rotk])
            P.add("dve", lambda e, rot=rot, k=k, g=g: e.scalar_tensor_tensor(out=rot[:, :], in0=j2[:, :], scalar=AKi[:, k, g:g + 1], in1=rot[:, :],
                                                                          op0=ALU.mult, op1=ALU.add),
                  r=["j2", ("AK", k), rotk], w=[rotk])
            rp, rpk = P.nxt("psr")
            P.add("pe", lambda e, rp=rp, rot=rot, X=X, d=d: e.matmul(rp[:, :NCH - d], lhsT=rot[:, :], rhs=X[:, :NCH - d], start=True, stop=True),
                  r=[rotk, Xk], w=[rpk])
            P.add("dve", lambda e, rp=rp, X=X, d=d: e.tensor_tensor(out=X[:, d:], in0=rp[:, :NCH - d], in1=X[:, d:], op=ALU.add),
                  r=[rpk, Xk], w=[Xk])
        if stage < 4:
            continue
        yp, ypk = P.nxt("ps")
        P.add("pe", lambda e, yp=yp, lm=lm, m=m, X=X, g=g: [e.matmul(yp[:, :], lhsT=lm[:, j, :], rhs=u_t[:, m, j::8], start=(j == 0), stop=False)
                                                           for j in range(8)] and
              e.matmul(yp[:, 1:NCH], lhsT=CS[:, g, :, :].rearrange("p a b -> p (a b)"), rhs=X[:, 0:NCH - 1], start=False, stop=True),
              r=[(lmk, j) for j in range(8)] + ukeys(m) + [Xk] + [("CS", t_) for t_ in range(8)], w=[ypk])
        ys, ysk = P.nxt("ys"); gw, gwk = P.nxt("gw"); gy, gyk = P.nxt("gy")
        P.add("act", lambda e, ys=ys, yp=yp: e.activation(out=ys[:, :], in_=yp[:, :], func=AF.Copy), r=[ypk], w=[ysk])
        P.add("dve", lambda e, ys=ys, gw=gw: e.tensor_tensor(out=gw[:, :], in0=ys[:, :], in1=ys[:, :], op=ALU.mult), r=[ysk], w=[gwk])
        P.add("dve", lambda e, gw=gw: e.tensor_scalar(out=gw[:, :], in0=gw[:, :], scalar1=0.044715, scalar2=1.0, op0=ALU.mult, op1=ALU.add),
              r=[gwk], w=[gwk])
        P.add("dve", lambda e, ys=ys, gw=gw: e.tensor_tensor(out=gw[:, :], in0=gw[:, :], in1=ys[:, :], op=ALU.mult), r=[ysk, gwk], w=[gwk])
        P.add("act", lambda e, gw=gw: e.activation(out=gw[:, :], in_=gw[:, :], func=AF.Sigmoid, scale=1.5957691216057308), r=[gwk], w=[gwk])
        P.add("dve", lambda e, ys=ys, gw=gw, gy=gy: e.tensor_tensor(out=gy[:, :], in0=ys[:, :], in1=gw[:, :], op=ALU.mult), r=[ysk, gwk], w=[gyk])
        for t_ in range(8):
            r0 = m * 128 + gl * 16
            P.dma(gTp[r0:r0 + 16, t_, :], gy[16 * t_:16 * t_ + 16, :], r=[gyk], w=[("out", g, t_)])
            outkeys.append(("out", g, t_))
    P.add("sp", None, r=outkeys)
    return P.finalize() if own else None


def s5a_layer(xfull, prm):
    NG = 32
    ident = np.eye(128, dtype=np.float32)
    j2 = np.zeros((128, 128), np.float32)
    j2[np.arange(64), np.arange(64) + 64] = 1.0
    j2[np.arange(64) + 64, np.arange(64)] = -1.0
    msk = np.zeros((128, 10), np.float32)
    msk[:64, 0] = 1.0; msk[64:, 1] = 1.0
    for gl in range(8):
        msk[16 * gl:16 * gl + 16, 2 + gl] = 1.0
    identg = np.zeros((128, 8, 16), np.float32)
    for gl in range(8):
        for h in range(16):
            identg[16 * gl + h, gl, h] = 1.0
    in_maps = []
    for c in range(NCORES):
        b, gh = c // 2, c % 2
        gs = slice(NG * gh, NG * gh + NG)
        def cl(a):
            a = np.broadcast_to(a.reshape(NG // 8, 8, 1, 64), (NG // 8, 8, 16, 64))
            return np.ascontiguousarray(a.transpose(1, 2, 0, 3).reshape(128, NG // 8, 64))
        def clb(a):
            a = a.reshape(NG // 8, 8, 64, 16)
            return np.ascontiguousarray(a.transpose(1, 3, 0, 2).reshape(128, NG // 8, 64))
        def plv(a):
            a = np.broadcast_to(a.T.reshape(1, 64, NG, 1), (2, 64, NG, 16))
            return np.ascontiguousarray(a.reshape(128, NG, 16))
        def plb(a):
            a = np.broadcast_to(a.transpose(1, 0, 2)[None], (2, 64, NG, 16))
            return np.ascontiguousarray(a.reshape(128, NG, 16))
        def plc(a):
            a = np.broadcast_to(a.transpose(2, 0, 1)[None], (2, 64, NG, 16))
            return np.ascontiguousarray(a.reshape(128, NG, 16))
        ldt = np.broadcast_to(prm["log_dt"][gs][:, None], (NG, 64))
        in_maps.append(dict(
            xT=xfull[b], w_u=np.ascontiguousarray(prm["w_in"][:, 512 * gh:512 * gh + 512]),
            lamre_cl=cl(prm["lam_re"][gs]), lamim_cl=cl(prm["lam_im"][gs]), ldt_cl=cl(ldt),
            bre_cl=clb(prm["b_re"][gs]), bim_cl=clb(prm["b_im"][gs]),
            lamre_pl=plv(prm["lam_re"][gs]), lamim_pl=plv(prm["lam_im"][gs]), ldt_pl=plv(ldt),
            bre_pl=plb(prm["b_re"][gs]), bim_pl=plb(prm["b_im"][gs]),
            cre_pl=plc(prm["c_re"][gs]), cim_pl=plc(prm["c_im"][gs]),
            dsk=pp(prm["d"][512 * gh:512 * gh + 512]), ident=ident, j2=j2, msk=msk, identg=identg))
    res = run("s5a", build_s5a, in_maps)
    out = []
    for b in range(4):
        halves = [res[2 * b + gh]["gTp"].transpose(0, 2, 1).reshape(512, 4096) for gh in range(2)]
        out.append(np.ascontiguousarray(np.concatenate(halves, axis=0)))
    return out


RW_GN_EPS = 64e-5


def build_rwa(NT=4096, TT=256, ntiles=None, pipeline=True, P=None):
    C = 64
    NCK = TT // C
    MT = 4
    own = P is None
    P = P or Prog()
    xT = P.dram("xT", [D, NT + 1])
    mu_d = P.dram("mu", [128, 6, 8])
    wq_d = {q: P.dram("w_" + q, [D, 512]) for q in "rkvz"}
    w1_d = P.dram("w1", [D, 64]); a1_d = P.dram("a1", [D, 64])
    w2_d = P.dram("w2", [64, 512]); a2_d = P.dram("a2", [64, 512])
    prm_d = P.dram("prm", [128, 7, 4])
    ident_d = P.dram("ident", [128, 128])
    onesb_d = P.dram("onesb", [128, 128])
    msk_d = P.dram("msk", [64, 4, 512])
    rst_d = P.dram("rst", [128, TT])
    mT = P.dram("mT", [512, NT], out=True)

    P.pool("wst", 1, [128, 512], F32)
    P.pool("ps", 4, [128, 512], F32, psum=True)
    P.pool("ps2", 4, [128, 512], F32, psum=True)
    wq = {q: load_weight_bf16(P, "w_" + q, wq_d[q], D, 512, "wst") for q in "rkvz"}
    w1 = load_weight_bf16(P, "w1", w1_d, D, 64, "wst")
    a1 = load_weight_bf16(P, "a1", a1_d, D, 64, "wst")

    def ld(name, shape, src, dtype=F32):
        t = P.sb(name, shape, dtype)
        P.dma(t[:], src, w=[name])
        return t
    w2f = ld("w2f", [64, 512], w2_d); a2f = ld("a2f", [64, 512], a2_d)
    w2 = P.sb("w2", [64, 512], BF16); a2 = P.sb("a2", [64, 512], BF16)
    P.add("pool", lambda e: e.tensor_copy(out=w2[:, :], in_=w2f[:, :]), r=["w2f"], w=["w2"])
    P.add("pool", lambda e: e.tensor_copy(out=a2[:, :], in_=a2f[:, :]), r=["a2f"], w=["a2"])
    mu = ld("mu", [128, 6, 8], mu_d)
    prm = ld("prm", [128, 7, 4], prm_d)
    ident = ld("ident", [128, 128], ident_d)
    onesb = ld("onesb", [128, 128], onesb_d)
    msk = ld("msk", [64, 4, 512], msk_d)
    rst = ld("rst", [128, TT], rst_d)
    maskU, maskUi, maskL, I8 = msk[:, 0, :], msk[:, 1, :], msk[:, 2, :], msk[:, 3, :]
    W0, A0, KK, KA, RK, LG, LB = range(7)

    ST = P.sb("ST", [128, MT, 64])
    P.add("pool", lambda e: e.memset(ST[:], 0.0), w=["ST"])

    P.pool("xin", 1, [128, 8, TT + 1], F32)
    dx = P.sb("dx", [128, 8, TT]); tmpx = P.sb("tmpx", [128, 8, TT])
    xs = [P.sb(f"xs{i}", [128, 8, TT], BF16) for i in range(6)]
    F = {n: P.sb("F_" + n, [128, MT, TT]) for n in ("r", "k", "v", "sz", "sw", "a", "f6", "f7", "f8", "f9", "f10")}
    t1b = P.sb("t1b", [64, TT], BF16); a1b = P.sb("a1b", [64, TT], BF16)
    P.pool("tm", 2, [64, 3, 512], F32)
    cs = {n: P.sb("c_" + n, [64, 512]) for n in ("N", "NT", "Aak", "Arb", "Ark", "T", "Xa", "XTa", "Xb", "XTb", "WT", "UT", "o", "sq", "on")}
    cs_odd = {n: P.sb("co_" + n, [64, 512]) for n in ("T", "Aak", "Arb", "Ark")}
    st = {n: P.sb("st_" + n, [64, 8]) for n in ("s1", "s2", "mean", "var", "rstd")}
    P.pool("mo", 1, [128, MT, TT], F32)
    xTv = xT.rearrange("(k p) t -> p k t", p=128)
    mTv = mT.rearrange("(m p) t -> p m t", p=128)
    outkeys = []

    def A(eng, fn, r, w):
        P.add(eng, fn, r=r, w=w)

    def do_tile(j):
            xin, xk = P.nxt("xin")
            P.dma(xin[:, :, :], xTv[:, :, j * TT:(j + 1) * TT + 1], w=[xk])
            A("pool", lambda e, xin=xin: e.tensor_tensor(out=dx[:, :, :], in0=xin[:, :, 0:TT], in1=xin[:, :, 1:TT + 1], op=ALU.subtract), [xk], ["dx"])
            for i in range(6):
                for k in range(8):
                    A("dve", lambda e, i=i, k=k: e.scalar_tensor_tensor(out=xs[i][:, k, :], in0=dx[:, k, :], scalar=mu[:, i, k:k + 1], in1=xin[:, k, 1:TT + 1],
                                                                        op0=ALU.mult, op1=ALU.add),
                      ["dx", "mu", xk], [(f"xs{i}", k)])
            for qi, q in enumerate("rkvz"):
                for m in range(MT):
                    pz, pzk = P.nxt("ps")
                    A("pe", lambda e, pz=pz, q=q, qi=qi, m=m: [e.matmul(pz[:, :TT], lhsT=wq[q][:, k, m * 128:(m + 1) * 128], rhs=xs[qi][:, k, :],
                                                                       start=(k == 0), stop=(k == 7)) for k in range(8)][-1],
                      [(f"xs{qi}", k) for k in range(8)] + [("w_" + q, k) for k in range(8)], [pzk])
                    dst = F[{"r": "r", "k": "k", "v": "v", "z": "sz"}[q]]
                    A("act", lambda e, pz=pz, dst=dst, m=m, q=q: e.activation(out=dst[:, m, :], in_=pz[:, :TT], func=AF.Silu if q == "z" else AF.Copy),
                      [pzk], [("F", {"r": "r", "k": "k", "v": "v", "z": "sz"}[q], m)])
            for (wa, wb, xi, tb, tbk, dstn, bias_i, fn1) in ((w1, w2, 4, t1b, "t1b", "sw", W0, AF.Tanh), (a1, a2, 5, a1b, "a1b", "a", A0, AF.Copy)):
                pz, pzk = P.nxt("ps")
                A("pe", lambda e, pz=pz, wa=wa, xi=xi: [e.matmul(pz[:64, :TT], lhsT=wa[:, k, :], rhs=xs[xi][:, k, :], start=(k == 0), stop=(k == 7))
                                                       for k in range(8)][-1],
                  [(f"xs{xi}", k) for k in range(8)] + [("w1" if wa is w1 else "a1", k) for k in range(8)], [pzk])
                A("act", lambda e, pz=pz, tb=tb, fn1=fn1: e.activation(out=tb[:, :], in_=pz[:64, :TT], func=fn1), [pzk], [tbk])
                for m in range(MT):
                    p2, p2k = P.nxt("ps")
                    A("pe", lambda e, p2=p2, wb=wb, tb=tb, m=m: e.matmul(p2[:, :TT], lhsT=wb[:, m * 128:(m + 1) * 128], rhs=tb[:, :], start=True, stop=True),
                      [tbk, "w2" if wb is w2 else "a2"], [p2k])
                    A("act", lambda e, p2=p2, dstn=dstn, m=m, bias_i=bias_i: e.activation(out=F[dstn][:, m, :], in_=p2[:, :TT], func=AF.Sigmoid,
                                                                                     bias=prm[:, bias_i, m:m + 1]),
                      [p2k, "prm"], [("F", dstn, m)])
            for m in range(MT):
                fk = lambda n: ("F", n, m)
                sl = lambda n: F[n][:, m, :]
                A("dve", lambda e, m=m: e.tensor_scalar(out=F["f6"][:, m, :], in0=F["k"][:, m, :], scalar1=prm[:, KK, m:m + 1], scalar2=None, op0=ALU.mult),
                  [fk("k"), "prm"], [fk("f6")])
                A("dve", lambda e, m=m: e.tensor_tensor(out=F["f7"][:, m, :], in0=F["f6"][:, m, :], in1=F["f6"][:, m, :], op=ALU.mult), [fk("f6")], [fk("f7")])
                pz, pzk = P.nxt("ps")
                A("pe", lambda e, pz=pz, m=m: e.matmul(pz[:, :TT], lhsT=onesb[:, :], rhs=F["f7"][:, m, :], start=True, stop=True), [fk("f7"), "onesb"], [pzk])
                A("act", lambda e, pz=pz, m=m: e.activation(out=F["f7"][:, m, :], in_=pz[:, :TT], func=AF.Sqrt), [pzk], [fk("f7")])
                A("dve", lambda e, m=m: e.tensor_scalar(out=F["f7"][:, m, :], in0=F["f7"][:, m, :], scalar1=1e-12, scalar2=None, op0=ALU.max), [fk("f7")], [fk("f7")])
                A("dve", lambda e, m=m: e.reciprocal(out=F["f7"][:, m, :], in_=F["f7"][:, m, :]), [fk("f7")], [fk("f7")])
                A("dve", lambda e, m=m: e.tensor_tensor(out=F["f6"][:, m, :], in0=F["f6"][:, m, :], in1=F["f7"][:, m, :], op=ALU.mult), [fk("f6"), fk("f7")], [fk("f6")])
                A("dve", lambda e, m=m: e.tensor_scalar(out=F["f7"][:, m, :], in0=F["a"][:, m, :], scalar1=-1.0, scalar2=prm[:, KA, m:m + 1], op0=ALU.add, op1=ALU.mult),
                  [fk("a"), fk("f7"), "prm"], [fk("f7")])
                A("dve", lambda e, m=m: e.tensor_tensor(out=F["f7"][:, m, :], in0=F["f7"][:, m, :], in1=F["k"][:, m, :], op=ALU.mult), [fk("f7"), fk("k")], [fk("f7")])
                A("dve", lambda e, m=m: e.tensor_tensor(out=F["f8"][:, m, :], in0=F["f7"][:, m, :], in1=F["k"][:, m, :], op=ALU.add), [fk("f7"), fk("k")], [fk("f8")])
                A("dve", lambda e, m=m: e.tensor_tensor(out=F["f7"][:, m, :], in0=F["f6"][:, m, :], in1=F["a"][:, m, :], op=ALU.mult), [fk("f6"), fk("a"), fk("f7")], [fk("f7")])
                A("dve", lambda e, m=m: e.tensor_scalar(out=F["sw"][:, m, :], in0=F["sw"][:, m, :], scalar1=-0.6065306597126334, scalar2=None, op0=ALU.mult),
                  [fk("sw")], [fk("sw")])
                A("dve", lambda e, m=m: e.tensor_tensor_scan(out=F["k"][:, m, :], data0=rst[:, :], data1=F["sw"][:, m, :], initial=0.0, op0=ALU.mult, op1=ALU.add),
                  [fk("sw"), "rst", fk("k"), fk("f8"), fk("f7")], [fk("k")])
                A("act", lambda e, m=m: e.activation(out=F["a"][:, m, :], in_=F["k"][:, m, :], func=AF.Exp), [fk("k"), fk("a"), fk("f7")], [fk("a")])
                A("act", lambda e, m=m: e.activation(out=F["f9"][:, m, :], in_=F["k"][:, m, :], func=AF.Exp, scale=-1.0), [fk("k")], [fk("f9")])
                A("dve", lambda e, m=m: e.tensor_tensor(out=F["k"][:, m, :], in0=F["k"][:, m, :], in1=F["sw"][:, m, :], op=ALU.subtract), [fk("k"), fk("sw"), fk("a"), fk("f9")], [fk("k")])
                A("act", lambda e, m=m: e.activation(out=F["k"][:, m, :], in_=F["k"][:, m, :], func=AF.Exp), [fk("k")], [fk("k")])
                A("dve", lambda e, m=m: e.scalar_tensor_tensor(out=F["f10"][:, m, :], in0=F["r"][:, m, :], scalar=prm[:, RK, m:m + 1], in1=F["f8"][:, m, :],
                                                               op0=ALU.mult, op1=ALU.mult), [fk("r"), fk("f8"), "prm"], [fk("f10")])
                pz, pzk = P.nxt("ps")
                A("pe", lambda e, pz=pz, m=m: e.matmul(pz[:, :TT], lhsT=onesb[:, :], rhs=F["f10"][:, m, :], start=True, stop=True), [fk("f10"), "onesb"], [pzk])
                A("dve", lambda e, pz=pz, m=m: e.tensor_tensor(out=F["f10"][:, m, :], in0=pz[:, :TT], in1=F["v"][:, m, :], op=ALU.mult), [pzk, fk("v")], [fk("f10")])
                A("dve", lambda e, m=m: e.tensor_tensor(out=F["r"][:, m, :], in0=F["r"][:, m, :], in1=F["a"][:, m, :], op=ALU.mult), [fk("r"), fk("a"), fk("f10")], [fk("r")])
                A("dve", lambda e, m=m: e.scalar_tensor_tensor(out=F["k"][:, m, :], in0=F["f6"][:, m, :], scalar=-1.0, in1=F["k"][:, m, :], op0=ALU.mult, op1=ALU.mult),
                  [fk("f6"), fk("k")], [fk("k")])
                A("dve", lambda e, m=m: e.tensor_tensor(out=F["f8"][:, m, :], in0=F["f8"][:, m, :], in1=F["f9"][:, m, :], op=ALU.mult), [fk("f8"), fk("f9"), fk("f10")], [fk("f8")])
                A("dve", lambda e, m=m: e.tensor_tensor(out=F["f7"][:, m, :], in0=F["f7"][:, m, :], in1=F["f9"][:, m, :], op=ALU.mult), [fk("f7"), fk("f9")], [fk("f7")])
                pcb = F["a"][:, m, C - 1::C].unsqueeze(2).broadcast_to([128, NCK, C])
                A("dve", lambda e, m=m, pcb=pcb: e.tensor_tensor(out=F["f9"][:, m, :].rearrange("p (c t) -> p c t", t=C),
                                                                 in0=F["f8"][:, m, :].rearrange("p (c t) -> p c t", t=C), in1=pcb, op=ALU.mult),
                  [fk("f8"), fk("a"), fk("f9"), fk("f7")], [fk("f9")])
                A("dve", lambda e, m=m, pcb=pcb: e.tensor_tensor(out=F["sw"][:, m, :].rearrange("p (c t) -> p c t", t=C),
                                                                 in0=F["f7"][:, m, :].rearrange("p (c t) -> p c t", t=C), in1=pcb, op=ALU.mult),
                  [fk("f7"), fk("a"), fk("sw"), fk("k")], [fk("sw")])
            RT, AT, KT, BT, VV, KH, BH, PP, BON, SZ = "r", "k", "f8", "f7", "v", "f9", "sw", "a", "f10", "sz"
            allm = lambda n: [("F", n, m) for m in range(MT)]
            mo, mok = P.nxt("mo")
            def do_chunk(c):
                    csl = slice(c * C, (c + 1) * C)
                    DBL = ("T", "Aak", "Arb", "Ark")
                    csx = dict(cs)
                    if c % 2 == 1:
                        csx.update({n: cs_odd[n] for n in DBL})
                    kname = lambda n: "c_" + n + ("_o" if (c % 2 == 1 and n in DBL) else "")
                    pspool = ["ps"]
                    nps = lambda: P.nxt(pspool[0])
                    tm, tmk = P.nxt("tm")
                    for qi, n in enumerate((VV, KH, BH)):
                        pz, pzk = nps()
                        A("pe", lambda e, pz=pz, n=n, csl=csl: [e.transpose(out=pz[:64, m * 128:(m + 1) * 128], in_=F[n][:, m, csl], identity=ident[:, :])
                                                               for m in range(MT)][-1], allm(n) + ["ident"], [pzk])
                        A("act", lambda e, pz=pz, tm=tm, qi=qi: e.activation(out=tm[:, qi, :], in_=pz[:64, :], func=AF.Copy), [pzk], [(tmk, qi)])
                    Vtm, Khtm, Bhtm = tm[:, 0, :], tm[:, 1, :], tm[:, 2, :]

                    def heads():
                        for h in range(8):
                            m, hl = h // 2, h % 2
                            yield h, m, hl, 64 * hl, slice(h * 64, (h + 1) * 64), slice((hl * 4 + m) * 64, (hl * 4 + m + 1) * 64)

                    def fm(n, m, bp):
                        return F[n][bp:bp + 64, m, csl]

                    for (name, ln, rn_, mask) in (("N", BT, AT, maskU), ("NT", AT, BT, maskL), ("Aak", KT, AT, maskU), ("Arb", BT, RT, maskUi), ("Ark", KT, RT, maskUi)):
                        pa, pak = nps()
                        pb, pbk = nps()

                        def f(e, pa=pa, pb=pb, ln=ln, rn_=rn_):
                            last = None
                            for h, m, hl, bp, hsn, hsb in heads():
                                dst = pa if hl == 0 else pb
                                last = e.matmul(dst[:64, m * 64:(m + 1) * 64], lhsT=fm(ln, m, bp), rhs=fm(rn_, m, bp), start=True, stop=True)
                            return last
                        A("pe", f, allm(ln) + allm(rn_), [pak, pbk])
                        A("dve", lambda e, pa=pa, name=name, mask=mask: e.tensor_tensor(out=csx[name][:, 0:256], in0=pa[:64, 0:256], in1=mask[:, 0:256], op=ALU.mult),
                          [pak, "msk"], [(kname(name), 0)])
                        A("dve", lambda e, pb=pb, name=name, mask=mask: e.tensor_tensor(out=csx[name][:, 256:512], in0=pb[:64, 0:256], in1=mask[:, 0:256], op=ALU.mult),
                          [pbk, "msk"], [(kname(name), 1)])
                    ck = lambda n: [(kname(n), 0), (kname(n), 1)]
                    A("dve", lambda e: e.tensor_tensor(out=csx["T"][:, :], in0=csx["N"][:, :], in1=I8, op=ALU.add), ck("N") + ["msk"], ck("T"))

                    def allheads(mmf, r, w):
                        def f(e):
                            last = None
                            for hh in heads():
                                last = mmf(e, *hh)
                            return last
                        A("pe", f, r, w)
                    X, XT = "N", "NT"
                    for lvl in range(5):
                        X2, X2T = ("Xa", "XTa") if lvl % 2 == 0 else ("Xb", "XTb")
                        if lvl < 4:
                            pz, pzk = nps()
                            allheads(lambda e, h, m, hl, bp, hsn, hsb, pz=pz, X=X, XT=XT: e.matmul(pz[:64, hsb], lhsT=csx[XT][:, hsb], rhs=csx[X][:, hsb], start=True, stop=True),
                                     ck(X) + ck(XT), [pzk])
                            A("act", lambda e, pz=pz, X2=X2: e.activation(out=csx[X2][:, :], in_=pz[:64, :], func=AF.Copy), [pzk], ck(X2))
                        pz, pzk = nps()
                        allheads(lambda e, h, m, hl, bp, hsn, hsb, pz=pz, X=X, XT=XT: e.matmul(pz[:64, hsb], lhsT=csx[X][:, hsb], rhs=csx[XT][:, hsb], start=True, stop=True),
                                 ck(X) + ck(XT), [pzk])
                        A("act", lambda e, pz=pz, X2T=X2T: e.activation(out=csx[X2T][:, :], in_=pz[:64, :], func=AF.Copy), [pzk], ck(X2T))
                        pz, pzk = nps()
                        allheads(lambda e, h, m, hl, bp, hsn, hsb, pz=pz, X2T=X2T: e.matmul(pz[:64, hsb], lhsT=csx[X2T][:, hsb], rhs=csx["T"][:, hsb], start=True, stop=True),
                                 ck(X2T) + ck("T"), [pzk])
                        A("dve", lambda e, pz=pz: e.tensor_tensor(out=csx["T"][:, :], in0=pz[:64, :], in1=csx["T"][:, :], op=ALU.add), [pzk] + ck("T"), ck("T"))
                        X, XT = X2, X2T

                    def state_mm(lhs_n, extra, out_name, outv_lo, outv_hi, rkeys):
                        pa, pak = nps()
                        pb, pbk = nps()

                        def f(e):
                            last = None
                            for h, m, hl, bp, hsn, hsb in heads():
                                if hl == 0:
                                    e.matmul(pa[:64, hsb], lhsT=fm(lhs_n, m, 0), rhs=ST[0:64, m, :], start=True, stop=False)
                                    last = extra(e, pa, hsn, hsb, False)
                                else:
                                    last = extra(e, pa, hsn, hsb, True)
                            for h, m, hl, bp, hsn, hsb in heads():
                                if hl == 1:
                                    last = e.matmul(pb[:64, m * 64:(m + 1) * 64], lhsT=fm(lhs_n, m, 64), rhs=ST[64:128, m, :], start=True, stop=True)
                            return last
                        A("pe", f, allm(lhs_n) + ["ST"] + rkeys, [pak, pbk])
                        A("act", lambda e: e.activation(out=csx["sq"][:, 0:256], in_=pb[:64, 0:256], func=AF.Copy), [pbk], ["c_sq"])
                        A("act", lambda e: e.activation(out=outv_lo, in_=pa[:64, 0:256].rearrange("p (m v) -> p m v", v=64), func=AF.Copy), [pak], [(out_name, 0)])
                        A("dve", lambda e: e.tensor_tensor(out=outv_hi, in0=pa[:64, 256:512].rearrange("p (m v) -> p m v", v=64),
                                                           in1=csx["sq"][:, 0:256].rearrange("p (m v) -> p m v", v=64), op=ALU.add), [pak, "c_sq"], [(out_name, 1)])

                    splits.append(len(P.ops))
                    pspool[0] = "ps2"
                    wt3 = csx["WT"][:, :].rearrange("p (b v) -> p b v", v=64)
                    state_mm(AT, lambda e, pa, hsn, hsb, first: e.matmul(pa[:64, hsb], lhsT=csx["Aak"][:, hsb], rhs=Vtm[:, hsn], start=first, stop=True),
                             "c_WT", wt3[:, 0:4, :], wt3[:, 4:8, :], ck("Aak") + [(tmk, 0)])
                    pz, pzk = nps()
                    allheads(lambda e, h, m, hl, bp, hsn, hsb, pz=pz: e.matmul(pz[:64, hsb], lhsT=csx["T"][:, hsb], rhs=csx["WT"][:, hsb], start=True, stop=True),
                             ck("T") + ck("WT"), [pzk])
                    A("act", lambda e, pz=pz: e.activation(out=csx["UT"][:, :], in_=pz[:64, :], func=AF.Copy), [pzk], ck("UT"))
                    o4 = csx["o"][:, :].rearrange("p (m hl v) -> p m hl v", hl=2, v=64)
                    state_mm(RT, lambda e, pa, hsn, hsb, first: [e.matmul(pa[:64, hsb], lhsT=csx["Arb"][:, hsb], rhs=csx["UT"][:, hsb], start=first, stop=False),
                                                                e.matmul(pa[:64, hsb], lhsT=csx["Ark"][:, hsb], rhs=Vtm[:, hsn], start=False, stop=True)][-1],
                             "c_o", o4[:, :, 0, :], o4[:, :, 1, :], ck("Arb") + ck("UT") + ck("Ark") + [(tmk, 0)])
                    pz, pzk = nps()
                    allheads(lambda e, h, m, hl, bp, hsn, hsb, pz=pz: [e.matmul(pz[bp:bp + 64, m * 64:(m + 1) * 64], lhsT=Bhtm[:, hsn], rhs=csx["UT"][:, hsb], start=True, stop=False),
                                                                       e.matmul(pz[bp:bp + 64, m * 64:(m + 1) * 64], lhsT=Khtm[:, hsn], rhs=Vtm[:, hsn], start=False, stop=True)][-1],
                             ck("UT") + [(tmk, 0), (tmk, 1), (tmk, 2)], [pzk])
                    for m in range(MT):
                        A("dve", lambda e, pz=pz, m=m, c=c: e.scalar_tensor_tensor(out=ST[:, m, :], in0=ST[:, m, :], scalar=F[PP][:, m, c * C + C - 1:c * C + C],
                                                                                 in1=pz[:, m * 64:(m + 1) * 64], op0=ALU.mult, op1=ALU.add),
                          ["ST", pzk, ("F", PP, m)], ["ST"])
                    o3 = csx["o"][:, :].rearrange("p (h v) -> p h v", v=64)
                    cok = [("c_o", 0), ("c_o", 1)]
                    A("dve", lambda e, o3=o3: e.tensor_reduce(out=st["s1"][:, :], in_=o3, axis=AX.X, op=ALU.add), cok, ["st_s1"])
                    A("dve", lambda e: e.tensor_tensor(out=csx["sq"][:, :], in0=csx["o"][:, :], in1=csx["o"][:, :], op=ALU.mult), cok, ["c_sq"])
                    A("dve", lambda e: e.tensor_reduce(out=st["s2"][:, :], in_=csx["sq"][:, :].rearrange("p (h v) -> p h v", v=64), axis=AX.X, op=ALU.add), ["c_sq"], ["st_s2"])
                    A("dve", lambda e: e.tensor_scalar(out=st["mean"][:, :], in0=st["s1"][:, :], scalar1=1.0 / 64, scalar2=None, op0=ALU.mult), ["st_s1"], ["st_mean"])
                    A("dve", lambda e: e.tensor_tensor(out=st["var"][:, :], in0=st["mean"][:, :], in1=st["mean"][:, :], op=ALU.mult), ["st_mean"], ["st_var"])
                    A("dve", lambda e: e.scalar_tensor_tensor(out=st["var"][:, :], in0=st["s2"][:, :], scalar=1.0 / 64, in1=st["var"][:, :], op0=ALU.mult, op1=ALU.subtract),
                      ["st_s2", "st_var"], ["st_var"])
                    A("dve", lambda e: e.tensor_scalar(out=st["var"][:, :], in0=st["var"][:, :], scalar1=RW_GN_EPS, scalar2=None, op0=ALU.add), ["st_var"], ["st_var"])
                    A("act", lambda e: e.activation(out=st["rstd"][:, :], in_=st["var"][:, :], func=AF.Sqrt), ["st_var"], ["st_rstd"])
                    A("dve", lambda e: e.reciprocal(out=st["rstd"][:, :], in_=st["rstd"][:, :]), ["st_rstd"], ["st_rstd"])
                    on3 = csx["on"][:, :].rearrange("p (h v) -> p h v", v=64)
                    A("dve", lambda e, o3=o3, on3=on3: e.tensor_tensor(out=on3, in0=o3, in1=st["mean"][:, :].unsqueeze(2).broadcast_to([64, 8, 64]), op=ALU.subtract),
                      cok + ["st_mean"], ["c_on"])
                    A("dve", lambda e, on3=on3: e.tensor_tensor(out=on3, in0=on3, in1=st["rstd"][:, :].unsqueeze(2).broadcast_to([64, 8, 64]), op=ALU.mult),
                      ["c_on", "st_rstd"], ["c_on"])
                    pz, pzk = nps()
                    A("pe", lambda e, pz=pz: [e.transpose(out=pz[:, m * 64:(m + 1) * 64], in_=csx["on"][:, m * 128:(m + 1) * 128], identity=ident[0:64, 0:64])
                                              for m in range(MT)][-1], ["c_on", "ident"], [pzk])
                    for m in range(MT):
                        A("dve", lambda e, pz=pz, m=m, csl=csl, mo=mo: e.tensor_scalar(out=mo[:, m, csl], in0=pz[:, m * 64:(m + 1) * 64], scalar1=prm[:, LG, m:m + 1],
                                                                                      scalar2=prm[:, LB, m:m + 1], op0=ALU.mult, op1=ALU.add),
                          [pzk, "prm"], [(mok, m)])
                        A("pool", lambda e, m=m, csl=csl, mo=mo: e.tensor_tensor(out=mo[:, m, csl], in0=mo[:, m, csl], in1=F[BON][:, m, csl], op=ALU.add),
                          [(mok, m), ("F", BON, m)], [(mok, m)])
                        A("pool", lambda e, m=m, csl=csl, mo=mo: e.tensor_tensor(out=mo[:, m, csl], in0=mo[:, m, csl], in1=F[SZ][:, m, csl], op=ALU.mult),
                          [(mok, m), ("F", SZ, m)], [(mok, m)])
            start = len(P.ops)
            splits, bounds = [], []
            for c in range(NCK):
                b0 = len(P.ops)
                do_chunk(c)
                bounds.append((b0, splits[-1], len(P.ops)))
            if pipeline:
                rec = P.ops[start:]
                del P.ops[start:]
                ph1 = [rec[b0 - start:sp_ - start] for (b0, sp_, b1) in bounds]
                ph2 = [rec[sp_ - start:b1 - start] for (b0, sp_, b1) in bounds]
                P.ops.extend(ph1[0])
                for c in range(NCK):
                    a_, b_ = ph2[c], (ph1[c + 1] if c + 1 < NCK else [])
                    ia = ib = 0
                    while ia < len(a_) or ib < len(b_):
                        if ib >= len(b_) or (ia < len(a_) and ia * len(b_) <= ib * len(a_)):
                            P.ops.append(a_[ia]); ia += 1
                        else:
                            P.ops.append(b_[ib]); ib += 1
            P.dma(mTv[:, :, j * TT:(j + 1) * TT], mo[:, :, :], r=[(mok, m) for m in range(MT)], w=[("out", j)])
            outkeys.append(("out", j))

    for j in range(NT // TT if ntiles is None else ntiles):
        do_tile(j)
    P.add("sp", None, r=outkeys)
    return P.finalize() if own else None


def rwa_layer(xfull, prm):
    TT = 256
    ident = np.eye(128, dtype=np.float32)
    onesb = np.zeros((128, 128), np.float32)
    onesb[:64, :64] = 1.0; onesb[64:, 64:] = 1.0
    i_, t_ = np.meshgrid(np.arange(64), np.arange(64), indexing="ij")
    msk = np.stack([np.tile((i_ < t_), (1, 8)), np.tile((i_ <= t_), (1, 8)), np.tile((i_ > t_), (1, 8)),
                    np.tile(np.eye(64), (1, 8))], axis=1).astype(np.float32)
    rst = np.ones((128, TT), np.float32); rst[:, ::64] = 0.0
    mu = np.stack([pp(prm["mu"][i]) for i in range(6)], axis=1)
    in_maps = []
    for c in range(NCORES):
        b, hh = c // 2, c % 2
        cs_ = slice(512 * hh, 512 * hh + 512)
        pv = [prm["w0"], prm["a0"], prm["k_k"], prm["k_a"], prm["r_k"].reshape(-1), prm["lnx_g"], prm["lnx_b"]]
        prmv = np.stack([pp(v_[cs_]) for v_ in pv], axis=1)
        d = dict(xT=np.ascontiguousarray(np.concatenate([np.zeros((D, 1), np.float32), xfull[b]], axis=1)), mu=np.ascontiguousarray(mu),
                 w1=prm["w1"], a1=prm["a1"], w2=np.ascontiguousarray(prm["w2"][:, cs_]), a2=np.ascontiguousarray(prm["a2"][:, cs_]),
                 prm=np.ascontiguousarray(prmv), ident=ident, onesb=onesb, msk=np.ascontiguousarray(msk), rst=rst)
        for qi, q in enumerate("rkvz"):
            d["w_" + q] = np.ascontiguousarray(prm["w_rkvz"][qi][:, cs_])
        in_maps.append(d)
    res = run("rwa", build_rwa, in_maps)
    return [np.ascontiguousarray(np.concatenate([res[2 * b]["mT"], res[2 * b + 1]["mT"]], axis=0)) for b in range(4)]


def build_outb(NT=2048, TT=256, sel=False, P=None):
    own = P is None
    P = P or Prog()
    xT = P.dram("xT", [D, NT])
    mT_d = P.dram("mT", [D, NT * (2 if sel else 1)])
    pT = P.dram("pT", [256, NT])
    w_out_d = P.dram("w_out", [D, D])
    w_pg_d = P.dram("w_pg", [D, D])
    w_pe_d = P.dram("w_pe", [256, D])
    lng_d = P.dram("ln_g", [128, 8])
    lnb_d = P.dram("ln_b", [128, 8])
    oT = P.dram("oT", [D, NT], out=True)
    P.pool("wst", 2, [128, 1024], F32)
    P.pool("ps", 6, [128, 512], F32, psum=True)
    P.pool("psA", 2, [128, 512], F32, psum=True)
    w_out = load_weight_bf16(P, "w_out", w_out_d, D, D, "wst")
    w_pg, w_pe, ones, lng, lnb = tail_setup(P, TT, w_pg_d, w_pe_d, lng_d, lnb_d)
    if sel:
        P.pool("g2", 1, [128, 8, TT], F32)
        selt = P.sb("sel", [128, 2], F32)
        P.dma(selt[:, :], P.dram("sel", [128, 2]), w=["sel"])
    P.pool("x", 2, [128, 8, TT], F32)
    P.pool("g", 2, [128, 8, TT], F32)
    P.pool("p", 2, [128, 2, TT], F32)
    P.pool("pb", 2, [128, 2, TT], BF16)
    P.pool("m", 2, [128, 8, TT], BF16)
    P.pool("o", 2, [128, 8, TT], F32)
    xTv = xT.rearrange("(k p) t -> p k t", p=128)
    mTv = mT_d.rearrange("(k p) t -> p k t", p=128)
    pTv = pT.rearrange("(k p) t -> p k t", p=128)
    oTv = oT.rearrange("(k p) t -> p k t", p=128)
    outkeys = []

    def do_tile(j):
        sl = slice(j * TT, (j + 1) * TT)
        x, xk = P.nxt("x"); g, gk = P.nxt("g"); pt, ptk = P.nxt("p"); pb, pbk = P.nxt("pb"); m, mk = P.nxt("m")
        P.dma(x[:, :, :], xTv[:, :, sl], w=[xk])
        P.dma(g[:, :, :], mTv[:, :, sl], w=[gk])
        if sel:
            g2, g2k = P.nxt("g2")
            P.dma(g2[:, :, :], mTv[:, :, NT + j * TT:NT + (j + 1) * TT], w=[g2k])
            P.add("pool", lambda e: e.tensor_scalar(out=g[:, :, :], in0=g[:, :, :], scalar1=selt[:, 0:1], scalar2=None, op0=ALU.mult),
                  r=[gk, "sel"], w=[gk])
            P.add("dve", lambda e: e.scalar_tensor_tensor(out=g[:, :, :], in0=g2[:, :, :], scalar=selt[:, 1:2], in1=g[:, :, :],
                                                          op0=ALU.mult, op1=ALU.add), r=[gk, g2k, "sel"], w=[gk])
        P.dma(pt[:, :, :], pTv[:, :, sl], w=[ptk])
        P.add("pool", lambda e: e.tensor_copy(out=pb[:, :, :], in_=pt[:, :, :]), r=[ptk], w=[pbk])
        for k in range(8):
            P.add("pool", lambda e, k=k: e.tensor_copy(out=m[:, k, :], in_=g[:, k, :]), r=[gk], w=[(mk, k)])

        def y_ps(d):
            yp, yk = P.nxt("ps")
            P.add("pe", lambda e: [e.matmul(yp[:, :TT], lhsT=w_out[:, k, d * 128:(d + 1) * 128], rhs=m[:, k, :],
                                            start=(k == 0), stop=(k == 7)) for k in range(8)][-1],
                  r=[(mk, k) for k in range(8)] + [("w_out", k) for k in range(8)], w=[yk])
            return yp[:, :TT], yk

        o, ok = P.nxt("o")
        tail(P, TT, x, xk, y_ps, pb, pbk, w_pg, w_pe, ones, lng, lnb, o, ok, "o")
        P.dma(oTv[:, :, sl], o[:, :, :], r=[(ok, d) for d in range(8)], w=[("out", j)])
        outkeys.append(("out", j))
    for j in range(NT // TT):
        do_tile(j)
    P.add("sp", None, r=outkeys)
    return P.finalize() if own else None


def outb_layer(xs, ms, p_l, w_out, w_pg, w_pe, ln_g, ln_b):
    in_maps = []
    for c in range(NCORES):
        b, h = c // 2, c % 2
        in_maps.append(dict(xT=xs[c], mT=ms[c], pT=np.ascontiguousarray(p_l[b, h * 2048:(h + 1) * 2048].T),
                            w_out=w_out, w_pg=w_pg, w_pe=w_pe, ln_g=pp(ln_g), ln_b=pp(ln_b)))
    res = run("outb", build_outb, in_maps)
    return [r["oT"] for r in res]


_NC_CACHE = {}


def run(name, builder, in_maps):
    if name not in _NC_CACHE:
        _NC_CACHE[name] = builder()
    res = run_bass_kernel_spmd(_NC_CACHE[name], in_maps, core_ids=list(range(NCORES)))
    return res.results


def conv_layer(xs, p_l, w_in, conv_k, w_out, w_pg, w_pe, ln_g, ln_b):
    ck = np.concatenate([pp(conv_k[j]) for j in range(3)], axis=1)
    in_maps = []
    for c in range(NCORES):
        b, h = c // 2, c % 2
        halo = np.zeros((D, 2), np.float32) if h == 0 else xs[c - 1][:, -2:]
        in_maps.append(dict(xT=np.ascontiguousarray(np.concatenate([halo, xs[c]], axis=1)),
                            pT=np.ascontiguousarray(p_l[b, h * 2048:(h + 1) * 2048].T),
                            w_in=w_in, conv_k=ck, w_out=w_out, w_pg=w_pg, w_pe=w_pe, ln_g=pp(ln_g), ln_b=pp(ln_b)))
    res = run("conv", build_conv, in_maps)
    return [r["oT"] for r in res]


NTF = 4096


def build_fused():
    NT = NTF
    P = Prog()
    x1s = P.scratch("x1s", [D, NT])
    gyp = P.scratch("gyp", [D, 8, NT // 8])
    x2s = P.scratch("x2s", [D, NT + 1])
    ms = P.scratch("ms", [D, NT])
    x3s = P.scratch("x3s", [D, NT + 2])
    with P.stage("Z_"):
        z = P.sb("z", [128, 2])
        P.add("pool", lambda e: e.memset(z[:, :], 0.0), w=["z"])
        for k in range(8):
            P.dma(x2s[k * 128:(k + 1) * 128, 0:1], z[:, 0:1], r=["z"], w=[("zz", k)], slow=True)
            P.dma(x3s[k * 128:(k + 1) * 128, 0:2], z[:, 0:2], r=["z"], w=[("zz", k)], slow=True)
    with P.stage("L0_", {"oT": x1s}):
        build_conv(NT=NT, P=P)
    for hf in range(2):
        with P.stage(f"L1a{hf}_", {"xT": x1s, "gTp": gyp[512 * hf:512 * hf + 512]}):
            build_s5a(P=P)
    with P.stage("L1b_", {"xT": x1s, "gT": gyp, "oT": x2s[:, 1:NT + 1]}):
        build_s5b(NT=NT, perm_g=True, P=P)
    for hf in range(2):
        with P.stage(f"L2a{hf}_", {"xT": x2s, "mT": ms[512 * hf:512 * hf + 512]}):
            build_rwa(NT=NT, P=P)
    with P.stage("L2b_", {"xT": x2s[:, 1:NT + 1], "mT": ms, "oT": x3s[:, 2:NT + 2]}):
        build_outb(NT=NT, P=P)
    with P.stage("L3_", {"xT": x3s}):
        build_conv(NT=NT, P=P)
    return P.finalize()


PAIRS = [[0, 1], [2, 3], [4, 5], [6, 7]]


def build_fused_pair():
    NT, NH = 4096, 2048
    P = Prog()
    x1h = P.scratch("x1h", [D, NH]); x1f = P.scratch("x1f", [D, NT])
    gym = P.scratch("gym", [512, 8, NT // 8]); gyg = P.scratch("gyg", [D, 8, NT // 8])
    x2h = P.scratch("x2h", [D, NH]); x2f = P.scratch("x2f", [D, NT + 1])
    msm = P.scratch("msm", [512, NT]); msg = P.scratch("msg", [D, NT])
    x3p = P.scratch("x3p", [D, NH + 2]); hlm = P.scratch("hlm", [D, 2]); hlg = P.scratch("hlg", [2 * D, 2])
    def gather(name, src, rows, place):
        R, C = src.shape
        for q in range(R // rows):
            gq = P.scratch(f"{name}_g{q}", [2 * rows, C])
            P.cc_allgather(src[q * rows:(q + 1) * rows, :], gq, PAIRS, w=[(name, q)])
            for rk in range(2):
                P.dma(place(q, rk), gq[rk * rows:(rk + 1) * rows, :], r=[(name, q)], w=[(name, q, rk)])

    with P.stage("Z_"):
        z = P.sb("z", [128, 2])
        P.add("pool", lambda e: e.memset(z[:, :], 0.0), w=["z"])
        for k in range(8):
            P.dma(x2f[k * 128:(k + 1) * 128, 0:1], z[:, 0:1], r=["z"], w=[("zz", k)], slow=True)
    with P.stage("L0_", {"oT": x1h}):
        build_conv(NT=NH, P=P)
    with P.stage("X1_"):
        gather("x1", x1h, 256, lambda q, rk: x1f[q * 256:(q + 1) * 256, rk * NH:(rk + 1) * NH])
    with P.stage("L1a_", {"xT": x1f, "gTp": gym}):
        build_s5a(P=P)
    with P.stage("X2_"):
        gyg2 = gyg.rearrange("c t n -> c (t n)")
        gather("gy", gym.rearrange("c t n -> c (t n)"), 128, lambda q, rk: gyg2[512 * rk + q * 128:512 * rk + (q + 1) * 128, :])
    with P.stage("L1b_", {"xT": x1h, "gT": gyg, "oT": x2h}):
        build_s5b(NT=NH, perm_g=True, sel=True, P=P)
    with P.stage("X3_"):
        gather("x2", x2h, 256, lambda q, rk: x2f[q * 256:(q + 1) * 256, 1 + rk * NH:1 + (rk + 1) * NH])
    with P.stage("L2a_", {"xT": x2f, "mT": msm}):
        build_rwa(NT=NT, P=P)
    with P.stage("X4_"):
        gather("ms", msm, 128, lambda q, rk: msg[512 * rk + q * 128:512 * rk + (q + 1) * 128, :])
    with P.stage("L2b_", {"xT": x2h, "mT": msg, "oT": x3p[:, 2:NH + 2]}):
        build_outb(NT=NH, sel=True, P=P)
    with P.stage("X5_"):
        P.dma(hlm[:, :], x3p[:, NH:NH + 2], w=["hlm"])
        P.cc_allgather(hlm, hlg, PAIRS, r=["hlm"], w=["hlg"])
        ht = P.sb("ht", [128, 8, 2]); selt = P.sb("sel", [128, 2])
        P.dma(selt[:, :], P.dram("sel", [128, 2]), w=["sel"])
        P.dma(ht[:, :, :], hlg[0:D, :].rearrange("(k p) t -> p k t", p=128), r=["hlg"], w=["ht"])
        P.add("dve", lambda e: e.tensor_scalar(out=ht[:, :, :], in0=ht[:, :, :], scalar1=selt[:, 1:2], scalar2=None, op0=ALU.mult),
              r=["ht", "sel"], w=["ht"])
        P.dma(x3p[:, 0:2].rearrange("(k p) t -> p k t", p=128), ht[:, :, :], r=["ht"], w=["x3halo"])
    with P.stage("L3_", {"xT": x3p}):
        build_conv(NT=NH, P=P)
    return P.finalize()


def kernel(**inp):
    inp = {k: np.asarray(v, np.float32) for k, v in inp.items()}
    x = inp["x"]
    sp = {k[4:]: inp[k][0] for k in inp if k.startswith("ssm_")}
    rp = {k[5:]: inp[k][0] for k in inp if k.startswith("rwkv_")}
    shared = {}

    def put(dst, prefix, d):
        for k, v in d.items():
            dst[prefix + k] = np.ascontiguousarray(v, dtype=np.float32)
    for (pre, i, j) in (("L0_", 0, 0), ("L3_", 3, 1)):
        put(shared, pre, dict(w_in=inp["conv_w_in"][j], conv_k=np.concatenate([pp(inp["conv_k"][j][t]) for t in range(3)], axis=1),
                              w_out=inp["conv_w_out"][j], **_tail_in(inp, i)))
    put(shared, "L1b_", dict(w_z=sp["w_in"][:, D:], w_glu=sp["w_glu"], b_glu=pp(sp["b_glu"]), w_out=sp["w_out"], **_tail_in(inp, 1)))
    put(shared, "L2b_", dict(w_out=rp["w_out"], **_tail_in(inp, 2)))
    halfp = [{}, {}]
    for hf in range(2):
        put(halfp[hf], "L1a_", _s5a_in(sp, hf))
        put(halfp[hf], "L2a_", _rwa_in(rp, hf))
    in_maps = []
    for c in range(NCORES):
        b, h = c // 2, c % 2
        ts = slice(2048 * h, 2048 * h + 2048)
        d = dict(shared)
        d.update(halfp[h])
        halo = np.zeros((D, 2), np.float32) if h == 0 else x[b, 2046:2048].T
        d["L0_xT"] = np.ascontiguousarray(np.concatenate([halo, x[b, ts].T], axis=1))
        for (pre, i) in (("L0_", 0), ("L1b_", 1), ("L2b_", 2), ("L3_", 3)):
            d[pre + "pT"] = np.ascontiguousarray(inp["p"][i][b, ts].T)
        selv = np.zeros((128, 2), np.float32); selv[:, h] = 1.0
        for pre in ("L1b_", "L2b_", "X5_"):
            d[pre + "sel"] = selv
        in_maps.append(d)
    res = run("fusedp", build_fused_pair, in_maps)
    out = np.zeros_like(x)
    for c in range(NCORES):
        out[c // 2, (c % 2) * 2048:(c % 2 + 1) * 2048] = res[c]["L3_oT"].T
    return out


def _tail_in(inp, i):
    return dict(w_pg=inp["ple_gate"][i], w_pe=inp["ple_proj"][i], ln_g=pp(inp["ln_g"][i]), ln_b=pp(inp["ln_b"][i]))


def _s5a_in(prm, gh):
    NG = 32
    ident = np.eye(128, dtype=np.float32)
    j2 = np.zeros((128, 128), np.float32)
    j2[np.arange(64), np.arange(64) + 64] = 1.0
    j2[np.arange(64) + 64, np.arange(64)] = -1.0
    msk = np.zeros((128, 10), np.float32)
    msk[:64, 0] = 1.0; msk[64:, 1] = 1.0
    identg = np.zeros((128, 8, 16), np.float32)
    for gl in range(8):
        msk[16 * gl:16 * gl + 16, 2 + gl] = 1.0
        for h in range(16):
            identg[16 * gl + h, gl, h] = 1.0
    gs = slice(NG * gh, NG * gh + NG)

    def cl(a):
        a = np.broadcast_to(a.reshape(NG // 8, 8, 1, 64), (NG // 8, 8, 16, 64))
        return np.ascontiguousarray(a.transpose(1, 2, 0, 3).reshape(128, NG // 8, 64))

    def clb(a):
        a = a.reshape(NG // 8, 8, 64, 16)
        return np.ascontiguousarray(a.transpose(1, 3, 0, 2).reshape(128, NG // 8, 64))

    def plv(a):
        a = np.broadcast_to(a.T.reshape(1, 64, NG, 1), (2, 64, NG, 16))
        return np.ascontiguousarray(a.reshape(128, NG, 16))

    def plb(a):
        a = np.broadcast_to(a.transpose(1, 0, 2)[None], (2, 64, NG, 16))
        return np.ascontiguousarray(a.reshape(128, NG, 16))

    def plc(a):
        a = np.broadcast_to(a.transpose(2, 0, 1)[None], (2, 64, NG, 16))
        return np.ascontiguousarray(a.reshape(128, NG, 16))
    ldt = np.broadcast_to(prm["log_dt"][gs][:, None], (NG, 64))
    return dict(
        w_u=np.ascontiguousarray(prm["w_in"][:, 512 * gh:512 * gh + 512]),
        lamre_cl=cl(prm["lam_re"][gs]), lamim_cl=cl(prm["lam_im"][gs]), ldt_cl=cl(ldt),
        bre_cl=clb(prm["b_re"][gs]), bim_cl=clb(prm["b_im"][gs]),
        lamre_pl=plv(prm["lam_re"][gs]), lamim_pl=plv(prm["lam_im"][gs]), ldt_pl=plv(ldt),
        bre_pl=plb(prm["b_re"][gs]), bim_pl=plb(prm["b_im"][gs]),
        cre_pl=plc(prm["c_re"][gs]), cim_pl=plc(prm["c_im"][gs]),
        dsk=pp(prm["d"][512 * gh:512 * gh + 512]), ident=ident, j2=j2, msk=msk, identg=identg)


def _rwa_in(prm, hh):
    TT = 256
    ident = np.eye(128, dtype=np.float32)
    onesb = np.zeros((128, 128), np.float32)
    onesb[:64, :64] = 1.0; onesb[64:, 64:] = 1.0
    i_, t_ = np.meshgrid(np.arange(64), np.arange(64), indexing="ij")
    msk = np.stack([np.tile((i_ < t_), (1, 8)), np.tile((i_ <= t_), (1, 8)), np.tile((i_ > t_), (1, 8)),
                    np.tile(np.eye(64), (1, 8))], axis=1).astype(np.float32)
    rst = np.ones((128, TT), np.float32); rst[:, ::64] = 0.0
    mu = np.stack([pp(prm["mu"][i]) for i in range(6)], axis=1)
    cs_ = slice(512 * hh, 512 * hh + 512)
    pv = [prm["w0"], prm["a0"], prm["k_k"], prm["k_a"], prm["r_k"].reshape(-1), prm["lnx_g"], prm["lnx_b"]]
    prmv = np.stack([pp(v_[cs_]) for v_ in pv], axis=1)
    d = dict(mu=np.ascontiguousarray(mu), w1=prm["w1"], a1=prm["a1"], w2=np.ascontiguousarray(prm["w2"][:, cs_]),
             a2=np.ascontiguousarray(prm["a2"][:, cs_]), prm=np.ascontiguousarray(prmv), ident=ident, onesb=onesb,
             msk=np.ascontiguousarray(msk), rst=rst)
    for qi, q in enumerate("rkvz"):
        d["w_" + q] = np.ascontiguousarray(prm["w_rkvz"][qi][:, cs_])
    return d


def kernel_fused_b(**inp):
    inp = {k: np.asarray(v, np.float32) for k, v in inp.items()}
    x = inp["x"]
    sp = {k[4:]: inp[k][0] for k in inp if k.startswith("ssm_")}
    rp = {k[5:]: inp[k][0] for k in inp if k.startswith("rwkv_")}
    shared = {}

    def put(prefix, d):
        for k, v in d.items():
            shared[prefix + k] = np.ascontiguousarray(v, dtype=np.float32)
    for (pre, i, j) in (("L0_", 0, 0), ("L3_", 3, 1)):
        put(pre, dict(w_in=inp["conv_w_in"][j], conv_k=np.concatenate([pp(inp["conv_k"][j][t]) for t in range(3)], axis=1),
                      w_out=inp["conv_w_out"][j], **_tail_in(inp, i)))
    for hf in range(2):
        put(f"L1a{hf}_", _s5a_in(sp, hf))
        put(f"L2a{hf}_", _rwa_in(rp, hf))
    put("L1b_", dict(w_z=sp["w_in"][:, D:], w_glu=sp["w_glu"], b_glu=pp(sp["b_glu"]), w_out=sp["w_out"], **_tail_in(inp, 1)))
    put("L2b_", dict(w_out=rp["w_out"], **_tail_in(inp, 2)))
    in_maps = []
    for c in range(NCORES):
        b = c % 4
        d = dict(shared)
        d["L0_xT"] = np.ascontiguousarray(np.concatenate([np.zeros((D, 2), np.float32), x[b].T], axis=1))
        for (pre, i) in (("L0_", 0), ("L1b_", 1), ("L2b_", 2), ("L3_", 3)):
            d[pre + "pT"] = np.ascontiguousarray(inp["p"][i][b].T)
        in_maps.append(d)
    res = run("fused", build_fused, in_maps)
    return np.ascontiguousarray(np.stack([res[b]["L3_oT"].T for b in range(4)]))


def _halves(full):
    return [np.ascontiguousarray(full[c // 2][:, (c % 2) * 2048:(c % 2 + 1) * 2048]) for c in range(NCORES)]


def _fulls(halves):
    return [np.ascontiguousarray(np.concatenate([halves[2 * b], halves[2 * b + 1]], axis=1)) for b in range(4)]


def kernel_unfused(**inp):
    inp = {k: np.asarray(v, np.float32) for k, v in inp.items()}
    x = inp["x"]
    xs = [np.ascontiguousarray(x[c // 2, (c % 2) * 2048:(c % 2 + 1) * 2048].T) for c in range(NCORES)]
    for i in range(DEPTH):
        kind, j = i % 3, i // 3
        tailw = (inp["ple_gate"][i], inp["ple_proj"][i], inp["ln_g"][i], inp["ln_b"][i])
        if kind == 0:
            xs = conv_layer(xs, inp["p"][i], inp["conv_w_in"][j], inp["conv_k"][j], inp["conv_w_out"][j], *tailw)
        elif kind == 1:
            prm = {k[4:]: inp[k][j] for k in inp if k.startswith("ssm_")}
            gy = s5a_layer(_fulls(xs), prm)
            xs = s5b_layer(xs, _halves(gy), inp["p"][i], prm["w_in"], prm["w_glu"], prm["b_glu"], prm["w_out"], *tailw)
        else:
            prm = {k[5:]: inp[k][j] for k in inp if k.startswith("rwkv_")}
            ms = rwa_layer(_fulls(xs), prm)
            xs = outb_layer(xs, _halves(ms), inp["p"][i], prm["w_out"], *tailw)
    out = np.zeros_like(x)
    for c in range(NCORES):
        out[c // 2, (c % 2) * 2048:(c % 2 + 1) * 2048] = xs[c].T
    return out
```

```python
from contextlib import ExitStack

import numpy as np
import concourse.bass as bass
import concourse.mybir as mybir
from concourse.bass_utils import run_bass_kernel_spmd

F32 = mybir.dt.float32
BF16 = mybir.dt.bfloat16
AF = mybir.ActivationFunctionType
ALU = mybir.AluOpType
AX = mybir.AxisListType

D = 1024
DEPTH = 4
DN_ALPHA = (2 * DEPTH) ** 0.25
LN_EPS = 1e-5
NCORES = 8


class Prog:
    COMPUTE = ("pe", "act", "dve", "pool")
    BLOCKNAME = {"pe": "tensor", "act": "scalar", "dve": "vector", "pool": "gpsimd", "sp": "sync"}
    NDMASEM = 12

    def __init__(self):
        self.nc = bass.Bass("TRN2", target_bir_lowering=False)
        self.ops = []
        self.stack = ExitStack()
        self._rr = {}
        self.prefix = ""
        self.bound = {}
        self.psum_keys = set()

    def dram(self, name, shape, dtype=F32, out=False):
        name = self.prefix + name
        if name in self.bound:
            ap = self.bound[name]
            assert list(ap.shape) == list(shape), (name, ap.shape, shape)
            return ap
        return self.nc.dram_tensor(name, list(shape), dtype,
                                   kind="ExternalOutput" if out else "ExternalInput").ap()

    def scratch(self, name, shape, dtype=F32):
        return self.nc.dram_tensor(name, list(shape), dtype).ap()

    def stage(self, prefix, bind=None):
        prog = self

        class _S:
            def __enter__(s_):
                s_.old = (prog.stack, prog.prefix)
                prog.stack = ExitStack()
                prog.prefix = prefix
                for k, v in (bind or {}).items():
                    prog.bound[prefix + k] = v

            def __exit__(s_, *a):
                prog.stack.close()
                prog.stack, prog.prefix = s_.old
                prog.ops.append(dict(barrier=True, eng=None, fn=None, r=[], w=[], dma=False))
                return False
        return _S()

    def sb(self, name, shape, dtype=F32):
        return self.stack.enter_context(self.nc.sbuf_tensor("s_" + self.prefix + name, list(shape), dtype))

    def ps(self, name, shape, dtype=F32):
        return self.stack.enter_context(self.nc.psum_tensor("p_" + self.prefix + name, list(shape), dtype))

    def pool(self, name, n, shape, dtype=F32, psum=False):
        mk = self.ps if psum else self.sb
        if psum:
            shape = [128, 512]
            self.psum_keys |= {f"{self.prefix}{name}{i}" for i in range(n)}
        tiles = [(mk(f"{name}{i}", shape, dtype), f"{self.prefix}{name}{i}") for i in range(n)]
        self._rr[name] = [tiles, 0]
        return tiles

    def nxt(self, name):
        ent = self._rr[name]
        t = ent[0][ent[1] % len(ent[0])]
        ent[1] += 1
        return t

    def add(self, eng, fn, r=(), w=(), dma=False):
        self.ops.append(dict(eng=eng, fn=fn, r=list(r), w=list(w), dma=dma))

    def dma(self, out, in_, r=(), w=(), q="sp", slow=False):
        if slow:
            self.add(q, lambda e: e.dma_start(out=out, in_=in_, allow_slow_non_contiguous=True), r=r, w=w, dma=True)
        else:
            self.add(q, lambda e: e.dma_start(out=out, in_=in_), r=r, w=w, dma=True)

    def cc_allgather(self, in_ap, out_ap, groups, r=(), w=()):
        self.ops.append(dict(eng="pool", fn=lambda e: e.collective_compute("AllGather", ALU.bypass, replica_groups=groups,
                                                                            ins=[in_ap.opt()], outs=[out_ap.opt()]),
                             r=list(r), w=list(w), dma=False, cc=True))

    def finalize(self):
        nc, ops = self.nc, self.ops
        last_w, readers = {}, {}
        bar = set()
        ccs = []
        last_c, last_d, dslot = {}, {}, {}
        for i, op in enumerate(ops):
            if op.get("barrier"):
                bar = set(last_c.values()) | set(last_d.values()) | set(ccs)
                ccs = []
                op["deps"] = set()
                continue
            if op.get("cc"):
                ccs.append(i)
            elif op["dma"]:
                sl_ = dslot.get(op["eng"], 0)
                dslot[op["eng"]] = sl_ + 1
                last_d[(op["eng"], sl_ % self.NDMASEM)] = i
            elif op["fn"] is not None:
                last_c[op["eng"]] = i
            deps = set(bar)
            raw = set()
            pk = self.psum_keys
            for k in op["r"]:
                if k in last_w:
                    deps.add(last_w[k]); raw.add(last_w[k])
                if k in pk:
                    for j in readers.get(k, ()):
                        if ops[j]["eng"] != op["eng"]:
                            deps.add(j)
            for k in op["w"]:
                if k in last_w:
                    deps.add(last_w[k])
                for j in readers.get(k, ()):
                    deps.add(j)
            deps.discard(i)
            op["deps"] = set(deps)
            if op["fn"] is not None:
                for k in op["r"]:
                    readers.setdefault(k, []).append(i)
            for k in op["w"]:
                last_w[k] = i
                readers[k] = []
        needed = set()
        for op in ops:
            needed |= op["deps"]
        engines = list(self.COMPUTE) + ["sp"]
        SEMMAX = 4000
        ccount = {e: 0 for e in engines}
        dcount = {}
        dn = {e: 0 for e in engines}
        semkeys = set()
        for i, op in enumerate(ops):
            e = op["eng"]
            if op.get("barrier"):
                op["sig"] = None
                continue
            if op.get("cc"):
                op["sig"] = (("cc", i), 1)
                semkeys.add(op["sig"][0])
                continue
            if op["dma"]:
                s = dn[e] % self.NDMASEM
                dn[e] += 1
                c = dcount.get((e, s), 0)
                per = SEMMAX // 16
                op["dprev"] = (("d", e, s, (c - 1) // per), 16 * ((c - 1) % per + 1)) if c > 0 else None
                dcount[(e, s)] = c + 1
                op["sig"] = (("d", e, s, c // per), 16 * (c % per + 1))
                semkeys.add(op["sig"][0])
            elif i in needed:
                c = ccount[e]
                ccount[e] += 1
                op["sig"] = (("c", e, c // SEMMAX), c % SEMMAX + 1)
                semkeys.add(op["sig"][0])
            else:
                op["sig"] = None
        self.nwait = {}
        with ExitStack() as st:
            sems = {k: st.enter_context(nc.semaphore("s_" + "_".join(str(x) for x in k))) for k in sorted(semkeys)}
            block = st.enter_context(nc.Block())
            for e in engines:
                mine = [op for op in ops if op["eng"] == e and not op.get("barrier")]
                if not mine:
                    continue

                def body(eng, mine=mine, eng_name=e):
                    known = {}

                    def wait(sig):
                        if sig is None:
                            return
                        key, val = sig
                        if known.get(key, 0) < val:
                            eng.wait_ge(sems[key], val)
                            known[key] = val
                            self.nwait[eng_name] = self.nwait.get(eng_name, 0) + 1

                    for op in mine:
                        need = {}
                        for j in op["deps"]:
                            sg = ops[j]["sig"]
                            if sg is not None and need.get(sg[0], 0) < sg[1]:
                                need[sg[0]] = sg[1]
                        if op["dma"] and op["dprev"] is not None:
                            sg = op["dprev"]
                            if need.get(sg[0], 0) < sg[1]:
                                need[sg[0]] = sg[1]
                        for kk_, vv_ in sorted(need.items(), key=lambda t: str(t[0])):
                            wait((kk_, vv_))
                        if op["fn"] is None:
                            continue
                        ins = op["fn"](eng)
                        if op["sig"] is not None:
                            ins.then_inc(sems[op["sig"][0]], 16 if op["dma"] else 1)

                getattr(block, self.BLOCKNAME[e])(body)
        self.counts = dict(ccount)
        self.stack.close()
        return nc


def pp(v):
    v = np.asarray(v, np.float32).reshape(-1, 128)
    return np.ascontiguousarray(v.T)


def load_weight_bf16(P, name, w_dram, K, N, stage_pool):
    kt = K // 128
    wt = P.sb(name, [128, kt, N], BF16)
    for k in range(kt):
        for c0 in range(0, N, 1024):
            c1 = min(N, c0 + 1024)
            st, sk = P.nxt(stage_pool)
            P.dma(st[:, : c1 - c0], w_dram[k * 128:(k + 1) * 128, c0:c1], w=[sk])
            if (k + c0 // 1024) % 2 == 0:
                P.add("dve", lambda e, st=st, k=k, c0=c0, c1=c1: e.tensor_copy(out=wt[:, k, c0:c1], in_=st[:, : c1 - c0]),
                      r=[sk], w=[(name, k)])
            else:
                P.add("act", lambda e, st=st, k=k, c0=c0, c1=c1: e.activation(out=wt[:, k, c0:c1], in_=st[:, : c1 - c0], func=AF.Copy),
                      r=[sk], w=[(name, k)])
    return wt


def tail(P, TT, x_res, xk, y_ps_fn, pT_bf, pk, w_pg, w_pe, ones, lng, lnb, out_tile, ok, tag):
    r, rk = P.nxt("r")
    rb, rbk = P.nxt("rb")
    for d in range(8):
        yp, yk = y_ps_fn(d)
        P.add("dve", lambda e, d=d, yp=yp: e.scalar_tensor_tensor(out=r[:, d, :], in0=x_res[:, d, :], scalar=float(DN_ALPHA),
                                                                 in1=yp, op0=ALU.mult, op1=ALU.add),
              r=[xk, yk], w=[(rk, d)])
        P.add("act", lambda e, d=d: e.activation(out=rb[:, d, :], in_=r[:, d, :], func=AF.Copy), r=[(rk, d)], w=[(rbk, d)])
    sq, sqk = P.nxt("sq")
    mean_ps, mk = P.nxt("psA")
    msq_ps, qk = P.nxt("psA")
    for d in range(8):
        gp, gk = P.nxt("ps")
        P.add("pe", lambda e, d=d, gp=gp: [e.matmul(gp[:, :TT], lhsT=w_pg[:, k, d * 128:(d + 1) * 128], rhs=rb[:, k, :],
                                                    start=(k == 0), stop=(k == 7)) for k in range(8)][-1],
              r=[(rbk, k) for k in range(8)] + [("w_pg", k) for k in range(8)], w=[gk])
        sg, sgk = P.nxt("tmp")
        P.add("act", lambda e, gp=gp, sg=sg: e.activation(out=sg[:, :TT], in_=gp[:, :TT], func=AF.Sigmoid), r=[gk], w=[sgk])
        ep, ek = P.nxt("ps")
        P.add("pe", lambda e, d=d, ep=ep: [e.matmul(ep[:, :TT], lhsT=w_pe[:, k, d * 128:(d + 1) * 128], rhs=pT_bf[:, k, :],
                                                    start=(k == 0), stop=(k == 1)) for k in range(2)][-1],
              r=[pk] + [("w_pe", k) for k in range(2)], w=[ek])
        P.add("dve", lambda e, ep=ep, sg=sg: e.tensor_tensor(out=sg[:, :TT], in0=ep[:, :TT], in1=sg[:, :TT], op=ALU.mult),
              r=[ek, sgk], w=[sgk])
        P.add("dve", lambda e, d=d, sg=sg: e.tensor_tensor(out=r[:, d, :], in0=r[:, d, :], in1=sg[:, :TT], op=ALU.add),
              r=[sgk, (rk, d)], w=[(rk, d)])
        P.add("act", lambda e, d=d: e.activation(out=sq[:, d, :], in_=r[:, d, :], func=AF.Square), r=[(rk, d)], w=[(sqk, d)])
    P.add("pe", lambda e: [e.matmul(mean_ps[:, :TT], lhsT=ones[:, :], rhs=r[:, k, :], start=(k == 0), stop=(k == 7))
                           for k in range(8)][-1], r=[(rk, k) for k in range(8)] + ["ones"], w=[mk])
    P.add("pe", lambda e: [e.matmul(msq_ps[:, :TT], lhsT=ones[:, :], rhs=sq[:, k, :], start=(k == 0), stop=(k == 7))
                           for k in range(8)][-1], r=[(sqk, k) for k in range(8)] + ["ones"], w=[qk])
    mean, mnk = P.nxt("st")
    rstd, rsk = P.nxt("st")
    P.add("act", lambda e: e.activation(out=mean[:, :TT], in_=mean_ps[:, :TT], func=AF.Copy), r=[mk], w=[mnk])
    P.add("dve", lambda e: e.tensor_tensor(out=rstd[:, :TT], in0=mean[:, :TT], in1=mean[:, :TT], op=ALU.mult), r=[mnk], w=[rsk])
    P.add("dve", lambda e: e.tensor_tensor(out=rstd[:, :TT], in0=msq_ps[:, :TT], in1=rstd[:, :TT], op=ALU.subtract),
          r=[qk, rsk], w=[rsk])
    P.add("dve", lambda e: e.tensor_scalar(out=rstd[:, :TT], in0=rstd[:, :TT], scalar1=float(LN_EPS), scalar2=None, op0=ALU.add),
          r=[rsk], w=[rsk])
    P.add("act", lambda e: e.activation(out=rstd[:, :TT], in_=rstd[:, :TT], func=AF.Sqrt), r=[rsk], w=[rsk])
    P.add("dve", lambda e: e.reciprocal(out=rstd[:, :TT], in_=rstd[:, :TT]), r=[rsk], w=[rsk])
    for d in range(8):
        P.add("dve", lambda e, d=d: e.tensor_tensor(out=r[:, d, :], in0=r[:, d, :], in1=mean[:, :TT], op=ALU.subtract),
              r=[(rk, d), mnk], w=[(rk, d)])
        P.add("dve", lambda e, d=d: e.tensor_tensor(out=r[:, d, :], in0=r[:, d, :], in1=rstd[:, :TT], op=ALU.mult),
              r=[(rk, d), rsk], w=[(rk, d)])
        P.add("act", lambda e, d=d: e.activation(out=out_tile[:, d, :], in_=r[:, d, :], func=AF.Identity,
                                                 scale=lng[:, d:d + 1], bias=lnb[:, d:d + 1]),
              r=[(rk, d), "lnp"], w=[(ok, d)])


def tail_setup(P, TT, w_pg_d, w_pe_d, lng_d, lnb_d):
    w_pg = load_weight_bf16(P, "w_pg", w_pg_d, 1024, 1024, "wst")
    w_pe = load_weight_bf16(P, "w_pe", w_pe_d, 256, 1024, "wst")
    ones = P.sb("ones", [128, 128], F32)
    P.add("pool", lambda e: e.memset(ones[:, :], 1.0 / D), w=["ones"])
    lng = P.sb("lng", [128, 8], F32)
    lnb = P.sb("lnb", [128, 8], F32)
    P.dma(lng[:, :], lng_d, w=["lnp"])
    P.dma(lnb[:, :], lnb_d, w=["lnp"])
    P.pool("r", 1, [128, 8, TT], F32)
    P.pool("rb", 1, [128, 8, TT], BF16)
    P.pool("sq", 1, [128, 8, TT], F32)
    P.pool("st", 2, [128, TT], F32)
    P.pool("tmp", 6, [128, TT + 2], F32)
    return w_pg, w_pe, ones, lng, lnb


def build_conv(NT=2048, TT=256, P=None):
    own = P is None
    P = P or Prog()
    xT = P.dram("xT", [D, NT + 2])
    pT = P.dram("pT", [256, NT])
    w_in_d = P.dram("w_in", [D, 4 * D])
    ck_d = P.dram("conv_k", [128, 24])
    w_out_d = P.dram("w_out", [D, D])
    w_pg_d = P.dram("w_pg", [D, D])
    w_pe_d = P.dram("w_pe", [256, D])
    lng_d = P.dram("ln_g", [128, 8])
    lnb_d = P.dram("ln_b", [128, 8])
    oT = P.dram("oT", [D, NT], out=True)

    P.pool("wst", 2, [128, 1024], F32)
    P.pool("ps", 6, [128, 512], F32, psum=True)
    P.pool("psA", 2, [128, 512], F32, psum=True)
    w_in = load_weight_bf16(P, "w_in", w_in_d, D, 4 * D, "wst")
    w_out = load_weight_bf16(P, "w_out", w_out_d, D, D, "wst")
    w_pg, w_pe, ones, lng, lnb = tail_setup(P, TT, w_pg_d, w_pe_d, lng_d, lnb_d)
    ck = P.sb("ck", [128, 24], F32)
    P.dma(ck[:, :], ck_d, w=["ck"])
    P.pool("x", 2, [128, 8, TT + 2], F32)
    P.pool("xb", 2, [128, 8, TT + 2], BF16)
    P.pool("p", 2, [128, 2, TT], F32)
    P.pool("pb", 2, [128, 2, TT], BF16)
    P.pool("m", 1, [128, 8, TT], BF16)
    P.pool("o", 2, [128, 8, TT], F32)
    xTv = xT.rearrange("(k p) t -> p k t", p=128)
    pTv = pT.rearrange("(k p) t -> p k t", p=128)
    oTv = oT.rearrange("(k p) t -> p k t", p=128)
    W = TT + 2
    outkeys = []
    for j in range(NT // TT):
        x, xk = P.nxt("x")
        xb, xbk = P.nxt("xb")
        pt, ptk = P.nxt("p")
        pb, pbk = P.nxt("pb")
        P.dma(x[:, :, :], xTv[:, :, j * TT: j * TT + W], w=[xk])
        P.dma(pt[:, :, :], pTv[:, :, j * TT:(j + 1) * TT], w=[ptk])
        P.add("pool", lambda e, x=x, xb=xb: e.tensor_copy(out=xb[:, :, :], in_=x[:, :, :]), r=[xk], w=[xbk])
        P.add("pool", lambda e, pt=pt, pb=pb: e.tensor_copy(out=pb[:, :, :], in_=pt[:, :, :]), r=[ptk], w=[pbk])
        m, mk = P.nxt("m")
        for et in range(8):
            pss = {}
            for qi, q in enumerate(("bg", "cg", "h", "z")):
                pz, pzk = P.nxt("ps")
                c0 = qi * D + et * 128
                lo = 0 if q in ("cg", "h") else 2
                P.add("pe", lambda e, pz=pz, c0=c0, lo=lo, xb=xb: [
                    e.matmul(pz[:, : W - lo], lhsT=w_in[:, k, c0:c0 + 128], rhs=xb[:, k, lo:W], start=(k == 0), stop=(k == 7))
                    for k in range(8)][-1], r=[xbk] + [("w_in", k) for k in range(8)], w=[pzk])
                pss[q] = (pz, pzk)
            cgs, cgk = P.nxt("tmp")
            P.add("act", lambda e, cgs=cgs, a=pss["cg"][0]: e.activation(out=cgs[:, :W], in_=a[:, :W], func=AF.Copy),
                  r=[pss["cg"][1]], w=[cgk])
            P.add("dve", lambda e, cgs=cgs, a=pss["h"][0]: e.tensor_tensor(out=cgs[:, :W], in0=a[:, :W], in1=cgs[:, :W], op=ALU.mult),
                  r=[pss["h"][1], cgk], w=[cgk])
            sz, szk = P.nxt("tmp")
            P.add("act", lambda e, sz=sz, a=pss["z"][0]: e.activation(out=sz[:, :TT], in_=a[:, :TT], func=AF.Silu),
                  r=[pss["z"][1]], w=[szk])
            P.add("dve", lambda e, sz=sz, a=pss["bg"][0]: e.tensor_tensor(out=sz[:, :TT], in0=a[:, :TT], in1=sz[:, :TT], op=ALU.mult),
                  r=[pss["bg"][1], szk], w=[szk])
            cv, cvk = P.nxt("tmp")
            P.add("act", lambda e, cv=cv, cgs=cgs, et=et: e.activation(out=cv[:, :TT], in_=cgs[:, 2:W], func=AF.Identity,
                                                                      scale=ck[:, 16 + et:17 + et], bias=0.0), r=[cgk, "ck"], w=[cvk])
            P.add("dve", lambda e, cv=cv, cgs=cgs, et=et: e.scalar_tensor_tensor(out=cv[:, :TT], in0=cgs[:, 1:W - 1], scalar=ck[:, 8 + et:9 + et],
                                                                                in1=cv[:, :TT], op0=ALU.mult, op1=ALU.add),
                  r=[cgk, cvk, "ck"], w=[cvk])
            P.add("dve", lambda e, cv=cv, cgs=cgs, et=et: e.scalar_tensor_tensor(out=cv[:, :TT], in0=cgs[:, 0:TT], scalar=ck[:, et:et + 1],
                                                                                in1=cv[:, :TT], op0=ALU.mult, op1=ALU.add),
                  r=[cgk, cvk, "ck"], w=[cvk])
            P.add("dve", lambda e, cv=cv, sz=sz, et=et, m=m: e.tensor_tensor(out=m[:, et, :], in0=cv[:, :TT], in1=sz[:, :TT], op=ALU.mult),
                  r=[cvk, szk], w=[(mk, et)])

        def y_ps(d, m=m, mk=mk):
            yp, yk = P.nxt("ps")
            P.add("pe", lambda e: [e.matmul(yp[:, :TT], lhsT=w_out[:, k, d * 128:(d + 1) * 128], rhs=m[:, k, :],
                                            start=(k == 0), stop=(k == 7)) for k in range(8)][-1],
                  r=[(mk, k) for k in range(8)] + [("w_out", k) for k in range(8)], w=[yk])
            return yp[:, :TT], yk

        o, ok = P.nxt("o")
        tail(P, TT, x[:, :, 2:W], xk, y_ps, pb, pbk, w_pg, w_pe, ones, lng, lnb, o, ok, "c")
        P.dma(oTv[:, :, j * TT:(j + 1) * TT], o[:, :, :], r=[(ok, d) for d in range(8)], w=[("out", j)])
        outkeys.append(("out", j))
    P.add("sp", None, r=outkeys)
    return P.finalize() if own else None


def build_s5b(NT=2048, TT=256, perm_g=False, sel=False, P=None):
    own = P is None
    P = P or Prog()
    xT = P.dram("xT", [D, NT])
    gcols = NT * (2 if sel else 1)
    gT = P.dram("gT", [D, 8, gcols // 8] if perm_g else [D, gcols])
    pT = P.dram("pT", [256, NT])
    w_z_d = P.dram("w_z", [D, D])
    w_glu_d = P.dram("w_glu", [D, D])
    bglu_d = P.dram("b_glu", [128, 8])
    w_out_d = P.dram("w_out", [D, D])
    w_pg_d = P.dram("w_pg", [D, D])
    w_pe_d = P.dram("w_pe", [256, D])
    lng_d = P.dram("ln_g", [128, 8])
    lnb_d = P.dram("ln_b", [128, 8])
    oT = P.dram("oT", [D, NT], out=True)
    P.pool("wst", 2, [128, 1024], F32)
    P.pool("ps", 6, [128, 512], F32, psum=True)
    P.pool("psA", 2, [128, 512], F32, psum=True)
    w_z = load_weight_bf16(P, "w_z", w_z_d, D, D, "wst")
    w_glu = load_weight_bf16(P, "w_glu", w_glu_d, D, D, "wst")
    w_out = load_weight_bf16(P, "w_out", w_out_d, D, D, "wst")
    w_pg, w_pe, ones, lng, lnb = tail_setup(P, TT, w_pg_d, w_pe_d, lng_d, lnb_d)
    bglu = P.sb("bglu", [128, 8], F32)
    P.dma(bglu[:, :], bglu_d, w=["bglu"])
    P.pool("x", 2, [128, 8, TT], F32)
    P.pool("xb", 2, [128, 8, TT], BF16)
    P.pool("g", 2, [128, 8, TT], F32)
    P.pool("gb", 2, [128, 8, TT], BF16)
    P.pool("p", 2, [128, 2, TT], F32)
    P.pool("pb", 2, [128, 2, TT], BF16)
    P.pool("m", 1, [128, 8, TT], BF16)
    P.pool("o", 2, [128, 8, TT], F32)
    if sel:
        P.pool("g2", 1, [128, 8, TT], F32)
        selt = P.sb("sel", [128, 2], F32)
        P.dma(selt[:, :], P.dram("sel", [128, 2]), w=["sel"])
    xTv = xT.rearrange("(k p) t -> p k t", p=128)
    gTv = gT.rearrange("(k p) t n -> p k t n", p=128) if perm_g else gT.rearrange("(k p) t -> p k t", p=128)
    pTv = pT.rearrange("(k p) t -> p k t", p=128)
    oTv = oT.rearrange("(k p) t -> p k t", p=128)
    outkeys = []
    CN = TT // 8
    gseq = (lambda ap: ap.rearrange("p (t n) -> p n t", t=8)) if perm_g else (lambda ap: ap)
    sseq = (lambda ap: ap.rearrange("p (n t) -> p n t", t=8)) if perm_g else (lambda ap: ap)
    for j in range(NT // TT):
        sl = slice(j * TT, (j + 1) * TT)
        x, xk = P.nxt("x"); xb, xbk = P.nxt("xb")
        g, gk = P.nxt("g"); gb, gbk = P.nxt("gb")
        pt, ptk = P.nxt("p"); pb, pbk = P.nxt("pb")
        P.dma(x[:, :, :], xTv[:, :, sl], w=[xk])
        if perm_g:
            for k in range(8):
                P.dma(g[:, k, :].rearrange("p (t n) -> p t n", t=8), gTv[:, k, :, j * CN:(j + 1) * CN], w=[gk])
            if sel:
                g2, g2k = P.nxt("g2")
                for k in range(8):
                    P.dma(g2[:, k, :].rearrange("p (t n) -> p t n", t=8), gTv[:, k, :, NT // 8 + j * CN:NT // 8 + (j + 1) * CN], w=[g2k])
                P.add("pool", lambda e, g=g: e.tensor_scalar(out=g[:, :, :], in0=g[:, :, :], scalar1=selt[:, 0:1], scalar2=None, op0=ALU.mult),
                      r=[gk, "sel"], w=[gk])
                P.add("dve", lambda e, g=g, g2=g2: e.scalar_tensor_tensor(out=g[:, :, :], in0=g2[:, :, :], scalar=selt[:, 1:2], in1=g[:, :, :],
                                                                          op0=ALU.mult, op1=ALU.add), r=[gk, g2k, "sel"], w=[gk])
        else:
            P.dma(g[:, :, :], gTv[:, :, sl], w=[gk])
        P.dma(pt[:, :, :], pTv[:, :, sl], w=[ptk])
        P.add("pool", lambda e, x=x, xb=xb: e.tensor_copy(out=xb[:, :, :], in_=x[:, :, :]), r=[xk], w=[xbk])
        for k in range(8):
            P.add("pool", lambda e, g=g, gb=gb, k=k: e.tensor_copy(out=sseq(gb[:, k, :]), in_=gseq(g[:, k, :])), r=[gk], w=[gbk])
        P.add("pool", lambda e, pt=pt, pb=pb: e.tensor_copy(out=pb[:, :, :], in_=pt[:, :, :]), r=[ptk], w=[pbk])
        m, mk = P.nxt("m")
        for et in range(8):
            sp_, spk = P.nxt("ps")
            P.add("pe", lambda e, sp_=sp_, et=et, gb=gb: [e.matmul(sp_[:, :TT], lhsT=w_glu[:, k, et * 128:(et + 1) * 128], rhs=gb[:, k, :],
                                                                  start=(k == 0), stop=(k == 7)) for k in range(8)][-1],
                  r=[gbk] + [("w_glu", k) for k in range(8)], w=[spk])
            zp, zpk = P.nxt("ps")
            P.add("pe", lambda e, zp=zp, et=et, xb=xb: [e.matmul(zp[:, :TT], lhsT=w_z[:, k, et * 128:(et + 1) * 128], rhs=xb[:, k, :],
                                                                start=(k == 0), stop=(k == 7)) for k in range(8)][-1],
                  r=[xbk] + [("w_z", k) for k in range(8)], w=[zpk])
            sg, sgk = P.nxt("tmp")
            P.add("act", lambda e, sg=sg, sp_=sp_, et=et: e.activation(out=sg[:, :TT], in_=sp_[:, :TT], func=AF.Sigmoid, bias=bglu[:, et:et + 1]),
                  r=[spk, "bglu"], w=[sgk])
            sz, szk = P.nxt("tmp")
            P.add("act", lambda e, sz=sz, zp=zp: e.activation(out=sz[:, :TT], in_=zp[:, :TT], func=AF.Silu), r=[zpk], w=[szk])
            P.add("dve", lambda e, sg=sg, g=g, et=et: e.tensor_tensor(out=sseq(sg[:, :TT]), in0=gseq(g[:, et, :]), in1=sseq(sg[:, :TT]), op=ALU.mult),
                  r=[gk, sgk], w=[sgk])
            P.add("dve", lambda e, sg=sg, sz=sz, m=m, et=et: e.tensor_tensor(out=m[:, et, :], in0=sg[:, :TT], in1=sz[:, :TT], op=ALU.mult),
                  r=[sgk, szk], w=[(mk, et)])

        def y_ps(d, m=m, mk=mk):
            yp, yk = P.nxt("ps")
            P.add("pe", lambda e: [e.matmul(yp[:, :TT], lhsT=w_out[:, k, d * 128:(d + 1) * 128], rhs=m[:, k, :],
                                            start=(k == 0), stop=(k == 7)) for k in range(8)][-1],
                  r=[(mk, k) for k in range(8)] + [("w_out", k) for k in range(8)], w=[yk])
            return yp[:, :TT], yk

        o, ok = P.nxt("o")
        tail(P, TT, x, xk, y_ps, pb, pbk, w_pg, w_pe, ones, lng, lnb, o, ok, "s")
        P.dma(oTv[:, :, sl], o[:, :, :], r=[(ok, d) for d in range(8)], w=[("out", j)])
        outkeys.append(("out", j))
    P.add("sp", None, r=outkeys)
    return P.finalize() if own else None


def s5b_layer(xs, gs, p_l, w_in, w_glu, b_glu, w_out, w_pg, w_pe, ln_g, ln_b):
    in_maps = []
    w_z = np.ascontiguousarray(w_in[:, D:])
    for c in range(NCORES):
        b, h = c // 2, c % 2
        in_maps.append(dict(xT=xs[c], gT=gs[c], pT=np.ascontiguousarray(p_l[b, h * 2048:(h + 1) * 2048].T),
                            w_z=w_z, w_glu=w_glu, b_glu=pp(b_glu), w_out=w_out, w_pg=w_pg, w_pe=w_pe,
                            ln_g=pp(ln_g), ln_b=pp(ln_b)))
    res = run("s5b", build_s5b, in_maps)
    return [r["oT"] for r in res]


PI = float(np.pi)


def _tt(P, eng, o, a, b, op):
    P.add(eng, lambda e: e.tensor_tensor(out=o[0], in0=a[0], in1=b[0], op=op), r=[a[1], b[1]], w=[o[1]])


def _ts(P, eng, o, a, s1, op0, s2=None, op1=None, extra=()):
    if op1 is None:
        P.add(eng, lambda e: e.tensor_scalar(out=o[0], in0=a[0], scalar1=s1, scalar2=None, op0=op0), r=[a[1]] + list(extra), w=[o[1]])
    else:
        P.add(eng, lambda e: e.tensor_scalar(out=o[0], in0=a[0], scalar1=s1, scalar2=s2, op0=op0, op1=op1), r=[a[1]] + list(extra), w=[o[1]])


def _stt(P, o, a, sc, b, op0, op1, extra=()):
    P.add("dve", lambda e: e.scalar_tensor_tensor(out=o[0], in0=a[0], scalar=sc, in1=b[0], op0=op0, op1=op1),
          r=[a[1], b[1]] + list(extra), w=[o[1]])


def _cmul(P, orr, oi, ar, ai, br, bi, t):
    _tt(P, "dve", orr, ar, br, ALU.mult)
    _tt(P, "dve", t, ai, bi, ALU.mult)
    _tt(P, "dve", orr, orr, t, ALU.subtract)
    _tt(P, "dve", oi, ar, bi, ALU.mult)
    _tt(P, "dve", t, ai, br, ALU.mult)
    _tt(P, "dve", oi, oi, t, ALU.add)


def _cplx_setup(P, tp, shape, lamre, lamim, ldt):
    nel = int(np.prod(shape))
    def T(n):
        if n not in tp:
            tp[n] = P.sb("tp_" + n, [128, 512])
        return (tp[n][:, :nel].rearrange("p (a b) -> p a b", b=shape[-1]), "tp_" + n)
    dt, thr, thi, tmp, x, x2, acc, sn, cs, t1, t2 = [T(n) for n in ("dt", "thr", "thi", "tmp", "x", "x2", "acc", "sn", "cs", "t1", "t2")]
    er, ar, ai, nr, den, cr, ci = [T(n) for n in ("er", "ar", "ai", "nr", "den", "cr", "ci")]
    P.add("act", lambda e: e.activation(out=dt[0], in_=ldt[0], func=AF.Exp), r=[ldt[1]], w=[dt[1]])
    _tt(P, "dve", thr, lamre, dt, ALU.mult)
    _tt(P, "dve", thi, lamim, dt, ALU.mult)
    for _ in range(4):
        _ts(P, "dve", tmp, thi, PI, ALU.is_gt, 2 * PI, ALU.mult)
        _tt(P, "dve", thi, thi, tmp, ALU.subtract)
    _ts(P, "dve", x, thi, 0.125, ALU.mult)
    _tt(P, "dve", x2, x, x, ALU.mult)
    _ts(P, "dve", acc, x2, 1.0 / 362880, ALU.mult)
    for c in (-1.0 / 5040, 1.0 / 120, -1.0 / 6):
        _stt(P, acc, acc, c, x2, ALU.add, ALU.mult)
    _stt(P, sn, acc, 1.0, x, ALU.add, ALU.mult)
    _ts(P, "dve", acc, x2, -1.0 / 3628800, ALU.mult)
    for c in (1.0 / 40320, -1.0 / 720, 1.0 / 24, -0.5):
        _stt(P, acc, acc, c, x2, ALU.add, ALU.mult)
    _ts(P, "dve", cs, acc, 1.0, ALU.add)
    for _ in range(3):
        _tt(P, "dve", t1, cs, cs, ALU.mult)
        _tt(P, "dve", t2, sn, sn, ALU.mult)
        _stt(P, sn, cs, 2.0, sn, ALU.mult, ALU.mult)
        _tt(P, "dve", cs, t1, t2, ALU.subtract)
    _ts(P, "dve", acc, thr, 1.0 / 120, ALU.mult)
    for c in (1.0 / 24, 1.0 / 6, 0.5, 1.0):
        _stt(P, acc, acc, c, thr, ALU.add, ALU.mult)
    _ts(P, "dve", er, acc, 1.0, ALU.add)
    _tt(P, "dve", ar, er, cs, ALU.mult)
    _tt(P, "dve", ai, er, sn, ALU.mult)
    _ts(P, "dve", nr, ar, -1.0, ALU.add)
    _tt(P, "dve", den, lamre, lamre, ALU.mult)
    _tt(P, "dve", t1, lamim, lamim, ALU.mult)
    _tt(P, "dve", den, den, t1, ALU.add)
    P.add("dve", lambda e: e.reciprocal(out=den[0], in_=den[0]), r=[den[1]], w=[den[1]])
    _tt(P, "dve", cr, nr, lamre, ALU.mult)
    _tt(P, "dve", t1, ai, lamim, ALU.mult)
    _tt(P, "dve", cr, cr, t1, ALU.add)
    _tt(P, "dve", cr, cr, den, ALU.mult)
    _tt(P, "dve", ci, ai, lamre, ALU.mult)
    _tt(P, "dve", t1, nr, lamim, ALU.mult)
    _tt(P, "dve", ci, ci, t1, ALU.subtract)
    _tt(P, "dve", ci, ci, den, ALU.mult)
    return ar, ai, cr, ci, t1, t2


def build_s5a(NG=32, NT=4096, TT=256, ngrun=None, stage=9, P=None):
    MT = NG // 8
    NCH = NT // 8
    assert NCH == 512
    own = P is None
    P = P or Prog()
    xT = P.dram("xT", [D, NT])
    w_u_d = P.dram("w_u", [D, MT * 128])
    cl_d = {n: P.dram(n, [128, MT, 64]) for n in ("lamre_cl", "lamim_cl", "ldt_cl", "bre_cl", "bim_cl")}
    pl_d = {n: P.dram(n, [128, NG, 16]) for n in ("lamre_pl", "lamim_pl", "ldt_pl", "bre_pl", "bim_pl", "cre_pl", "cim_pl")}
    dsk_d = P.dram("dsk", [128, MT])
    ident_d = P.dram("ident", [128, 128])
    j2_d = P.dram("j2", [128, 128])
    msk_d = P.dram("msk", [128, 2 + 8])
    identg_d = P.dram("identg", [128, 8, 16])
    gTp = P.dram("gTp", [MT * 128, 8, NCH], out=True)

    P.pool("wst", 1, [128, 1024], F32)
    P.pool("ps", 4, [128, 512], F32, psum=True)
    P.pool("psk", 2, [128, 128], F32, psum=True)
    P.pool("psr", 2, [128, 512], F32, psum=True)
    w_u = load_weight_bf16(P, "w_u", w_u_d, D, MT * 128, "wst")

    def ld(name, shape, src):
        t = P.sb(name, shape)
        P.dma(t[:], src, w=[name])
        return (t[:], name), t
    cl = {n: ld(n, [128, MT, 64], cl_d[n])[0] for n in cl_d}
    pl = {n: ld(n, [128, NG, 16], pl_d[n])[0] for n in pl_d}
    _, dsk = ld("dsk", [128, MT], dsk_d)
    _, ident = ld("ident", [128, 128], ident_d)
    _, j2 = ld("j2", [128, 128], j2_d)
    _, msk = ld("msk", [128, 10], msk_d)
    _, identg = ld("identg", [128, 8, 16], identg_d)
    mlo, mhi = msk[:, 0:1], msk[:, 1:2]

    u_t = P.sb("u", [128, MT, NT], BF16)
    P.pool("x", 1, [128, 8, TT], F32)
    P.pool("xb", 2, [128, 8, TT], BF16)
    xTv = xT.rearrange("(k p) t -> p k t", p=128)
    for j in range(NT // TT):
        x, xk = P.nxt("x"); xb, xbk = P.nxt("xb")
        P.dma(x[:, :, :], xTv[:, :, j * TT:(j + 1) * TT], w=[xk])
        P.add("pool", lambda e, x=x, xb=xb: e.tensor_copy(out=xb[:, :, :], in_=x[:, :, :]), r=[xk], w=[xbk])
        for m in range(MT):
            up, upk = P.nxt("ps")
            P.add("pe", lambda e, up=up, m=m, xb=xb: [e.matmul(up[:, :TT], lhsT=w_u[:, k, m * 128:(m + 1) * 128], rhs=xb[:, k, :],
                                                             start=(k == 0), stop=(k == 7)) for k in range(8)][-1],
                  r=[xbk] + [("w_u", k) for k in range(8)], w=[upk])
            P.add("act", lambda e, up=up, m=m, j=j: e.activation(out=u_t[:, m, j * TT:(j + 1) * TT], in_=up[:, :TT], func=AF.Copy),
                  r=[upk], w=[("u", m, j)])
    ukeys = lambda m: [("u", m, j) for j in range(NT // TT)]

    tp = {}
    ar, ai, cr, ci, t1, t2 = _cplx_setup(P, tp, [MT, 64], cl["lamre_cl"], cl["lamim_cl"], cl["ldt_cl"])
    EB = P.sb("EB", [128, 8, MT, 2, 64])
    ebr = lambda j: (EB[:, j, :, 0, :], ("EB", j))
    ebi = lambda j: (EB[:, j, :, 1, :], ("EB", j))
    _cmul(P, ebr(7), ebi(7), cr, ci, cl["bre_cl"], cl["bim_cl"], t1)
    for j in range(6, -1, -1):
        _cmul(P, ebr(j), ebi(j), ebr(j + 1), ebi(j + 1), ar, ai, t1)

    par, pai, pcr, pci, pt1, pt2 = _cplx_setup(P, tp, [NG, 16], pl["lamre_pl"], pl["lamim_pl"], pl["ldt_pl"])
    def T(n, old=None):
        if old is None:
            t = P.sb(n, [128, NG, 16])
            return (t[:], n)
        return (tp[old][:, :NG * 16].rearrange("p (a b) -> p a b", b=16), "tp_" + old)
    qr, qi, qr2, qi2 = T("qr", "dt"), T("qi", "thr"), T("qr2", "thi"), T("qi2", "tmp")
    car, cai, car2, cai2 = T("car", "x"), T("cai", "x2"), T("car2", "acc"), T("cai2", "sn")
    pr, pi_, pr2, pi2 = T("pr", "cs"), T("pi", "er"), T("pr2", "nr"), T("pi2", "den")
    Cc = T("Cc")
    Xs = P.sb("Xs", [128, 8, NG, 16])
    CS = P.sb("CS", [128, NG, 8, 16])
    AKr = P.sb("AKr", [128, 9, NG]); AKi = P.sb("AKi", [128, 9, NG])
    _ts(P, "dve", pt1, pl["cim_pl"], mhi, ALU.mult, extra=["msk"])
    _stt(P, Cc, pl["cre_pl"], mlo, pt1, ALU.mult, ALU.subtract, extra=["msk"])
    _cmul(P, qr, qi, pcr, pci, pl["bre_pl"], pl["bim_pl"], pt1)

    def stack(o, re, im, sub):
        _ts(P, "dve", pt2, im, mhi, ALU.mult, extra=["msk"])
        _stt(P, o, re, mlo, pt2, ALU.mult, ALU.subtract if sub else ALU.add, extra=["msk"])
    stack((Xs[:, 0, :, :], ("Xs", 0)), qr, qi, False)
    cur_q, nxt_q = (qr, qi), (qr2, qi2)
    cur_c, nxt_c = (pl["cre_pl"], pl["cim_pl"]), (car, cai)
    alt_c = (car2, cai2)
    cur_p, nxt_p = None, (pr, pi_)
    for tau in range(1, 9):
        if tau <= 7:
            _cmul(P, nxt_q[0], nxt_q[1], cur_q[0], cur_q[1], par, pai, pt1)
            cur_q, nxt_q = nxt_q, cur_q
            stack((Xs[:, tau, :, :], ("Xs", tau)), cur_q[0], cur_q[1], False)
        _cmul(P, nxt_c[0], nxt_c[1], cur_c[0], cur_c[1], par, pai, pt1)
        cur_c = nxt_c
        nxt_c = alt_c if cur_c[0][1] == car[1] else (car, cai)
        stack((CS[:, :, tau - 1, :], ("CS", tau - 1)), cur_c[0], cur_c[1], True)
        if cur_p is None:
            cur_p = (par, pai)
        else:
            _cmul(P, nxt_p[0], nxt_p[1], cur_p[0], cur_p[1], par, pai, pt1)
            cur_p = nxt_p
            nxt_p = (pr2, pi2) if cur_p[0][1] == pr[1] else (pr, pi_)
    akr = lambda k: (AKr[:, k, :], ("AK", k))
    aki = lambda k: (AKi[:, k, :], ("AK", k))
    P.add("dve", lambda e: e.tensor_copy(out=AKr[:, 0, :], in_=cur_p[0][0][:, :, 0]), r=[cur_p[0][1]], w=[("AK", 0)])
    P.add("dve", lambda e: e.tensor_copy(out=AKi[:, 0, :], in_=cur_p[1][0][:, :, 0]), r=[cur_p[1][1]], w=[("AK", 0)])
    s1 = P.sb("aks1", [128, NG]); s2 = P.sb("aks2", [128, NG])
    s1 = (s1[:], "aks1"); s2 = (s2[:], "aks2")
    for k in range(1, 9):
        _tt(P, "dve", s1, akr(k - 1), akr(k - 1), ALU.mult)
        _tt(P, "dve", s2, aki(k - 1), aki(k - 1), ALU.mult)
        _tt(P, "dve", akr(k), s1, s2, ALU.subtract)
        _stt(P, aki(k), akr(k - 1), 2.0, aki(k - 1), ALU.mult, ALU.mult)

    P.pool("LE", 2, [128, 8, 128], BF16)
    lm_tiles = P.pool("LM", 2, [128, 8, 128], BF16)
    for t, k_ in lm_tiles:
        P.add("pool", lambda e, t=t: e.memset(t[:], 0.0), w=[k_])
    P.pool("XPA", 2, [128, 8, 128], F32)
    P.pool("ROT", 4, [128, 128], F32)
    P.pool("KS", 2, [128, 128], F32)
    P.pool("X", 2, [128, NCH], F32)
    P.pool("ys", 2, [128, NCH], F32)
    P.pool("gw", 1, [128, NCH], F32)
    P.pool("gy", 2, [128, NCH], F32)
    outkeys = []
    for g in range(NG if ngrun is None else ngrun):
        m, gl = g // 8, g % 8
        gm = msk[:, 2 + gl:3 + gl]
        le, lek = P.nxt("LE")
        for j in range(8):
            P.add("dve", lambda e, le=le, j=j, m=m, gm=gm: e.tensor_scalar(out=le[:, j, :], in0=EB[:, j, m, :, :].rearrange("p a b -> p (a b)"),
                                                                         scalar1=gm, scalar2=None, op0=ALU.mult),
                  r=[("EB", j), "msk"], w=[(lek, j)])
        kp, kpk = P.nxt("psk")
        xp, xpk = P.nxt("XPA")
        P.add("pool", lambda e, xp=xp: e.memset(xp[:, :, :], 0.0), w=[xpk])
        P.add("dve", lambda e, xp=xp, g=g, gl=gl: e.tensor_copy(out=xp[:, :, 16 * gl:16 * gl + 16], in_=Xs[:, :, g, :]),
              r=[("Xs", t_) for t_ in range(8)] + [xpk], w=[xpk])
        P.add("pe", lambda e, xp=xp, g=g, kp=kp: [e.matmul(kp[:, 16 * tau:16 * tau + 16], lhsT=xp[:, tau, :], rhs=Cc[0][:, g, :],
                                                          start=True, stop=True) for tau in range(8)][-1],
              r=[xpk, "Cc"], w=[kpk])
        ks, ksk = P.nxt("KS")
        P.add("dve", lambda e, ks=ks, kp=kp, m=m, gl=gl: e.scalar_tensor_tensor(out=ks[:, 0:16], in0=identg[:, gl, :], scalar=dsk[:, m:m + 1],
                                                                             in1=kp[:, 0:16], op0=ALU.mult, op1=ALU.add),
              r=["identg", "dsk", kpk], w=[(ksk, 0)])
        P.add("dve", lambda e, ks=ks, kp=kp: e.tensor_copy(out=ks[:, 16:128], in_=kp[:, 16:128]),
              r=[kpk], w=[(ksk, 1)])
        lm, lmk = P.nxt("LM")
        for j in range(8):
            P.add("pool", lambda e, lm=lm, ks=ks, j=j: e.tensor_copy(out=lm[:, j, 16 * j:128], in_=ks[:, 0:128 - 16 * j]),
                  r=[(ksk, 0), (ksk, 1)], w=[(lmk, j)])
        if stage < 2:
            continue
        ep, epk = P.nxt("ps")
        P.add("pe", lambda e, ep=ep, le=le, m=m: [e.matmul(ep[:, :], lhsT=le[:, j, :], rhs=u_t[:, m, j::8], start=(j == 0), stop=(j == 7))
                                                  for j in range(8)][-1],
              r=[(lek, j) for j in range(8)] + ukeys(m), w=[epk])
        X, Xk = P.nxt("X")
        P.add("act", lambda e, X=X, ep=ep: e.activation(out=X[:, :], in_=ep[:, :], func=AF.Copy), r=[epk], w=[Xk])
        for k in range(9 if stage >= 3 else 0):
            d = 1 << k
            rot, rotk = P.nxt("ROT")
            P.add("pool", lambda e, rot=rot, k=k, g=g: e.tensor_scalar(out=rot[:, :], in0=ident[:, :], scalar1=AKr[:, k, g:g + 1], scalar2=None, op0=ALU.mult),
                  r=["ident", ("AK", k)], w=[rotk])
            P.add("dve", lambda e, rot=rot, k=k, g=g: e.scalar_tensor_tensor(out=rot[:, :], in0=j2[:, :], scalar=AKi[:, k, g:g + 1], in1=rot[:, :],
                                                                          op0=ALU.mult, op1=ALU.add),
                  r=["j2", ("AK", k), rotk], w=[rotk])
            rp, rpk = P.nxt("psr")
            P.add("pe", lambda e, rp=rp, rot=rot, X=X, d=d: e.matmul(rp[:, :NCH - d], lhsT=rot[:, :], rhs=X[:, :NCH - d], start=True, stop=True),
                  r=[rotk, Xk], w=[rpk])
            P.add("dve", lambda e, rp=rp, X=X, d=d: e.tensor_tensor(out=X[:, d:], in0=rp[:, :NCH - d], in1=X[:, d:], op=ALU.add),
                  r=[rpk, Xk], w=[Xk])
        if stage < 4:
            continue
        yp, ypk = P.nxt("ps")
        P.add("pe", lambda e, yp=yp, lm=lm, m=m, X=X, g=g: [e.matmul(yp[:, :], lhsT=lm[:, j, :], rhs=u_t[:, m, j::8], start=(j == 0), stop=False)
                                                           for j in range(8)] and
              e.matmul(yp[:, 1:NCH], lhsT=CS[:, g, :, :].rearrange("p a b -> p (a b)"), rhs=X[:, 0:NCH - 1], start=False, stop=True),
              r=[(lmk, j) for j in range(8)] + ukeys(m) + [Xk] + [("CS", t_) for t_ in range(8)], w=[ypk])
        ys, ysk = P.nxt("ys"); gw, gwk = P.nxt("gw"); gy, gyk = P.nxt("gy")
        P.add("act", lambda e, ys=ys, yp=yp: e.activation(out=ys[:, :], in_=yp[:, :], func=AF.Copy), r=[ypk], w=[ysk])
        P.add("dve", lambda e, ys=ys, gw=gw: e.tensor_tensor(out=gw[:, :], in0=ys[:, :], in1=ys[:, :], op=ALU.mult), r=[ysk], w=[gwk])
        P.add("dve", lambda e, gw=gw: e.tensor_scalar(out=gw[:, :], in0=gw[:, :], scalar1=0.044715, scalar2=1.0, op0=ALU.mult, op1=ALU.add),
              r=[gwk], w=[gwk])
        P.add("dve", lambda e, ys=ys, gw=gw: e.tensor_tensor(out=gw[:, :], in0=gw[:, :], in1=ys[:, :], op=ALU.mult), r=[ysk, gwk], w=[gwk])
        P.add("act", lambda e, gw=gw: e.activation(out=gw[:, :], in_=gw[:, :], func=AF.Sigmoid, scale=1.5957691216057308), r=[gwk], w=[gwk])
        P.add("dve", lambda e, ys=ys, gw=gw, gy=gy: e.tensor_tensor(out=gy[:, :], in0=ys[:, :], in1=gw[:, :], op=ALU.mult), r=[ysk, gwk], w=[gyk])
        for t_ in range(8):
            r0 = m * 128 + gl * 16
            P.dma(gTp[r0:r0 + 16, t_, :], gy[16 * t_:16 * t_ + 16, :], r=[gyk], w=[("out", g, t_)])
            outkeys.append(("out", g, t_))
    P.add("sp", None, r=outkeys)
    return P.finalize() if own else None


def s5a_layer(xfull, prm):
    NG = 32
    ident = np.eye(128, dtype=np.float32)
    j2 = np.zeros((128, 128), np.float32)
    j2[np.arange(64), np.arange(64) + 64] = 1.0
    j2[np.arange(64) + 64, np.arange(64)] = -1.0
    msk = np.zeros((128, 10), np.float32)
    msk[:64, 0] = 1.0; msk[64:, 1] = 1.0
    for gl in range(8):
        msk[16 * gl:16 * gl + 16, 2 + gl] = 1.0
    identg = np.zeros((128, 8, 16), np.float32)
    for gl in range(8):
        for h in range(16):
            identg[16 * gl + h, gl, h] = 1.0
    in_maps = []
    for c in range(NCORES):
        b, gh = c // 2, c % 2
        gs = slice(NG * gh, NG * gh + NG)
        def cl(a):
            a = np.broadcast_to(a.reshape(NG // 8, 8, 1, 64), (NG // 8, 8, 16, 64))
            return np.ascontiguousarray(a.transpose(1, 2, 0, 3).reshape(128, NG // 8, 64))
        def clb(a):
            a = a.reshape(NG // 8, 8, 64, 16)
            return np.ascontiguousarray(a.transpose(1, 3, 0, 2).reshape(128, NG // 8, 64))
        def plv(a):
            a = np.broadcast_to(a.T.reshape(1, 64, NG, 1), (2, 64, NG, 16))
            return np.ascontiguousarray(a.reshape(128, NG, 16))
        def plb(a):
            a = np.broadcast_to(a.transpose(1, 0, 2)[None], (2, 64, NG, 16))
            return np.ascontiguousarray(a.reshape(128, NG, 16))
        def plc(a):
            a = np.broadcast_to(a.transpose(2, 0, 1)[None], (2, 64, NG, 16))
            return np.ascontiguousarray(a.reshape(128, NG, 16))
        ldt = np.broadcast_to(prm["log_dt"][gs][:, None], (NG, 64))
        in_maps.append(dict(
            xT=xfull[b], w_u=np.ascontiguousarray(prm["w_in"][:, 512 * gh:512 * gh + 512]),
            lamre_cl=cl(prm["lam_re"][gs]), lamim_cl=cl(prm["lam_im"][gs]), ldt_cl=cl(ldt),
            bre_cl=clb(prm["b_re"][gs]), bim_cl=clb(prm["b_im"][gs]),
            lamre_pl=plv(prm["lam_re"][gs]), lamim_pl=plv(prm["lam_im"][gs]), ldt_pl=plv(ldt),
            bre_pl=plb(prm["b_re"][gs]), bim_pl=plb(prm["b_im"][gs]),
            cre_pl=plc(prm["c_re"][gs]), cim_pl=plc(prm["c_im"][gs]),
            dsk=pp(prm["d"][512 * gh:512 * gh + 512]), ident=ident, j2=j2, msk=msk, identg=identg))
    res = run("s5a", build_s5a, in_maps)
    out = []
    for b in range(4):
        halves = [res[2 * b + gh]["gTp"].transpose(0, 2, 1).reshape(512, 4096) for gh in range(2)]
        out.append(np.ascontiguousarray(np.concatenate(halves, axis=0)))
    return out


RW_GN_EPS = 64e-5


def build_rwa(NT=4096, TT=256, ntiles=None, pipeline=True, P=None):
    C = 64
    NCK = TT // C
    MT = 4
    own = P is None
    P = P or Prog()
    xT = P.dram("xT", [D, NT + 1])
    mu_d = P.dram("mu", [128, 6, 8])
    wq_d = {q: P.dram("w_" + q, [D, 512]) for q in "rkvz"}
    w1_d = P.dram("w1", [D, 64]); a1_d = P.dram("a1", [D, 64])
    w2_d = P.dram("w2", [64, 512]); a2_d = P.dram("a2", [64, 512])
    prm_d = P.dram("prm", [128, 7, 4])
    ident_d = P.dram("ident", [128, 128])
    onesb_d = P.dram("onesb", [128, 128])
    msk_d = P.dram("msk", [64, 4, 512])
    rst_d = P.dram("rst", [128, TT])
    mT = P.dram("mT", [512, NT], out=True)

    P.pool("wst", 1, [128, 512], F32)
    P.pool("ps", 4, [128, 512], F32, psum=True)
    P.pool("ps2", 4, [128, 512], F32, psum=True)
    wq = {q: load_weight_bf16(P, "w_" + q, wq_d[q], D, 512, "wst") for q in "rkvz"}
    w1 = load_weight_bf16(P, "w1", w1_d, D, 64, "wst")
    a1 = load_weight_bf16(P, "a1", a1_d, D, 64, "wst")

    def ld(name, shape, src, dtype=F32):
        t = P.sb(name, shape, dtype)
        P.dma(t[:], src, w=[name])
        return t
    w2f = ld("w2f", [64, 512], w2_d); a2f = ld("a2f", [64, 512], a2_d)
    w2 = P.sb("w2", [64, 512], BF16); a2 = P.sb("a2", [64, 512], BF16)
    P.add("pool", lambda e: e.tensor_copy(out=w2[:, :], in_=w2f[:, :]), r=["w2f"], w=["w2"])
    P.add("pool", lambda e: e.tensor_copy(out=a2[:, :], in_=a2f[:, :]), r=["a2f"], w=["a2"])
    mu = ld("mu", [128, 6, 8], mu_d)
    prm = ld("prm", [128, 7, 4], prm_d)
    ident = ld("ident", [128, 128], ident_d)
    onesb = ld("onesb", [128, 128], onesb_d)
    msk = ld("msk", [64, 4, 512], msk_d)
    rst = ld("rst", [128, TT], rst_d)
    maskU, maskUi, maskL, I8 = msk[:, 0, :], msk[:, 1, :], msk[:, 2, :], msk[:, 3, :]
    W0, A0, KK, KA, RK, LG, LB = range(7)

    ST = P.sb("ST", [128, MT, 64])
    P.add("pool", lambda e: e.memset(ST[:], 0.0), w=["ST"])

    P.pool("xin", 1, [128, 8, TT + 1], F32)
    dx = P.sb("dx", [128, 8, TT]); tmpx = P.sb("tmpx", [128, 8, TT])
    xs = [P.sb(f"xs{i}", [128, 8, TT], BF16) for i in range(6)]
    F = {n: P.sb("F_" + n, [128, MT, TT]) for n in ("r", "k", "v", "sz", "sw", "a", "f6", "f7", "f8", "f9", "f10")}
    t1b = P.sb("t1b", [64, TT], BF16); a1b = P.sb("a1b", [64, TT], BF16)
    P.pool("tm", 2, [64, 3, 512], F32)
    cs = {n: P.sb("c_" + n, [64, 512]) for n in ("N", "NT", "Aak", "Arb", "Ark", "T", "Xa", "XTa", "Xb", "XTb", "WT", "UT", "o", "sq", "on")}
    cs_odd = {n: P.sb("co_" + n, [64, 512]) for n in ("T", "Aak", "Arb", "Ark")}
    st = {n: P.sb("st_" + n, [64, 8]) for n in ("s1", "s2", "mean", "var", "rstd")}
    P.pool("mo", 1, [128, MT, TT], F32)
    xTv = xT.rearrange("(k p) t -> p k t", p=128)
    mTv = mT.rearrange("(m p) t -> p m t", p=128)
    outkeys = []

    def A(eng, fn, r, w):
        P.add(eng, fn, r=r, w=w)

    def do_tile(j):
            xin, xk = P.nxt("xin")
            P.dma(xin[:, :, :], xTv[:, :, j * TT:(j + 1) * TT + 1], w=[xk])
            A("pool", lambda e, xin=xin: e.tensor_tensor(out=dx[:, :, :], in0=xin[:, :, 0:TT], in1=xin[:, :, 1:TT + 1], op=ALU.subtract), [xk], ["dx"])
            for i in range(6):
                for k in range(8):
                    A("dve", lambda e, i=i, k=k: e.scalar_tensor_tensor(out=xs[i][:, k, :], in0=dx[:, k, :], scalar=mu[:, i, k:k + 1], in1=xin[:, k, 1:TT + 1],
                                                                        op0=ALU.mult, op1=ALU.add),
                      ["dx", "mu", xk], [(f"xs{i}", k)])
            for qi, q in enumerate("rkvz"):
                for m in range(MT):
                    pz, pzk = P.nxt("ps")
                    A("pe", lambda e, pz=pz, q=q, qi=qi, m=m: [e.matmul(pz[:, :TT], lhsT=wq[q][:, k, m * 128:(m + 1) * 128], rhs=xs[qi][:, k, :],
                                                                       start=(k == 0), stop=(k == 7)) for k in range(8)][-1],
                      [(f"xs{qi}", k) for k in range(8)] + [("w_" + q, k) for k in range(8)], [pzk])
                    dst = F[{"r": "r", "k": "k", "v": "v", "z": "sz"}[q]]
                    A("act", lambda e, pz=pz, dst=dst, m=m, q=q: e.activation(out=dst[:, m, :], in_=pz[:, :TT], func=AF.Silu if q == "z" else AF.Copy),
                      [pzk], [("F", {"r": "r", "k": "k", "v": "v", "z": "sz"}[q], m)])
            for (wa, wb, xi, tb, tbk, dstn, bias_i, fn1) in ((w1, w2, 4, t1b, "t1b", "sw", W0, AF.Tanh), (a1, a2, 5, a1b, "a1b", "a", A0, AF.Copy)):
                pz, pzk = P.nxt("ps")
                A("pe", lambda e, pz=pz, wa=wa, xi=xi: [e.matmul(pz[:64, :TT], lhsT=wa[:, k, :], rhs=xs[xi][:, k, :], start=(k == 0), stop=(k == 7))
                                                       for k in range(8)][-1],
                  [(f"xs{xi}", k) for k in range(8)] + [("w1" if wa is w1 else "a1", k) for k in range(8)], [pzk])
                A("act", lambda e, pz=pz, tb=tb, fn1=fn1: e.activation(out=tb[:, :], in_=pz[:64, :TT], func=fn1), [pzk], [tbk])
                for m in range(MT):
                    p2, p2k = P.nxt("ps")
                    A("pe", lambda e, p2=p2, wb=wb, tb=tb, m=m: e.matmul(p2[:, :TT], lhsT=wb[:, m * 128:(m + 1) * 128], rhs=tb[:, :], start=True, stop=True),
                      [tbk, "w2" if wb is w2 else "a2"], [p2k])
                    A("act", lambda e, p2=p2, dstn=dstn, m=m, bias_i=bias_i: e.activation(out=F[dstn][:, m, :], in_=p2[:, :TT], func=AF.Sigmoid,
                                                                                     bias=prm[:, bias_i, m:m + 1]),
                      [p2k, "prm"], [("F", dstn, m)])
            for m in range(MT):
                fk = lambda n: ("F", n, m)
                sl = lambda n: F[n][:, m, :]
                A("dve", lambda e, m=m: e.tensor_scalar(out=F["f6"][:, m, :], in0=F["k"][:, m, :], scalar1=prm[:, KK, m:m + 1], scalar2=None, op0=ALU.mult),
                  [fk("k"), "prm"], [fk("f6")])
                A("dve", lambda e, m=m: e.tensor_tensor(out=F["f7"][:, m, :], in0=F["f6"][:, m, :], in1=F["f6"][:, m, :], op=ALU.mult), [fk("f6")], [fk("f7")])
                pz, pzk = P.nxt("ps")
                A("pe", lambda e, pz=pz, m=m: e.matmul(pz[:, :TT], lhsT=onesb[:, :], rhs=F["f7"][:, m, :], start=True, stop=True), [fk("f7"), "onesb"], [pzk])
                A("act", lambda e, pz=pz, m=m: e.activation(out=F["f7"][:, m, :], in_=pz[:, :TT], func=AF.Sqrt), [pzk], [fk("f7")])
                A("dve", lambda e, m=m: e.tensor_scalar(out=F["f7"][:, m, :], in0=F["f7"][:, m, :], scalar1=1e-12, scalar2=None, op0=ALU.max), [fk("f7")], [fk("f7")])
                A("dve", lambda e, m=m: e.reciprocal(out=F["f7"][:, m, :], in_=F["f7"][:, m, :]), [fk("f7")], [fk("f7")])
                A("dve", lambda e, m=m: e.tensor_tensor(out=F["f6"][:, m, :], in0=F["f6"][:, m, :], in1=F["f7"][:, m, :], op=ALU.mult), [fk("f6"), fk("f7")], [fk("f6")])
                A("dve", lambda e, m=m: e.tensor_scalar(out=F["f7"][:, m, :], in0=F["a"][:, m, :], scalar1=-1.0, scalar2=prm[:, KA, m:m + 1], op0=ALU.add, op1=ALU.mult),
                  [fk("a"), fk("f7"), "prm"], [fk("f7")])
                A("dve", lambda e, m=m: e.tensor_tensor(out=F["f7"][:, m, :], in0=F["f7"][:, m, :], in1=F["k"][:, m, :], op=ALU.mult), [fk("f7"), fk("k")], [fk("f7")])
                A("dve", lambda e, m=m: e.tensor_tensor(out=F["f8"][:, m, :], in0=F["f7"][:, m, :], in1=F["k"][:, m, :], op=ALU.add), [fk("f7"), fk("k")], [fk("f8")])
                A("dve", lambda e, m=m: e.tensor_tensor(out=F["f7"][:, m, :], in0=F["f6"][:, m, :], in1=F["a"][:, m, :], op=ALU.mult), [fk("f6"), fk("a"), fk("f7")], [fk("f7")])
                A("dve", lambda e, m=m: e.tensor_scalar(out=F["sw"][:, m, :], in0=F["sw"][:, m, :], scalar1=-0.6065306597126334, scalar2=None, op0=ALU.mult),
                  [fk("sw")], [fk("sw")])
                A("dve", lambda e, m=m: e.tensor_tensor_scan(out=F["k"][:, m, :], data0=rst[:, :], data1=F["sw"][:, m, :], initial=0.0, op0=ALU.mult, op1=ALU.add),
                  [fk("sw"), "rst", fk("k"), fk("f8"), fk("f7")], [fk("k")])
                A("act", lambda e, m=m: e.activation(out=F["a"][:, m, :], in_=F["k"][:, m, :], func=AF.Exp), [fk("k"), fk("a"), fk("f7")], [fk("a")])
                A("act", lambda e, m=m: e.activation(out=F["f9"][:, m, :], in_=F["k"][:, m, :], func=AF.Exp, scale=-1.0), [fk("k")], [fk("f9")])
                A("dve", lambda e, m=m: e.tensor_tensor(out=F["k"][:, m, :], in0=F["k"][:, m, :], in1=F["sw"][:, m, :], op=ALU.subtract), [fk("k"), fk("sw"), fk("a"), fk("f9")], [fk("k")])
                A("act", lambda e, m=m: e.activation(out=F["k"][:, m, :], in_=F["k"][:, m, :], func=AF.Exp), [fk("k")], [fk("k")])
                A("dve", lambda e, m=m: e.scalar_tensor_tensor(out=F["f10"][:, m, :], in0=F["r"][:, m, :], scalar=prm[:, RK, m:m + 1], in1=F["f8"][:, m, :],
                                                               op0=ALU.mult, op1=ALU.mult), [fk("r"), fk("f8"), "prm"], [fk("f10")])
                pz, pzk = P.nxt("ps")
                A("pe", lambda e, pz=pz, m=m: e.matmul(pz[:, :TT], lhsT=onesb[:, :], rhs=F["f10"][:, m, :], start=True, stop=True), [fk("f10"), "onesb"], [pzk])
                A("dve", lambda e, pz=pz, m=m: e.tensor_tensor(out=F["f10"][:, m, :], in0=pz[:, :TT], in1=F["v"][:, m, :], op=ALU.mult), [pzk, fk("v")], [fk("f10")])
                A("dve", lambda e, m=m: e.tensor_tensor(out=F["r"][:, m, :], in0=F["r"][:, m, :], in1=F["a"][:, m, :], op=ALU.mult), [fk("r"), fk("a"), fk("f10")], [fk("r")])
                A("dve", lambda e, m=m: e.scalar_tensor_tensor(out=F["k"][:, m, :], in0=F["f6"][:, m, :], scalar=-1.0, in1=F["k"][:, m, :], op0=ALU.mult, op1=ALU.mult),
                  [fk("f6"), fk("k")], [fk("k")])
                A("dve", lambda e, m=m: e.tensor_tensor(out=F["f8"][:, m, :], in0=F["f8"][:, m, :], in1=F["f9"][:, m, :], op=ALU.mult), [fk("f8"), fk("f9"), fk("f10")], [fk("f8")])
                A("dve", lambda e, m=m: e.tensor_tensor(out=F["f7"][:, m, :], in0=F["f7"][:, m, :], in1=F["f9"][:, m, :], op=ALU.mult), [fk("f7"), fk("f9")], [fk("f7")])
                pcb = F["a"][:, m, C - 1::C].unsqueeze(2).broadcast_to([128, NCK, C])
                A("dve", lambda e, m=m, pcb=pcb: e.tensor_tensor(out=F["f9"][:, m, :].rearrange("p (c t) -> p c t", t=C),
                                                                 in0=F["f8"][:, m, :].rearrange("p (c t) -> p c t", t=C), in1=pcb, op=ALU.mult),
                  [fk("f8"), fk("a"), fk("f9"), fk("f7")], [fk("f9")])
                A("dve", lambda e, m=m, pcb=pcb: e.tensor_tensor(out=F["sw"][:, m, :].rearrange("p (c t) -> p c t", t=C),
                                                                 in0=F["f7"][:, m, :].rearrange("p (c t) -> p c t", t=C), in1=pcb, op=ALU.mult),
                  [fk("f7"), fk("a"), fk("sw"), fk("k")], [fk("sw")])
            RT, AT, KT, BT, VV, KH, BH, PP, BON, SZ = "r", "k", "f8", "f7", "v", "f9", "sw", "a", "f10", "sz"
            allm = lambda n: [("F", n, m) for m in range(MT)]
            mo, mok = P.nxt("mo")
            def do_chunk(c):
                    csl = slice(c * C, (c + 1) * C)
                    DBL = ("T", "Aak", "Arb", "Ark")
                    csx = dict(cs)
                    if c % 2 == 1:
                        csx.update({n: cs_odd[n] for n in DBL})
                    kname = lambda n: "c_" + n + ("_o" if (c % 2 == 1 and n in DBL) else "")
                    pspool = ["ps"]
                    nps = lambda: P.nxt(pspool[0])
                    tm, tmk = P.nxt("tm")
                    for qi, n in enumerate((VV, KH, BH)):
                        pz, pzk = nps()
                        A("pe", lambda e, pz=pz, n=n, csl=csl: [e.transpose(out=pz[:64, m * 128:(m + 1) * 128], in_=F[n][:, m, csl], identity=ident[:, :])
                                                               for m in range(MT)][-1], allm(n) + ["ident"], [pzk])
                        A("act", lambda e, pz=pz, tm=tm, qi=qi: e.activation(out=tm[:, qi, :], in_=pz[:64, :], func=AF.Copy), [pzk], [(tmk, qi)])
                    Vtm, Khtm, Bhtm = tm[:, 0, :], tm[:, 1, :], tm[:, 2, :]

                    def heads():
                        for h in range(8):
                            m, hl = h // 2, h % 2
                            yield h, m, hl, 64 * hl, slice(h * 64, (h + 1) * 64), slice((hl * 4 + m) * 64, (hl * 4 + m + 1) * 64)

                    def fm(n, m, bp):
                        return F[n][bp:bp + 64, m, csl]

                    for (name, ln, rn_, mask) in (("N", BT, AT, maskU), ("NT", AT, BT, maskL), ("Aak", KT, AT, maskU), ("Arb", BT, RT, maskUi), ("Ark", KT, RT, maskUi)):
                        pa, pak = nps()
                        pb, pbk = nps()

                        def f(e, pa=pa, pb=pb, ln=ln, rn_=rn_):
                            last = None
                            for h, m, hl, bp, hsn, hsb in heads():
                                dst = pa if hl == 0 else pb
                                last = e.matmul(dst[:64, m * 64:(m + 1) * 64], lhsT=fm(ln, m, bp), rhs=fm(rn_, m, bp), start=True, stop=True)
                            return last
                        A("pe", f, allm(ln) + allm(rn_), [pak, pbk])
                        A("dve", lambda e, pa=pa, name=name, mask=mask: e.tensor_tensor(out=csx[name][:, 0:256], in0=pa[:64, 0:256], in1=mask[:, 0:256], op=ALU.mult),
                          [pak, "msk"], [(kname(name), 0)])
                        A("dve", lambda e, pb=pb, name=name, mask=mask: e.tensor_tensor(out=csx[name][:, 256:512], in0=pb[:64, 0:256], in1=mask[:, 0:256], op=ALU.mult),
                          [pbk, "msk"], [(kname(name), 1)])
                    ck = lambda n: [(kname(n), 0), (kname(n), 1)]
                    A("dve", lambda e: e.tensor_tensor(out=csx["T"][:, :], in0=csx["N"][:, :], in1=I8, op=ALU.add), ck("N") + ["msk"], ck("T"))

                    def allheads(mmf, r, w):
                        def f(e):
                            last = None
                            for hh in heads():
                                last = mmf(e, *hh)
                            return last
                        A("pe", f, r, w)
                    X, XT = "N", "NT"
                    for lvl in range(5):
                        X2, X2T = ("Xa", "XTa") if lvl % 2 == 0 else ("Xb", "XTb")
                        if lvl < 4:
                            pz, pzk = nps()
                            allheads(lambda e, h, m, hl, bp, hsn, hsb, pz=pz, X=X, XT=XT: e.matmul(pz[:64, hsb], lhsT=csx[XT][:, hsb], rhs=csx[X][:, hsb], start=True, stop=True),
                                     ck(X) + ck(XT), [pzk])
                            A("act", lambda e, pz=pz, X2=X2: e.activation(out=csx[X2][:, :], in_=pz[:64, :], func=AF.Copy), [pzk], ck(X2))
                        pz, pzk = nps()
                        allheads(lambda e, h, m, hl, bp, hsn, hsb, pz=pz, X=X, XT=XT: e.matmul(pz[:64, hsb], lhsT=csx[X][:, hsb], rhs=csx[XT][:, hsb], start=True, stop=True),
                                 ck(X) + ck(XT), [pzk])
                        A("act", lambda e, pz=pz, X2T=X2T: e.activation(out=csx[X2T][:, :], in_=pz[:64, :], func=AF.Copy), [pzk], ck(X2T))
                        pz, pzk = nps()
                        allheads(lambda e, h, m, hl, bp, hsn, hsb, pz=pz, X2T=X2T: e.matmul(pz[:64, hsb], lhsT=csx[X2T][:, hsb], rhs=csx["T"][:, hsb], start=True, stop=True),
                                 ck(X2T) + ck("T"), [pzk])
                        A("dve", lambda e, pz=pz: e.tensor_tensor(out=csx["T"][:, :], in0=pz[:64, :], in1=csx["T"][:, :], op=ALU.add), [pzk] + ck("T"), ck("T"))
                        X, XT = X2, X2T

                    def state_mm(lhs_n, extra, out_name, outv_lo, outv_hi, rkeys):
                        pa, pak = nps()
                        pb, pbk = nps()

                        def f(e):
                            last = None
                            for h, m, hl, bp, hsn, hsb in heads():
                                if hl == 0:
                                    e.matmul(pa[:64, hsb], lhsT=fm(lhs_n, m, 0), rhs=ST[0:64, m, :], start=True, stop=False)
                                    last = extra(e, pa, hsn, hsb, False)
                                else:
                                    last = extra(e, pa, hsn, hsb, True)
                            for h, m, hl, bp, hsn, hsb in heads():
                                if hl == 1:
                                    last = e.matmul(pb[:64, m * 64:(m + 1) * 64], lhsT=fm(lhs_n, m, 64), rhs=ST[64:128, m, :], start=True, stop=True)
                            return last
                        A("pe", f, allm(lhs_n) + ["ST"] + rkeys, [pak, pbk])
                        A("act", lambda e: e.activation(out=csx["sq"][:, 0:256], in_=pb[:64, 0:256], func=AF.Copy), [pbk], ["c_sq"])
                        A("act", lambda e: e.activation(out=outv_lo, in_=pa[:64, 0:256].rearrange("p (m v) -> p m v", v=64), func=AF.Copy), [pak], [(out_name, 0)])
                        A("dve", lambda e: e.tensor_tensor(out=outv_hi, in0=pa[:64, 256:512].rearrange("p (m v) -> p m v", v=64),
                                                           in1=csx["sq"][:, 0:256].rearrange("p (m v) -> p m v", v=64), op=ALU.add), [pak, "c_sq"], [(out_name, 1)])

                    splits.append(len(P.ops))
                    pspool[0] = "ps2"
                    wt3 = csx["WT"][:, :].rearrange("p (b v) -> p b v", v=64)
                    state_mm(AT, lambda e, pa, hsn, hsb, first: e.matmul(pa[:64, hsb], lhsT=csx["Aak"][:, hsb], rhs=Vtm[:, hsn], start=first, stop=True),
                             "c_WT", wt3[:, 0:4, :], wt3[:, 4:8, :], ck("Aak") + [(tmk, 0)])
                    pz, pzk = nps()
                    allheads(lambda e, h, m, hl, bp, hsn, hsb, pz=pz: e.matmul(pz[:64, hsb], lhsT=csx["T"][:, hsb], rhs=csx["WT"][:, hsb], start=True, stop=True),
                             ck("T") + ck("WT"), [pzk])
                    A("act", lambda e, pz=pz: e.activation(out=csx["UT"][:, :], in_=pz[:64, :], func=AF.Copy), [pzk], ck("UT"))
                    o4 = csx["o"][:, :].rearrange("p (m hl v) -> p m hl v", hl=2, v=64)
                    state_mm(RT, lambda e, pa, hsn, hsb, first: [e.matmul(pa[:64, hsb], lhsT=csx["Arb"][:, hsb], rhs=csx["UT"][:, hsb], start=first, stop=False),
                                                                e.matmul(pa[:64, hsb], lhsT=csx["Ark"][:, hsb], rhs=Vtm[:, hsn], start=False, stop=True)][-1],
                             "c_o", o4[:, :, 0, :], o4[:, :, 1, :], ck("Arb") + ck("UT") + ck("Ark") + [(tmk, 0)])
                    pz, pzk = nps()
                    allheads(lambda e, h, m, hl, bp, hsn, hsb, pz=pz: [e.matmul(pz[bp:bp + 64, m * 64:(m + 1) * 64], lhsT=Bhtm[:, hsn], rhs=csx["UT"][:, hsb], start=True, stop=False),
                                                                       e.matmul(pz[bp:bp + 64, m * 64:(m + 1) * 64], lhsT=Khtm[:, hsn], rhs=Vtm[:, hsn], start=False, stop=True)][-1],
                             ck("UT") + [(tmk, 0), (tmk, 1), (tmk, 2)], [pzk])
                    for m in range(MT):
                        A("dve", lambda e, pz=pz, m=m, c=c: e.scalar_tensor_tensor(out=ST[:, m, :], in0=ST[:, m, :], scalar=F[PP][:, m, c * C + C - 1:c * C + C],
                                                                                 in1=pz[:, m * 64:(m + 1) * 64], op0=ALU.mult, op1=ALU.add),
                          ["ST", pzk, ("F", PP, m)], ["ST"])
                    o3 = csx["o"][:, :].rearrange("p (h v) -> p h v", v=64)
                    cok = [("c_o", 0), ("c_o", 1)]
                    A("dve", lambda e, o3=o3: e.tensor_reduce(out=st["s1"][:, :], in_=o3, axis=AX.X, op=ALU.add), cok, ["st_s1"])
                    A("dve", lambda e: e.tensor_tensor(out=csx["sq"][:, :], in0=csx["o"][:, :], in1=csx["o"][:, :], op=ALU.mult), cok, ["c_sq"])
                    A("dve", lambda e: e.tensor_reduce(out=st["s2"][:, :], in_=csx["sq"][:, :].rearrange("p (h v) -> p h v", v=64), axis=AX.X, op=ALU.add), ["c_sq"], ["st_s2"])
                    A("dve", lambda e: e.tensor_scalar(out=st["mean"][:, :], in0=st["s1"][:, :], scalar1=1.0 / 64, scalar2=None, op0=ALU.mult), ["st_s1"], ["st_mean"])
                    A("dve", lambda e: e.tensor_tensor(out=st["var"][:, :], in0=st["mean"][:, :], in1=st["mean"][:, :], op=ALU.mult), ["st_mean"], ["st_var"])
                    A("dve", lambda e: e.scalar_tensor_tensor(out=st["var"][:, :], in0=st["s2"][:, :], scalar=1.0 / 64, in1=st["var"][:, :], op0=ALU.mult, op1=ALU.subtract),
                      ["st_s2", "st_var"], ["st_var"])
                    A("dve", lambda e: e.tensor_scalar(out=st["var"][:, :], in0=st["var"][:, :], scalar1=RW_GN_EPS, scalar2=None, op0=ALU.add), ["st_var"], ["st_var"])
                    A("act", lambda e: e.activation(out=st["rstd"][:, :], in_=st["var"][:, :], func=AF.Sqrt), ["st_var"], ["st_rstd"])
                    A("dve", lambda e: e.reciprocal(out=st["rstd"][:, :], in_=st["rstd"][:, :]), ["st_rstd"], ["st_rstd"])
                    on3 = csx["on"][:, :].rearrange("p (h v) -> p h v", v=64)
                    A("dve", lambda e, o3=o3, on3=on3: e.tensor_tensor(out=on3, in0=o3, in1=st["mean"][:, :].unsqueeze(2).broadcast_to([64, 8, 64]), op=ALU.subtract),
                      cok + ["st_mean"], ["c_on"])
                    A("dve", lambda e, on3=on3: e.tensor_tensor(out=on3, in0=on3, in1=st["rstd"][:, :].unsqueeze(2).broadcast_to([64, 8, 64]), op=ALU.mult),
                      ["c_on", "st_rstd"], ["c_on"])
                    pz, pzk = nps()
                    A("pe", lambda e, pz=pz: [e.transpose(out=pz[:, m * 64:(m + 1) * 64], in_=csx["on"][:, m * 128:(m + 1) * 128], identity=ident[0:64, 0:64])
                                              for m in range(MT)][-1], ["c_on", "ident"], [pzk])
                    for m in range(MT):
                        A("dve", lambda e, pz=pz, m=m, csl=csl, mo=mo: e.tensor_scalar(out=mo[:, m, csl], in0=pz[:, m * 64:(m + 1) * 64], scalar1=prm[:, LG, m:m + 1],
                                                                                      scalar2=prm[:, LB, m:m + 1], op0=ALU.mult, op1=ALU.add),
                          [pzk, "prm"], [(mok, m)])
                        A("pool", lambda e, m=m, csl=csl, mo=mo: e.tensor_tensor(out=mo[:, m, csl], in0=mo[:, m, csl], in1=F[BON][:, m, csl], op=ALU.add),
                          [(mok, m), ("F", BON, m)], [(mok, m)])
                        A("pool", lambda e, m=m, csl=csl, mo=mo: e.tensor_tensor(out=mo[:, m, csl], in0=mo[:, m, csl], in1=F[SZ][:, m, csl], op=ALU.mult),
                          [(mok, m), ("F", SZ, m)], [(mok, m)])
            start = len(P.ops)
            splits, bounds = [], []
            for c in range(NCK):
                b0 = len(P.ops)
                do_chunk(c)
                bounds.append((b0, splits[-1], len(P.ops)))
            if pipeline:
                rec = P.ops[start:]
                del P.ops[start:]
                ph1 = [rec[b0 - start:sp_ - start] for (b0, sp_, b1) in bounds]
                ph2 = [rec[sp_ - start:b1 - start] for (b0, sp_, b1) in bounds]
                P.ops.extend(ph1[0])
                for c in range(NCK):
                    a_, b_ = ph2[c], (ph1[c + 1] if c + 1 < NCK else [])
                    ia = ib = 0
                    while ia < len(a_) or ib < len(b_):
                        if ib >= len(b_) or (ia < len(a_) and ia * len(b_) <= ib * len(a_)):
                            P.ops.append(a_[ia]); ia += 1
                        else:
                            P.ops.append(b_[ib]); ib += 1
            P.dma(mTv[:, :, j * TT:(j + 1) * TT], mo[:, :, :], r=[(mok, m) for m in range(MT)], w=[("out", j)])
            outkeys.append(("out", j))

    for j in range(NT // TT if ntiles is None else ntiles):
        do_tile(j)
    P.add("sp", None, r=outkeys)
    return P.finalize() if own else None


def rwa_layer(xfull, prm):
    TT = 256
    ident = np.eye(128, dtype=np.float32)
    onesb = np.zeros((128, 128), np.float32)
    onesb[:64, :64] = 1.0; onesb[64:, 64:] = 1.0
    i_, t_ = np.meshgrid(np.arange(64), np.arange(64), indexing="ij")
    msk = np.stack([np.tile((i_ < t_), (1, 8)), np.tile((i_ <= t_), (1, 8)), np.tile((i_ > t_), (1, 8)),
                    np.tile(np.eye(64), (1, 8))], axis=1).astype(np.float32)
    rst = np.ones((128, TT), np.float32); rst[:, ::64] = 0.0
    mu = np.stack([pp(prm["mu"][i]) for i in range(6)], axis=1)
    in_maps = []
    for c in range(NCORES):
        b, hh = c // 2, c % 2
        cs_ = slice(512 * hh, 512 * hh + 512)
        pv = [prm["w0"], prm["a0"], prm["k_k"], prm["k_a"], prm["r_k"].reshape(-1), prm["lnx_g"], prm["lnx_b"]]
        prmv = np.stack([pp(v_[cs_]) for v_ in pv], axis=1)
        d = dict(xT=np.ascontiguousarray(np.concatenate([np.zeros((D, 1), np.float32), xfull[b]], axis=1)), mu=np.ascontiguousarray(mu),
                 w1=prm["w1"], a1=prm["a1"], w2=np.ascontiguousarray(prm["w2"][:, cs_]), a2=np.ascontiguousarray(prm["a2"][:, cs_]),
                 prm=np.ascontiguousarray(prmv), ident=ident, onesb=onesb, msk=np.ascontiguousarray(msk), rst=rst)
        for qi, q in enumerate("rkvz"):
            d["w_" + q] = np.ascontiguousarray(prm["w_rkvz"][qi][:, cs_])
        in_maps.append(d)
    res = run("rwa", build_rwa, in_maps)
    return [np.ascontiguousarray(np.concatenate([res[2 * b]["mT"], res[2 * b + 1]["mT"]], axis=0)) for b in range(4)]


def build_outb(NT=2048, TT=256, sel=False, P=None):
    own = P is None
    P = P or Prog()
    xT = P.dram("xT", [D, NT])
    mT_d = P.dram("mT", [D, NT * (2 if sel else 1)])
    pT = P.dram("pT", [256, NT])
    w_out_d = P.dram("w_out", [D, D])
    w_pg_d = P.dram("w_pg", [D, D])
    w_pe_d = P.dram("w_pe", [256, D])
    lng_d = P.dram("ln_g", [128, 8])
    lnb_d = P.dram("ln_b", [128, 8])
    oT = P.dram("oT", [D, NT], out=True)
    P.pool("wst", 2, [128, 1024], F32)
    P.pool("ps", 6, [128, 512], F32, psum=True)
    P.pool("psA", 2, [128, 512], F32, psum=True)
    w_out = load_weight_bf16(P, "w_out", w_out_d, D, D, "wst")
    w_pg, w_pe, ones, lng, lnb = tail_setup(P, TT, w_pg_d, w_pe_d, lng_d, lnb_d)
    if sel:
        P.pool("g2", 1, [128, 8, TT], F32)
        selt = P.sb("sel", [128, 2], F32)
        P.dma(selt[:, :], P.dram("sel", [128, 2]), w=["sel"])
    P.pool("x", 2, [128, 8, TT], F32)
    P.pool("g", 2, [128, 8, TT], F32)
    P.pool("p", 2, [128, 2, TT], F32)
    P.pool("pb", 2, [128, 2, TT], BF16)
    P.pool("m", 2, [128, 8, TT], BF16)
    P.pool("o", 2, [128, 8, TT], F32)
    xTv = xT.rearrange("(k p) t -> p k t", p=128)
    mTv = mT_d.rearrange("(k p) t -> p k t", p=128)
    pTv = pT.rearrange("(k p) t -> p k t", p=128)
    oTv = oT.rearrange("(k p) t -> p k t", p=128)
    outkeys = []

    def do_tile(j):
        sl = slice(j * TT, (j + 1) * TT)
        x, xk = P.nxt("x"); g, gk = P.nxt("g"); pt, ptk = P.nxt("p"); pb, pbk = P.nxt("pb"); m, mk = P.nxt("m")
        P.dma(x[:, :, :], xTv[:, :, sl], w=[xk])
        P.dma(g[:, :, :], mTv[:, :, sl], w=[gk])
        if sel:
            g2, g2k = P.nxt("g2")
            P.dma(g2[:, :, :], mTv[:, :, NT + j * TT:NT + (j + 1) * TT], w=[g2k])
            P.add("pool", lambda e: e.tensor_scalar(out=g[:, :, :], in0=g[:, :, :], scalar1=selt[:, 0:1], scalar2=None, op0=ALU.mult),
                  r=[gk, "sel"], w=[gk])
            P.add("dve", lambda e: e.scalar_tensor_tensor(out=g[:, :, :], in0=g2[:, :, :], scalar=selt[:, 1:2], in1=g[:, :, :],
                                                          op0=ALU.mult, op1=ALU.add), r=[gk, g2k, "sel"], w=[gk])
        P.dma(pt[:, :, :], pTv[:, :, sl], w=[ptk])
        P.add("pool", lambda e: e.tensor_copy(out=pb[:, :, :], in_=pt[:, :, :]), r=[ptk], w=[pbk])
        for k in range(8):
            P.add("act" if k % 2 else "dve", (lambda e, k=k: e.activation(out=m[:, k, :], in_=g[:, k, :], func=AF.Copy)) if k % 2 else
                  (lambda e, k=k: e.tensor_copy(out=m[:, k, :], in_=g[:, k, :])), r=[gk], w=[(mk, k)])

        def y_ps(d):
            yp, yk = P.nxt("ps")
            P.add("pe", lambda e: [e.matmul(yp[:, :TT], lhsT=w_out[:, k, d * 128:(d + 1) * 128], rhs=m[:, k, :],
                                            start=(k == 0), stop=(k == 7)) for k in range(8)][-1],
                  r=[(mk, k) for k in range(8)] + [("w_out", k) for k in range(8)], w=[yk])
            return yp[:, :TT], yk

        o, ok = P.nxt("o")
        tail(P, TT, x, xk, y_ps, pb, pbk, w_pg, w_pe, ones, lng, lnb, o, ok, "o")
        P.dma(oTv[:, :, sl], o[:, :, :], r=[(ok, d) for d in range(8)], w=[("out", j)])
        outkeys.append(("out", j))
    for j in range(NT // TT):
        do_tile(j)
    P.add("sp", None, r=outkeys)
    return P.finalize() if own else None


def outb_layer(xs, ms, p_l, w_out, w_pg, w_pe, ln_g, ln_b):
    in_maps = []
    for c in range(NCORES):
        b, h = c // 2, c % 2
        in_maps.append(dict(xT=xs[c], mT=ms[c], pT=np.ascontiguousarray(p_l[b, h * 2048:(h + 1) * 2048].T),
                            w_out=w_out, w_pg=w_pg, w_pe=w_pe, ln_g=pp(ln_g), ln_b=pp(ln_b)))
    res = run("outb", build_outb, in_maps)
    return [r["oT"] for r in res]


_NC_CACHE = {}


def run(name, builder, in_maps):
    if name not in _NC_CACHE:
        _NC_CACHE[name] = builder()
    res = run_bass_kernel_spmd(_NC_CACHE[name], in_maps, core_ids=list(range(NCORES)))
    return res.results


def conv_layer(xs, p_l, w_in, conv_k, w_out, w_pg, w_pe, ln_g, ln_b):
    ck = np.concatenate([pp(conv_k[j]) for j in range(3)], axis=1)
    in_maps = []
    for c in range(NCORES):
        b, h = c // 2, c % 2
        halo = np.zeros((D, 2), np.float32) if h == 0 else xs[c - 1][:, -2:]
        in_maps.append(dict(xT=np.ascontiguousarray(np.concatenate([halo, xs[c]], axis=1)),
                            pT=np.ascontiguousarray(p_l[b, h * 2048:(h + 1) * 2048].T),
                            w_in=w_in, conv_k=ck, w_out=w_out, w_pg=w_pg, w_pe=w_pe, ln_g=pp(ln_g), ln_b=pp(ln_b)))
    res = run("conv", build_conv, in_maps)
    return [r["oT"] for r in res]


NTF = 4096


def build_fused():
    NT = NTF
    P = Prog()
    x1s = P.scratch("x1s", [D, NT])
    gyp = P.scratch("gyp", [D, 8, NT // 8])
    x2s = P.scratch("x2s", [D, NT + 1])
    ms = P.scratch("ms", [D, NT])
    x3s = P.scratch("x3s", [D, NT + 2])
    with P.stage("Z_"):
        z = P.sb("z", [128, 2])
        P.add("pool", lambda e: e.memset(z[:, :], 0.0), w=["z"])
        for k in range(8):
            P.dma(x2s[k * 128:(k + 1) * 128, 0:1], z[:, 0:1], r=["z"], w=[("zz", k)], slow=True)
            P.dma(x3s[k * 128:(k + 1) * 128, 0:2], z[:, 0:2], r=["z"], w=[("zz", k)], slow=True)
    with P.stage("L0_", {"oT": x1s}):
        build_conv(NT=NT, P=P)
    for hf in range(2):
        with P.stage(f"L1a{hf}_", {"xT": x1s, "gTp": gyp[512 * hf:512 * hf + 512]}):
            build_s5a(P=P)
    with P.stage("L1b_", {"xT": x1s, "gT": gyp, "oT": x2s[:, 1:NT + 1]}):
        build_s5b(NT=NT, perm_g=True, P=P)
    for hf in range(2):
        with P.stage(f"L2a{hf}_", {"xT": x2s, "mT": ms[512 * hf:512 * hf + 512]}):
            build_rwa(NT=NT, P=P)
    with P.stage("L2b_", {"xT": x2s[:, 1:NT + 1], "mT": ms, "oT": x3s[:, 2:NT + 2]}):
        build_outb(NT=NT, P=P)
    with P.stage("L3_", {"xT": x3s}):
        build_conv(NT=NT, P=P)
    return P.finalize()


PAIRS = [[0, 1], [2, 3], [4, 5], [6, 7]]


def build_fused_pair():
    NT, NH = 4096, 2048
    P = Prog()
    x1h = P.scratch("x1h", [D, NH]); x1f = P.scratch("x1f", [D, NT])
    gym = P.scratch("gym", [512, 8, NT // 8]); gyg = P.scratch("gyg", [D, 8, NT // 8])
    x2h = P.scratch("x2h", [D, NH]); x2f = P.scratch("x2f", [D, NT + 1])
    msm = P.scratch("msm", [512, NT]); msg = P.scratch("msg", [D, NT])
    x3p = P.scratch("x3p", [D, NH + 2]); hlm = P.scratch("hlm", [D, 2]); hlg = P.scratch("hlg", [2 * D, 2])
    def gather(name, src, rows, place):
        R, C = src.shape
        for q in range(R // rows):
            gq = P.scratch(f"{name}_g{q}", [2 * rows, C])
            P.cc_allgather(src[q * rows:(q + 1) * rows, :], gq, PAIRS, w=[(name, q)])
            for rk in range(2):
                P.dma(place(q, rk), gq[rk * rows:(rk + 1) * rows, :], r=[(name, q)], w=[(name, q, rk)])

    with P.stage("Z_"):
        z = P.sb("z", [128, 2])
        P.add("pool", lambda e: e.memset(z[:, :], 0.0), w=["z"])
        for k in range(8):
            P.dma(x2f[k * 128:(k + 1) * 128, 0:1], z[:, 0:1], r=["z"], w=[("zz", k)], slow=True)
    with P.stage("L0_", {"oT": x1h}):
        build_conv(NT=NH, P=P)
    with P.stage("X1_"):
        gather("x1", x1h, 256, lambda q, rk: x1f[q * 256:(q + 1) * 256, rk * NH:(rk + 1) * NH])
    with P.stage("L1a_", {"xT": x1f, "gTp": gym}):
        build_s5a(P=P)
    with P.stage("X2_"):
        gyg2 = gyg.rearrange("c t n -> c (t n)")
        gather("gy", gym.rearrange("c t n -> c (t n)"), 128, lambda q, rk: gyg2[512 * rk + q * 128:512 * rk + (q + 1) * 128, :])
    with P.stage("L1b_", {"xT": x1h, "gT": gyg, "oT": x2h}):
        build_s5b(NT=NH, perm_g=True, sel=True, P=P)
    with P.stage("X3_"):
        gather("x2", x2h, 256, lambda q, rk: x2f[q * 256:(q + 1) * 256, 1 + rk * NH:1 + (rk + 1) * NH])
    with P.stage("L2a_", {"xT": x2f, "mT": msm}):
        build_rwa(NT=NT, P=P)
    with P.stage("X4_"):
        gather("ms", msm, 128, lambda q, rk: msg[512 * rk + q * 128:512 * rk + (q + 1) * 128, :])
    with P.stage("L2b_", {"xT": x2h, "mT": msg, "oT": x3p[:, 2:NH + 2]}):
        build_outb(NT=NH, sel=True, P=P)
    with P.stage("X5_"):
        P.dma(hlm[:, :], x3p[:, NH:NH + 2], w=["hlm"])
        P.cc_allgather(hlm, hlg, PAIRS, r=["hlm"], w=["hlg"])
        ht = P.sb("ht", [128, 8, 2]); selt = P.sb("sel", [128, 2])
        P.dma(selt[:, :], P.dram("sel", [128, 2]), w=["sel"])
        P.dma(ht[:, :, :], hlg[0:D, :].rearrange("(k p) t -> p k t", p=128), r=["hlg"], w=["ht"])
        P.add("dve", lambda e: e.tensor_scalar(out=ht[:, :, :], in0=ht[:, :, :], scalar1=selt[:, 1:2], scalar2=None, op0=ALU.mult),
              r=["ht", "sel"], w=["ht"])
        P.dma(x3p[:, 0:2].rearrange("(k p) t -> p k t", p=128), ht[:, :, :], r=["ht"], w=["x3halo"])
    with P.stage("L3_", {"xT": x3p}):
        build_conv(NT=NH, P=P)
    return P.finalize()


def kernel(**inp):
    inp = {k: np.asarray(v, np.float32) for k, v in inp.items()}
    x = inp["x"]
    sp = {k[4:]: inp[k][0] for k in inp if k.startswith("ssm_")}
    rp = {k[5:]: inp[k][0] for k in inp if k.startswith("rwkv_")}
    shared = {}

    def put(dst, prefix, d):
        for k, v in d.items():
            dst[prefix + k] = np.ascontiguousarray(v, dtype=np.float32)
    for (pre, i, j) in (("L0_", 0, 0), ("L3_", 3, 1)):
        put(shared, pre, dict(w_in=inp["conv_w_in"][j], conv_k=np.concatenate([pp(inp["conv_k"][j][t]) for t in range(3)], axis=1),
                              w_out=inp["conv_w_out"][j], **_tail_in(inp, i)))
    put(shared, "L1b_", dict(w_z=sp["w_in"][:, D:], w_glu=sp["w_glu"], b_glu=pp(sp["b_glu"]), w_out=sp["w_out"], **_tail_in(inp, 1)))
    put(shared, "L2b_", dict(w_out=rp["w_out"], **_tail_in(inp, 2)))
    halfp = [{}, {}]
    for hf in range(2):
        put(halfp[hf], "L1a_", _s5a_in(sp, hf))
        put(halfp[hf], "L2a_", _rwa_in(rp, hf))
    in_maps = []
    for c in range(NCORES):
        b, h = c // 2, c % 2
        ts = slice(2048 * h, 2048 * h + 2048)
        d = dict(shared)
        d.update(halfp[h])
        halo = np.zeros((D, 2), np.float32) if h == 0 else x[b, 2046:2048].T
        d["L0_xT"] = np.ascontiguousarray(np.concatenate([halo, x[b, ts].T], axis=1))
        for (pre, i) in (("L0_", 0), ("L1b_", 1), ("L2b_", 2), ("L3_", 3)):
            d[pre + "pT"] = np.ascontiguousarray(inp["p"][i][b, ts].T)
        selv = np.zeros((128, 2), np.float32); selv[:, h] = 1.0
        for pre in ("L1b_", "L2b_", "X5_"):
            d[pre + "sel"] = selv
        in_maps.append(d)
    res = run("fusedp", build_fused_pair, in_maps)
    out = np.zeros_like(x)
    for c in range(NCORES):
        out[c // 2, (c % 2) * 2048:(c % 2 + 1) * 2048] = res[c]["L3_oT"].T
    return out


def _tail_in(inp, i):
    return dict(w_pg=inp["ple_gate"][i], w_pe=inp["ple_proj"][i], ln_g=pp(inp["ln_g"][i]), ln_b=pp(inp["ln_b"][i]))


def _s5a_in(prm, gh):
    NG = 32
    ident = np.eye(128, dtype=np.float32)
    j2 = np.zeros((128, 128), np.float32)
    j2[np.arange(64), np.arange(64) + 64] = 1.0
    j2[np.arange(64) + 64, np.arange(64)] = -1.0
    msk = np.zeros((128, 10), np.float32)
    msk[:64, 0] = 1.0; msk[64:, 1] = 1.0
    identg = np.zeros((128, 8, 16), np.float32)
    for gl in range(8):
        msk[16 * gl:16 * gl + 16, 2 + gl] = 1.0
        for h in range(16):
            identg[16 * gl + h, gl, h] = 1.0
    gs = slice(NG * gh, NG * gh + NG)

    def cl(a):
        a = np.broadcast_to(a.reshape(NG // 8, 8, 1, 64), (NG // 8, 8, 16, 64))
        return np.ascontiguousarray(a.transpose(1, 2, 0, 3).reshape(128, NG // 8, 64))

    def clb(a):
        a = a.reshape(NG // 8, 8, 64, 16)
        return np.ascontiguousarray(a.transpose(1, 3, 0, 2).reshape(128, NG // 8, 64))

    def plv(a):
        a = np.broadcast_to(a.T.reshape(1, 64, NG, 1), (2, 64, NG, 16))
        return np.ascontiguousarray(a.reshape(128, NG, 16))

    def plb(a):
        a = np.broadcast_to(a.transpose(1, 0, 2)[None], (2, 64, NG, 16))
        return np.ascontiguousarray(a.reshape(128, NG, 16))

    def plc(a):
        a = np.broadcast_to(a.transpose(2, 0, 1)[None], (2, 64, NG, 16))
        return np.ascontiguousarray(a.reshape(128, NG, 16))
    ldt = np.broadcast_to(prm["log_dt"][gs][:, None], (NG, 64))
    return dict(
        w_u=np.ascontiguousarray(prm["w_in"][:, 512 * gh:512 * gh + 512]),
        lamre_cl=cl(prm["lam_re"][gs]), lamim_cl=cl(prm["lam_im"][gs]), ldt_cl=cl(ldt),
        bre_cl=clb(prm["b_re"][gs]), bim_cl=clb(prm["b_im"][gs]),
        lamre_pl=plv(prm["lam_re"][gs]), lamim_pl=plv(prm["lam_im"][gs]), ldt_pl=plv(ldt),
        bre_pl=plb(prm["b_re"][gs]), bim_pl=plb(prm["b_im"][gs]),
        cre_pl=plc(prm["c_re"][gs]), cim_pl=plc(prm["c_im"][gs]),
        dsk=pp(prm["d"][512 * gh:512 * gh + 512]), ident=ident, j2=j2, msk=msk, identg=identg)


def _rwa_in(prm, hh):
    TT = 256
    ident = np.eye(128, dtype=np.float32)
    onesb = np.zeros((128, 128), np.float32)
    onesb[:64, :64] = 1.0; onesb[64:, 64:] = 1.0
    i_, t_ = np.meshgrid(np.arange(64), np.arange(64), indexing="ij")
    msk = np.stack([np.tile((i_ < t_), (1, 8)), np.tile((i_ <= t_), (1, 8)), np.tile((i_ > t_), (1, 8)),
                    np.tile(np.eye(64), (1, 8))], axis=1).astype(np.float32)
    rst = np.ones((128, TT), np.float32); rst[:, ::64] = 0.0
    mu = np.stack([pp(prm["mu"][i]) for i in range(6)], axis=1)
    cs_ = slice(512 * hh, 512 * hh + 512)
    pv = [prm["w0"], prm["a0"], prm["k_k"], prm["k_a"], prm["r_k"].reshape(-1), prm["lnx_g"], prm["lnx_b"]]
    prmv = np.stack([pp(v_[cs_]) for v_ in pv], axis=1)
    d = dict(mu=np.ascontiguousarray(mu), w1=prm["w1"], a1=prm["a1"], w2=np.ascontiguousarray(prm["w2"][:, cs_]),
             a2=np.ascontiguousarray(prm["a2"][:, cs_]), prm=np.ascontiguousarray(prmv), ident=ident, onesb=onesb,
             msk=np.ascontiguousarray(msk), rst=rst)
    for qi, q in enumerate("rkvz"):
        d["w_" + q] = np.ascontiguousarray(prm["w_rkvz"][qi][:, cs_])
    return d


def kernel_fused_b(**inp):
    inp = {k: np.asarray(v, np.float32) for k, v in inp.items()}
    x = inp["x"]
    sp = {k[4:]: inp[k][0] for k in inp if k.startswith("ssm_")}
    rp = {k[5:]: inp[k][0] for k in inp if k.startswith("rwkv_")}
    shared = {}

    def put(prefix, d):
        for k, v in d.items():
            shared[prefix + k] = np.ascontiguousarray(v, dtype=np.float32)
    for (pre, i, j) in (("L0_", 0, 0), ("L3_", 3, 1)):
        put(pre, dict(w_in=inp["conv_w_in"][j], conv_k=np.concatenate([pp(inp["conv_k"][j][t]) for t in range(3)], axis=1),
                      w_out=inp["conv_w_out"][j], **_tail_in(inp, i)))
    for hf in range(2):
        put(f"L1a{hf}_", _s5a_in(sp, hf))
        put(f"L2a{hf}_", _rwa_in(rp, hf))
    put("L1b_", dict(w_z=sp["w_in"][:, D:], w_glu=sp["w_glu"], b_glu=pp(sp["b_glu"]), w_out=sp["w_out"], **_tail_in(inp, 1)))
    put("L2b_", dict(w_out=rp["w_out"], **_tail_in(inp, 2)))
    in_maps = []
    for c in range(NCORES):
        b = c % 4
        d = dict(shared)
        d["L0_xT"] = np.ascontiguousarray(np.concatenate([np.zeros((D, 2), np.float32), x[b].T], axis=1))
        for (pre, i) in (("L0_", 0), ("L1b_", 1), ("L2b_", 2), ("L3_", 3)):
            d[pre + "pT"] = np.ascontiguousarray(inp["p"][i][b].T)
        in_maps.append(d)
    res = run("fused", build_fused, in_maps)
    return np.ascontiguousarray(np.stack([res[b]["L3_oT"].T for b in range(4)]))


def _halves(full):
    return [np.ascontiguousarray(full[c // 2][:, (c % 2) * 2048:(c % 2 + 1) * 2048]) for c in range(NCORES)]


def _fulls(halves):
    return [np.ascontiguousarray(np.concatenate([halves[2 * b], halves[2 * b + 1]], axis=1)) for b in range(4)]


def kernel_unfused(**inp):
    inp = {k: np.asarray(v, np.float32) for k, v in inp.items()}
    x = inp["x"]
    xs = [np.ascontiguousarray(x[c // 2, (c % 2) * 2048:(c % 2 + 1) * 2048].T) for c in range(NCORES)]
    for i in range(DEPTH):
        kind, j = i % 3, i // 3
        tailw = (inp["ple_gate"][i], inp["ple_proj"][i], inp["ln_g"][i], inp["ln_b"][i])
        if kind == 0:
            xs = conv_layer(xs, inp["p"][i], inp["conv_w_in"][j], inp["conv_k"][j], inp["conv_w_out"][j], *tailw)
        elif kind == 1:
            prm = {k[4:]: inp[k][j] for k in inp if k.startswith("ssm_")}
            gy = s5a_layer(_fulls(xs), prm)
            xs = s5b_layer(xs, _halves(gy), inp["p"][i], prm["w_in"], prm["w_glu"], prm["b_glu"], prm["w_out"], *tailw)
        else:
            prm = {k[5:]: inp[k][j] for k in inp if k.startswith("rwkv_")}
            ms = rwa_layer(_fulls(xs), prm)
            xs = outb_layer(xs, _halves(ms), inp["p"][i], prm["w_out"], *tailw)
    out = np.zeros_like(x)
    for c in range(NCORES):
        out[c // 2, (c % 2) * 2048:(c % 2 + 1) * 2048] = xs[c].T
    return out
```
